# Optimizing a Trainium2 kernel written in Bass

```python
import math
import jax, jax.numpy as jnp
from jax import lax
import numpy as np

D_MODEL = 2048
BATCH = 4
SEQ = 4096
DEPTH = 2

HEAD_DIM = 128
SB_HEADS = 8
DIFF_HEADS = 4
DIFF_V_DIM = 2 * HEAD_DIM
SB_WIDTH = SB_HEADS * HEAD_DIM
DIFF_QK_WIDTH = DIFF_HEADS * 2 * HEAD_DIM
DIFF_V_WIDTH = DIFF_HEADS * DIFF_V_DIM
N_BRANCHES = 2
IN_WIDTH = 3 * SB_WIDTH + 2 * DIFF_QK_WIDTH + DIFF_V_WIDTH + N_BRANCHES * D_MODEL
D_FF = 5632
PLE_DIM = 256
ROPE_THETA = 500000.0
ROT_DIM = HEAD_DIM // 4
Q_BLOCK = 128
EPS = 1e-6
FFN_RES_WEIGHT = 0.5

_OFF_SB_K = SB_WIDTH
_OFF_SB_V = 2 * SB_WIDTH
_OFF_DF_Q = 3 * SB_WIDTH
_OFF_DF_K = _OFF_DF_Q + DIFF_QK_WIDTH
_OFF_DF_V = _OFF_DF_K + DIFF_QK_WIDTH
_OFF_G_A = _OFF_DF_V + DIFF_V_WIDTH
_OFF_G_B = _OFF_G_A + D_MODEL

kernel_name = "hybrid_stickbreak_diffattn_macaron_ple"


def rmsnorm(x, g):
    xf = x.astype(jnp.float32)
    ms = jnp.mean(xf * xf, axis=-1, keepdims=True)
    return (xf * lax.rsqrt(ms + EPS) * g.astype(jnp.float32)).astype(x.dtype)


def swiglu_ffn(x, w_gu, w_down):
    gate, up = jnp.split(x @ w_gu, 2, axis=-1)
    return (jax.nn.silu(gate) * up) @ w_down


def rope_tables(seq):
    pos = jnp.arange(seq, dtype=jnp.float32)
    inv_freq = ROPE_THETA ** (-jnp.arange(0, ROT_DIM, 2, dtype=jnp.float32) / ROT_DIM)
    ang = pos[:, None] * inv_freq[None, :]
    return jnp.cos(ang), jnp.sin(ang)


def partial_rope(x, cos, sin):
    x_rot, x_pass = x[..., :ROT_DIM], x[..., ROT_DIM:]
    x1, x2 = jnp.split(x_rot, 2, axis=-1)
    c = cos[None, :, None, :].astype(x.dtype)
    s = sin[None, :, None, :].astype(x.dtype)
    return jnp.concatenate([x1 * c - x2 * s, x2 * c + x1 * s, x_pass], axis=-1)


def stick_breaking_attention(q, k, v):
    seq = q.shape[2]
    scale = HEAD_DIM ** -0.5
    outs = []
    for qb in range(seq // Q_BLOCK):
        q0 = qb * Q_BLOCK
        klen = q0 + Q_BLOCK
        z = jnp.einsum('bhqd,bhkd->bhqk', q[:, :, q0:klen], k[:, :, :klen]).astype(jnp.float32) * scale
        t_idx = q0 + jnp.arange(Q_BLOCK)[:, None]
        s_idx = jnp.arange(klen)[None, :]
        mask = s_idx < t_idx
        log_beta = jax.nn.log_sigmoid(z)
        log_keep = jnp.where(mask, jax.nn.log_sigmoid(-z), 0.0)
        later = lax.cumsum(log_keep, axis=3, reverse=True) - log_keep
        attn = jnp.where(mask, jnp.exp(log_beta + later), 0.0)
        outs.append(jnp.einsum('bhqk,bhkd->bhqd', attn.astype(v.dtype), v[:, :, :klen]))
    return jnp.concatenate(outs, axis=2)


def differential_attention(q1, q2, k1, k2, v, lam):
    seq = q1.shape[2]
    scale = HEAD_DIM ** -0.5
    outs = []
    for qb in range(seq // Q_BLOCK):
        q0 = qb * Q_BLOCK
        klen = q0 + Q_BLOCK
        mask = jnp.arange(klen)[None, :] <= (q0 + jnp.arange(Q_BLOCK))[:, None]
        s1 = jnp.einsum('bhqd,bhkd->bhqk', q1[:, :, q0:klen], k1[:, :, :klen]).astype(jnp.float32) * scale
        s2 = jnp.einsum('bhqd,bhkd->bhqk', q2[:, :, q0:klen], k2[:, :, :klen]).astype(jnp.float32) * scale
        a1 = jax.nn.softmax(jnp.where(mask, s1, -jnp.inf), axis=-1)
        a2 = jax.nn.softmax(jnp.where(mask, s2, -jnp.inf), axis=-1)
        attn = a1 - lam * a2
        outs.append(jnp.einsum('bhqk,bhkd->bhqd', attn.astype(v.dtype), v[:, :, :klen]))
    return jnp.concatenate(outs, axis=2)


def setup_inputs(seed: int = 0) -> dict:
    key = jax.random.key(seed)
    ks = iter(jax.random.split(key, 32))

    def w(shape, fan_in):
        return jax.random.normal(next(ks), shape, jnp.float32) * (fan_in ** -0.5)

    def gain(shape):
        return 1.0 + 0.02 * jax.random.normal(next(ks), shape, jnp.float32)

    def small(shape, s):
        return s * jax.random.normal(next(ks), shape, jnp.float32)

    return {
        "x": jax.random.normal(next(ks), (BATCH, SEQ, D_MODEL), jnp.float32),
        "p": jax.random.normal(next(ks), (DEPTH, BATCH, SEQ, PLE_DIM), jnp.float32),
        "ffn1_norm": gain((DEPTH, D_MODEL)),
        "ffn1_w_gu": w((DEPTH, D_MODEL, 2 * D_FF), D_MODEL),
        "ffn1_w_down": w((DEPTH, D_FF, D_MODEL), D_FF),
        "mix_norm": gain((DEPTH, D_MODEL)),
        "w_in": w((DEPTH, D_MODEL, IN_WIDTH), D_MODEL),
        "diff_q_norm": gain((DEPTH, HEAD_DIM)),
        "diff_k_norm": gain((DEPTH, HEAD_DIM)),
        "diff_lambda_q1": small((DEPTH, HEAD_DIM), 0.1),
        "diff_lambda_k1": small((DEPTH, HEAD_DIM), 0.1),
        "diff_lambda_q2": small((DEPTH, HEAD_DIM), 0.1),
        "diff_lambda_k2": small((DEPTH, HEAD_DIM), 0.1),
        "diff_sub_norm": gain((DEPTH, DIFF_V_DIM)),
        "w_branch_a": w((DEPTH, SB_WIDTH, D_MODEL), SB_WIDTH),
        "w_branch_b": w((DEPTH, DIFF_V_WIDTH, D_MODEL), DIFF_V_WIDTH),
        "w_out": w((DEPTH, D_MODEL, D_MODEL), D_MODEL),
        "ffn2_norm": gain((DEPTH, D_MODEL)),
        "ffn2_w_gu": w((DEPTH, D_MODEL, 2 * D_FF), D_MODEL),
        "ffn2_w_down": w((DEPTH, D_FF, D_MODEL), D_FF),
        "ple_norm": gain((DEPTH, D_MODEL)),
        "ple_w_gate": w((DEPTH, D_MODEL, D_MODEL), D_MODEL),
        "ple_w_proj": w((DEPTH, PLE_DIM, D_MODEL), PLE_DIM),
        "ple_out_norm": gain((DEPTH, D_MODEL)),
    }


def reference(x, p, ffn1_norm, ffn1_w_gu, ffn1_w_down, mix_norm, w_in,
              diff_q_norm, diff_k_norm, diff_lambda_q1, diff_lambda_k1,
              diff_lambda_q2, diff_lambda_k2, diff_sub_norm, w_branch_a,
              w_branch_b, w_out, ffn2_norm, ffn2_w_gu, ffn2_w_down, ple_norm,
              ple_w_gate, ple_w_proj, ple_out_norm):
    b, s, _ = x.shape
    cos, sin = rope_tables(s)
    h = x
    for i in range(DEPTH):
        lambda_init = 0.8 - 0.6 * math.exp(-0.3 * i)

        h = h + FFN_RES_WEIGHT * swiglu_ffn(rmsnorm(h, ffn1_norm[i]), ffn1_w_gu[i], ffn1_w_down[i])

        u = rmsnorm(h, mix_norm[i])
        proj = u @ w_in[i]
        sb_q = proj[..., :_OFF_SB_K].reshape(b, s, SB_HEADS, HEAD_DIM)
        sb_k = proj[..., _OFF_SB_K:_OFF_SB_V].reshape(b, s, SB_HEADS, HEAD_DIM)
        sb_v = proj[..., _OFF_SB_V:_OFF_DF_Q].reshape(b, s, SB_HEADS, HEAD_DIM)
        df_q = proj[..., _OFF_DF_Q:_OFF_DF_K].reshape(b, s, DIFF_HEADS, 2, HEAD_DIM)
        df_k = proj[..., _OFF_DF_K:_OFF_DF_V].reshape(b, s, DIFF_HEADS, 2, HEAD_DIM)
        df_v = proj[..., _OFF_DF_V:_OFF_G_A].reshape(b, s, DIFF_HEADS, DIFF_V_DIM)
        gate_a = jax.nn.sigmoid(proj[..., _OFF_G_A:_OFF_G_B])
        gate_b = jax.nn.sigmoid(proj[..., _OFF_G_B:])

        y_a = stick_breaking_attention(sb_q.transpose(0, 2, 1, 3), sb_k.transpose(0, 2, 1, 3),
                                       sb_v.transpose(0, 2, 1, 3))
        y_a = y_a.transpose(0, 2, 1, 3).reshape(b, s, SB_WIDTH)

        def prep(t, g):
            return partial_rope(rmsnorm(t, g), cos, sin).transpose(0, 2, 1, 3)
        q1 = prep(df_q[..., 0, :], diff_q_norm[i])
        q2 = prep(df_q[..., 1, :], diff_q_norm[i])
        k1 = prep(df_k[..., 0, :], diff_k_norm[i])
        k2 = prep(df_k[..., 1, :], diff_k_norm[i])
        lam = (jnp.exp(jnp.sum(diff_lambda_q1[i].astype(jnp.float32) * diff_lambda_k1[i].astype(jnp.float32)))
               - jnp.exp(jnp.sum(diff_lambda_q2[i].astype(jnp.float32) * diff_lambda_k2[i].astype(jnp.float32)))
               + lambda_init)
        y_b = differential_attention(q1, q2, k1, k2, df_v.transpose(0, 2, 1, 3), lam)
        y_b = rmsnorm(y_b.transpose(0, 2, 1, 3), diff_sub_norm[i]) * (1.0 - lambda_init)
        y_b = y_b.reshape(b, s, DIFF_V_WIDTH)

        merged = gate_a * (y_a @ w_branch_a[i]) + gate_b * (y_b @ w_branch_b[i])
        h = h + merged @ w_out[i]

        h = h + FFN_RES_WEIGHT * swiglu_ffn(rmsnorm(h, ffn2_norm[i]), ffn2_w_gu[i], ffn2_w_down[i])

        ple_gate = jax.nn.sigmoid(rmsnorm(h, ple_norm[i]) @ ple_w_gate[i])
        ple = rmsnorm(p[i] @ ple_w_proj[i], ple_out_norm[i])
        h = h + ple_gate * ple
    return h
```

```python
import math
import types
import numpy as np
import ml_dtypes
import concourse.bass as bass
import concourse.mybir as mybir
from concourse.bass_utils import run_bass_kernel_spmd

F32 = mybir.dt.float32
BF16 = mybir.dt.bfloat16
AF = mybir.ActivationFunctionType
ALU = mybir.AluOpType

D = 2048
DC = 16
DFF = 5632
FC = 44
G = 512
DEPTH = 2
INW = 10240
EPS = 1e-6
CAP = 30000
NDS = 16
SCALE = 128 ** -0.5
import os as _os
SKEW_SB = tuple(int(x) for x in _os.environ.get('SKEW_SB', '1,1').split(','))
SKEW_DF = int(_os.environ.get('SKEW_DF', '1'))

GC_FFN1, GC_MIX, GC_FFN2, GC_PLE, GC_PLEO = 0, 16, 32, 48, 64
GC_DQ, GC_DK, GC_L, GC_SUB, GC_LI, GC_OML = 80, 81, 82, 86, 88, 89
NGC = 90
CC_ONES, CC_NTRI, CC_ROT, CC_MSB, CC_MDF = 0, 128, 256, 384, 384 + 2048
NCC = 384 + 4096


def _freeze(fn):
    if fn is None or fn.__closure__ is None:
        return fn
    cells = []
    for c in fn.__closure__:
        try:
            cells.append(types.CellType(c.cell_contents))
        except ValueError:
            cells.append(c)
    return types.FunctionType(fn.__code__, fn.__globals__, fn.__name__, fn.__defaults__, tuple(cells))


class Buf:
    __slots__ = ("w", "r", "name")

    def __init__(self, name=""):
        self.w = None
        self.r = {}
        self.name = name


class Prog:
    ENGS = ["pe", "act", "dve", "pool", "sp"]

    def __init__(self, nc):
        self.nc = nc
        self.streams = {e: [] for e in self.ENGS}
        self.cnt = {e: 0 for e in self.ENGS}
        self.seen = {e: {} for e in self.ENGS}
        self.ndma = {}
        self.dma_tokens_live = {}
        self.extra_sems = []

    def _need_wait(self, eng, tok):
        if tok[0] == "e":
            _, te, k = tok
            if te == eng and eng == "pe":
                return None
            if self.seen[eng].get(te, 0) >= k:
                return None
            self.seen[eng][te] = k
            return ("e", te, k)
        elif tok[0] == "d":
            _, q, i = tok
            slot = i % NDS
            val = 16 * (i // NDS + 1)
            key = ("d", q, slot)
            if self.seen[eng].get(key, 0) >= val:
                return None
            self.seen[eng][key] = val
            return ("d", q, slot, val)
        else:
            key = tok
            if self.seen[eng].get(key, 0) >= 1:
                return None
            self.seen[eng][key] = 1
            return tok

    def _deps(self, eng, reads, writes):
        toks = []
        for b in reads:
            if b.w is not None:
                toks.append(b.w)
        for b in writes:
            if b.w is not None:
                toks.append(b.w)
            toks.extend(b.r.values())
        waits = []
        for t in toks:
            w = self._need_wait(eng, t)
            if w is not None:
                waits.append(w)
        return waits

    def _mark(self, tok, key, reads, writes):
        for b in reads:
            b.r[key] = tok
        for b in writes:
            b.w = tok
            b.r = {}

    def op(self, eng, fn, reads=(), writes=(), inc=True):
        if eng != "pe":
            inc = True
        waits = self._deps(eng, reads, writes)
        k = self.cnt[eng] + 1
        tok = ("e", eng, k)
        if inc:
            self.cnt[eng] = k
        self.streams[eng].append((waits, _freeze(fn), k if inc else None, None))
        self._mark(tok, eng, reads, writes)
        return tok

    def dma(self, eng, out, in_, reads=(), writes=()):
        waits = self._deps(eng, reads, writes)
        i = self.ndma.get(eng, 0)
        self.ndma[eng] = i + 1
        if i >= NDS:
            w = self._need_wait(eng, ("d", eng, i - NDS))
            if w is not None:
                waits.append(w)
        tok = ("d", eng, i)
        self.streams[eng].append((waits, lambda e: e.dma_start(out=out, in_=in_), None, i))
        self._mark(tok, tok, reads, writes)
        live = self.dma_tokens_live.setdefault(eng, [])
        live.append(tok)
        if len(live) > NDS:
            del live[:-NDS]
        return tok

    def barrier(self, extra=()):
        toks = [("e", e, self.cnt[e]) for e in self.ENGS if self.cnt[e] > 0]
        for live in self.dma_tokens_live.values():
            toks += list(live)
        toks += list(extra)
        for eng in self.ENGS:
            waits = []
            for t in toks:
                if t[0] == "e" and t[1] == eng and eng == "pe":
                    continue
                w = self._need_wait(eng, t)
                if w is not None:
                    waits.append(w)
            if waits:
                self.streams[eng].append((waits, None, None, None))

    def emit(self):
        nc = self.nc
        nsem = {e: (self.cnt[e] + CAP - 1) // CAP + 1 for e in self.ENGS}
        sems = {e: [nc.alloc_semaphore(f"s_{e}_{j}") for j in range(nsem[e])] for e in self.ENGS}
        dsems = {q: [nc.alloc_semaphore(f"s_dma_{q}_{j}") for j in range(NDS)] for q in self.ndma}
        xsems = self.extra_sems
        handles = {"pe": "tensor", "act": "scalar", "dve": "vector", "pool": "gpsimd", "sp": "sync"}

        def run(ename, eng):
            for waits, fn, k, di in self.streams[ename]:
                for w in waits:
                    if w[0] == "e":
                        _, te, kk = w
                        eng.wait_ge(sems[te][(kk - 1) // CAP], (kk - 1) % CAP + 1)
                    elif w[0] == "d":
                        eng.wait_ge(dsems[w[1]][w[2]], w[3])
                    else:
                        eng.wait_ge(xsems[w[1]], 1)
                if fn is None:
                    continue
                ins = fn(eng)
                if k is not None:
                    ins.then_inc(sems[ename][(k - 1) // CAP], 1)
                if di is not None:
                    ins.then_inc(dsems[ename][di % NDS], 16)

        with nc.Block() as block:
            @block.tensor
            def _(e):
                run("pe", e)

            @block.scalar
            def _(e):
                run("act", e)

            @block.vector
            def _(e):
                run("dve", e)

            @block.gpsimd
            def _(e):
                run("pool", e)

            @block.sync
            def _(e):
                run("sp", e)


class Arena:
    def __init__(self, nc, nbytes):
        self.t = nc.alloc_sbuf_tensor("arena", [128, nbytes // 2], BF16)
        self.nbytes = nbytes
        self.off = 0
        self.marks = []

    def push(self):
        self.marks.append(self.off)

    def pop(self):
        self.off = self.marks.pop()

    def alloc(self, shape, dtype):
        es = 4 if dtype == F32 else 2
        n = int(np.prod(shape[1:]))
        nb = n * es
        nb = (nb + 63) // 64 * 64
        assert self.off + nb <= self.nbytes, f"arena overflow {self.off + nb} > {self.nbytes}"
        o = self.off // 2
        ap = self.t[:, o:o + nb // 2]
        self.off += nb
        if dtype == F32:
            ap = ap.bitcast(F32)
        ap = ap[:, 0:n]
        if len(shape) == 3:
            ap = ap.rearrange("p (a b) -> p a b", a=shape[1])
        return ap


def build(TG, layers, dbg=None):
    T = TG * G
    NKB = T // 128
    L = len(layers)
    nc = bass.Bass("TRN2", target_bir_lowering=False)
    P = Prog(nc)

    def din(name, shape, dt=F32):
        return nc.dram_tensor(name, shape, dt, kind="ExternalInput").ap()

    def dscr(name, shape, dt, out=False):
        kind = "ExternalOutput" if (out or (dbg and name in dbg)) else "Internal"
        return nc.dram_tensor(name, shape, dt, kind=kind)

    xT = din("xT", [D, T])
    pT = din("pT", [L, 256, T])
    w_gu = [din("w_gu1", [L, D, 2 * DFF]), din("w_gu2", [L, D, 2 * DFF])]
    w_dn = [din("w_d1", [L, DFF, D]), din("w_d2", [L, DFF, D])]
    w_in = din("w_in", [L, D, INW])
    w_a = din("w_a", [L, 1024, D])
    w_b = din("w_b", [L, 1024, D])
    w_o = din("w_o", [L, D, D])
    w_pg = din("w_pg", [L, D, D])
    w_pp = din("w_pp", [L, 256, D])
    gains_d = din("gains", [L, 128, NGC])
    cst_d = din("cst", [128, NCC])
    rope_d = din("rope", [128, 2, T])
    farb_d = din("farbias", [128, 1])

    outT = nc.dram_tensor("outT", [D, T], F32, kind="ExternalOutput").ap()
    hS = dscr("hS", [D, T], F32).ap()
    qA = dscr("qA", [1024, T], BF16).ap()
    qB = dscr("qB", [1024, T], BF16).ap()
    gts = dscr("gts", [4096, T], BF16).ap()
    kloc = dscr("kloc", [2048, T], BF16)
    vloc = dscr("vloc", [T, 2048], BF16)
    kall = dscr("kall", [4096, T], BF16)
    vall = dscr("vall", [2 * T, 2048], BF16)
    yAd = dscr("yAd", [1024, T], BF16).ap()
    yBd = dscr("yBd", [1024, T], BF16).ap()

    ar = Arena(nc, 188 * 1024)
    ps = [nc.alloc_psum_tensor(f"ps{i}", [128, 512], F32).ap() for i in range(8)]
    psb = [Buf(f"ps{i}") for i in range(8)]

    cst = ar.alloc([128, 384], BF16)
    cstb = Buf("cst")
    gains = ar.alloc([128, L * NGC], F32)
    gainsb = Buf("gains")
    farb = ar.alloc([128, 1], F32)
    lam = ar.alloc([128, 2 * L], F32)
    lamb = Buf("lam")
    P.dma("pool", cst, cst_d[:, 0:384], writes=[cstb])
    for l in range(L):
        P.dma("sp", gains[:, l * NGC:(l + 1) * NGC], gains_d[l], writes=[gainsb])
    P.dma("sp", farb, farb_d, writes=[gainsb])
    ones = cst[:, CC_ONES:CC_ONES + 128]
    ntri = cst[:, CC_NTRI:CC_NTRI + 128]
    rotm = cst[:, CC_ROT:CC_ROT + 128]

    def gcol(l, c, n=1):
        return gains[:, l * NGC + c:l * NGC + c + n]

    ar.push()
    ltmp = ar.alloc([128, 8], F32)
    ltb = Buf("ltmp")
    ones32 = ar.alloc([128, 128], F32)
    o32b = Buf("ones32")
    P.op("dve", lambda e: e.memset(ones32, 1.0), writes=[o32b])
    for l in range(L):
        P.op("dve", lambda e, l=l: e.tensor_tensor(out=ltmp[:, 0:1], in0=gcol(l, GC_L), in1=gcol(l, GC_L + 1), op=ALU.mult),
             reads=[gainsb], writes=[ltb])
        P.op("dve", lambda e, l=l: e.tensor_tensor(out=ltmp[:, 1:2], in0=gcol(l, GC_L + 2), in1=gcol(l, GC_L + 3), op=ALU.mult),
             reads=[gainsb], writes=[ltb])
        P.op("pe", lambda e: e.matmul(ps[7][:, 0:2], ones32, ltmp[:, 0:2], start=True, stop=True),
             reads=[ltb, o32b], writes=[psb[7]])
        P.op("act", lambda e: e.activation(out=ltmp[:, 2:4], in_=ps[7][:, 0:2], func=AF.Exp), reads=[psb[7]], writes=[ltb])
        P.op("dve", lambda e: e.tensor_tensor(out=ltmp[:, 4:5], in0=ltmp[:, 2:3], in1=ltmp[:, 3:4], op=ALU.subtract),
             reads=[ltb], writes=[ltb])
        P.op("dve", lambda e, l=l: e.tensor_tensor(out=lam[:, 2 * l:2 * l + 1], in0=ltmp[:, 4:5], in1=gcol(l, GC_LI), op=ALU.add),
             reads=[ltb, gainsb], writes=[lamb])
        P.op("dve", lambda e, l=l: e.tensor_scalar(out=lam[:, 2 * l + 1:2 * l + 2], in0=lam[:, 2 * l:2 * l + 1], scalar1=-1.0, scalar2=None, op0=ALU.mult),
             reads=[lamb], writes=[lamb])
    P.barrier()
    ar.pop()
    stop_after = dbg.get("stop") if dbg else None

    dmaq = ["sp", "act"]
    rr = [0]

    def hwq():
        return "sp"

    def stq():
        return "act"

    def rms_bc(src, nch, bank, out_rstd, rstdb, srcb, Dn, tmps, tmpbs):
        nt = len(tmps)
        for c in range(nch):
            sq, sqb = tmps[c % nt], tmpbs[c % nt]
            eng = "dve" if c % 2 == 0 else "pool"
            P.op(eng, lambda e, c=c, sq=sq: e.tensor_tensor(out=sq, in0=src[:, c, :], in1=src[:, c, :], op=ALU.mult),
                 reads=[srcb], writes=[sqb])
            P.op("pe", lambda e, c=c, sq=sq: e.matmul(ps[bank], ones, sq, start=(c == 0), stop=(c == nch - 1)),
                 reads=[sqb, cstb], writes=[psb[bank]], inc=True)
        P.op("act", lambda e: e.activation(out=out_rstd, in_=ps[bank], func=AF.Ln, bias=EPS, scale=1.0 / Dn),
             reads=[psb[bank]], writes=[rstdb])
        P.op("act", lambda e: e.activation(out=out_rstd, in_=out_rstd, func=AF.Exp, scale=-0.5),
             reads=[rstdb], writes=[rstdb])

    class WStream:
        def __init__(self, nslots, nelem=8192):
            self.slots = [ar.alloc([128, nelem], BF16) for _ in range(nslots)]
            self.bufs = [Buf(f"w{i}") for i in range(nslots)]
            self.i = 0

        def load(self, src_ap, kc, ncol):
            s = self.i % len(self.slots)
            self.i += 1
            dst = self.slots[s][:, 0:kc * ncol].rearrange("p (c n) -> p c n", c=kc)
            P.dma("pool", dst, src_ap, writes=[self.bufs[s]])
            return dst, self.bufs[s]

    def wsrc(w2d, c0, ncol):
        return w2d[:, c0:c0 + ncol].rearrange("(c p) n -> p c n", p=128)

    def ffn(l, which, hT, hTb, xn, xnb, hid, hidb, ws, rstd, rstdb, tmps, tmpbs, sg, sgb):
        gc = GC_FFN1 if which == 0 else GC_FFN2
        rms_bc(hT, DC, 6, rstd, rstdb, hTb, float(D), tmps, tmpbs)
        for c in range(DC):
            eng = "dve"
            P.op(eng, lambda e, c=c: e.scalar_tensor_tensor(out=xn[:, c, :], in0=hT[:, c, :], scalar=gcol(l, gc + c),
                                                            in1=rstd, op0=ALU.mult, op1=ALU.mult),
                 reads=[hTb, rstdb, gainsb], writes=[xnb])
        wgu = w_gu[which][l]
        wdn = w_dn[which][l]
        for fb in range(FC // 4):
            wg, wgb = ws.load(wsrc(wgu, fb * 512, 512), DC, 512)
            wu, wub = ws.load(wsrc(wgu, DFF + fb * 512, 512), DC, 512)
            for j in range(4):
                fc = fb * 4 + j
                pg, pu = (0, 1) if fc % 2 == 0 else (2, 3)
                for kc in range(DC):
                    P.op("pe", lambda e, kc=kc, j=j, wg=wg, pg=pg: e.matmul(ps[pg], wg[:, kc, j * 128:(j + 1) * 128], xn[:, kc, :],
                                                                            start=(kc == 0), stop=(kc == DC - 1)),
                         reads=[wgb, xnb], writes=[psb[pg]], inc=(kc == DC - 1))
                for kc in range(DC):
                    P.op("pe", lambda e, kc=kc, j=j, wu=wu, pu=pu: e.matmul(ps[pu], wu[:, kc, j * 128:(j + 1) * 128], xn[:, kc, :],
                                                                            start=(kc == 0), stop=(kc == DC - 1)),
                         reads=[wub, xnb], writes=[psb[pu]], inc=(kc == DC - 1))
                s = fc % 2
                P.op("act", lambda e, pg=pg, s=s: e.activation(out=sg[s], in_=ps[pg], func=AF.Silu), reads=[psb[pg]], writes=[sgb[s]])
                P.op("dve", lambda e, pu=pu, s=s, fc=fc: e.tensor_tensor(out=hid[:, fc, :], in0=sg[s], in1=ps[pu], op=ALU.mult),
                     reads=[sgb[s], psb[pu]], writes=[hidb])
        dbanks = [4, 5, 0, 1]
        for db in range(8):
            halves = []
            for kh in range(2):
                halves.append(ws.load(wdn[kh * 2816:(kh + 1) * 2816, db * 256:(db + 1) * 256].rearrange("(c p) n -> p c n", p=128), 22, 256))
            for kh in range(2):
                wd, wdb = halves[kh]
                for j in range(2):
                    pd = dbanks[(db % 2) * 2 + j]
                    for fc in range(22):
                        last = (kh == 1 and fc == 21)
                        P.op("pe", lambda e, fc=fc, kh=kh, j=j, wd=wd, pd=pd, last=last: e.matmul(
                            ps[pd], wd[:, fc, j * 128:(j + 1) * 128], hid[:, kh * 22 + fc, :], start=(kh == 0 and fc == 0), stop=last),
                             reads=[wdb, hidb], writes=[psb[pd]], inc=last)
            for j in range(2):
                dc = db * 2 + j
                pd = dbanks[(db % 2) * 2 + j]
                P.op("dve", lambda e, dc=dc, pd=pd: e.scalar_tensor_tensor(out=hT[:, dc, :], in0=ps[pd], scalar=0.5, in1=hT[:, dc, :],
                                                                           op0=ALU.mult, op1=ALU.add),
                     reads=[psb[pd], hTb], writes=[hTb])

    def phase_A(li, l, src_h):
        ar.push()
        hT = ar.alloc([128, DC, G], F32); hTb = Buf("hT")
        xn = ar.alloc([128, DC, G], BF16); xnb = Buf("xn")
        hid = ar.alloc([128, FC, G], BF16); hidb = Buf("hid")
        ws = WStream(4)
        rstd = ar.alloc([128, G], F32); rstdb = Buf("rstd")
        rs2 = [ar.alloc([128, G], F32) for _ in range(2)]; rs2b = [Buf("rs20"), Buf("rs21")]
        tmps = [ar.alloc([128, G], BF16) for _ in range(4)]; tmpbs = [Buf(f"tmp{i}") for i in range(4)]
        sg = [ar.alloc([128, G], F32) for _ in range(2)]; sgb = [Buf("sg0"), Buf("sg1")]
        stg = [ar.alloc([128, G], BF16) for _ in range(4)]; stgb = [Buf(f"stg{i}") for i in range(4)]
        xg = [ar.alloc([128, G], BF16) for _ in range(2)]; xgb = [Buf("xg0"), Buf("xg1")]
        x32 = [ar.alloc([128, G], F32) for _ in range(2)]; x32b = [Buf("x320"), Buf("x321")]
        rp = ar.alloc([128, 2, G], F32); rpb = Buf("rope")
        sti = [0]

        def stage():
            i = sti[0] % 4
            sti[0] += 1
            return stg[i], stgb[i]

        for g in range(TG):
            t0 = g * G
            for q4 in range(4):
                P.dma(hwq(), hT[:, q4 * 4:(q4 + 1) * 4, :],
                      src_h[q4 * 512:(q4 + 1) * 512, t0:t0 + G].rearrange("(c p) t -> p c t", p=128), writes=[hTb])
            P.dma(hwq(), rp, rope_d[:, :, t0:t0 + G], writes=[rpb])
            if stop_after != "Aload":
                ffn(li, 0, hT, hTb, xn, xnb, hid, hidb, ws, rstd, rstdb, tmps, tmpbs, sg, sgb)
            for q4 in range(4):
                P.dma(stq(), hS[q4 * 512:(q4 + 1) * 512, t0:t0 + G].rearrange("(c p) t -> p c t", p=128),
                      hT[:, q4 * 4:(q4 + 1) * 4, :], reads=[hTb])
            if stop_after in ("Aload", "Affn"):
                continue
            rms_bc(hT, DC, 6, rstd, rstdb, hTb, float(D), tmps, tmpbs)
            for c in range(DC):
                eng = "dve"
                P.op(eng, lambda e, c=c: e.scalar_tensor_tensor(out=xn[:, c, :], in0=hT[:, c, :], scalar=gcol(li, GC_MIX + c),
                                                                in1=rstd, op0=ALU.mult, op1=ALU.mult),
                     reads=[hTb, rstdb, gainsb], writes=[xnb])
            win = w_in[li]
            pbank = [0]

            def nextbank():
                b = pbank[0] % 4
                pbank[0] += 1
                return b

            for cb in range(20):
                wb_, wbb = ws.load(wsrc(win, cb * 512, 512), DC, 512)
                kind = ["sbq", "sbk", "sbv", "dfq", "dfk", "dfv", "ga", "ga", "gb", "gb"][cb // 2]
                if kind in ("sbv", "dfv"):
                    for tt in range(4):
                        b = nextbank()
                        for kc in range(DC):
                            P.op("pe", lambda e, kc=kc, tt=tt, b=b, wb_=wb_: e.matmul(ps[b], xn[:, kc, tt * 128:(tt + 1) * 128], wb_[:, kc, :],
                                                                                      start=(kc == 0), stop=(kc == DC - 1)),
                                 reads=[wbb, xnb], writes=[psb[b]], inc=(kc == DC - 1))
                        st, stb = stage()
                        P.op("act", lambda e, b=b, st=st: e.activation(out=st, in_=ps[b], func=AF.Copy), reads=[psb[b]], writes=[stb])
                        col0 = (0 if kind == "sbv" else 1024) + (cb % 2) * 512
                        P.dma(stq(), vloc[t0 + tt * 128:t0 + (tt + 1) * 128, col0:col0 + 512], st, reads=[stb])
                    continue
                for j in range(4):
                    b = nextbank()
                    oc = (cb % 2) * 4 + j
                    for kc in range(DC):
                        P.op("pe", lambda e, kc=kc, j=j, b=b, wb_=wb_: e.matmul(ps[b], wb_[:, kc, j * 128:(j + 1) * 128], xn[:, kc, :],
                                                                                start=(kc == 0), stop=(kc == DC - 1)),
                             reads=[wbb, xnb], writes=[psb[b]], inc=(kc == DC - 1))
                    st, stb = stage()
                    if kind == "sbq":
                        P.op("act", lambda e, b=b, st=st: e.activation(out=st, in_=ps[b], func=AF.Copy, scale=SCALE), reads=[psb[b]], writes=[stb])
                        P.dma(stq(), qA[oc * 128:(oc + 1) * 128, t0:t0 + G], st, reads=[stb])
                    elif kind == "sbk":
                        P.op("act", lambda e, b=b, st=st: e.activation(out=st, in_=ps[b], func=AF.Copy), reads=[psb[b]], writes=[stb])
                        P.dma(stq(), kloc[oc * 128:(oc + 1) * 128, t0:t0 + G], st, reads=[stb])
                    elif kind in ("ga", "gb"):
                        gch = ((cb - 12) * 4 + j)
                        P.op("act", lambda e, b=b, st=st: e.activation(out=st, in_=ps[b], func=AF.Sigmoid), reads=[psb[b]], writes=[stb])
                        P.dma(stq(), gts[gch * 128:(gch + 1) * 128, t0:t0 + G], st, reads=[stb])
                    else:
                        isq = kind == "dfq"
                        gcn = GC_DQ if isq else GC_DK
                        s = oc % 2
                        bss, brt = 4 + s, 6 + s
                        P.op("act", lambda e, b=b, s=s: e.activation(out=tmps[s], in_=ps[b], func=AF.Square), reads=[psb[b]], writes=[tmpbs[s]])
                        P.op("act", lambda e, b=b, s=s, gcn=gcn: e.activation(out=x32[s], in_=ps[b], func=AF.Copy, scale=gcol(li, gcn)),
                             reads=[psb[b], gainsb], writes=[x32b[s]])
                        P.op("pe", lambda e, s=s, bss=bss: e.matmul(ps[bss], ones, tmps[s], start=True, stop=True), reads=[tmpbs[s], cstb], writes=[psb[bss]])
                        P.op("dve", lambda e, s=s: e.tensor_copy(out=xg[s], in_=x32[s]), reads=[x32b[s]], writes=[xgb[s]])
                        P.op("pe", lambda e, s=s, brt=brt: e.matmul(ps[brt], rotm, xg[s], start=True, stop=True), reads=[xgb[s], cstb], writes=[psb[brt]])
                        P.op("act", lambda e, s=s, bss=bss: e.activation(out=rs2[s], in_=ps[bss], func=AF.Ln, bias=EPS, scale=1.0 / 128),
                             reads=[psb[bss]], writes=[rs2b[s]])
                        P.op("act", lambda e, s=s, isq=isq: e.activation(out=rs2[s], in_=rs2[s], func=AF.Exp, scale=-0.5, bias=(math.log(SCALE) if isq else 0.0)),
                             reads=[rs2b[s]], writes=[rs2b[s]])
                        P.op("dve", lambda e, s=s: e.tensor_tensor(out=x32[s], in0=x32[s], in1=rp[:, 0, :], op=ALU.mult),
                             reads=[x32b[s], rpb], writes=[x32b[s]])
                        P.op("dve", lambda e, s=s, brt=brt: e.tensor_tensor(out=sg[s], in0=ps[brt], in1=rp[:, 1, :], op=ALU.mult),
                             reads=[psb[brt], rpb], writes=[sgb[s]])
                        P.op("dve", lambda e, s=s: e.tensor_tensor(out=x32[s], in0=x32[s], in1=sg[s], op=ALU.add),
                             reads=[x32b[s], sgb[s]], writes=[x32b[s]])
                        P.op("dve", lambda e, s=s, st=st: e.tensor_tensor(out=st, in0=x32[s], in1=rs2[s], op=ALU.mult),
                             reads=[x32b[s], rs2b[s]], writes=[stb])
                        if isq:
                            P.dma(stq(), qB[oc * 128:(oc + 1) * 128, t0:t0 + G], st, reads=[stb])
                        else:
                            P.dma(stq(), kloc[1024 + oc * 128:1024 + (oc + 1) * 128, t0:t0 + G], st, reads=[stb])
        P.barrier()
        ar.pop()

    RK = min(2048, (1 << 19) // T)
    NKC = 2048 // RK
    RV = 256
    NVC = T // RV

    def kfar(r0, n):
        j, w = r0 // RK, r0 % RK
        return kall[j * 2 * RK + w:j * 2 * RK + w + n, :]

    def exchange():
        groups = [[0, 1], [2, 3], [4, 5], [6, 7]]
        toks = []
        jobs = [(kloc, kall, RK, j) for j in range(NKC)] + [(vloc, vall, RV, j) for j in range(NVC)]
        for (src, dst, R_, j) in jobs:
            sem = nc.alloc_semaphore(f"cc{len(P.extra_sems)}")
            idx = len(P.extra_sems)
            P.extra_sems.append(sem)
            P.streams["pool"].append(([], (lambda e, src=src, dst=dst, sem=sem, R_=R_, j=j: e.collective_compute(
                "AllGather", ALU.bypass, replica_groups=groups, ins=[src[j * R_:(j + 1) * R_, :]],
                outs=[dst[j * 2 * R_:(j + 1) * 2 * R_, :]]).then_inc(sem, 1)), None, None))
            toks.append(("x", idx))
        P.barrier(extra=toks)

    def phase_B(li, l, last):
        ar.push()
        ar.push()
        msk = ar.alloc([128, 4096], BF16); mskb = Buf("msk")
        P.dma("pool", msk, cst_d[:, 384:384 + 4096], writes=[mskb])
        yst = [ar.alloc([128, G], BF16) for _ in range(4)]; ystb = [Buf(f"yst{i}") for i in range(4)]
        ysti = [0]
        kT = [ar.alloc([128, 4, T], BF16) for _ in range(2)]; kTb = [Buf("kT0"), Buf("kT1")]
        vv = [ar.alloc([128, 2 * NKB, 256], BF16) for _ in range(2)]; vvb = [Buf("v0"), Buf("v1")]
        qt = [ar.alloc([128, G], BF16) for _ in range(3)]; qtb = [Buf(f"qt{i}") for i in range(3)]
        e32 = [ar.alloc([128, G], F32) for _ in range(2)]; e32b = [Buf("e0"), Buf("e1")]
        sp = [ar.alloc([128, G], BF16) for _ in range(4)]; spb = [Buf(f"sp{i}") for i in range(4)]
        at = [ar.alloc([128, G], BF16) for _ in range(3)]; atb = [Buf(f"at{i}") for i in range(3)]
        R32 = ar.alloc([128, G], F32); R32b = Buf("R32")
        Rbf = [ar.alloc([128, G], BF16) for _ in range(3)]; Rbfb = [Buf(f"Rbf{i}") for i in range(3)]
        n1 = ar.alloc([128, 2, G], F32); n1b = Buf("n1")
        rc = ar.alloc([128, G], F32); rcb = Buf("rc")
        y32 = ar.alloc([128, 2, G], F32); y32b = Buf("y32")
        tq = ar.alloc([128, G], BF16); tqb = Buf("tq")
        msb = [msk[:, k * 512:(k + 1) * 512] for k in range(4)]
        mdf = [msk[:, 2048 + k * 512:2048 + (k + 1) * 512] for k in range(4)]
        cnt = {"q": 0, "e": 0, "sp": 0, "at": 0, "z": 0, "p": 0, "y": 0, "R": 0, "hd": 0}

        def rot(name, n):
            i = cnt[name] % n
            cnt[name] += 1
            return i

        for h in range(8):
            hb = rot("hd", 2)
            k_, k_b = kT[hb], kTb[hb]
            v_, v_b = vv[hb], vvb[hb]
            P.dma(hwq(), k_[:, 0, :], kloc[h * 128:(h + 1) * 128, :], writes=[k_b])
            P.dma(hwq(), k_[:, 1, :], kfar(h * 128, 128), writes=[k_b])
            P.dma(hwq(), v_[:, 0:NKB, 0:128], vloc[:, h * 128:(h + 1) * 128].rearrange("(b p) c -> p b c", p=128), writes=[v_b])
            for j in range(NVC):
                P.dma(hwq(), v_[:, NKB + 2 * j:NKB + 2 * j + 2, 0:128],
                      vall[j * 512:j * 512 + 256, h * 128:(h + 1) * 128].rearrange("(b p) c -> p b c", p=128), writes=[v_b])
            for g in range(TG):
                t0 = g * G
                qi = rot("q", 3)
                P.dma(hwq(), qt[qi], qA[h * 128:(h + 1) * 128, t0:t0 + G], writes=[qtb[qi]])
                yb = 4 + rot("y", 2)
                blocks = [(0, kb) for kb in range(4 * g + 3, -1, -1)] + [(1, kb) for kb in range(NKB - 1, -1, -1)]
                nblk = len(blocks)
                st_ = {}

                def stA(bi):
                    far, kb = blocks[bi]
                    kblk = k_[:, far, kb * 128:(kb + 1) * 128]
                    dk = (kb - 4 * g) if (far == 0 and kb >= 4 * g) else None
                    zb = rot("z", 2)
                    P.op("pe", lambda e: e.matmul(ps[zb], kblk, qt[qi], start=True, stop=True),
                         reads=[k_b, qtb[qi]], writes=[psb[zb]])
                    ei = rot("e", 2)
                    P.op("act", lambda e: e.activation(out=e32[ei], in_=ps[zb], func=AF.Exp), reads=[psb[zb]], writes=[e32b[ei]])
                    si = rot("sp", 4)
                    P.op("act", lambda e: e.activation(out=sp[si], in_=e32[ei], func=AF.Ln, bias=1.0, scale=1.0),
                         reads=[e32b[ei]], writes=[spb[si]])
                    if dk is not None:
                        P.op("dve", lambda e: e.tensor_tensor(out=sp[si], in0=sp[si], in1=msb[dk], op=ALU.mult),
                             reads=[spb[si], mskb], writes=[spb[si]])
                    Ri = None
                    if bi < nblk - 1:
                        if bi == 0:
                            P.op("dve", lambda e: e.tensor_scalar(out=R32, in0=sp[si], scalar1=-1.0, scalar2=None, op0=ALU.mult),
                                 reads=[spb[si]], writes=[R32b])
                        else:
                            P.op("dve", lambda e: e.tensor_tensor(out=R32, in0=R32, in1=sp[si], op=ALU.subtract),
                                 reads=[spb[si], R32b], writes=[R32b])
                        Ri = rot("R", 3)
                        P.op("dve", lambda e: e.tensor_copy(out=Rbf[Ri], in_=R32), reads=[R32b], writes=[Rbfb[Ri]])
                    st_[bi] = dict(kblk=kblk, dk=dk, si=si, Ri=Ri, far=far, kb=kb)

                def stB(bi):
                    d = st_[bi]
                    kblk, dk, si, far = d["kblk"], d["dk"], d["si"], d["far"]
                    pb = 2 + rot("p", 2)
                    P.op("pe", lambda e: e.matmul(ps[pb], kblk, qt[qi], start=True, stop=False),
                         reads=[k_b, qtb[qi]], writes=[psb[pb]], inc=False)
                    P.op("pe", lambda e: e.matmul(ps[pb], ntri, sp[si], start=False, stop=(bi == 0)),
                         reads=[spb[si], cstb], writes=[psb[pb]], inc=(bi == 0))
                    if bi > 0:
                        Rp = st_[bi - 1]["Ri"]
                        P.op("pe", lambda e: e.matmul(ps[pb], ones, Rbf[Rp], start=False, stop=True),
                             reads=[Rbfb[Rp], cstb], writes=[psb[pb]])
                    ai = rot("at", 3)
                    if far:
                        P.op("act", lambda e: e.activation(out=at[ai], in_=ps[pb], func=AF.Exp, bias=farb[:, 0:1], scale=1.0),
                             reads=[psb[pb], gainsb], writes=[atb[ai]])
                    else:
                        P.op("act", lambda e: e.activation(out=at[ai], in_=ps[pb], func=AF.Exp), reads=[psb[pb]], writes=[atb[ai]])
                    if dk is not None:
                        P.op("dve", lambda e: e.tensor_tensor(out=at[ai], in0=at[ai], in1=msb[dk], op=ALU.mult),
                             reads=[atb[ai], mskb], writes=[atb[ai]])
                    d["ai"] = ai

                def stC(bi):
                    d = st_[bi]
                    ai = d["ai"]
                    vblk = v_[:, d["far"] * NKB + d["kb"], 0:128]
                    P.op("pe", lambda e: e.matmul(ps[yb], vblk, at[ai], start=(bi == 0), stop=(bi == nblk - 1)),
                         reads=[v_b, atb[ai]], writes=[psb[yb]], inc=(bi == nblk - 1))

                sB, sC = SKEW_SB
                for it in range(nblk + sC):
                    if it < nblk:
                        stA(it)
                    if sB <= it < nblk + sB:
                        stB(it - sB)
                    if it >= sC:
                        stC(it - sC)
                yi = ysti[0] % 4
                ysti[0] += 1
                P.op("act", lambda e, yb=yb, yi=yi: e.activation(out=yst[yi], in_=ps[yb], func=AF.Copy), reads=[psb[yb]], writes=[ystb[yi]])
                P.dma(stq(), yAd[h * 128:(h + 1) * 128, t0:t0 + G], yst[yi], reads=[ystb[yi]])

        for h in range(4):
            hb = rot("hd", 2)
            k_, k_b = kT[hb], kTb[hb]
            v_, v_b = vv[hb], vvb[hb]
            for half in range(2):
                r0 = 1024 + (2 * h + half) * 128
                P.dma(hwq(), k_[:, half, :], kloc[r0:r0 + 128, :], writes=[k_b])
                P.dma(hwq(), k_[:, 2 + half, :], kfar(r0, 128), writes=[k_b])
            c0 = 1024 + h * 256
            P.dma(hwq(), v_[:, 0:NKB, :], vloc[:, c0:c0 + 256].rearrange("(b p) c -> p b c", p=128), writes=[v_b])
            for j in range(NVC):
                P.dma(hwq(), v_[:, NKB + 2 * j:NKB + 2 * j + 2, :],
                      vall[j * 512:j * 512 + 256, c0:c0 + 256].rearrange("(b p) c -> p b c", p=128), writes=[v_b])
            for g in range(TG):
                t0 = g * G
                blocks = [(0, kb) for kb in range(4 * g + 3, -1, -1)] + [(1, kb) for kb in range(NKB - 1, -1, -1)]
                nblk = len(blocks)
                for half in range(2):
                    qi = rot("q", 3)
                    r0 = (2 * h + half) * 128
                    P.dma(hwq(), qt[qi], qB[r0:r0 + 128, t0:t0 + G], writes=[qtb[qi]])
                    ysel = rot("y", 2)
                    ya, yb2, db = (2, 3, 6) if ysel == 0 else (4, 5, 7)
                    dst_ = {}

                    def dA(bi):
                        far, kb = blocks[bi]
                        kblk = k_[:, 2 * far + half, kb * 128:(kb + 1) * 128]
                        dk = (kb - 4 * g) if (far == 0 and kb >= 4 * g) else None
                        zb = rot("z", 2)
                        P.op("pe", lambda e: e.matmul(ps[zb], kblk, qt[qi], start=True, stop=True),
                             reads=[k_b, qtb[qi]], writes=[psb[zb]])
                        ai = rot("at", 3)
                        if far:
                            P.op("act", lambda e: e.activation(out=at[ai], in_=ps[zb], func=AF.Exp, bias=farb[:, 0:1], scale=1.0),
                                 reads=[psb[zb], gainsb], writes=[atb[ai]])
                        else:
                            P.op("act", lambda e: e.activation(out=at[ai], in_=ps[zb], func=AF.Exp), reads=[psb[zb]], writes=[atb[ai]])
                        if dk is not None:
                            P.op("dve", lambda e: e.tensor_tensor(out=at[ai], in0=at[ai], in1=mdf[dk], op=ALU.mult),
                                 reads=[atb[ai], mskb], writes=[atb[ai]])
                        dst_[bi] = (far, kb, ai)

                    def dB(bi):
                        far, kb, ai = dst_[bi]
                        st, sp_ = (bi == 0), (bi == nblk - 1)
                        P.op("pe", lambda e: e.matmul(ps[ya], v_[:, far * NKB + kb, 0:128], at[ai], start=st, stop=sp_),
                             reads=[v_b, atb[ai]], writes=[psb[ya]], inc=sp_)
                        P.op("pe", lambda e: e.matmul(ps[yb2], v_[:, far * NKB + kb, 128:256], at[ai], start=st, stop=sp_),
                             reads=[v_b, atb[ai]], writes=[psb[yb2]], inc=sp_)
                        P.op("pe", lambda e: e.matmul(ps[db], ones, at[ai], start=st, stop=sp_),
                             reads=[cstb, atb[ai]], writes=[psb[db]], inc=sp_)

                    for it in range(nblk + SKEW_DF):
                        if it < nblk:
                            dA(it)
                        if it >= SKEW_DF:
                            dB(it - SKEW_DF)
                    P.op("dve", lambda e, db=db: e.reciprocal(out=rc, in_=ps[db]), reads=[psb[db]], writes=[rcb])
                    if half == 0:
                        P.op("dve", lambda e, ya=ya: e.tensor_tensor(out=n1[:, 0, :], in0=ps[ya], in1=rc, op=ALU.mult), reads=[psb[ya], rcb], writes=[n1b])
                        P.op("dve", lambda e, yb2=yb2: e.tensor_tensor(out=n1[:, 1, :], in0=ps[yb2], in1=rc, op=ALU.mult), reads=[psb[yb2], rcb], writes=[n1b])
                    else:
                        for c, bk in ((0, ya), (1, yb2)):
                            P.op("dve", lambda e, c=c, bk=bk: e.tensor_tensor(out=y32[:, c, :], in0=ps[bk], in1=rc, op=ALU.mult),
                                 reads=[psb[bk], rcb], writes=[y32b])
                            P.op("dve", lambda e, c=c: e.scalar_tensor_tensor(out=y32[:, c, :], in0=y32[:, c, :], scalar=lam[:, 2 * li + 1:2 * li + 2],
                                                                             in1=n1[:, c, :], op0=ALU.mult, op1=ALU.add),
                                 reads=[y32b, n1b, lamb], writes=[y32b])
                        for c in range(2):
                            P.op("pool", lambda e, c=c: e.tensor_tensor(out=sp[c], in0=y32[:, c, :], in1=y32[:, c, :], op=ALU.mult),
                                 reads=[y32b], writes=[spb[c]])
                            P.op("pe", lambda e, c=c: e.matmul(ps[0], ones, sp[c], start=(c == 0), stop=(c == 1)),
                                 reads=[spb[c], cstb], writes=[psb[0]], inc=(c == 1))
                        P.op("act", lambda e: e.activation(out=rc, in_=ps[0], func=AF.Ln, bias=EPS, scale=1.0 / 256),
                             reads=[psb[0]], writes=[rcb])
                        P.op("act", lambda e: e.activation(out=rc, in_=rc, func=AF.Exp, scale=-0.5),
                             reads=[rcb], writes=[rcb])
                        P.op("dve", lambda e: e.tensor_scalar(out=rc, in0=rc, scalar1=gcol(li, GC_OML), scalar2=None, op0=ALU.mult),
                             reads=[rcb, gainsb], writes=[rcb])
                        for c in range(2):
                            yi = ysti[0] % 4
                            ysti[0] += 1
                            P.op("dve", lambda e, c=c, yi=yi: e.scalar_tensor_tensor(out=yst[yi], in0=y32[:, c, :],
                                                                                   scalar=gcol(li, GC_SUB + c), in1=rc, op0=ALU.mult, op1=ALU.mult),
                                 reads=[y32b, rcb, gainsb], writes=[ystb[yi]])
                            P.dma(stq(), yBd[(2 * h + c) * 128:(2 * h + c + 1) * 128, t0:t0 + G], yst[yi], reads=[ystb[yi]])
        P.barrier()
        ar.pop()

        hT = ar.alloc([128, DC, G], F32); hTb = Buf("hT")
        xn = ar.alloc([128, DC, G], BF16); xnb = Buf("xn")
        hid = ar.alloc([128, FC, G], BF16); hidb = Buf("hid")
        ws = WStream(4)
        rstd = ar.alloc([128, G], F32); rstdb = Buf("rstd")
        tmps = [ar.alloc([128, G], BF16) for _ in range(4)]; tmpbs = [Buf(f"tmp{i}") for i in range(4)]
        sg = [ar.alloc([128, G], F32) for _ in range(2)]; sgb = [Buf("sg0"), Buf("sg1")]
        gt = [ar.alloc([128, 2, G], BF16) for _ in range(4)]; gtb = [Buf(f"gt{i}") for i in range(4)]
        mg = [ar.alloc([128, G], F32) for _ in range(2)]; mgb = [Buf("mg0"), Buf("mg1")]
        ptl = ar.alloc([128, 2, G], BF16); ptlb = Buf("ptl")
        yg = hid[:, 0:16, :]; ygb = hidb
        for g in range(TG):
            t0 = g * G
            for q4 in range(4):
                P.dma(hwq(), hT[:, q4 * 4:(q4 + 1) * 4, :],
                      hS[q4 * 512:(q4 + 1) * 512, t0:t0 + G].rearrange("(c p) t -> p c t", p=128), writes=[hTb])
            P.dma("pool", ptl, pT[li][:, t0:t0 + G].rearrange("(c p) t -> p c t", p=128), writes=[ptlb])
            P.dma(hwq(), yg[:, 0:8, :], yAd[:, t0:t0 + G].rearrange("(c p) t -> p c t", p=128), writes=[ygb])
            P.dma(hwq(), yg[:, 8:16, :], yBd[:, t0:t0 + G].rearrange("(c p) t -> p c t", p=128), writes=[ygb])
            for ob in range(4):
                wa_, wab = ws.load(wsrc(w_a[li], ob * 512, 512), 8, 512)
                wb2, wbb2 = ws.load(wsrc(w_b[li], ob * 512, 512), 8, 512)
                for j in range(4):
                    oc = ob * 4 + j
                    gi = oc % 4
                    P.dma(hwq(), gt[gi], gts.rearrange("(two r) t -> r two t", two=2)[oc * 128:(oc + 1) * 128, :, t0:t0 + G], writes=[gtb[gi]])
                    pa, pb_ = (0, 1) if oc % 2 == 0 else (2, 3)
                    for kc in range(8):
                        P.op("pe", lambda e, kc=kc, j=j, pa=pa, wa_=wa_: e.matmul(ps[pa], wa_[:, kc, j * 128:(j + 1) * 128], yg[:, kc, :],
                                                                                  start=(kc == 0), stop=(kc == 7)),
                             reads=[wab, ygb], writes=[psb[pa]], inc=(kc == 7))
                    for kc in range(8):
                        P.op("pe", lambda e, kc=kc, j=j, pb_=pb_, wb2=wb2: e.matmul(ps[pb_], wb2[:, kc, j * 128:(j + 1) * 128], yg[:, 8 + kc, :],
                                                                                   start=(kc == 0), stop=(kc == 7)),
                             reads=[wbb2, ygb], writes=[psb[pb_]], inc=(kc == 7))
                    s = oc % 2
                    P.op("dve", lambda e, s=s, pa=pa, gi=gi: e.tensor_tensor(out=sg[s], in0=ps[pa], in1=gt[gi][:, 0, :], op=ALU.mult),
                         reads=[psb[pa], gtb[gi]], writes=[sgb[s]])
                    P.op("dve", lambda e, s=s, pb_=pb_, gi=gi: e.tensor_tensor(out=mg[s], in0=ps[pb_], in1=gt[gi][:, 1, :], op=ALU.mult),
                         reads=[psb[pb_], gtb[gi]], writes=[mgb[s]])
                    P.op("pool", lambda e, s=s, oc=oc: e.tensor_tensor(out=xn[:, oc, :], in0=sg[s], in1=mg[s], op=ALU.add),
                         reads=[sgb[s], mgb[s]], writes=[xnb])
            for ob in range(4):
                wo_, wob = ws.load(wsrc(w_o[li], ob * 512, 512), DC, 512)
                for j in range(4):
                    oc = ob * 4 + j
                    pd = 4 + oc % 2
                    for kc in range(DC):
                        P.op("pe", lambda e, kc=kc, j=j, pd=pd, wo_=wo_: e.matmul(ps[pd], wo_[:, kc, j * 128:(j + 1) * 128], xn[:, kc, :],
                                                                                  start=(kc == 0), stop=(kc == DC - 1)),
                             reads=[wob, xnb], writes=[psb[pd]], inc=(kc == DC - 1))
                    P.op("dve", lambda e, oc=oc, pd=pd: e.tensor_tensor(out=hT[:, oc, :], in0=ps[pd], in1=hT[:, oc, :], op=ALU.add),
                         reads=[psb[pd], hTb], writes=[hTb])
            ffn(li, 1, hT, hTb, xn, xnb, hid, hidb, ws, rstd, rstdb, tmps, tmpbs, sg, sgb)
            rms_bc(hT, DC, 6, rstd, rstdb, hTb, float(D), tmps, tmpbs)
            for c in range(DC):
                eng = "dve"
                P.op(eng, lambda e, c=c: e.scalar_tensor_tensor(out=xn[:, c, :], in0=hT[:, c, :], scalar=gcol(li, GC_PLE + c),
                                                                in1=rstd, op0=ALU.mult, op1=ALU.mult),
                     reads=[hTb, rstdb, gainsb], writes=[xnb])
            ppv = hid[:, 0:32, :].rearrange("p a b -> p (a b)").bitcast(F32).rearrange("p (a b) -> p a b", a=16)
            for ob in range(4):
                wp_, wpb = ws.load(wsrc(w_pp[li], ob * 512, 512), 2, 512)
                for j in range(4):
                    oc = ob * 4 + j
                    pd = 4 + oc % 2
                    for kc in range(2):
                        P.op("pe", lambda e, kc=kc, j=j, pd=pd, wp_=wp_: e.matmul(ps[pd], wp_[:, kc, j * 128:(j + 1) * 128], ptl[:, kc, :],
                                                                                  start=(kc == 0), stop=(kc == 1)),
                             reads=[wpb, ptlb], writes=[psb[pd]], inc=(kc == 1))
                    P.op("act", lambda e, oc=oc, pd=pd: e.activation(out=ppv[:, oc, :], in_=ps[pd], func=AF.Copy), reads=[psb[pd]], writes=[hidb])
            rms_bc(ppv, DC, 6, rstd, rstdb, hidb, float(D), tmps, tmpbs)
            for ob in range(4):
                wg_, wgb_ = ws.load(wsrc(w_pg[li], ob * 512, 512), DC, 512)
                for j in range(4):
                    oc = ob * 4 + j
                    pd = (0, 1, 2, 3)[oc % 4]
                    for kc in range(DC):
                        P.op("pe", lambda e, kc=kc, j=j, pd=pd, wg_=wg_: e.matmul(ps[pd], wg_[:, kc, j * 128:(j + 1) * 128], xn[:, kc, :],
                                                                                  start=(kc == 0), stop=(kc == DC - 1)),
                             reads=[wgb_, xnb], writes=[psb[pd]], inc=(kc == DC - 1))
                    s = oc % 2
                    P.op("act", lambda e, s=s, pd=pd: e.activation(out=sg[s], in_=ps[pd], func=AF.Sigmoid), reads=[psb[pd]], writes=[sgb[s]])
                    P.op("dve", lambda e, oc=oc: e.scalar_tensor_tensor(out=ppv[:, oc, :], in0=ppv[:, oc, :], scalar=gcol(li, GC_PLEO + oc),
                                                                        in1=rstd, op0=ALU.mult, op1=ALU.mult),
                         reads=[hidb, rstdb, gainsb], writes=[hidb])
                    P.op("dve", lambda e, s=s, oc=oc: e.tensor_tensor(out=sg[s], in0=sg[s], in1=ppv[:, oc, :], op=ALU.mult),
                         reads=[sgb[s], hidb], writes=[sgb[s]])
                    P.op("dve", lambda e, s=s, oc=oc: e.tensor_tensor(out=hT[:, oc, :], in0=hT[:, oc, :], in1=sg[s], op=ALU.add),
                         reads=[sgb[s], hTb], writes=[hTb])
            dst = outT if last else hS
            for q4 in range(4):
                P.dma(stq(), dst[q4 * 512:(q4 + 1) * 512, t0:t0 + G].rearrange("(c p) t -> p c t", p=128),
                      hT[:, q4 * 4:(q4 + 1) * 4, :], reads=[hTb])
        P.barrier()
        ar.pop()

    for li, l in enumerate(layers):
        if stop_after == "pre":
            break
        phase_A(li, l, xT if li == 0 else hS)
        if stop_after in ("A", "Aload", "Affn"):
            break
        exchange()
        phase_B(li, l, last=(li == L - 1))
    P.barrier()
    P.emit()
    return nc


def _consts():
    c = np.zeros((128, NCC), np.float32)
    c[:, CC_ONES:CC_ONES + 128] = 1.0
    j = np.arange(128)[:, None]
    s = np.arange(128)[None, :]
    c[:, CC_NTRI:CC_NTRI + 128] = -(j >= s).astype(np.float32)
    rot = np.zeros((128, 128), np.float32)
    for i in range(16):
        rot[16 + i, i] = -1.0
        rot[i, 16 + i] = 1.0
    c[:, CC_ROT:CC_ROT + 128] = rot
    tt = np.arange(512)[None, :]
    sp = np.arange(128)[:, None]
    for k in range(4):
        c[:, CC_MSB + k * 512:CC_MSB + (k + 1) * 512] = ((128 * k + sp) < tt).astype(np.float32)
        c[:, CC_MDF + k * 512:CC_MDF + (k + 1) * 512] = ((128 * k + sp) <= tt).astype(np.float32)
    return c


def _rope(pos0, T):
    pos = (pos0 + np.arange(T)).astype(np.float32)
    inv = (np.float32(500000.0) ** (-np.arange(0, 32, 2, dtype=np.float32) / np.float32(32))).astype(np.float32)
    ang = pos[:, None] * inv[None, :]
    cos = np.cos(ang).astype(np.float32).T
    sin = np.sin(ang).astype(np.float32).T
    r = np.zeros((128, 2, T), np.float32)
    r[:, 0, :] = 1.0
    r[0:16, 0, :] = cos
    r[16:32, 0, :] = cos
    r[0:16, 1, :] = sin
    r[16:32, 1, :] = sin
    return r


def _gains(inp, l, lam_layer):
    g = np.zeros((128, NGC), np.float32)

    def col(v):
        return np.asarray(v, np.float32).reshape(-1, 128).T

    g[:, GC_FFN1:GC_FFN1 + 16] = col(inp["ffn1_norm"][l])
    g[:, GC_MIX:GC_MIX + 16] = col(inp["mix_norm"][l])
    g[:, GC_FFN2:GC_FFN2 + 16] = col(inp["ffn2_norm"][l])
    g[:, GC_PLE:GC_PLE + 16] = col(inp["ple_norm"][l])
    g[:, GC_PLEO:GC_PLEO + 16] = col(inp["ple_out_norm"][l])
    g[:, GC_DQ:GC_DQ + 1] = col(inp["diff_q_norm"][l])
    g[:, GC_DK:GC_DK + 1] = col(inp["diff_k_norm"][l])
    g[:, GC_L + 0:GC_L + 1] = col(inp["diff_lambda_q1"][l])
    g[:, GC_L + 1:GC_L + 2] = col(inp["diff_lambda_k1"][l])
    g[:, GC_L + 2:GC_L + 3] = col(inp["diff_lambda_q2"][l])
    g[:, GC_L + 3:GC_L + 4] = col(inp["diff_lambda_k2"][l])
    g[:, GC_SUB:GC_SUB + 2] = col(inp["diff_sub_norm"][l])
    li = 0.8 - 0.6 * math.exp(-0.3 * lam_layer)
    g[:, GC_LI] = li
    g[:, GC_OML] = 1.0 - li
    return g


_NC_CACHE = {}


def _run(inp, TG, layers_list, dbg=None):
    T = TG * G
    S = 2 * T
    x = np.asarray(inp["x"], np.float32)
    B = x.shape[0]
    ncores = 2 * B
    cst = _consts()
    hT = [np.ascontiguousarray(x[c // 2, (c % 2) * T:(c % 2 + 1) * T, :].T) for c in range(ncores)]
    res = None
    for layers in layers_list:
        key = (TG, len(layers), str(dbg))
        if key not in _NC_CACHE:
            _NC_CACHE[key] = build(TG, layers, dbg)
        nc = _NC_CACHE[key]
        sl = slice(layers[0], layers[-1] + 1)
        shared = {
            "w_gu1": np.asarray(inp["ffn1_w_gu"][sl], np.float32), "w_gu2": np.asarray(inp["ffn2_w_gu"][sl], np.float32),
            "w_d1": np.asarray(inp["ffn1_w_down"][sl], np.float32), "w_d2": np.asarray(inp["ffn2_w_down"][sl], np.float32),
            "w_in": np.asarray(inp["w_in"][sl], np.float32), "w_a": np.asarray(inp["w_branch_a"][sl], np.float32),
            "w_b": np.asarray(inp["w_branch_b"][sl], np.float32), "w_o": np.asarray(inp["w_out"][sl], np.float32),
            "w_pg": np.asarray(inp["ple_w_gate"][sl], np.float32), "w_pp": np.asarray(inp["ple_w_proj"][sl], np.float32),
            "gains": np.stack([_gains(inp, l, l) for l in layers]), "cst": cst,
        }
        p = np.asarray(inp["p"], np.float32)
        in_maps = []
        for c in range(ncores):
            b, half = c // 2, c % 2
            m = dict(shared)
            m["xT"] = hT[c]
            m["pT"] = np.ascontiguousarray(np.stack([p[l, b, half * T:(half + 1) * T, :].T for l in layers]))
            m["rope"] = _rope(half * T, T)
            m["farbias"] = np.full((128, 1), 0.0 if half == 1 else -30000.0, np.float32)
            in_maps.append(m)
        res = run_bass_kernel_spmd(nc, in_maps, core_ids=list(range(ncores)))
        hT = [np.asarray(res.results[c]["outT"]) for c in range(ncores)]
    out = np.empty((B, S, D), np.float32)
    for c in range(ncores):
        out[c // 2, (c % 2) * T:(c % 2 + 1) * T, :] = hT[c].T
    return out, res


def kernel(**inputs):
    out, _ = _run(inputs, 4, [[0, 1]])
    return out
```

```python
import math
import types
import numpy as np
import ml_dtypes
import concourse.bass as bass
import concourse.mybir as mybir
from concourse.bass_utils import run_bass_kernel_spmd

F32 = mybir.dt.float32
BF16 = mybir.dt.bfloat16
AF = mybir.ActivationFunctionType
ALU = mybir.AluOpType

D = 2048
DC = 16
DFF = 5632
FC = 44
G = 512
DEPTH = 2
INW = 10240
EPS = 1e-6
CAP = 30000
NDS = 16
SCALE = 128 ** -0.5
import os as _os
SKEW_SB = tuple(int(x) for x in _os.environ.get('SKEW_SB', '1,1').split(','))
SKEW_DF = int(_os.environ.get('SKEW_DF', '1'))

GC_FFN1, GC_MIX, GC_FFN2, GC_PLE, GC_PLEO = 0, 16, 32, 48, 64
GC_DQ, GC_DK, GC_L, GC_SUB, GC_LI, GC_OML = 80, 81, 82, 86, 88, 89
NGC = 90
CC_ONES, CC_NTRI, CC_ROT, CC_MSB, CC_MDF = 0, 128, 256, 384, 384 + 2048
NCC = 384 + 4096


def _freeze(fn):
    if fn is None or fn.__closure__ is None:
        return fn
    cells = []
    for c in fn.__closure__:
        try:
            cells.append(types.CellType(c.cell_contents))
        except ValueError:
            cells.append(c)
    return types.FunctionType(fn.__code__, fn.__globals__, fn.__name__, fn.__defaults__, tuple(cells))


class Buf:
    __slots__ = ("w", "r", "name")

    def __init__(self, name=""):
        self.w = None
        self.r = {}
        self.name = name


class Prog:
    ENGS = ["pe", "act", "dve", "pool", "sp"]

    def __init__(self, nc):
        self.nc = nc
        self.streams = {e: [] for e in self.ENGS}
        self.cnt = {e: 0 for e in self.ENGS}
        self.seen = {e: {} for e in self.ENGS}
        self.ndma = {}
        self.dma_tokens_live = {}
        self.extra_sems = []

    def _need_wait(self, eng, tok):
        if tok[0] == "e":
            _, te, k = tok
            if te == eng and eng == "pe":
                return None
            if self.seen[eng].get(te, 0) >= k:
                return None
            self.seen[eng][te] = k
            return ("e", te, k)
        elif tok[0] == "d":
            _, q, i = tok
            slot = i % NDS
            val = 16 * (i // NDS + 1)
            key = ("d", q, slot)
            if self.seen[eng].get(key, 0) >= val:
                return None
            self.seen[eng][key] = val
            return ("d", q, slot, val)
        else:
            key = tok
            if self.seen[eng].get(key, 0) >= 1:
                return None
            self.seen[eng][key] = 1
            return tok

    def _deps(self, eng, reads, writes):
        toks = []
        for b in reads:
            if b.w is not None:
                toks.append(b.w)
        for b in writes:
            if b.w is not None:
                toks.append(b.w)
            toks.extend(b.r.values())
        waits = []
        for t in toks:
            w = self._need_wait(eng, t)
            if w is not None:
                waits.append(w)
        return waits

    def _mark(self, tok, key, reads, writes):
        for b in reads:
            b.r[key] = tok
        for b in writes:
            b.w = tok
            b.r = {}

    def op(self, eng, fn, reads=(), writes=(), inc=True):
        if eng != "pe":
            inc = True
        waits = self._deps(eng, reads, writes)
        k = self.cnt[eng] + 1
        tok = ("e", eng, k)
        if inc:
            self.cnt[eng] = k
        self.streams[eng].append((waits, _freeze(fn), k if inc else None, None))
        self._mark(tok, eng, reads, writes)
        return tok

    def dma(self, eng, out, in_, reads=(), writes=()):
        waits = self._deps(eng, reads, writes)
        i = self.ndma.get(eng, 0)
        self.ndma[eng] = i + 1
        if i >= NDS:
            w = self._need_wait(eng, ("d", eng, i - NDS))
            if w is not None:
                waits.append(w)
        tok = ("d", eng, i)
        self.streams[eng].append((waits, lambda e: e.dma_start(out=out, in_=in_), None, i))
        self._mark(tok, tok, reads, writes)
        live = self.dma_tokens_live.setdefault(eng, [])
        live.append(tok)
        if len(live) > NDS:
            del live[:-NDS]
        return tok

    def barrier(self, extra=()):
        toks = [("e", e, self.cnt[e]) for e in self.ENGS if self.cnt[e] > 0]
        for live in self.dma_tokens_live.values():
            toks += list(live)
        toks += list(extra)
        for eng in self.ENGS:
            waits = []
            for t in toks:
                if t[0] == "e" and t[1] == eng and eng == "pe":
                    continue
                w = self._need_wait(eng, t)
                if w is not None:
                    waits.append(w)
            if waits:
                self.streams[eng].append((waits, None, None, None))

    def emit(self):
        nc = self.nc
        nsem = {e: (self.cnt[e] + CAP - 1) // CAP + 1 for e in self.ENGS}
        sems = {e: [nc.alloc_semaphore(f"s_{e}_{j}") for j in range(nsem[e])] for e in self.ENGS}
        dsems = {q: [nc.alloc_semaphore(f"s_dma_{q}_{j}") for j in range(NDS)] for q in self.ndma}
        xsems = self.extra_sems
        handles = {"pe": "tensor", "act": "scalar", "dve": "vector", "pool": "gpsimd", "sp": "sync"}

        def run(ename, eng):
            for waits, fn, k, di in self.streams[ename]:
                for w in waits:
                    if w[0] == "e":
                        _, te, kk = w
                        eng.wait_ge(sems[te][(kk - 1) // CAP], (kk - 1) % CAP + 1)
                    elif w[0] == "d":
                        eng.wait_ge(dsems[w[1]][w[2]], w[3])
                    else:
                        eng.wait_ge(xsems[w[1]], 1)
                if fn is None:
                    continue
                ins = fn(eng)
                if k is not None:
                    ins.then_inc(sems[ename][(k - 1) // CAP], 1)
                if di is not None:
                    ins.then_inc(dsems[ename][di % NDS], 16)

        with nc.Block() as block:
            @block.tensor
            def _(e):
                run("pe", e)

            @block.scalar
            def _(e):
                run("act", e)

            @block.vector
            def _(e):
                run("dve", e)

            @block.gpsimd
            def _(e):
                run("pool", e)

            @block.sync
            def _(e):
                run("sp", e)


class Arena:
    def __init__(self, nc, nbytes):
        self.t = nc.alloc_sbuf_tensor("arena", [128, nbytes // 2], BF16)
        self.nbytes = nbytes
        self.off = 0
        self.marks = []

    def push(self):
        self.marks.append(self.off)

    def pop(self):
        self.off = self.marks.pop()

    def alloc(self, shape, dtype):
        es = 4 if dtype == F32 else 2
        n = int(np.prod(shape[1:]))
        nb = n * es
        nb = (nb + 63) // 64 * 64
        assert self.off + nb <= self.nbytes, f"arena overflow {self.off + nb} > {self.nbytes}"
        o = self.off // 2
        ap = self.t[:, o:o + nb // 2]
        self.off += nb
        if dtype == F32:
            ap = ap.bitcast(F32)
        ap = ap[:, 0:n]
        if len(shape) == 3:
            ap = ap.rearrange("p (a b) -> p a b", a=shape[1])
        return ap


def build(TG, layers, dbg=None):
    T = TG * G
    NKB = T // 128
    L = len(layers)
    nc = bass.Bass("TRN2", target_bir_lowering=False)
    P = Prog(nc)

    def din(name, shape, dt=F32):
        return nc.dram_tensor(name, shape, dt, kind="ExternalInput").ap()

    def dscr(name, shape, dt, out=False):
        kind = "ExternalOutput" if (out or (dbg and name in dbg)) else "Internal"
        return nc.dram_tensor(name, shape, dt, kind=kind)

    xT = din("xT", [D, T])
    pT = din("pT", [L, 256, T])
    w_gu = [din("w_gu1", [L, D, 2 * DFF]), din("w_gu2", [L, D, 2 * DFF])]
    w_dn = [din("w_d1", [L, DFF, D]), din("w_d2", [L, DFF, D])]
    w_in = din("w_in", [L, D, INW])
    w_a = din("w_a", [L, 1024, D])
    w_b = din("w_b", [L, 1024, D])
    w_o = din("w_o", [L, D, D])
    w_pg = din("w_pg", [L, D, D])
    w_pp = din("w_pp", [L, 256, D])
    gains_d = din("gains", [L, 128, NGC])
    cst_d = din("cst", [128, NCC])
    rope_d = din("rope", [128, 2, T])
    farb_d = din("farbias", [128, 1])

    outT = nc.dram_tensor("outT", [D, T], F32, kind="ExternalOutput").ap()
    hS = dscr("hS", [D, T], F32).ap()
    qA = dscr("qA", [1024, T], BF16).ap()
    qB = dscr("qB", [1024, T], BF16).ap()
    gts = dscr("gts", [4096, T], BF16).ap()
    kloc = dscr("kloc", [2048, T], BF16)
    vloc = dscr("vloc", [T, 2048], BF16)
    kall = dscr("kall", [4096, T], BF16)
    vall = dscr("vall", [2 * T, 2048], BF16)
    yAd = dscr("yAd", [1024, T], BF16).ap()
    yBd = dscr("yBd", [1024, T], BF16).ap()

    ar = Arena(nc, 188 * 1024)
    ps = [nc.alloc_psum_tensor(f"ps{i}", [128, 512], F32).ap() for i in range(8)]
    psb = [Buf(f"ps{i}") for i in range(8)]

    cst = ar.alloc([128, 384], BF16)
    cstb = Buf("cst")
    gains = ar.alloc([128, L * NGC], F32)
    gainsb = Buf("gains")
    farb = ar.alloc([128, 1], F32)
    lam = ar.alloc([128, 2 * L], F32)
    lamb = Buf("lam")
    P.dma("pool", cst, cst_d[:, 0:384], writes=[cstb])
    for l in range(L):
        P.dma("sp", gains[:, l * NGC:(l + 1) * NGC], gains_d[l], writes=[gainsb])
    P.dma("sp", farb, farb_d, writes=[gainsb])
    ones = cst[:, CC_ONES:CC_ONES + 128]
    ntri = cst[:, CC_NTRI:CC_NTRI + 128]
    rotm = cst[:, CC_ROT:CC_ROT + 128]

    def gcol(l, c, n=1):
        return gains[:, l * NGC + c:l * NGC + c + n]

    ar.push()
    ltmp = ar.alloc([128, 8], F32)
    ltb = Buf("ltmp")
    ones32 = ar.alloc([128, 128], F32)
    o32b = Buf("ones32")
    P.op("dve", lambda e: e.memset(ones32, 1.0), writes=[o32b])
    for l in range(L):
        P.op("dve", lambda e, l=l: e.tensor_tensor(out=ltmp[:, 0:1], in0=gcol(l, GC_L), in1=gcol(l, GC_L + 1), op=ALU.mult),
             reads=[gainsb], writes=[ltb])
        P.op("dve", lambda e, l=l: e.tensor_tensor(out=ltmp[:, 1:2], in0=gcol(l, GC_L + 2), in1=gcol(l, GC_L + 3), op=ALU.mult),
             reads=[gainsb], writes=[ltb])
        P.op("pe", lambda e: e.matmul(ps[7][:, 0:2], ones32, ltmp[:, 0:2], start=True, stop=True),
             reads=[ltb, o32b], writes=[psb[7]])
        P.op("act", lambda e: e.activation(out=ltmp[:, 2:4], in_=ps[7][:, 0:2], func=AF.Exp), reads=[psb[7]], writes=[ltb])
        P.op("dve", lambda e: e.tensor_tensor(out=ltmp[:, 4:5], in0=ltmp[:, 2:3], in1=ltmp[:, 3:4], op=ALU.subtract),
             reads=[ltb], writes=[ltb])
        P.op("dve", lambda e, l=l: e.tensor_tensor(out=lam[:, 2 * l:2 * l + 1], in0=ltmp[:, 4:5], in1=gcol(l, GC_LI), op=ALU.add),
             reads=[ltb, gainsb], writes=[lamb])
        P.op("dve", lambda e, l=l: e.tensor_scalar(out=lam[:, 2 * l + 1:2 * l + 2], in0=lam[:, 2 * l:2 * l + 1], scalar1=-1.0, scalar2=None, op0=ALU.mult),
             reads=[lamb], writes=[lamb])
    P.barrier()
    ar.pop()
    stop_after = dbg.get("stop") if dbg else None

    dmaq = ["sp", "act"]
    rr = [0]

    def hwq():
        return "sp"

    def stq():
        return "act"

    def rms_bc(src, nch, bank, out_rstd, rstdb, srcb, Dn, tmps, tmpbs):
        nt = len(tmps)
        for c in range(nch):
            sq, sqb = tmps[c % nt], tmpbs[c % nt]
            eng = "dve" if c % 2 == 0 else "pool"
            P.op(eng, lambda e, c=c, sq=sq: e.tensor_tensor(out=sq, in0=src[:, c, :], in1=src[:, c, :], op=ALU.mult),
                 reads=[srcb], writes=[sqb])
            P.op("pe", lambda e, c=c, sq=sq: e.matmul(ps[bank], ones, sq, start=(c == 0), stop=(c == nch - 1)),
                 reads=[sqb, cstb], writes=[psb[bank]], inc=True)
        P.op("act", lambda e: e.activation(out=out_rstd, in_=ps[bank], func=AF.Ln, bias=EPS, scale=1.0 / Dn),
             reads=[psb[bank]], writes=[rstdb])
        P.op("act", lambda e: e.activation(out=out_rstd, in_=out_rstd, func=AF.Exp, scale=-0.5),
             reads=[rstdb], writes=[rstdb])

    class WStream:
        def __init__(self, nslots, nelem=8192):
            self.slots = [ar.alloc([128, nelem], BF16) for _ in range(nslots)]
            self.bufs = [Buf(f"w{i}") for i in range(nslots)]
            self.i = 0

        def load(self, src_ap, kc, ncol):
            s = self.i % len(self.slots)
            self.i += 1
            dst = self.slots[s][:, 0:kc * ncol].rearrange("p (c n) -> p c n", c=kc)
            P.dma("pool", dst, src_ap, writes=[self.bufs[s]])
            return dst, self.bufs[s]

    def wsrc(w2d, c0, ncol):
        return w2d[:, c0:c0 + ncol].rearrange("(c p) n -> p c n", p=128)

    def ffn(l, which, hT, hTb, xn, xnb, hid, hidb, ws, rstd, rstdb, tmps, tmpbs, sg, sgb):
        gc = GC_FFN1 if which == 0 else GC_FFN2
        rms_bc(hT, DC, 6, rstd, rstdb, hTb, float(D), tmps, tmpbs)
        for c in range(DC):
            eng = "dve"
            P.op(eng, lambda e, c=c: e.scalar_tensor_tensor(out=xn[:, c, :], in0=hT[:, c, :], scalar=gcol(l, gc + c),
                                                            in1=rstd, op0=ALU.mult, op1=ALU.mult),
                 reads=[hTb, rstdb, gainsb], writes=[xnb])
        wgu = w_gu[which][l]
        wdn = w_dn[which][l]
        for fb in range(FC // 4):
            wg, wgb = ws.load(wsrc(wgu, fb * 512, 512), DC, 512)
            wu, wub = ws.load(wsrc(wgu, DFF + fb * 512, 512), DC, 512)
            for j in range(4):
                fc = fb * 4 + j
                pg, pu = (0, 1) if fc % 2 == 0 else (2, 3)
                for kc in range(DC):
                    P.op("pe", lambda e, kc=kc, j=j, wg=wg, pg=pg: e.matmul(ps[pg], wg[:, kc, j * 128:(j + 1) * 128], xn[:, kc, :],
                                                                            start=(kc == 0), stop=(kc == DC - 1)),
                         reads=[wgb, xnb], writes=[psb[pg]], inc=(kc == DC - 1))
                for kc in range(DC):
                    P.op("pe", lambda e, kc=kc, j=j, wu=wu, pu=pu: e.matmul(ps[pu], wu[:, kc, j * 128:(j + 1) * 128], xn[:, kc, :],
                                                                            start=(kc == 0), stop=(kc == DC - 1)),
                         reads=[wub, xnb], writes=[psb[pu]], inc=(kc == DC - 1))
                s = fc % 2
                P.op("act", lambda e, pg=pg, s=s: e.activation(out=sg[s], in_=ps[pg], func=AF.Silu), reads=[psb[pg]], writes=[sgb[s]])
                P.op("dve", lambda e, pu=pu, s=s, fc=fc: e.tensor_tensor(out=hid[:, fc, :], in0=sg[s], in1=ps[pu], op=ALU.mult),
                     reads=[sgb[s], psb[pu]], writes=[hidb])
        dbanks = [4, 5, 0, 1]
        for db in range(8):
            halves = []
            for kh in range(2):
                halves.append(ws.load(wdn[kh * 2816:(kh + 1) * 2816, db * 256:(db + 1) * 256].rearrange("(c p) n -> p c n", p=128), 22, 256))
            for kh in range(2):
                wd, wdb = halves[kh]
                for j in range(2):
                    pd = dbanks[(db % 2) * 2 + j]
                    for fc in range(22):
                        last = (kh == 1 and fc == 21)
                        P.op("pe", lambda e, fc=fc, kh=kh, j=j, wd=wd, pd=pd, last=last: e.matmul(
                            ps[pd], wd[:, fc, j * 128:(j + 1) * 128], hid[:, kh * 22 + fc, :], start=(kh == 0 and fc == 0), stop=last),
                             reads=[wdb, hidb], writes=[psb[pd]], inc=last)
            for j in range(2):
                dc = db * 2 + j
                pd = dbanks[(db % 2) * 2 + j]
                P.op("dve", lambda e, dc=dc, pd=pd: e.scalar_tensor_tensor(out=hT[:, dc, :], in0=ps[pd], scalar=0.5, in1=hT[:, dc, :],
                                                                           op0=ALU.mult, op1=ALU.add),
                     reads=[psb[pd], hTb], writes=[hTb])

    def phase_A(li, l, src_h):
        ar.push()
        hT = ar.alloc([128, DC, G], F32); hTb = Buf("hT")
        xn = ar.alloc([128, DC, G], BF16); xnb = Buf("xn")
        hid = ar.alloc([128, FC, G], BF16); hidb = Buf("hid")
        ws = WStream(4)
        rstd = ar.alloc([128, G], F32); rstdb = Buf("rstd")
        rs2 = [ar.alloc([128, G], F32) for _ in range(2)]; rs2b = [Buf("rs20"), Buf("rs21")]
        tmps = [ar.alloc([128, G], BF16) for _ in range(4)]; tmpbs = [Buf(f"tmp{i}") for i in range(4)]
        sg = [ar.alloc([128, G], F32) for _ in range(2)]; sgb = [Buf("sg0"), Buf("sg1")]
        stg = [ar.alloc([128, G], BF16) for _ in range(4)]; stgb = [Buf(f"stg{i}") for i in range(4)]
        xg = [ar.alloc([128, G], BF16) for _ in range(2)]; xgb = [Buf("xg0"), Buf("xg1")]
        x32 = [ar.alloc([128, G], F32) for _ in range(2)]; x32b = [Buf("x320"), Buf("x321")]
        rp = ar.alloc([128, 2, G], F32); rpb = Buf("rope")
        sti = [0]

        def stage():
            i = sti[0] % 4
            sti[0] += 1
            return stg[i], stgb[i]

        for g in range(TG):
            t0 = g * G
            for q4 in range(4):
                P.dma(hwq(), hT[:, q4 * 4:(q4 + 1) * 4, :],
                      src_h[q4 * 512:(q4 + 1) * 512, t0:t0 + G].rearrange("(c p) t -> p c t", p=128), writes=[hTb])
            P.dma(hwq(), rp, rope_d[:, :, t0:t0 + G], writes=[rpb])
            if stop_after != "Aload":
                ffn(li, 0, hT, hTb, xn, xnb, hid, hidb, ws, rstd, rstdb, tmps, tmpbs, sg, sgb)
            for q4 in range(4):
                P.dma(stq(), hS[q4 * 512:(q4 + 1) * 512, t0:t0 + G].rearrange("(c p) t -> p c t", p=128),
                      hT[:, q4 * 4:(q4 + 1) * 4, :], reads=[hTb])
            if stop_after in ("Aload", "Affn"):
                continue
            rms_bc(hT, DC, 6, rstd, rstdb, hTb, float(D), tmps, tmpbs)
            for c in range(DC):
                eng = "dve"
                P.op(eng, lambda e, c=c: e.scalar_tensor_tensor(out=xn[:, c, :], in0=hT[:, c, :], scalar=gcol(li, GC_MIX + c),
                                                                in1=rstd, op0=ALU.mult, op1=ALU.mult),
                     reads=[hTb, rstdb, gainsb], writes=[xnb])
            win = w_in[li]
            pbank = [0]

            def nextbank():
                b = pbank[0] % 4
                pbank[0] += 1
                return b

            for cb in range(20):
                wb_, wbb = ws.load(wsrc(win, cb * 512, 512), DC, 512)
                kind = ["sbq", "sbk", "sbv", "dfq", "dfk", "dfv", "ga", "ga", "gb", "gb"][cb // 2]
                if kind in ("sbv", "dfv"):
                    for tt in range(4):
                        b = nextbank()
                        for kc in range(DC):
                            P.op("pe", lambda e, kc=kc, tt=tt, b=b, wb_=wb_: e.matmul(ps[b], xn[:, kc, tt * 128:(tt + 1) * 128], wb_[:, kc, :],
                                                                                      start=(kc == 0), stop=(kc == DC - 1)),
                                 reads=[wbb, xnb], writes=[psb[b]], inc=(kc == DC - 1))
                        st, stb = stage()
                        P.op("act", lambda e, b=b, st=st: e.activation(out=st, in_=ps[b], func=AF.Copy), reads=[psb[b]], writes=[stb])
                        col0 = (0 if kind == "sbv" else 1024) + (cb % 2) * 512
                        P.dma(stq(), vloc[t0 + tt * 128:t0 + (tt + 1) * 128, col0:col0 + 512], st, reads=[stb])
                    continue
                for j in range(4):
                    b = nextbank()
                    oc = (cb % 2) * 4 + j
                    for kc in range(DC):
                        P.op("pe", lambda e, kc=kc, j=j, b=b, wb_=wb_: e.matmul(ps[b], wb_[:, kc, j * 128:(j + 1) * 128], xn[:, kc, :],
                                                                                start=(kc == 0), stop=(kc == DC - 1)),
                             reads=[wbb, xnb], writes=[psb[b]], inc=(kc == DC - 1))
                    st, stb = stage()
                    if kind == "sbq":
                        P.op("act", lambda e, b=b, st=st: e.activation(out=st, in_=ps[b], func=AF.Copy, scale=SCALE), reads=[psb[b]], writes=[stb])
                        P.dma(stq(), qA[oc * 128:(oc + 1) * 128, t0:t0 + G], st, reads=[stb])
                    elif kind == "sbk":
                        P.op("act", lambda e, b=b, st=st: e.activation(out=st, in_=ps[b], func=AF.Copy), reads=[psb[b]], writes=[stb])
                        P.dma(stq(), kloc[oc * 128:(oc + 1) * 128, t0:t0 + G], st, reads=[stb])
                    elif kind in ("ga", "gb"):
                        gch = ((cb - 12) * 4 + j)
                        P.op("act", lambda e, b=b, st=st: e.activation(out=st, in_=ps[b], func=AF.Sigmoid), reads=[psb[b]], writes=[stb])
                        P.dma(stq(), gts[gch * 128:(gch + 1) * 128, t0:t0 + G], st, reads=[stb])
                    else:
                        isq = kind == "dfq"
                        gcn = GC_DQ if isq else GC_DK
                        s = oc % 2
                        bss, brt = 4 + s, 6 + s
                        P.op("act", lambda e, b=b, s=s: e.activation(out=tmps[s], in_=ps[b], func=AF.Square), reads=[psb[b]], writes=[tmpbs[s]])
                        P.op("act", lambda e, b=b, s=s, gcn=gcn: e.activation(out=x32[s], in_=ps[b], func=AF.Copy, scale=gcol(li, gcn)),
                             reads=[psb[b], gainsb], writes=[x32b[s]])
                        P.op("pe", lambda e, s=s, bss=bss: e.matmul(ps[bss], ones, tmps[s], start=True, stop=True), reads=[tmpbs[s], cstb], writes=[psb[bss]])
                        P.op("dve", lambda e, s=s: e.tensor_copy(out=xg[s], in_=x32[s]), reads=[x32b[s]], writes=[xgb[s]])
                        P.op("pe", lambda e, s=s, brt=brt: e.matmul(ps[brt], rotm, xg[s], start=True, stop=True), reads=[xgb[s], cstb], writes=[psb[brt]])
                        P.op("act", lambda e, s=s, bss=bss: e.activation(out=rs2[s], in_=ps[bss], func=AF.Ln, bias=EPS, scale=1.0 / 128),
                             reads=[psb[bss]], writes=[rs2b[s]])
                        P.op("act", lambda e, s=s, isq=isq: e.activation(out=rs2[s], in_=rs2[s], func=AF.Exp, scale=-0.5, bias=(math.log(SCALE) if isq else 0.0)),
                             reads=[rs2b[s]], writes=[rs2b[s]])
                        P.op("dve", lambda e, s=s: e.tensor_tensor(out=x32[s], in0=x32[s], in1=rp[:, 0, :], op=ALU.mult),
                             reads=[x32b[s], rpb], writes=[x32b[s]])
                        P.op("dve", lambda e, s=s, brt=brt: e.tensor_tensor(out=sg[s], in0=ps[brt], in1=rp[:, 1, :], op=ALU.mult),
                             reads=[psb[brt], rpb], writes=[sgb[s]])
                        P.op("dve", lambda e, s=s: e.tensor_tensor(out=x32[s], in0=x32[s], in1=sg[s], op=ALU.add),
                             reads=[x32b[s], sgb[s]], writes=[x32b[s]])
                        P.op("dve", lambda e, s=s, st=st: e.tensor_tensor(out=st, in0=x32[s], in1=rs2[s], op=ALU.mult),
                             reads=[x32b[s], rs2b[s]], writes=[stb])
                        if isq:
                            P.dma(stq(), qB[oc * 128:(oc + 1) * 128, t0:t0 + G], st, reads=[stb])
                        else:
                            P.dma(stq(), kloc[1024 + oc * 128:1024 + (oc + 1) * 128, t0:t0 + G], st, reads=[stb])
        P.barrier()
        ar.pop()

    RK = min(2048, (1 << 19) // T)
    NKC = 2048 // RK
    RV = 256
    NVC = T // RV

    def kfar(r0, n):
        j, w = r0 // RK, r0 % RK
        return kall[j * 2 * RK + w:j * 2 * RK + w + n, :]

    def exchange():
        groups = [[0, 1], [2, 3], [4, 5], [6, 7]]
        toks = []
        jobs = [(kloc, kall, RK, j) for j in range(NKC)] + [(vloc, vall, RV, j) for j in range(NVC)]
        for (src, dst, R_, j) in jobs:
            sem = nc.alloc_semaphore(f"cc{len(P.extra_sems)}")
            idx = len(P.extra_sems)
            P.extra_sems.append(sem)
            P.streams["pool"].append(([], (lambda e, src=src, dst=dst, sem=sem, R_=R_, j=j: e.collective_compute(
                "AllGather", ALU.bypass, replica_groups=groups, ins=[src[j * R_:(j + 1) * R_, :]],
                outs=[dst[j * 2 * R_:(j + 1) * 2 * R_, :]]).then_inc(sem, 1)), None, None))
            toks.append(("x", idx))
        P.barrier(extra=toks)

    def phase_B(li, l, last):
        ar.push()
        ar.push()
        msk = ar.alloc([128, 4096], BF16); mskb = Buf("msk")
        P.dma("pool", msk, cst_d[:, 384:384 + 4096], writes=[mskb])
        yst = [ar.alloc([128, G], BF16) for _ in range(4)]; ystb = [Buf(f"yst{i}") for i in range(4)]
        ysti = [0]
        kT = [ar.alloc([128, 4, T], BF16) for _ in range(2)]; kTb = [Buf("kT0"), Buf("kT1")]
        vv = [ar.alloc([128, 2 * NKB, 256], BF16) for _ in range(2)]; vvb = [Buf("v0"), Buf("v1")]
        qt = [ar.alloc([128, G], BF16) for _ in range(3)]; qtb = [Buf(f"qt{i}") for i in range(3)]
        e32 = [ar.alloc([128, G], F32) for _ in range(2)]; e32b = [Buf("e0"), Buf("e1")]
        sp = [ar.alloc([128, G], BF16) for _ in range(4)]; spb = [Buf(f"sp{i}") for i in range(4)]
        at = [ar.alloc([128, G], BF16) for _ in range(3)]; atb = [Buf(f"at{i}") for i in range(3)]
        R32 = ar.alloc([128, G], F32); R32b = Buf("R32")
        Rbf = [ar.alloc([128, G], BF16) for _ in range(3)]; Rbfb = [Buf(f"Rbf{i}") for i in range(3)]
        n1 = ar.alloc([128, 2, G], F32); n1b = Buf("n1")
        rc = ar.alloc([128, G], F32); rcb = Buf("rc")
        y32 = ar.alloc([128, 2, G], F32); y32b = Buf("y32")
        tq = ar.alloc([128, G], BF16); tqb = Buf("tq")
        msb = [msk[:, k * 512:(k + 1) * 512] for k in range(4)]
        mdf = [msk[:, 2048 + k * 512:2048 + (k + 1) * 512] for k in range(4)]
        cnt = {"q": 0, "e": 0, "sp": 0, "at": 0, "z": 0, "zs": 0, "p": 0, "y": 0, "R": 0, "hd": 0}

        def rot(name, n):
            i = cnt[name] % n
            cnt[name] += 1
            return i

        for h in range(8):
            hb = rot("hd", 2)
            k_, k_b = kT[hb], kTb[hb]
            v_, v_b = vv[hb], vvb[hb]
            P.dma(hwq(), k_[:, 0, :], kloc[h * 128:(h + 1) * 128, :], writes=[k_b])
            P.dma(hwq(), k_[:, 1, :], kfar(h * 128, 128), writes=[k_b])
            P.dma(hwq(), v_[:, 0:NKB, 0:128], vloc[:, h * 128:(h + 1) * 128].rearrange("(b p) c -> p b c", p=128), writes=[v_b])
            for j in range(NVC):
                P.dma(hwq(), v_[:, NKB + 2 * j:NKB + 2 * j + 2, 0:128],
                      vall[j * 512:j * 512 + 256, h * 128:(h + 1) * 128].rearrange("(b p) c -> p b c", p=128), writes=[v_b])
            for g in range(TG):
                t0 = g * G
                qi = rot("q", 3)
                P.dma(hwq(), qt[qi], qA[h * 128:(h + 1) * 128, t0:t0 + G], writes=[qtb[qi]])
                yb = 4 + rot("y", 2)
                blocks = [(0, kb) for kb in range(4 * g + 3, -1, -1)] + [(1, kb) for kb in range(NKB - 1, -1, -1)]
                nblk = len(blocks)
                st_ = {}

                def stA(bi):
                    far, kb = blocks[bi]
                    kblk = k_[:, far, kb * 128:(kb + 1) * 128]
                    dk = (kb - 4 * g) if (far == 0 and kb >= 4 * g) else None
                    zb = rot("zs", 4)
                    P.op("pe", lambda e: e.matmul(ps[zb], kblk, qt[qi], start=True, stop=True),
                         reads=[k_b, qtb[qi]], writes=[psb[zb]])
                    ei = rot("e", 2)
                    P.op("act", lambda e: e.activation(out=e32[ei], in_=ps[zb], func=AF.Exp), reads=[psb[zb]], writes=[e32b[ei]])
                    si = rot("sp", 4)
                    P.op("act", lambda e: e.activation(out=sp[si], in_=e32[ei], func=AF.Ln, bias=1.0, scale=1.0),
                         reads=[e32b[ei]], writes=[spb[si]])
                    if dk is not None:
                        P.op("dve", lambda e: e.tensor_tensor(out=sp[si], in0=sp[si], in1=msb[dk], op=ALU.mult),
                             reads=[spb[si], mskb], writes=[spb[si]])
                    Ri = None
                    if bi < nblk - 1:
                        if bi == 0:
                            P.op("dve", lambda e: e.tensor_scalar(out=R32, in0=sp[si], scalar1=-1.0, scalar2=None, op0=ALU.mult),
                                 reads=[spb[si]], writes=[R32b])
                        else:
                            P.op("dve", lambda e: e.tensor_tensor(out=R32, in0=R32, in1=sp[si], op=ALU.subtract),
                                 reads=[spb[si], R32b], writes=[R32b])
                        Ri = rot("R", 3)
                        P.op("dve", lambda e: e.tensor_copy(out=Rbf[Ri], in_=R32), reads=[R32b], writes=[Rbfb[Ri]])
                    st_[bi] = dict(kblk=kblk, dk=dk, si=si, Ri=Ri, far=far, kb=kb, zb=zb)

                def stB(bi):
                    d = st_[bi]
                    kblk, dk, si, far = d["kblk"], d["dk"], d["si"], d["far"]
                    pb = d["zb"]
                    P.op("pe", lambda e: e.matmul(ps[pb], ntri, sp[si], start=False, stop=(bi == 0)),
                         reads=[spb[si], cstb], writes=[psb[pb]], inc=(bi == 0))
                    if bi > 0:
                        Rp = st_[bi - 1]["Ri"]
                        P.op("pe", lambda e: e.matmul(ps[pb], ones, Rbf[Rp], start=False, stop=True),
                             reads=[Rbfb[Rp], cstb], writes=[psb[pb]])
                    ai = rot("at", 3)
                    if far:
                        P.op("act", lambda e: e.activation(out=at[ai], in_=ps[pb], func=AF.Exp, bias=farb[:, 0:1], scale=1.0),
                             reads=[psb[pb], gainsb], writes=[atb[ai]])
                    else:
                        P.op("act", lambda e: e.activation(out=at[ai], in_=ps[pb], func=AF.Exp), reads=[psb[pb]], writes=[atb[ai]])
                    if dk is not None:
                        P.op("dve", lambda e: e.tensor_tensor(out=at[ai], in0=at[ai], in1=msb[dk], op=ALU.mult),
                             reads=[atb[ai], mskb], writes=[atb[ai]])
                    d["ai"] = ai

                def stC(bi):
                    d = st_[bi]
                    ai = d["ai"]
                    vblk = v_[:, d["far"] * NKB + d["kb"], 0:128]
                    P.op("pe", lambda e: e.matmul(ps[yb], vblk, at[ai], start=(bi == 0), stop=(bi == nblk - 1)),
                         reads=[v_b, atb[ai]], writes=[psb[yb]], inc=(bi == nblk - 1))

                sB, sC = SKEW_SB
                for it in range(nblk + sC):
                    if it < nblk:
                        stA(it)
                    if sB <= it < nblk + sB:
                        stB(it - sB)
                    if it >= sC:
                        stC(it - sC)
                yi = ysti[0] % 4
                ysti[0] += 1
                P.op("act", lambda e, yb=yb, yi=yi: e.activation(out=yst[yi], in_=ps[yb], func=AF.Copy), reads=[psb[yb]], writes=[ystb[yi]])
                P.dma(stq(), yAd[h * 128:(h + 1) * 128, t0:t0 + G], yst[yi], reads=[ystb[yi]])

        for h in range(4):
            hb = rot("hd", 2)
            k_, k_b = kT[hb], kTb[hb]
            v_, v_b = vv[hb], vvb[hb]
            for half in range(2):
                r0 = 1024 + (2 * h + half) * 128
                P.dma(hwq(), k_[:, half, :], kloc[r0:r0 + 128, :], writes=[k_b])
                P.dma(hwq(), k_[:, 2 + half, :], kfar(r0, 128), writes=[k_b])
            c0 = 1024 + h * 256
            P.dma(hwq(), v_[:, 0:NKB, :], vloc[:, c0:c0 + 256].rearrange("(b p) c -> p b c", p=128), writes=[v_b])
            for j in range(NVC):
                P.dma(hwq(), v_[:, NKB + 2 * j:NKB + 2 * j + 2, :],
                      vall[j * 512:j * 512 + 256, c0:c0 + 256].rearrange("(b p) c -> p b c", p=128), writes=[v_b])
            for g in range(TG):
                t0 = g * G
                blocks = [(0, kb) for kb in range(4 * g + 3, -1, -1)] + [(1, kb) for kb in range(NKB - 1, -1, -1)]
                nblk = len(blocks)
                for half in range(2):
                    qi = rot("q", 3)
                    r0 = (2 * h + half) * 128
                    P.dma(hwq(), qt[qi], qB[r0:r0 + 128, t0:t0 + G], writes=[qtb[qi]])
                    ysel = rot("y", 2)
                    ya, yb2, db = (2, 3, 6) if ysel == 0 else (4, 5, 7)
                    dst_ = {}

                    def dA(bi):
                        far, kb = blocks[bi]
                        kblk = k_[:, 2 * far + half, kb * 128:(kb + 1) * 128]
                        dk = (kb - 4 * g) if (far == 0 and kb >= 4 * g) else None
                        zb = rot("z", 2)
                        P.op("pe", lambda e: e.matmul(ps[zb], kblk, qt[qi], start=True, stop=True),
                             reads=[k_b, qtb[qi]], writes=[psb[zb]])
                        ai = rot("at", 3)
                        if far:
                            P.op("act", lambda e: e.activation(out=at[ai], in_=ps[zb], func=AF.Exp, bias=farb[:, 0:1], scale=1.0),
                                 reads=[psb[zb], gainsb], writes=[atb[ai]])
                        else:
                            P.op("act", lambda e: e.activation(out=at[ai], in_=ps[zb], func=AF.Exp), reads=[psb[zb]], writes=[atb[ai]])
                        if dk is not None:
                            P.op("dve", lambda e: e.tensor_tensor(out=at[ai], in0=at[ai], in1=mdf[dk], op=ALU.mult),
                                 reads=[atb[ai], mskb], writes=[atb[ai]])
                        dst_[bi] = (far, kb, ai)

                    def dB(bi):
                        far, kb, ai = dst_[bi]
                        st, sp_ = (bi == 0), (bi == nblk - 1)
                        P.op("pe", lambda e: e.matmul(ps[ya], v_[:, far * NKB + kb, 0:128], at[ai], start=st, stop=sp_),
                             reads=[v_b, atb[ai]], writes=[psb[ya]], inc=sp_)
                        P.op("pe", lambda e: e.matmul(ps[yb2], v_[:, far * NKB + kb, 128:256], at[ai], start=st, stop=sp_),
                             reads=[v_b, atb[ai]], writes=[psb[yb2]], inc=sp_)
                        P.op("pe", lambda e: e.matmul(ps[db], ones, at[ai], start=st, stop=sp_),
                             reads=[cstb, atb[ai]], writes=[psb[db]], inc=sp_)

                    for it in range(nblk + SKEW_DF):
                        if it < nblk:
                            dA(it)
                        if it >= SKEW_DF:
                            dB(it - SKEW_DF)
                    P.op("dve", lambda e, db=db: e.reciprocal(out=rc, in_=ps[db]), reads=[psb[db]], writes=[rcb])
                    if half == 0:
                        P.op("dve", lambda e, ya=ya: e.tensor_tensor(out=n1[:, 0, :], in0=ps[ya], in1=rc, op=ALU.mult), reads=[psb[ya], rcb], writes=[n1b])
                        P.op("dve", lambda e, yb2=yb2: e.tensor_tensor(out=n1[:, 1, :], in0=ps[yb2], in1=rc, op=ALU.mult), reads=[psb[yb2], rcb], writes=[n1b])
                    else:
                        for c, bk in ((0, ya), (1, yb2)):
                            P.op("dve", lambda e, c=c, bk=bk: e.tensor_tensor(out=y32[:, c, :], in0=ps[bk], in1=rc, op=ALU.mult),
                                 reads=[psb[bk], rcb], writes=[y32b])
                            P.op("dve", lambda e, c=c: e.scalar_tensor_tensor(out=y32[:, c, :], in0=y32[:, c, :], scalar=lam[:, 2 * li + 1:2 * li + 2],
                                                                             in1=n1[:, c, :], op0=ALU.mult, op1=ALU.add),
                                 reads=[y32b, n1b, lamb], writes=[y32b])
                        for c in range(2):
                            P.op("pool", lambda e, c=c: e.tensor_tensor(out=sp[c], in0=y32[:, c, :], in1=y32[:, c, :], op=ALU.mult),
                                 reads=[y32b], writes=[spb[c]])
                            P.op("pe", lambda e, c=c: e.matmul(ps[0], ones, sp[c], start=(c == 0), stop=(c == 1)),
                                 reads=[spb[c], cstb], writes=[psb[0]], inc=(c == 1))
                        P.op("act", lambda e: e.activation(out=rc, in_=ps[0], func=AF.Ln, bias=EPS, scale=1.0 / 256),
                             reads=[psb[0]], writes=[rcb])
                        P.op("act", lambda e: e.activation(out=rc, in_=rc, func=AF.Exp, scale=-0.5),
                             reads=[rcb], writes=[rcb])
                        P.op("dve", lambda e: e.tensor_scalar(out=rc, in0=rc, scalar1=gcol(li, GC_OML), scalar2=None, op0=ALU.mult),
                             reads=[rcb, gainsb], writes=[rcb])
                        for c in range(2):
                            yi = ysti[0] % 4
                            ysti[0] += 1
                            P.op("dve", lambda e, c=c, yi=yi: e.scalar_tensor_tensor(out=yst[yi], in0=y32[:, c, :],
                                                                                   scalar=gcol(li, GC_SUB + c), in1=rc, op0=ALU.mult, op1=ALU.mult),
                                 reads=[y32b, rcb, gainsb], writes=[ystb[yi]])
                            P.dma(stq(), yBd[(2 * h + c) * 128:(2 * h + c + 1) * 128, t0:t0 + G], yst[yi], reads=[ystb[yi]])
        P.barrier()
        ar.pop()

        hT = ar.alloc([128, DC, G], F32); hTb = Buf("hT")
        xn = ar.alloc([128, DC, G], BF16); xnb = Buf("xn")
        hid = ar.alloc([128, FC, G], BF16); hidb = Buf("hid")
        ws = WStream(4)
        rstd = ar.alloc([128, G], F32); rstdb = Buf("rstd")
        tmps = [ar.alloc([128, G], BF16) for _ in range(4)]; tmpbs = [Buf(f"tmp{i}") for i in range(4)]
        sg = [ar.alloc([128, G], F32) for _ in range(2)]; sgb = [Buf("sg0"), Buf("sg1")]
        gt = [ar.alloc([128, 2, G], BF16) for _ in range(4)]; gtb = [Buf(f"gt{i}") for i in range(4)]
        mg = [ar.alloc([128, G], F32) for _ in range(2)]; mgb = [Buf("mg0"), Buf("mg1")]
        ptl = ar.alloc([128, 2, G], BF16); ptlb = Buf("ptl")
        yg = hid[:, 0:16, :]; ygb = hidb
        for g in range(TG):
            t0 = g * G
            for q4 in range(4):
                P.dma(hwq(), hT[:, q4 * 4:(q4 + 1) * 4, :],
                      hS[q4 * 512:(q4 + 1) * 512, t0:t0 + G].rearrange("(c p) t -> p c t", p=128), writes=[hTb])
            P.dma("pool", ptl, pT[li][:, t0:t0 + G].rearrange("(c p) t -> p c t", p=128), writes=[ptlb])
            P.dma(hwq(), yg[:, 0:8, :], yAd[:, t0:t0 + G].rearrange("(c p) t -> p c t", p=128), writes=[ygb])
            P.dma(hwq(), yg[:, 8:16, :], yBd[:, t0:t0 + G].rearrange("(c p) t -> p c t", p=128), writes=[ygb])
            for ob in range(4):
                wa_, wab = ws.load(wsrc(w_a[li], ob * 512, 512), 8, 512)
                wb2, wbb2 = ws.load(wsrc(w_b[li], ob * 512, 512), 8, 512)
                for j in range(4):
                    oc = ob * 4 + j
                    gi = oc % 4
                    P.dma(hwq(), gt[gi], gts.rearrange("(two r) t -> r two t", two=2)[oc * 128:(oc + 1) * 128, :, t0:t0 + G], writes=[gtb[gi]])
                    pa, pb_ = (0, 1) if oc % 2 == 0 else (2, 3)
                    for kc in range(8):
                        P.op("pe", lambda e, kc=kc, j=j, pa=pa, wa_=wa_: e.matmul(ps[pa], wa_[:, kc, j * 128:(j + 1) * 128], yg[:, kc, :],
                                                                                  start=(kc == 0), stop=(kc == 7)),
                             reads=[wab, ygb], writes=[psb[pa]], inc=(kc == 7))
                    for kc in range(8):
                        P.op("pe", lambda e, kc=kc, j=j, pb_=pb_, wb2=wb2: e.matmul(ps[pb_], wb2[:, kc, j * 128:(j + 1) * 128], yg[:, 8 + kc, :],
                                                                                   start=(kc == 0), stop=(kc == 7)),
                             reads=[wbb2, ygb], writes=[psb[pb_]], inc=(kc == 7))
                    s = oc % 2
                    P.op("dve", lambda e, s=s, pa=pa, gi=gi: e.tensor_tensor(out=sg[s], in0=ps[pa], in1=gt[gi][:, 0, :], op=ALU.mult),
                         reads=[psb[pa], gtb[gi]], writes=[sgb[s]])
                    P.op("dve", lambda e, s=s, pb_=pb_, gi=gi: e.tensor_tensor(out=mg[s], in0=ps[pb_], in1=gt[gi][:, 1, :], op=ALU.mult),
                         reads=[psb[pb_], gtb[gi]], writes=[mgb[s]])
                    P.op("pool", lambda e, s=s, oc=oc: e.tensor_tensor(out=xn[:, oc, :], in0=sg[s], in1=mg[s], op=ALU.add),
                         reads=[sgb[s], mgb[s]], writes=[xnb])
            for ob in range(4):
                wo_, wob = ws.load(wsrc(w_o[li], ob * 512, 512), DC, 512)
                for j in range(4):
                    oc = ob * 4 + j
                    pd = 4 + oc % 2
                    for kc in range(DC):
                        P.op("pe", lambda e, kc=kc, j=j, pd=pd, wo_=wo_: e.matmul(ps[pd], wo_[:, kc, j * 128:(j + 1) * 128], xn[:, kc, :],
                                                                                  start=(kc == 0), stop=(kc == DC - 1)),
                             reads=[wob, xnb], writes=[psb[pd]], inc=(kc == DC - 1))
                    P.op("dve", lambda e, oc=oc, pd=pd: e.tensor_tensor(out=hT[:, oc, :], in0=ps[pd], in1=hT[:, oc, :], op=ALU.add),
                         reads=[psb[pd], hTb], writes=[hTb])
            ffn(li, 1, hT, hTb, xn, xnb, hid, hidb, ws, rstd, rstdb, tmps, tmpbs, sg, sgb)
            rms_bc(hT, DC, 6, rstd, rstdb, hTb, float(D), tmps, tmpbs)
            for c in range(DC):
                eng = "dve"
                P.op(eng, lambda e, c=c: e.scalar_tensor_tensor(out=xn[:, c, :], in0=hT[:, c, :], scalar=gcol(li, GC_PLE + c),
                                                                in1=rstd, op0=ALU.mult, op1=ALU.mult),
                     reads=[hTb, rstdb, gainsb], writes=[xnb])
            ppv = hid[:, 0:32, :].rearrange("p a b -> p (a b)").bitcast(F32).rearrange("p (a b) -> p a b", a=16)
            for ob in range(4):
                wp_, wpb = ws.load(wsrc(w_pp[li], ob * 512, 512), 2, 512)
                for j in range(4):
                    oc = ob * 4 + j
                    pd = 4 + oc % 2
                    for kc in range(2):
                        P.op("pe", lambda e, kc=kc, j=j, pd=pd, wp_=wp_: e.matmul(ps[pd], wp_[:, kc, j * 128:(j + 1) * 128], ptl[:, kc, :],
                                                                                  start=(kc == 0), stop=(kc == 1)),
                             reads=[wpb, ptlb], writes=[psb[pd]], inc=(kc == 1))
                    P.op("act", lambda e, oc=oc, pd=pd: e.activation(out=ppv[:, oc, :], in_=ps[pd], func=AF.Copy), reads=[psb[pd]], writes=[hidb])
            rms_bc(ppv, DC, 6, rstd, rstdb, hidb, float(D), tmps, tmpbs)
            for ob in range(4):
                wg_, wgb_ = ws.load(wsrc(w_pg[li], ob * 512, 512), DC, 512)
                for j in range(4):
                    oc = ob * 4 + j
                    pd = (0, 1, 2, 3)[oc % 4]
                    for kc in range(DC):
                        P.op("pe", lambda e, kc=kc, j=j, pd=pd, wg_=wg_: e.matmul(ps[pd], wg_[:, kc, j * 128:(j + 1) * 128], xn[:, kc, :],
                                                                                  start=(kc == 0), stop=(kc == DC - 1)),
                             reads=[wgb_, xnb], writes=[psb[pd]], inc=(kc == DC - 1))
                    s = oc % 2
                    P.op("act", lambda e, s=s, pd=pd: e.activation(out=sg[s], in_=ps[pd], func=AF.Sigmoid), reads=[psb[pd]], writes=[sgb[s]])
                    P.op("dve", lambda e, oc=oc: e.scalar_tensor_tensor(out=ppv[:, oc, :], in0=ppv[:, oc, :], scalar=gcol(li, GC_PLEO + oc),
                                                                        in1=rstd, op0=ALU.mult, op1=ALU.mult),
                         reads=[hidb, rstdb, gainsb], writes=[hidb])
                    P.op("dve", lambda e, s=s, oc=oc: e.tensor_tensor(out=sg[s], in0=sg[s], in1=ppv[:, oc, :], op=ALU.mult),
                         reads=[sgb[s], hidb], writes=[sgb[s]])
                    P.op("dve", lambda e, s=s, oc=oc: e.tensor_tensor(out=hT[:, oc, :], in0=hT[:, oc, :], in1=sg[s], op=ALU.add),
                         reads=[sgb[s], hTb], writes=[hTb])
            dst = outT if last else hS
            for q4 in range(4):
                P.dma(stq(), dst[q4 * 512:(q4 + 1) * 512, t0:t0 + G].rearrange("(c p) t -> p c t", p=128),
                      hT[:, q4 * 4:(q4 + 1) * 4, :], reads=[hTb])
        P.barrier()
        ar.pop()

    for li, l in enumerate(layers):
        if stop_after == "pre":
            break
        phase_A(li, l, xT if li == 0 else hS)
        if stop_after in ("A", "Aload", "Affn"):
            break
        exchange()
        phase_B(li, l, last=(li == L - 1))
    P.barrier()
    P.emit()
    return nc


def _consts():
    c = np.zeros((128, NCC), np.float32)
    c[:, CC_ONES:CC_ONES + 128] = 1.0
    j = np.arange(128)[:, None]
    s = np.arange(128)[None, :]
    c[:, CC_NTRI:CC_NTRI + 128] = -(j >= s).astype(np.float32)
    rot = np.zeros((128, 128), np.float32)
    for i in range(16):
        rot[16 + i, i] = -1.0
        rot[i, 16 + i] = 1.0
    c[:, CC_ROT:CC_ROT + 128] = rot
    tt = np.arange(512)[None, :]
    sp = np.arange(128)[:, None]
    for k in range(4):
        c[:, CC_MSB + k * 512:CC_MSB + (k + 1) * 512] = ((128 * k + sp) < tt).astype(np.float32)
        c[:, CC_MDF + k * 512:CC_MDF + (k + 1) * 512] = ((128 * k + sp) <= tt).astype(np.float32)
    return c


def _rope(pos0, T):
    pos = (pos0 + np.arange(T)).astype(np.float32)
    inv = (np.float32(500000.0) ** (-np.arange(0, 32, 2, dtype=np.float32) / np.float32(32))).astype(np.float32)
    ang = pos[:, None] * inv[None, :]
    cos = np.cos(ang).astype(np.float32).T
    sin = np.sin(ang).astype(np.float32).T
    r = np.zeros((128, 2, T), np.float32)
    r[:, 0, :] = 1.0
    r[0:16, 0, :] = cos
    r[16:32, 0, :] = cos
    r[0:16, 1, :] = sin
    r[16:32, 1, :] = sin
    return r


def _gains(inp, l, lam_layer):
    g = np.zeros((128, NGC), np.float32)

    def col(v):
        return np.asarray(v, np.float32).reshape(-1, 128).T

    g[:, GC_FFN1:GC_FFN1 + 16] = col(inp["ffn1_norm"][l])
    g[:, GC_MIX:GC_MIX + 16] = col(inp["mix_norm"][l])
    g[:, GC_FFN2:GC_FFN2 + 16] = col(inp["ffn2_norm"][l])
    g[:, GC_PLE:GC_PLE + 16] = col(inp["ple_norm"][l])
    g[:, GC_PLEO:GC_PLEO + 16] = col(inp["ple_out_norm"][l])
    g[:, GC_DQ:GC_DQ + 1] = col(inp["diff_q_norm"][l])
    g[:, GC_DK:GC_DK + 1] = col(inp["diff_k_norm"][l])
    g[:, GC_L + 0:GC_L + 1] = col(inp["diff_lambda_q1"][l])
    g[:, GC_L + 1:GC_L + 2] = col(inp["diff_lambda_k1"][l])
    g[:, GC_L + 2:GC_L + 3] = col(inp["diff_lambda_q2"][l])
    g[:, GC_L + 3:GC_L + 4] = col(inp["diff_lambda_k2"][l])
    g[:, GC_SUB:GC_SUB + 2] = col(inp["diff_sub_norm"][l])
    li = 0.8 - 0.6 * math.exp(-0.3 * lam_layer)
    g[:, GC_LI] = li
    g[:, GC_OML] = 1.0 - li
    return g


_NC_CACHE = {}


def _run(inp, TG, layers_list, dbg=None):
    T = TG * G
    S = 2 * T
    x = np.asarray(inp["x"], np.float32)
    B = x.shape[0]
    ncores = 2 * B
    cst = _consts()
    hT = [np.ascontiguousarray(x[c // 2, (c % 2) * T:(c % 2 + 1) * T, :].T) for c in range(ncores)]
    res = None
    for layers in layers_list:
        key = (TG, len(layers), str(dbg))
        if key not in _NC_CACHE:
            _NC_CACHE[key] = build(TG, layers, dbg)
        nc = _NC_CACHE[key]
        sl = slice(layers[0], layers[-1] + 1)
        shared = {
            "w_gu1": np.asarray(inp["ffn1_w_gu"][sl], np.float32), "w_gu2": np.asarray(inp["ffn2_w_gu"][sl], np.float32),
            "w_d1": np.asarray(inp["ffn1_w_down"][sl], np.float32), "w_d2": np.asarray(inp["ffn2_w_down"][sl], np.float32),
            "w_in": np.asarray(inp["w_in"][sl], np.float32), "w_a": np.asarray(inp["w_branch_a"][sl], np.float32),
            "w_b": np.asarray(inp["w_branch_b"][sl], np.float32), "w_o": np.asarray(inp["w_out"][sl], np.float32),
            "w_pg": np.asarray(inp["ple_w_gate"][sl], np.float32), "w_pp": np.asarray(inp["ple_w_proj"][sl], np.float32),
            "gains": np.stack([_gains(inp, l, l) for l in layers]), "cst": cst,
        }
        p = np.asarray(inp["p"], np.float32)
        in_maps = []
        for c in range(ncores):
            b, half = c // 2, c % 2
            m = dict(shared)
            m["xT"] = hT[c]
            m["pT"] = np.ascontiguousarray(np.stack([p[l, b, half * T:(half + 1) * T, :].T for l in layers]))
            m["rope"] = _rope(half * T, T)
            m["farbias"] = np.full((128, 1), 0.0 if half == 1 else -30000.0, np.float32)
            in_maps.append(m)
        res = run_bass_kernel_spmd(nc, in_maps, core_ids=list(range(ncores)))
        hT = [np.asarray(res.results[c]["outT"]) for c in range(ncores)]
    out = np.empty((B, S, D), np.float32)
    for c in range(ncores):
        out[c // 2, (c % 2) * T:(c % 2 + 1) * T, :] = hT[c].T
    return out, res


def kernel(**inputs):
    out, _ = _run(inputs, 4, [[0, 1]])
    return out
```

```python
import math
import types
import numpy as np
import ml_dtypes
import concourse.bass as bass
import concourse.mybir as mybir
from concourse.bass_utils import run_bass_kernel_spmd

F32 = mybir.dt.float32
BF16 = mybir.dt.bfloat16
AF = mybir.ActivationFunctionType
ALU = mybir.AluOpType

D = 2048
DC = 16
DFF = 5632
FC = 44
G = 512
DEPTH = 2
INW = 10240
EPS = 1e-6
CAP = 30000
NDS = 16
SCALE = 128 ** -0.5
import os as _os
SKEW_SB = tuple(int(x) for x in _os.environ.get('SKEW_SB', '1,1').split(','))
SKEW_DF = int(_os.environ.get('SKEW_DF', '1'))

GC_FFN1, GC_MIX, GC_FFN2, GC_PLE, GC_PLEO = 0, 16, 32, 48, 64
GC_DQ, GC_DK, GC_L, GC_SUB, GC_LI, GC_OML = 80, 81, 82, 86, 88, 89
NGC = 90
CC_ONES, CC_NTRI, CC_ROT, CC_MSB, CC_MDF = 0, 128, 256, 384, 384 + 2048
NCC = 384 + 4096


def _freeze(fn):
    if fn is None or fn.__closure__ is None:
        return fn
    cells = []
    for c in fn.__closure__:
        try:
            cells.append(types.CellType(c.cell_contents))
        except ValueError:
            cells.append(c)
    return types.FunctionType(fn.__code__, fn.__globals__, fn.__name__, fn.__defaults__, tuple(cells))


class Buf:
    __slots__ = ("w", "r", "name")

    def __init__(self, name=""):
        self.w = None
        self.r = {}
        self.name = name


class Prog:
    ENGS = ["pe", "act", "dve", "pool", "sp"]

    def __init__(self, nc):
        self.nc = nc
        self.streams = {e: [] for e in self.ENGS}
        self.cnt = {e: 0 for e in self.ENGS}
        self.seen = {e: {} for e in self.ENGS}
        self.ndma = {}
        self.dma_tokens_live = {}
        self.extra_sems = []

    def _need_wait(self, eng, tok):
        if tok[0] == "e":
            _, te, k = tok
            if te == eng and eng == "pe":
                return None
            if self.seen[eng].get(te, 0) >= k:
                return None
            self.seen[eng][te] = k
            return ("e", te, k)
        elif tok[0] == "d":
            _, q, i = tok
            slot = i % NDS
            val = 16 * (i // NDS + 1)
            key = ("d", q, slot)
            if self.seen[eng].get(key, 0) >= val:
                return None
            self.seen[eng][key] = val
            return ("d", q, slot, val)
        else:
            key = tok
            if self.seen[eng].get(key, 0) >= 1:
                return None
            self.seen[eng][key] = 1
            return tok

    def _deps(self, eng, reads, writes):
        toks = []
        for b in reads:
            if b.w is not None:
                toks.append(b.w)
        for b in writes:
            if b.w is not None:
                toks.append(b.w)
            toks.extend(b.r.values())
        waits = []
        for t in toks:
            w = self._need_wait(eng, t)
            if w is not None:
                waits.append(w)
        return waits

    def _mark(self, tok, key, reads, writes):
        for b in reads:
            b.r[key] = tok
        for b in writes:
            b.w = tok
            b.r = {}

    def op(self, eng, fn, reads=(), writes=(), inc=True):
        if eng != "pe":
            inc = True
        waits = self._deps(eng, reads, writes)
        k = self.cnt[eng] + 1
        tok = ("e", eng, k)
        if inc:
            self.cnt[eng] = k
        self.streams[eng].append((waits, _freeze(fn), k if inc else None, None))
        self._mark(tok, eng, reads, writes)
        return tok

    def dma(self, eng, out, in_, reads=(), writes=()):
        waits = self._deps(eng, reads, writes)
        i = self.ndma.get(eng, 0)
        self.ndma[eng] = i + 1
        if i >= NDS:
            w = self._need_wait(eng, ("d", eng, i - NDS))
            if w is not None:
                waits.append(w)
        tok = ("d", eng, i)
        self.streams[eng].append((waits, lambda e: e.dma_start(out=out, in_=in_), None, i))
        self._mark(tok, tok, reads, writes)
        live = self.dma_tokens_live.setdefault(eng, [])
        live.append(tok)
        if len(live) > NDS:
            del live[:-NDS]
        return tok

    def barrier(self, extra=()):
        toks = [("e", e, self.cnt[e]) for e in self.ENGS if self.cnt[e] > 0]
        for live in self.dma_tokens_live.values():
            toks += list(live)
        toks += list(extra)
        for eng in self.ENGS:
            waits = []
            for t in toks:
                if t[0] == "e" and t[1] == eng and eng == "pe":
                    continue
                w = self._need_wait(eng, t)
                if w is not None:
                    waits.append(w)
            if waits:
                self.streams[eng].append((waits, None, None, None))

    def emit(self):
        nc = self.nc
        nsem = {e: (self.cnt[e] + CAP - 1) // CAP + 1 for e in self.ENGS}
        sems = {e: [nc.alloc_semaphore(f"s_{e}_{j}") for j in range(nsem[e])] for e in self.ENGS}
        dsems = {q: [nc.alloc_semaphore(f"s_dma_{q}_{j}") for j in range(NDS)] for q in self.ndma}
        xsems = self.extra_sems
        handles = {"pe": "tensor", "act": "scalar", "dve": "vector", "pool": "gpsimd", "sp": "sync"}

        def run(ename, eng):
            for waits, fn, k, di in self.streams[ename]:
                for w in waits:
                    if w[0] == "e":
                        _, te, kk = w
                        eng.wait_ge(sems[te][(kk - 1) // CAP], (kk - 1) % CAP + 1)
                    elif w[0] == "d":
                        eng.wait_ge(dsems[w[1]][w[2]], w[3])
                    else:
                        eng.wait_ge(xsems[w[1]], 1)
                if fn is None:
                    continue
                ins = fn(eng)
                if k is not None:
                    ins.then_inc(sems[ename][(k - 1) // CAP], 1)
                if di is not None:
                    ins.then_inc(dsems[ename][di % NDS], 16)

        with nc.Block() as block:
            @block.tensor
            def _(e):
                run("pe", e)

            @block.scalar
            def _(e):
                run("act", e)

            @block.vector
            def _(e):
                run("dve", e)

            @block.gpsimd
            def _(e):
                run("pool", e)

            @block.sync
            def _(e):
                run("sp", e)


class Arena:
    def __init__(self, nc, nbytes):
        self.t = nc.alloc_sbuf_tensor("arena", [128, nbytes // 2], BF16)
        self.nbytes = nbytes
        self.off = 0
        self.marks = []

    def push(self):
        self.marks.append(self.off)

    def pop(self):
        self.off = self.marks.pop()

    def alloc(self, shape, dtype):
        es = 4 if dtype == F32 else 2
        n = int(np.prod(shape[1:]))
        nb = n * es
        nb = (nb + 63) // 64 * 64
        assert self.off + nb <= self.nbytes, f"arena overflow {self.off + nb} > {self.nbytes}"
        o = self.off // 2
        ap = self.t[:, o:o + nb // 2]
        self.off += nb
        if dtype == F32:
            ap = ap.bitcast(F32)
        ap = ap[:, 0:n]
        if len(shape) == 3:
            ap = ap.rearrange("p (a b) -> p a b", a=shape[1])
        return ap


def build(TG, layers, dbg=None):
    T = TG * G
    NKB = T // 128
    L = len(layers)
    nc = bass.Bass("TRN2", target_bir_lowering=False)
    P = Prog(nc)

    def din(name, shape, dt=F32):
        return nc.dram_tensor(name, shape, dt, kind="ExternalInput").ap()

    def dscr(name, shape, dt, out=False):
        kind = "ExternalOutput" if (out or (dbg and name in dbg)) else "Internal"
        return nc.dram_tensor(name, shape, dt, kind=kind)

    xT = din("xT", [D, T])
    pT = din("pT", [L, 256, T])
    w_gu = [din("w_gu1", [L, D, 2 * DFF]), din("w_gu2", [L, D, 2 * DFF])]
    w_dn = [din("w_d1", [L, DFF, D]), din("w_d2", [L, DFF, D])]
    w_in = din("w_in", [L, D, INW])
    w_a = din("w_a", [L, 1024, D])
    w_b = din("w_b", [L, 1024, D])
    w_o = din("w_o", [L, D, D])
    w_pg = din("w_pg", [L, D, D])
    w_pp = din("w_pp", [L, 256, D])
    gains_d = din("gains", [L, 128, NGC])
    cst_d = din("cst", [128, NCC])
    rope_d = din("rope", [128, 2, T])
    farb_d = din("farbias", [128, 1])

    outT = nc.dram_tensor("outT", [D, T], F32, kind="ExternalOutput").ap()
    hS = dscr("hS", [D, T], F32).ap()
    qA = dscr("qA", [1024, T], BF16).ap()
    qB = dscr("qB", [1024, T], BF16).ap()
    gts = dscr("gts", [4096, T], BF16).ap()
    kloc = dscr("kloc", [2048, T], BF16)
    vloc = dscr("vloc", [T, 2048], BF16)
    kall = dscr("kall", [4096, T], BF16)
    vall = dscr("vall", [2 * T, 2048], BF16)
    yAd = dscr("yAd", [1024, T], BF16).ap()
    yBd = dscr("yBd", [1024, T], BF16).ap()

    ar = Arena(nc, 188 * 1024)
    ps = [nc.alloc_psum_tensor(f"ps{i}", [128, 512], F32).ap() for i in range(8)]
    psb = [Buf(f"ps{i}") for i in range(8)]

    cst = ar.alloc([128, 384], BF16)
    cstb = Buf("cst")
    gains = ar.alloc([128, L * NGC], F32)
    gainsb = Buf("gains")
    farb = ar.alloc([128, 1], F32)
    lam = ar.alloc([128, 2 * L], F32)
    lamb = Buf("lam")
    P.dma("pool", cst, cst_d[:, 0:384], writes=[cstb])
    for l in range(L):
        P.dma("sp", gains[:, l * NGC:(l + 1) * NGC], gains_d[l], writes=[gainsb])
    P.dma("sp", farb, farb_d, writes=[gainsb])
    ones = cst[:, CC_ONES:CC_ONES + 128]
    ntri = cst[:, CC_NTRI:CC_NTRI + 128]
    rotm = cst[:, CC_ROT:CC_ROT + 128]

    def gcol(l, c, n=1):
        return gains[:, l * NGC + c:l * NGC + c + n]

    ar.push()
    ltmp = ar.alloc([128, 8], F32)
    ltb = Buf("ltmp")
    ones32 = ar.alloc([128, 128], F32)
    o32b = Buf("ones32")
    P.op("dve", lambda e: e.memset(ones32, 1.0), writes=[o32b])
    for l in range(L):
        P.op("dve", lambda e, l=l: e.tensor_tensor(out=ltmp[:, 0:1], in0=gcol(l, GC_L), in1=gcol(l, GC_L + 1), op=ALU.mult),
             reads=[gainsb], writes=[ltb])
        P.op("dve", lambda e, l=l: e.tensor_tensor(out=ltmp[:, 1:2], in0=gcol(l, GC_L + 2), in1=gcol(l, GC_L + 3), op=ALU.mult),
             reads=[gainsb], writes=[ltb])
        P.op("pe", lambda e: e.matmul(ps[7][:, 0:2], ones32, ltmp[:, 0:2], start=True, stop=True),
             reads=[ltb, o32b], writes=[psb[7]])
        P.op("act", lambda e: e.activation(out=ltmp[:, 2:4], in_=ps[7][:, 0:2], func=AF.Exp), reads=[psb[7]], writes=[ltb])
        P.op("dve", lambda e: e.tensor_tensor(out=ltmp[:, 4:5], in0=ltmp[:, 2:3], in1=ltmp[:, 3:4], op=ALU.subtract),
             reads=[ltb], writes=[ltb])
        P.op("dve", lambda e, l=l: e.tensor_tensor(out=lam[:, 2 * l:2 * l + 1], in0=ltmp[:, 4:5], in1=gcol(l, GC_LI), op=ALU.add),
             reads=[ltb, gainsb], writes=[lamb])
        P.op("dve", lambda e, l=l: e.tensor_scalar(out=lam[:, 2 * l + 1:2 * l + 2], in0=lam[:, 2 * l:2 * l + 1], scalar1=-1.0, scalar2=None, op0=ALU.mult),
             reads=[lamb], writes=[lamb])
    P.barrier()
    ar.pop()
    stop_after = dbg.get("stop") if dbg else None

    dmaq = ["sp", "act"]
    rr = [0]

    def hwq():
        return "sp"

    def stq():
        return "act"

    def rms_bc(src, nch, bank, out_rstd, rstdb, srcb, Dn, tmps, tmpbs):
        nt = len(tmps)
        for c in range(nch):
            sq, sqb = tmps[c % nt], tmpbs[c % nt]
            eng = "dve" if c % 2 == 0 else "pool"
            P.op(eng, lambda e, c=c, sq=sq: e.tensor_tensor(out=sq, in0=src[:, c, :], in1=src[:, c, :], op=ALU.mult),
                 reads=[srcb], writes=[sqb])
            P.op("pe", lambda e, c=c, sq=sq: e.matmul(ps[bank], ones, sq, start=(c == 0), stop=(c == nch - 1)),
                 reads=[sqb, cstb], writes=[psb[bank]], inc=True)
        P.op("act", lambda e: e.activation(out=out_rstd, in_=ps[bank], func=AF.Ln, bias=EPS, scale=1.0 / Dn),
             reads=[psb[bank]], writes=[rstdb])
        P.op("act", lambda e: e.activation(out=out_rstd, in_=out_rstd, func=AF.Exp, scale=-0.5),
             reads=[rstdb], writes=[rstdb])

    class WStream:
        def __init__(self, nslots, nelem=8192):
            self.slots = [ar.alloc([128, nelem], BF16) for _ in range(nslots)]
            self.bufs = [Buf(f"w{i}") for i in range(nslots)]
            self.i = 0

        def load(self, src_ap, kc, ncol):
            s = self.i % len(self.slots)
            self.i += 1
            dst = self.slots[s][:, 0:kc * ncol].rearrange("p (c n) -> p c n", c=kc)
            P.dma("pool", dst, src_ap, writes=[self.bufs[s]])
            return dst, self.bufs[s]

    def wsrc(w2d, c0, ncol):
        return w2d[:, c0:c0 + ncol].rearrange("(c p) n -> p c n", p=128)

    def ffn(l, which, hT, hTb, xn, xnb, hid, hidb, ws, rstd, rstdb, tmps, tmpbs, sg, sgb):
        gc = GC_FFN1 if which == 0 else GC_FFN2
        rms_bc(hT, DC, 6, rstd, rstdb, hTb, float(D), tmps, tmpbs)
        for c in range(DC):
            eng = "dve"
            P.op(eng, lambda e, c=c: e.scalar_tensor_tensor(out=xn[:, c, :], in0=hT[:, c, :], scalar=gcol(l, gc + c),
                                                            in1=rstd, op0=ALU.mult, op1=ALU.mult),
                 reads=[hTb, rstdb, gainsb], writes=[xnb])
        wgu = w_gu[which][l]
        wdn = w_dn[which][l]
        for fb in range(FC // 4):
            wg, wgb = ws.load(wsrc(wgu, fb * 512, 512), DC, 512)
            wu, wub = ws.load(wsrc(wgu, DFF + fb * 512, 512), DC, 512)
            for j in range(4):
                fc = fb * 4 + j
                pg, pu = (0, 1) if fc % 2 == 0 else (2, 3)
                for kc in range(DC):
                    P.op("pe", lambda e, kc=kc, j=j, wg=wg, pg=pg: e.matmul(ps[pg], wg[:, kc, j * 128:(j + 1) * 128], xn[:, kc, :],
                                                                            start=(kc == 0), stop=(kc == DC - 1)),
                         reads=[wgb, xnb], writes=[psb[pg]], inc=(kc == DC - 1))
                for kc in range(DC):
                    P.op("pe", lambda e, kc=kc, j=j, wu=wu, pu=pu: e.matmul(ps[pu], wu[:, kc, j * 128:(j + 1) * 128], xn[:, kc, :],
                                                                            start=(kc == 0), stop=(kc == DC - 1)),
                         reads=[wub, xnb], writes=[psb[pu]], inc=(kc == DC - 1))
                s = fc % 2
                P.op("act", lambda e, pg=pg, s=s: e.activation(out=sg[s], in_=ps[pg], func=AF.Silu), reads=[psb[pg]], writes=[sgb[s]])
                P.op("dve", lambda e, pu=pu, s=s, fc=fc: e.tensor_tensor(out=hid[:, fc, :], in0=sg[s], in1=ps[pu], op=ALU.mult),
                     reads=[sgb[s], psb[pu]], writes=[hidb])
        dbanks = [4, 5, 0, 1]
        for db in range(8):
            halves = []
            for kh in range(2):
                halves.append(ws.load(wdn[kh * 2816:(kh + 1) * 2816, db * 256:(db + 1) * 256].rearrange("(c p) n -> p c n", p=128), 22, 256))
            for kh in range(2):
                wd, wdb = halves[kh]
                for j in range(2):
                    pd = dbanks[(db % 2) * 2 + j]
                    for fc in range(22):
                        last = (kh == 1 and fc == 21)
                        P.op("pe", lambda e, fc=fc, kh=kh, j=j, wd=wd, pd=pd, last=last: e.matmul(
                            ps[pd], wd[:, fc, j * 128:(j + 1) * 128], hid[:, kh * 22 + fc, :], start=(kh == 0 and fc == 0), stop=last),
                             reads=[wdb, hidb], writes=[psb[pd]], inc=last)
            for j in range(2):
                dc = db * 2 + j
                pd = dbanks[(db % 2) * 2 + j]
                P.op("dve", lambda e, dc=dc, pd=pd: e.scalar_tensor_tensor(out=hT[:, dc, :], in0=ps[pd], scalar=0.5, in1=hT[:, dc, :],
                                                                           op0=ALU.mult, op1=ALU.add),
                     reads=[psb[pd], hTb], writes=[hTb])

    def phase_A(li, l, src_h):
        ar.push()
        hT = ar.alloc([128, DC, G], F32); hTb = Buf("hT")
        xn = ar.alloc([128, DC, G], BF16); xnb = Buf("xn")
        hid = ar.alloc([128, FC, G], BF16); hidb = Buf("hid")
        ws = WStream(4)
        rstd = ar.alloc([128, G], F32); rstdb = Buf("rstd")
        rs2 = [ar.alloc([128, G], F32) for _ in range(2)]; rs2b = [Buf("rs20"), Buf("rs21")]
        tmps = [ar.alloc([128, G], BF16) for _ in range(4)]; tmpbs = [Buf(f"tmp{i}") for i in range(4)]
        sg = [ar.alloc([128, G], F32) for _ in range(2)]; sgb = [Buf("sg0"), Buf("sg1")]
        stg = [ar.alloc([128, G], BF16) for _ in range(4)]; stgb = [Buf(f"stg{i}") for i in range(4)]
        xg = [ar.alloc([128, G], BF16) for _ in range(2)]; xgb = [Buf("xg0"), Buf("xg1")]
        x32 = [ar.alloc([128, G], F32) for _ in range(2)]; x32b = [Buf("x320"), Buf("x321")]
        rp = ar.alloc([128, 2, G], F32); rpb = Buf("rope")
        sti = [0]

        def stage():
            i = sti[0] % 4
            sti[0] += 1
            return stg[i], stgb[i]

        for g in range(TG):
            t0 = g * G
            for q4 in range(4):
                P.dma(hwq(), hT[:, q4 * 4:(q4 + 1) * 4, :],
                      src_h[q4 * 512:(q4 + 1) * 512, t0:t0 + G].rearrange("(c p) t -> p c t", p=128), writes=[hTb])
            P.dma(hwq(), rp, rope_d[:, :, t0:t0 + G], writes=[rpb])
            if stop_after != "Aload":
                ffn(li, 0, hT, hTb, xn, xnb, hid, hidb, ws, rstd, rstdb, tmps, tmpbs, sg, sgb)
            for q4 in range(4):
                P.dma(stq(), hS[q4 * 512:(q4 + 1) * 512, t0:t0 + G].rearrange("(c p) t -> p c t", p=128),
                      hT[:, q4 * 4:(q4 + 1) * 4, :], reads=[hTb])
            if stop_after in ("Aload", "Affn"):
                continue
            rms_bc(hT, DC, 6, rstd, rstdb, hTb, float(D), tmps, tmpbs)
            for c in range(DC):
                eng = "dve"
                P.op(eng, lambda e, c=c: e.scalar_tensor_tensor(out=xn[:, c, :], in0=hT[:, c, :], scalar=gcol(li, GC_MIX + c),
                                                                in1=rstd, op0=ALU.mult, op1=ALU.mult),
                     reads=[hTb, rstdb, gainsb], writes=[xnb])
            win = w_in[li]
            pbank = [0]

            def nextbank():
                b = pbank[0] % 4
                pbank[0] += 1
                return b

            for cb in range(20):
                wb_, wbb = ws.load(wsrc(win, cb * 512, 512), DC, 512)
                kind = ["sbq", "sbk", "sbv", "dfq", "dfk", "dfv", "ga", "ga", "gb", "gb"][cb // 2]
                if kind in ("sbv", "dfv"):
                    for tt in range(4):
                        b = nextbank()
                        for kc in range(DC):
                            P.op("pe", lambda e, kc=kc, tt=tt, b=b, wb_=wb_: e.matmul(ps[b], xn[:, kc, tt * 128:(tt + 1) * 128], wb_[:, kc, :],
                                                                                      start=(kc == 0), stop=(kc == DC - 1)),
                                 reads=[wbb, xnb], writes=[psb[b]], inc=(kc == DC - 1))
                        st, stb = stage()
                        P.op("act", lambda e, b=b, st=st: e.activation(out=st, in_=ps[b], func=AF.Copy), reads=[psb[b]], writes=[stb])
                        col0 = (0 if kind == "sbv" else 1024) + (cb % 2) * 512
                        P.dma(stq(), vloc[t0 + tt * 128:t0 + (tt + 1) * 128, col0:col0 + 512], st, reads=[stb])
                    continue
                for j in range(4):
                    b = nextbank()
                    oc = (cb % 2) * 4 + j
                    for kc in range(DC):
                        P.op("pe", lambda e, kc=kc, j=j, b=b, wb_=wb_: e.matmul(ps[b], wb_[:, kc, j * 128:(j + 1) * 128], xn[:, kc, :],
                                                                                start=(kc == 0), stop=(kc == DC - 1)),
                             reads=[wbb, xnb], writes=[psb[b]], inc=(kc == DC - 1))
                    st, stb = stage()
                    if kind == "sbq":
                        P.op("act", lambda e, b=b, st=st: e.activation(out=st, in_=ps[b], func=AF.Copy, scale=SCALE), reads=[psb[b]], writes=[stb])
                        P.dma(stq(), qA[oc * 128:(oc + 1) * 128, t0:t0 + G], st, reads=[stb])
                    elif kind == "sbk":
                        P.op("act", lambda e, b=b, st=st: e.activation(out=st, in_=ps[b], func=AF.Copy), reads=[psb[b]], writes=[stb])
                        P.dma(stq(), kloc[oc * 128:(oc + 1) * 128, t0:t0 + G], st, reads=[stb])
                    elif kind in ("ga", "gb"):
                        gch = ((cb - 12) * 4 + j)
                        P.op("act", lambda e, b=b, st=st: e.activation(out=st, in_=ps[b], func=AF.Sigmoid), reads=[psb[b]], writes=[stb])
                        P.dma(stq(), gts[gch * 128:(gch + 1) * 128, t0:t0 + G], st, reads=[stb])
                    else:
                        isq = kind == "dfq"
                        gcn = GC_DQ if isq else GC_DK
                        s = oc % 2
                        bss, brt = 4 + s, 6 + s
                        P.op("act", lambda e, b=b, s=s: e.activation(out=tmps[s], in_=ps[b], func=AF.Square), reads=[psb[b]], writes=[tmpbs[s]])
                        P.op("act", lambda e, b=b, s=s, gcn=gcn: e.activation(out=x32[s], in_=ps[b], func=AF.Copy, scale=gcol(li, gcn)),
                             reads=[psb[b], gainsb], writes=[x32b[s]])
                        P.op("pe", lambda e, s=s, bss=bss: e.matmul(ps[bss], ones, tmps[s], start=True, stop=True), reads=[tmpbs[s], cstb], writes=[psb[bss]])
                        P.op("dve", lambda e, s=s: e.tensor_copy(out=xg[s], in_=x32[s]), reads=[x32b[s]], writes=[xgb[s]])
                        P.op("pe", lambda e, s=s, brt=brt: e.matmul(ps[brt], rotm, xg[s], start=True, stop=True), reads=[xgb[s], cstb], writes=[psb[brt]])
                        P.op("act", lambda e, s=s, bss=bss: e.activation(out=rs2[s], in_=ps[bss], func=AF.Ln, bias=EPS, scale=1.0 / 128),
                             reads=[psb[bss]], writes=[rs2b[s]])
                        P.op("act", lambda e, s=s, isq=isq: e.activation(out=rs2[s], in_=rs2[s], func=AF.Exp, scale=-0.5, bias=(math.log(SCALE) if isq else 0.0)),
                             reads=[rs2b[s]], writes=[rs2b[s]])
                        P.op("dve", lambda e, s=s: e.tensor_tensor(out=x32[s], in0=x32[s], in1=rp[:, 0, :], op=ALU.mult),
                             reads=[x32b[s], rpb], writes=[x32b[s]])
                        P.op("dve", lambda e, s=s, brt=brt: e.tensor_tensor(out=sg[s], in0=ps[brt], in1=rp[:, 1, :], op=ALU.mult),
                             reads=[psb[brt], rpb], writes=[sgb[s]])
                        P.op("dve", lambda e, s=s: e.tensor_tensor(out=x32[s], in0=x32[s], in1=sg[s], op=ALU.add),
                             reads=[x32b[s], sgb[s]], writes=[x32b[s]])
                        P.op("dve", lambda e, s=s, st=st: e.tensor_tensor(out=st, in0=x32[s], in1=rs2[s], op=ALU.mult),
                             reads=[x32b[s], rs2b[s]], writes=[stb])
                        if isq:
                            P.dma(stq(), qB[oc * 128:(oc + 1) * 128, t0:t0 + G], st, reads=[stb])
                        else:
                            P.dma(stq(), kloc[1024 + oc * 128:1024 + (oc + 1) * 128, t0:t0 + G], st, reads=[stb])
        P.barrier()
        ar.pop()

    RK = min(2048, (1 << 19) // T)
    NKC = 2048 // RK
    RV = 256
    NVC = T // RV

    def kfar(r0, n):
        j, w = r0 // RK, r0 % RK
        return kall[j * 2 * RK + w:j * 2 * RK + w + n, :]

    def exchange():
        groups = [[0, 1], [2, 3], [4, 5], [6, 7]]
        toks = []
        jobs = [(kloc, kall, RK, j) for j in range(NKC)] + [(vloc, vall, RV, j) for j in range(NVC)]
        for (src, dst, R_, j) in jobs:
            sem = nc.alloc_semaphore(f"cc{len(P.extra_sems)}")
            idx = len(P.extra_sems)
            P.extra_sems.append(sem)
            P.streams["pool"].append(([], (lambda e, src=src, dst=dst, sem=sem, R_=R_, j=j: e.collective_compute(
                "AllGather", ALU.bypass, replica_groups=groups, ins=[src[j * R_:(j + 1) * R_, :]],
                outs=[dst[j * 2 * R_:(j + 1) * 2 * R_, :]]).then_inc(sem, 1)), None, None))
            toks.append(("x", idx))
        P.barrier(extra=toks)

    def phase_B(li, l, last):
        ar.push()
        ar.push()
        msk = ar.alloc([128, 4096], BF16); mskb = Buf("msk")
        P.dma("pool", msk, cst_d[:, 384:384 + 4096], writes=[mskb])
        yst = [ar.alloc([128, G], BF16) for _ in range(4)]; ystb = [Buf(f"yst{i}") for i in range(4)]
        ysti = [0]
        kT = [ar.alloc([128, 4, T], BF16) for _ in range(2)]; kTb = [Buf("kT0"), Buf("kT1")]
        vv = [ar.alloc([128, 2 * NKB, 256], BF16) for _ in range(2)]; vvb = [Buf("v0"), Buf("v1")]
        qt = [ar.alloc([128, G], BF16) for _ in range(4)]; qtb = [Buf(f"qt{i}") for i in range(4)]
        e32 = [ar.alloc([128, G], F32) for _ in range(4)]; e32b = [Buf(f"e{i}") for i in range(4)]
        sp = [ar.alloc([128, G], BF16) for _ in range(8)]; spb = [Buf(f"sp{i}") for i in range(8)]
        at = [ar.alloc([128, G], BF16) for _ in range(6)]; atb = [Buf(f"at{i}") for i in range(6)]
        R32 = [ar.alloc([128, G], F32) for _ in range(2)]; R32b = [Buf("R320"), Buf("R321")]
        Rbf = [ar.alloc([128, G], BF16) for _ in range(6)]; Rbfb = [Buf(f"Rbf{i}") for i in range(6)]
        n1 = ar.alloc([128, 2, G], F32); n1b = Buf("n1")
        rc = ar.alloc([128, G], F32); rcb = Buf("rc")
        y32 = ar.alloc([128, 2, G], F32); y32b = Buf("y32")
        tq = ar.alloc([128, G], BF16); tqb = Buf("tq")
        msb = [msk[:, k * 512:(k + 1) * 512] for k in range(4)]
        mdf = [msk[:, 2048 + k * 512:2048 + (k + 1) * 512] for k in range(4)]
        cnt = {"q": 0, "e": 0, "sp": 0, "at": 0, "z": 0, "zs": 0, "p": 0, "y": 0, "R": 0, "hd": 0}

        def rot(name, n):
            i = cnt.get(name, 0) % n
            cnt[name] = cnt.get(name, 0) + 1
            return i

        for h in range(8):
            hb = rot("hd", 2)
            k_, k_b = kT[hb], kTb[hb]
            v_, v_b = vv[hb], vvb[hb]
            P.dma(hwq(), k_[:, 0, :], kloc[h * 128:(h + 1) * 128, :], writes=[k_b])
            P.dma(hwq(), k_[:, 1, :], kfar(h * 128, 128), writes=[k_b])
            P.dma(hwq(), v_[:, 0:NKB, 0:128], vloc[:, h * 128:(h + 1) * 128].rearrange("(b p) c -> p b c", p=128), writes=[v_b])
            for j in range(NVC):
                P.dma(hwq(), v_[:, NKB + 2 * j:NKB + 2 * j + 2, 0:128],
                      vall[j * 512:j * 512 + 256, h * 128:(h + 1) * 128].rearrange("(b p) c -> p b c", p=128), writes=[v_b])
            NSTR = 2 if TG >= 2 else 1
            for g0 in range(0, TG, NSTR):
                streams = []
                for sidx in range(NSTR):
                    g = g0 + sidx
                    qi = rot("q", 4)
                    P.dma(hwq(), qt[qi], qA[h * 128:(h + 1) * 128, g * G:(g + 1) * G], writes=[qtb[qi]])
                    blocks = [(0, kb) for kb in range(4 * g + 3, -1, -1)] + [(1, kb) for kb in range(NKB - 1, -1, -1)]
                    streams.append(dict(g=g, qi=qi, yb=4 + sidx, blocks=blocks, nblk=len(blocks), st={}, sx=sidx))

                def stA(S, bi):
                    sx, g, qi = S["sx"], S["g"], S["qi"]
                    far, kb = S["blocks"][bi]
                    kblk = k_[:, far, kb * 128:(kb + 1) * 128]
                    dk = (kb - 4 * g) if (far == 0 and kb >= 4 * g) else None
                    zb = 2 * sx + rot(f"zs{sx}", 2)
                    P.op("pe", lambda e: e.matmul(ps[zb], kblk, qt[qi], start=True, stop=True),
                         reads=[k_b, qtb[qi]], writes=[psb[zb]])
                    ei = 2 * sx + rot(f"e{sx}", 2)
                    P.op("act", lambda e: e.activation(out=e32[ei], in_=ps[zb], func=AF.Exp), reads=[psb[zb]], writes=[e32b[ei]])
                    si = 4 * sx + rot(f"sp{sx}", 4)
                    P.op("act", lambda e: e.activation(out=sp[si], in_=e32[ei], func=AF.Ln, bias=1.0, scale=1.0),
                         reads=[e32b[ei]], writes=[spb[si]])
                    if dk is not None:
                        P.op("dve", lambda e: e.tensor_tensor(out=sp[si], in0=sp[si], in1=msb[dk], op=ALU.mult),
                             reads=[spb[si], mskb], writes=[spb[si]])
                    Ri = None
                    if bi < S["nblk"] - 1:
                        R32s, R32sb = R32[sx], R32b[sx]
                        if bi == 0:
                            P.op("dve", lambda e: e.tensor_scalar(out=R32s, in0=sp[si], scalar1=-1.0, scalar2=None, op0=ALU.mult),
                                 reads=[spb[si]], writes=[R32sb])
                        else:
                            P.op("dve", lambda e: e.tensor_tensor(out=R32s, in0=R32s, in1=sp[si], op=ALU.subtract),
                                 reads=[spb[si], R32sb], writes=[R32sb])
                        Ri = 3 * sx + rot(f"R{sx}", 3)
                        P.op("dve", lambda e: e.tensor_copy(out=Rbf[Ri], in_=R32s), reads=[R32sb], writes=[Rbfb[Ri]])
                    S["st"][bi] = dict(kblk=kblk, dk=dk, si=si, Ri=Ri, far=far, kb=kb, zb=zb)

                def stB(S, bi):
                    sx = S["sx"]
                    d = S["st"][bi]
                    dk, si, far = d["dk"], d["si"], d["far"]
                    pb = d["zb"]
                    P.op("pe", lambda e: e.matmul(ps[pb], ntri, sp[si], start=False, stop=(bi == 0)),
                         reads=[spb[si], cstb], writes=[psb[pb]], inc=(bi == 0))
                    if bi > 0:
                        Rp = S["st"][bi - 1]["Ri"]
                        P.op("pe", lambda e: e.matmul(ps[pb], ones, Rbf[Rp], start=False, stop=True),
                             reads=[Rbfb[Rp], cstb], writes=[psb[pb]])
                    ai = 3 * sx + rot(f"at{sx}", 3)
                    if far:
                        P.op("act", lambda e: e.activation(out=at[ai], in_=ps[pb], func=AF.Exp, bias=farb[:, 0:1], scale=1.0),
                             reads=[psb[pb], gainsb], writes=[atb[ai]])
                    else:
                        P.op("act", lambda e: e.activation(out=at[ai], in_=ps[pb], func=AF.Exp), reads=[psb[pb]], writes=[atb[ai]])
                    if dk is not None:
                        P.op("dve", lambda e: e.tensor_tensor(out=at[ai], in0=at[ai], in1=msb[dk], op=ALU.mult),
                             reads=[atb[ai], mskb], writes=[atb[ai]])
                    d["ai"] = ai

                def stC(S, bi):
                    d = S["st"][bi]
                    ai, yb, nblk = d["ai"], S["yb"], S["nblk"]
                    vblk = v_[:, d["far"] * NKB + d["kb"], 0:128]
                    P.op("pe", lambda e: e.matmul(ps[yb], vblk, at[ai], start=(bi == 0), stop=(bi == nblk - 1)),
                         reads=[v_b, atb[ai]], writes=[psb[yb]], inc=(bi == nblk - 1))

                sB, sC = SKEW_SB
                nmax = max(S["nblk"] for S in streams)
                for it in range(nmax + sC):
                    for S in streams:
                        if it < S["nblk"]:
                            stA(S, it)
                    for S in streams:
                        if sB <= it < S["nblk"] + sB:
                            stB(S, it - sB)
                    for S in streams:
                        if sC <= it < S["nblk"] + sC:
                            stC(S, it - sC)
                for S in streams:
                    yi = ysti[0] % 4
                    ysti[0] += 1
                    yb, g = S["yb"], S["g"]
                    P.op("act", lambda e, yb=yb, yi=yi: e.activation(out=yst[yi], in_=ps[yb], func=AF.Copy), reads=[psb[yb]], writes=[ystb[yi]])
                    P.dma(stq(), yAd[h * 128:(h + 1) * 128, g * G:(g + 1) * G], yst[yi], reads=[ystb[yi]])

        for h in range(4):
            hb = rot("hd", 2)
            k_, k_b = kT[hb], kTb[hb]
            v_, v_b = vv[hb], vvb[hb]
            for half in range(2):
                r0 = 1024 + (2 * h + half) * 128
                P.dma(hwq(), k_[:, half, :], kloc[r0:r0 + 128, :], writes=[k_b])
                P.dma(hwq(), k_[:, 2 + half, :], kfar(r0, 128), writes=[k_b])
            c0 = 1024 + h * 256
            P.dma(hwq(), v_[:, 0:NKB, :], vloc[:, c0:c0 + 256].rearrange("(b p) c -> p b c", p=128), writes=[v_b])
            for j in range(NVC):
                P.dma(hwq(), v_[:, NKB + 2 * j:NKB + 2 * j + 2, :],
                      vall[j * 512:j * 512 + 256, c0:c0 + 256].rearrange("(b p) c -> p b c", p=128), writes=[v_b])
            for g in range(TG):
                t0 = g * G
                blocks = [(0, kb) for kb in range(4 * g + 3, -1, -1)] + [(1, kb) for kb in range(NKB - 1, -1, -1)]
                nblk = len(blocks)
                for half in range(2):
                    qi = rot("q", 4)
                    r0 = (2 * h + half) * 128
                    P.dma(hwq(), qt[qi], qB[r0:r0 + 128, t0:t0 + G], writes=[qtb[qi]])
                    ysel = rot("y", 2)
                    ya, yb2, db = (2, 3, 6) if ysel == 0 else (4, 5, 7)
                    dst_ = {}

                    def dA(bi):
                        far, kb = blocks[bi]
                        kblk = k_[:, 2 * far + half, kb * 128:(kb + 1) * 128]
                        dk = (kb - 4 * g) if (far == 0 and kb >= 4 * g) else None
                        zb = rot("z", 2)
                        P.op("pe", lambda e: e.matmul(ps[zb], kblk, qt[qi], start=True, stop=True),
                             reads=[k_b, qtb[qi]], writes=[psb[zb]])
                        ai = rot("at", 3)
                        if far:
                            P.op("act", lambda e: e.activation(out=at[ai], in_=ps[zb], func=AF.Exp, bias=farb[:, 0:1], scale=1.0),
                                 reads=[psb[zb], gainsb], writes=[atb[ai]])
                        else:
                            P.op("act", lambda e: e.activation(out=at[ai], in_=ps[zb], func=AF.Exp), reads=[psb[zb]], writes=[atb[ai]])
                        if dk is not None:
                            P.op("dve", lambda e: e.tensor_tensor(out=at[ai], in0=at[ai], in1=mdf[dk], op=ALU.mult),
                                 reads=[atb[ai], mskb], writes=[atb[ai]])
                        dst_[bi] = (far, kb, ai)

                    def dB(bi):
                        far, kb, ai = dst_[bi]
                        st, sp_ = (bi == 0), (bi == nblk - 1)
                        P.op("pe", lambda e: e.matmul(ps[ya], v_[:, far * NKB + kb, 0:128], at[ai], start=st, stop=sp_),
                             reads=[v_b, atb[ai]], writes=[psb[ya]], inc=sp_)
                        P.op("pe", lambda e: e.matmul(ps[yb2], v_[:, far * NKB + kb, 128:256], at[ai], start=st, stop=sp_),
                             reads=[v_b, atb[ai]], writes=[psb[yb2]], inc=sp_)
                        P.op("pe", lambda e: e.matmul(ps[db], ones, at[ai], start=st, stop=sp_),
                             reads=[cstb, atb[ai]], writes=[psb[db]], inc=sp_)

                    for it in range(nblk + SKEW_DF):
                        if it < nblk:
                            dA(it)
                        if it >= SKEW_DF:
                            dB(it - SKEW_DF)
                    P.op("dve", lambda e, db=db: e.reciprocal(out=rc, in_=ps[db]), reads=[psb[db]], writes=[rcb])
                    if half == 0:
                        P.op("dve", lambda e, ya=ya: e.tensor_tensor(out=n1[:, 0, :], in0=ps[ya], in1=rc, op=ALU.mult), reads=[psb[ya], rcb], writes=[n1b])
                        P.op("dve", lambda e, yb2=yb2: e.tensor_tensor(out=n1[:, 1, :], in0=ps[yb2], in1=rc, op=ALU.mult), reads=[psb[yb2], rcb], writes=[n1b])
                    else:
                        for c, bk in ((0, ya), (1, yb2)):
                            P.op("dve", lambda e, c=c, bk=bk: e.tensor_tensor(out=y32[:, c, :], in0=ps[bk], in1=rc, op=ALU.mult),
                                 reads=[psb[bk], rcb], writes=[y32b])
                            P.op("dve", lambda e, c=c: e.scalar_tensor_tensor(out=y32[:, c, :], in0=y32[:, c, :], scalar=lam[:, 2 * li + 1:2 * li + 2],
                                                                             in1=n1[:, c, :], op0=ALU.mult, op1=ALU.add),
                                 reads=[y32b, n1b, lamb], writes=[y32b])
                        for c in range(2):
                            P.op("pool", lambda e, c=c: e.tensor_tensor(out=sp[c], in0=y32[:, c, :], in1=y32[:, c, :], op=ALU.mult),
                                 reads=[y32b], writes=[spb[c]])
                            P.op("pe", lambda e, c=c: e.matmul(ps[0], ones, sp[c], start=(c == 0), stop=(c == 1)),
                                 reads=[spb[c], cstb], writes=[psb[0]], inc=(c == 1))
                        P.op("act", lambda e: e.activation(out=rc, in_=ps[0], func=AF.Ln, bias=EPS, scale=1.0 / 256),
                             reads=[psb[0]], writes=[rcb])
                        P.op("act", lambda e: e.activation(out=rc, in_=rc, func=AF.Exp, scale=-0.5),
                             reads=[rcb], writes=[rcb])
                        P.op("dve", lambda e: e.tensor_scalar(out=rc, in0=rc, scalar1=gcol(li, GC_OML), scalar2=None, op0=ALU.mult),
                             reads=[rcb, gainsb], writes=[rcb])
                        for c in range(2):
                            yi = ysti[0] % 4
                            ysti[0] += 1
                            P.op("dve", lambda e, c=c, yi=yi: e.scalar_tensor_tensor(out=yst[yi], in0=y32[:, c, :],
                                                                                   scalar=gcol(li, GC_SUB + c), in1=rc, op0=ALU.mult, op1=ALU.mult),
                                 reads=[y32b, rcb, gainsb], writes=[ystb[yi]])
                            P.dma(stq(), yBd[(2 * h + c) * 128:(2 * h + c + 1) * 128, t0:t0 + G], yst[yi], reads=[ystb[yi]])
        P.barrier()
        ar.pop()

        hT = ar.alloc([128, DC, G], F32); hTb = Buf("hT")
        xn = ar.alloc([128, DC, G], BF16); xnb = Buf("xn")
        hid = ar.alloc([128, FC, G], BF16); hidb = Buf("hid")
        ws = WStream(4)
        rstd = ar.alloc([128, G], F32); rstdb = Buf("rstd")
        tmps = [ar.alloc([128, G], BF16) for _ in range(4)]; tmpbs = [Buf(f"tmp{i}") for i in range(4)]
        sg = [ar.alloc([128, G], F32) for _ in range(2)]; sgb = [Buf("sg0"), Buf("sg1")]
        gt = [ar.alloc([128, 2, G], BF16) for _ in range(4)]; gtb = [Buf(f"gt{i}") for i in range(4)]
        mg = [ar.alloc([128, G], F32) for _ in range(2)]; mgb = [Buf("mg0"), Buf("mg1")]
        ptl = ar.alloc([128, 2, G], BF16); ptlb = Buf("ptl")
        yg = hid[:, 0:16, :]; ygb = hidb
        for g in range(TG):
            t0 = g * G
            for q4 in range(4):
                P.dma(hwq(), hT[:, q4 * 4:(q4 + 1) * 4, :],
                      hS[q4 * 512:(q4 + 1) * 512, t0:t0 + G].rearrange("(c p) t -> p c t", p=128), writes=[hTb])
            P.dma("pool", ptl, pT[li][:, t0:t0 + G].rearrange("(c p) t -> p c t", p=128), writes=[ptlb])
            P.dma(hwq(), yg[:, 0:8, :], yAd[:, t0:t0 + G].rearrange("(c p) t -> p c t", p=128), writes=[ygb])
            P.dma(hwq(), yg[:, 8:16, :], yBd[:, t0:t0 + G].rearrange("(c p) t -> p c t", p=128), writes=[ygb])
            for ob in range(4):
                wa_, wab = ws.load(wsrc(w_a[li], ob * 512, 512), 8, 512)
                wb2, wbb2 = ws.load(wsrc(w_b[li], ob * 512, 512), 8, 512)
                for j in range(4):
                    oc = ob * 4 + j
                    gi = oc % 4
                    P.dma(hwq(), gt[gi], gts.rearrange("(two r) t -> r two t", two=2)[oc * 128:(oc + 1) * 128, :, t0:t0 + G], writes=[gtb[gi]])
                    pa, pb_ = (0, 1) if oc % 2 == 0 else (2, 3)
                    for kc in range(8):
                        P.op("pe", lambda e, kc=kc, j=j, pa=pa, wa_=wa_: e.matmul(ps[pa], wa_[:, kc, j * 128:(j + 1) * 128], yg[:, kc, :],
                                                                                  start=(kc == 0), stop=(kc == 7)),
                             reads=[wab, ygb], writes=[psb[pa]], inc=(kc == 7))
                    for kc in range(8):
                        P.op("pe", lambda e, kc=kc, j=j, pb_=pb_, wb2=wb2: e.matmul(ps[pb_], wb2[:, kc, j * 128:(j + 1) * 128], yg[:, 8 + kc, :],
                                                                                   start=(kc == 0), stop=(kc == 7)),
                             reads=[wbb2, ygb], writes=[psb[pb_]], inc=(kc == 7))
                    s = oc % 2
                    P.op("dve", lambda e, s=s, pa=pa, gi=gi: e.tensor_tensor(out=sg[s], in0=ps[pa], in1=gt[gi][:, 0, :], op=ALU.mult),
                         reads=[psb[pa], gtb[gi]], writes=[sgb[s]])
                    P.op("dve", lambda e, s=s, pb_=pb_, gi=gi: e.tensor_tensor(out=mg[s], in0=ps[pb_], in1=gt[gi][:, 1, :], op=ALU.mult),
                         reads=[psb[pb_], gtb[gi]], writes=[mgb[s]])
                    P.op("pool", lambda e, s=s, oc=oc: e.tensor_tensor(out=xn[:, oc, :], in0=sg[s], in1=mg[s], op=ALU.add),
                         reads=[sgb[s], mgb[s]], writes=[xnb])
            for ob in range(4):
                wo_, wob = ws.load(wsrc(w_o[li], ob * 512, 512), DC, 512)
                for j in range(4):
                    oc = ob * 4 + j
                    pd = 4 + oc % 2
                    for kc in range(DC):
                        P.op("pe", lambda e, kc=kc, j=j, pd=pd, wo_=wo_: e.matmul(ps[pd], wo_[:, kc, j * 128:(j + 1) * 128], xn[:, kc, :],
                                                                                  start=(kc == 0), stop=(kc == DC - 1)),
                             reads=[wob, xnb], writes=[psb[pd]], inc=(kc == DC - 1))
                    P.op("dve", lambda e, oc=oc, pd=pd: e.tensor_tensor(out=hT[:, oc, :], in0=ps[pd], in1=hT[:, oc, :], op=ALU.add),
                         reads=[psb[pd], hTb], writes=[hTb])
            ffn(li, 1, hT, hTb, xn, xnb, hid, hidb, ws, rstd, rstdb, tmps, tmpbs, sg, sgb)
            rms_bc(hT, DC, 6, rstd, rstdb, hTb, float(D), tmps, tmpbs)
            for c in range(DC):
                eng = "dve"
                P.op(eng, lambda e, c=c: e.scalar_tensor_tensor(out=xn[:, c, :], in0=hT[:, c, :], scalar=gcol(li, GC_PLE + c),
                                                                in1=rstd, op0=ALU.mult, op1=ALU.mult),
                     reads=[hTb, rstdb, gainsb], writes=[xnb])
            ppv = hid[:, 0:32, :].rearrange("p a b -> p (a b)").bitcast(F32).rearrange("p (a b) -> p a b", a=16)
            for ob in range(4):
                wp_, wpb = ws.load(wsrc(w_pp[li], ob * 512, 512), 2, 512)
                for j in range(4):
                    oc = ob * 4 + j
                    pd = 4 + oc % 2
                    for kc in range(2):
                        P.op("pe", lambda e, kc=kc, j=j, pd=pd, wp_=wp_: e.matmul(ps[pd], wp_[:, kc, j * 128:(j + 1) * 128], ptl[:, kc, :],
                                                                                  start=(kc == 0), stop=(kc == 1)),
                             reads=[wpb, ptlb], writes=[psb[pd]], inc=(kc == 1))
                    P.op("act", lambda e, oc=oc, pd=pd: e.activation(out=ppv[:, oc, :], in_=ps[pd], func=AF.Copy), reads=[psb[pd]], writes=[hidb])
            rms_bc(ppv, DC, 6, rstd, rstdb, hidb, float(D), tmps, tmpbs)
            for ob in range(4):
                wg_, wgb_ = ws.load(wsrc(w_pg[li], ob * 512, 512), DC, 512)
                for j in range(4):
                    oc = ob * 4 + j
                    pd = (0, 1, 2, 3)[oc % 4]
                    for kc in range(DC):
                        P.op("pe", lambda e, kc=kc, j=j, pd=pd, wg_=wg_: e.matmul(ps[pd], wg_[:, kc, j * 128:(j + 1) * 128], xn[:, kc, :],
                                                                                  start=(kc == 0), stop=(kc == DC - 1)),
                             reads=[wgb_, xnb], writes=[psb[pd]], inc=(kc == DC - 1))
                    s = oc % 2
                    P.op("act", lambda e, s=s, pd=pd: e.activation(out=sg[s], in_=ps[pd], func=AF.Sigmoid), reads=[psb[pd]], writes=[sgb[s]])
                    P.op("dve", lambda e, oc=oc: e.scalar_tensor_tensor(out=ppv[:, oc, :], in0=ppv[:, oc, :], scalar=gcol(li, GC_PLEO + oc),
                                                                        in1=rstd, op0=ALU.mult, op1=ALU.mult),
                         reads=[hidb, rstdb, gainsb], writes=[hidb])
                    P.op("dve", lambda e, s=s, oc=oc: e.tensor_tensor(out=sg[s], in0=sg[s], in1=ppv[:, oc, :], op=ALU.mult),
                         reads=[sgb[s], hidb], writes=[sgb[s]])
                    P.op("dve", lambda e, s=s, oc=oc: e.tensor_tensor(out=hT[:, oc, :], in0=hT[:, oc, :], in1=sg[s], op=ALU.add),
                         reads=[sgb[s], hTb], writes=[hTb])
            dst = outT if last else hS
            for q4 in range(4):
                P.dma(stq(), dst[q4 * 512:(q4 + 1) * 512, t0:t0 + G].rearrange("(c p) t -> p c t", p=128),
                      hT[:, q4 * 4:(q4 + 1) * 4, :], reads=[hTb])
        P.barrier()
        ar.pop()

    for li, l in enumerate(layers):
        if stop_after == "pre":
            break
        phase_A(li, l, xT if li == 0 else hS)
        if stop_after in ("A", "Aload", "Affn"):
            break
        exchange()
        phase_B(li, l, last=(li == L - 1))
    P.barrier()
    P.emit()
    return nc


def _consts():
    c = np.zeros((128, NCC), np.float32)
    c[:, CC_ONES:CC_ONES + 128] = 1.0
    j = np.arange(128)[:, None]
    s = np.arange(128)[None, :]
    c[:, CC_NTRI:CC_NTRI + 128] = -(j >= s).astype(np.float32)
    rot = np.zeros((128, 128), np.float32)
    for i in range(16):
        rot[16 + i, i] = -1.0
        rot[i, 16 + i] = 1.0
    c[:, CC_ROT:CC_ROT + 128] = rot
    tt = np.arange(512)[None, :]
    sp = np.arange(128)[:, None]
    for k in range(4):
        c[:, CC_MSB + k * 512:CC_MSB + (k + 1) * 512] = ((128 * k + sp) < tt).astype(np.float32)
        c[:, CC_MDF + k * 512:CC_MDF + (k + 1) * 512] = ((128 * k + sp) <= tt).astype(np.float32)
    return c


def _rope(pos0, T):
    pos = (pos0 + np.arange(T)).astype(np.float32)
    inv = (np.float32(500000.0) ** (-np.arange(0, 32, 2, dtype=np.float32) / np.float32(32))).astype(np.float32)
    ang = pos[:, None] * inv[None, :]
    cos = np.cos(ang).astype(np.float32).T
    sin = np.sin(ang).astype(np.float32).T
    r = np.zeros((128, 2, T), np.float32)
    r[:, 0, :] = 1.0
    r[0:16, 0, :] = cos
    r[16:32, 0, :] = cos
    r[0:16, 1, :] = sin
    r[16:32, 1, :] = sin
    return r


def _gains(inp, l, lam_layer):
    g = np.zeros((128, NGC), np.float32)

    def col(v):
        return np.asarray(v, np.float32).reshape(-1, 128).T

    g[:, GC_FFN1:GC_FFN1 + 16] = col(inp["ffn1_norm"][l])
    g[:, GC_MIX:GC_MIX + 16] = col(inp["mix_norm"][l])
    g[:, GC_FFN2:GC_FFN2 + 16] = col(inp["ffn2_norm"][l])
    g[:, GC_PLE:GC_PLE + 16] = col(inp["ple_norm"][l])
    g[:, GC_PLEO:GC_PLEO + 16] = col(inp["ple_out_norm"][l])
    g[:, GC_DQ:GC_DQ + 1] = col(inp["diff_q_norm"][l])
    g[:, GC_DK:GC_DK + 1] = col(inp["diff_k_norm"][l])
    g[:, GC_L + 0:GC_L + 1] = col(inp["diff_lambda_q1"][l])
    g[:, GC_L + 1:GC_L + 2] = col(inp["diff_lambda_k1"][l])
    g[:, GC_L + 2:GC_L + 3] = col(inp["diff_lambda_q2"][l])
    g[:, GC_L + 3:GC_L + 4] = col(inp["diff_lambda_k2"][l])
    g[:, GC_SUB:GC_SUB + 2] = col(inp["diff_sub_norm"][l])
    li = 0.8 - 0.6 * math.exp(-0.3 * lam_layer)
    g[:, GC_LI] = li
    g[:, GC_OML] = 1.0 - li
    return g


_NC_CACHE = {}


def _run(inp, TG, layers_list, dbg=None):
    T = TG * G
    S = 2 * T
    x = np.asarray(inp["x"], np.float32)
    B = x.shape[0]
    ncores = 2 * B
    cst = _consts()
    hT = [np.ascontiguousarray(x[c // 2, (c % 2) * T:(c % 2 + 1) * T, :].T) for c in range(ncores)]
    res = None
    for layers in layers_list:
        key = (TG, len(layers), str(dbg))
        if key not in _NC_CACHE:
            _NC_CACHE[key] = build(TG, layers, dbg)
        nc = _NC_CACHE[key]
        sl = slice(layers[0], layers[-1] + 1)
        shared = {
            "w_gu1": np.asarray(inp["ffn1_w_gu"][sl], np.float32), "w_gu2": np.asarray(inp["ffn2_w_gu"][sl], np.float32),
            "w_d1": np.asarray(inp["ffn1_w_down"][sl], np.float32), "w_d2": np.asarray(inp["ffn2_w_down"][sl], np.float32),
            "w_in": np.asarray(inp["w_in"][sl], np.float32), "w_a": np.asarray(inp["w_branch_a"][sl], np.float32),
            "w_b": np.asarray(inp["w_branch_b"][sl], np.float32), "w_o": np.asarray(inp["w_out"][sl], np.float32),
            "w_pg": np.asarray(inp["ple_w_gate"][sl], np.float32), "w_pp": np.asarray(inp["ple_w_proj"][sl], np.float32),
            "gains": np.stack([_gains(inp, l, l) for l in layers]), "cst": cst,
        }
        p = np.asarray(inp["p"], np.float32)
        in_maps = []
        for c in range(ncores):
            b, half = c // 2, c % 2
            m = dict(shared)
            m["xT"] = hT[c]
            m["pT"] = np.ascontiguousarray(np.stack([p[l, b, half * T:(half + 1) * T, :].T for l in layers]))
            m["rope"] = _rope(half * T, T)
            m["farbias"] = np.full((128, 1), 0.0 if half == 1 else -30000.0, np.float32)
            in_maps.append(m)
        res = run_bass_kernel_spmd(nc, in_maps, core_ids=list(range(ncores)))
        hT = [np.asarray(res.results[c]["outT"]) for c in range(ncores)]
    out = np.empty((B, S, D), np.float32)
    for c in range(ncores):
        out[c // 2, (c % 2) * T:(c % 2 + 1) * T, :] = hT[c].T
    return out, res


def kernel(**inputs):
    out, _ = _run(inputs, 4, [[0, 1]])
    return out
```

```python
import math
import types
import numpy as np
import ml_dtypes
import concourse.bass as bass
import concourse.mybir as mybir
from concourse.bass_utils import run_bass_kernel_spmd

F32 = mybir.dt.float32
BF16 = mybir.dt.bfloat16
AF = mybir.ActivationFunctionType
ALU = mybir.AluOpType

D = 2048
DC = 16
DFF = 5632
FC = 44
G = 512
DEPTH = 2
INW = 10240
EPS = 1e-6
CAP = 30000
NDS = 16
SCALE = 128 ** -0.5
import os as _os
SKEW_SB = tuple(int(x) for x in _os.environ.get('SKEW_SB', '1,1').split(','))
SKEW_DF = int(_os.environ.get('SKEW_DF', '1'))

GC_FFN1, GC_MIX, GC_FFN2, GC_PLE, GC_PLEO = 0, 16, 32, 48, 64
GC_DQ, GC_DK, GC_L, GC_SUB, GC_LI, GC_OML = 80, 81, 82, 86, 88, 89
NGC = 90
CC_ONES, CC_NTRI, CC_ROT, CC_MSB, CC_MDF = 0, 128, 256, 384, 384 + 2048
NCC = 384 + 4096


def _freeze(fn):
    if fn is None or fn.__closure__ is None:
        return fn
    cells = []
    for c in fn.__closure__:
        try:
            cells.append(types.CellType(c.cell_contents))
        except ValueError:
            cells.append(c)
    return types.FunctionType(fn.__code__, fn.__globals__, fn.__name__, fn.__defaults__, tuple(cells))


class Buf:
    __slots__ = ("w", "r", "name")

    def __init__(self, name=""):
        self.w = None
        self.r = {}
        self.name = name


class Prog:
    ENGS = ["pe", "act", "dve", "pool", "sp"]

    def __init__(self, nc):
        self.nc = nc
        self.streams = {e: [] for e in self.ENGS}
        self.cnt = {e: 0 for e in self.ENGS}
        self.seen = {e: {} for e in self.ENGS}
        self.ndma = {}
        self.dma_tokens_live = {}
        self.extra_sems = []

    def _need_wait(self, eng, tok):
        if tok[0] == "e":
            _, te, k = tok
            if te == eng and eng == "pe":
                return None
            if self.seen[eng].get(te, 0) >= k:
                return None
            self.seen[eng][te] = k
            return ("e", te, k)
        elif tok[0] == "d":
            _, q, i = tok
            slot = i % NDS
            val = 16 * (i // NDS + 1)
            key = ("d", q, slot)
            if self.seen[eng].get(key, 0) >= val:
                return None
            self.seen[eng][key] = val
            return ("d", q, slot, val)
        else:
            key = tok
            if self.seen[eng].get(key, 0) >= 1:
                return None
            self.seen[eng][key] = 1
            return tok

    def _deps(self, eng, reads, writes):
        toks = []
        for b in reads:
            if b.w is not None:
                toks.append(b.w)
        for b in writes:
            if b.w is not None:
                toks.append(b.w)
            toks.extend(b.r.values())
        waits = []
        for t in toks:
            w = self._need_wait(eng, t)
            if w is not None:
                waits.append(w)
        return waits

    def _mark(self, tok, key, reads, writes):
        for b in reads:
            b.r[key] = tok
        for b in writes:
            b.w = tok
            b.r = {}

    def op(self, eng, fn, reads=(), writes=(), inc=True):
        if eng != "pe":
            inc = True
        waits = self._deps(eng, reads, writes)
        k = self.cnt[eng] + 1
        tok = ("e", eng, k)
        if inc:
            self.cnt[eng] = k
        self.streams[eng].append((waits, _freeze(fn), k if inc else None, None))
        self._mark(tok, eng, reads, writes)
        return tok

    def dma(self, eng, out, in_, reads=(), writes=()):
        waits = self._deps(eng, reads, writes)
        i = self.ndma.get(eng, 0)
        self.ndma[eng] = i + 1
        if i >= NDS:
            w = self._need_wait(eng, ("d", eng, i - NDS))
            if w is not None:
                waits.append(w)
        tok = ("d", eng, i)
        self.streams[eng].append((waits, lambda e: e.dma_start(out=out, in_=in_), None, i))
        self._mark(tok, tok, reads, writes)
        live = self.dma_tokens_live.setdefault(eng, [])
        live.append(tok)
        if len(live) > NDS:
            del live[:-NDS]
        return tok

    def barrier(self, extra=()):
        toks = [("e", e, self.cnt[e]) for e in self.ENGS if self.cnt[e] > 0]
        for live in self.dma_tokens_live.values():
            toks += list(live)
        toks += list(extra)
        for eng in self.ENGS:
            waits = []
            for t in toks:
                if t[0] == "e" and t[1] == eng and eng == "pe":
                    continue
                w = self._need_wait(eng, t)
                if w is not None:
                    waits.append(w)
            if waits:
                self.streams[eng].append((waits, None, None, None))

    def emit(self):
        nc = self.nc
        nsem = {e: (self.cnt[e] + CAP - 1) // CAP + 1 for e in self.ENGS}
        sems = {e: [nc.alloc_semaphore(f"s_{e}_{j}") for j in range(nsem[e])] for e in self.ENGS}
        dsems = {q: [nc.alloc_semaphore(f"s_dma_{q}_{j}") for j in range(NDS)] for q in self.ndma}
        xsems = self.extra_sems
        handles = {"pe": "tensor", "act": "scalar", "dve": "vector", "pool": "gpsimd", "sp": "sync"}

        def run(ename, eng):
            for waits, fn, k, di in self.streams[ename]:
                for w in waits:
                    if w[0] == "e":
                        _, te, kk = w
                        eng.wait_ge(sems[te][(kk - 1) // CAP], (kk - 1) % CAP + 1)
                    elif w[0] == "d":
                        eng.wait_ge(dsems[w[1]][w[2]], w[3])
                    else:
                        eng.wait_ge(xsems[w[1]], 1)
                if fn is None:
                    continue
                ins = fn(eng)
                if k is not None:
                    ins.then_inc(sems[ename][(k - 1) // CAP], 1)
                if di is not None:
                    ins.then_inc(dsems[ename][di % NDS], 16)

        with nc.Block() as block:
            @block.tensor
            def _(e):
                run("pe", e)

            @block.scalar
            def _(e):
                run("act", e)

            @block.vector
            def _(e):
                run("dve", e)

            @block.gpsimd
            def _(e):
                run("pool", e)

            @block.sync
            def _(e):
                run("sp", e)


class Arena:
    def __init__(self, nc, nbytes):
        self.t = nc.alloc_sbuf_tensor("arena", [128, nbytes // 2], BF16)
        self.nbytes = nbytes
        self.off = 0
        self.marks = []

    def push(self):
        self.marks.append(self.off)

    def pop(self):
        self.off = self.marks.pop()

    def alloc(self, shape, dtype):
        es = 4 if dtype == F32 else 2
        n = int(np.prod(shape[1:]))
        nb = n * es
        nb = (nb + 63) // 64 * 64
        assert self.off + nb <= self.nbytes, f"arena overflow {self.off + nb} > {self.nbytes}"
        o = self.off // 2
        ap = self.t[:, o:o + nb // 2]
        self.off += nb
        if dtype == F32:
            ap = ap.bitcast(F32)
        ap = ap[:, 0:n]
        if len(shape) == 3:
            ap = ap.rearrange("p (a b) -> p a b", a=shape[1])
        return ap


def build(TG, layers, dbg=None):
    T = TG * G
    NKB = T // 128
    L = len(layers)
    nc = bass.Bass("TRN2", target_bir_lowering=False)
    P = Prog(nc)

    def din(name, shape, dt=F32):
        return nc.dram_tensor(name, shape, dt, kind="ExternalInput").ap()

    def dscr(name, shape, dt, out=False):
        kind = "ExternalOutput" if (out or (dbg and name in dbg)) else "Internal"
        return nc.dram_tensor(name, shape, dt, kind=kind)

    xT = din("xT", [D, T])
    pT = din("pT", [L, 256, T])
    w_gu = [din("w_gu1", [L, D, 2 * DFF]), din("w_gu2", [L, D, 2 * DFF])]
    w_dn = [din("w_d1", [L, DFF, D]), din("w_d2", [L, DFF, D])]
    w_in = din("w_in", [L, D, INW])
    w_a = din("w_a", [L, 1024, D])
    w_b = din("w_b", [L, 1024, D])
    w_o = din("w_o", [L, D, D])
    w_pg = din("w_pg", [L, D, D])
    w_pp = din("w_pp", [L, 256, D])
    gains_d = din("gains", [L, 128, NGC])
    cst_d = din("cst", [128, NCC])
    rope_d = din("rope", [128, 2, T])
    farb_d = din("farbias", [128, 1])

    outT = nc.dram_tensor("outT", [D, T], F32, kind="ExternalOutput").ap()
    hS = dscr("hS", [D, T], F32).ap()
    qA = dscr("qA", [1024, T], BF16).ap()
    qB = dscr("qB", [1024, T], BF16).ap()
    gts = dscr("gts", [4096, T], BF16).ap()
    kloc = dscr("kloc", [2048, T], BF16)
    vloc = dscr("vloc", [T, 2048], BF16)
    kall = dscr("kall", [4096, T], BF16)
    vall = dscr("vall", [2 * T, 2048], BF16)
    yAd = dscr("yAd", [1024, T], BF16).ap()
    yBd = dscr("yBd", [1024, T], BF16).ap()

    ar = Arena(nc, 188 * 1024)
    ps = [nc.alloc_psum_tensor(f"ps{i}", [128, 512], F32).ap() for i in range(8)]
    psb = [Buf(f"ps{i}") for i in range(8)]

    cst = ar.alloc([128, 384], BF16)
    cstb = Buf("cst")
    gains = ar.alloc([128, L * NGC], F32)
    gainsb = Buf("gains")
    farb = ar.alloc([128, 1], F32)
    lam = ar.alloc([128, 2 * L], F32)
    lamb = Buf("lam")
    P.dma("pool", cst, cst_d[:, 0:384], writes=[cstb])
    for l in range(L):
        P.dma("sp", gains[:, l * NGC:(l + 1) * NGC], gains_d[l], writes=[gainsb])
    P.dma("sp", farb, farb_d, writes=[gainsb])
    ones = cst[:, CC_ONES:CC_ONES + 128]
    ntri = cst[:, CC_NTRI:CC_NTRI + 128]
    rotm = cst[:, CC_ROT:CC_ROT + 128]

    def gcol(l, c, n=1):
        return gains[:, l * NGC + c:l * NGC + c + n]

    ar.push()
    ltmp = ar.alloc([128, 8], F32)
    ltb = Buf("ltmp")
    ones32 = ar.alloc([128, 128], F32)
    o32b = Buf("ones32")
    P.op("dve", lambda e: e.memset(ones32, 1.0), writes=[o32b])
    for l in range(L):
        P.op("dve", lambda e, l=l: e.tensor_tensor(out=ltmp[:, 0:1], in0=gcol(l, GC_L), in1=gcol(l, GC_L + 1), op=ALU.mult),
             reads=[gainsb], writes=[ltb])
        P.op("dve", lambda e, l=l: e.tensor_tensor(out=ltmp[:, 1:2], in0=gcol(l, GC_L + 2), in1=gcol(l, GC_L + 3), op=ALU.mult),
             reads=[gainsb], writes=[ltb])
        P.op("pe", lambda e: e.matmul(ps[7][:, 0:2], ones32, ltmp[:, 0:2], start=True, stop=True),
             reads=[ltb, o32b], writes=[psb[7]])
        P.op("act", lambda e: e.activation(out=ltmp[:, 2:4], in_=ps[7][:, 0:2], func=AF.Exp), reads=[psb[7]], writes=[ltb])
        P.op("dve", lambda e: e.tensor_tensor(out=ltmp[:, 4:5], in0=ltmp[:, 2:3], in1=ltmp[:, 3:4], op=ALU.subtract),
             reads=[ltb], writes=[ltb])
        P.op("dve", lambda e, l=l: e.tensor_tensor(out=lam[:, 2 * l:2 * l + 1], in0=ltmp[:, 4:5], in1=gcol(l, GC_LI), op=ALU.add),
             reads=[ltb, gainsb], writes=[lamb])
        P.op("dve", lambda e, l=l: e.tensor_scalar(out=lam[:, 2 * l + 1:2 * l + 2], in0=lam[:, 2 * l:2 * l + 1], scalar1=-1.0, scalar2=None, op0=ALU.mult),
             reads=[lamb], writes=[lamb])
    P.barrier()
    ar.pop()
    stop_after = dbg.get("stop") if dbg else None

    dmaq = ["sp", "act"]
    rr = [0]

    def hwq():
        return "sp"

    def stq():
        return "act"

    def rms_bc(src, nch, bank, out_rstd, rstdb, srcb, Dn, tmps, tmpbs):
        nt = len(tmps)
        for c in range(nch):
            sq, sqb = tmps[c % nt], tmpbs[c % nt]
            eng = "dve" if c % 2 == 0 else "pool"
            P.op(eng, lambda e, c=c, sq=sq: e.tensor_tensor(out=sq, in0=src[:, c, :], in1=src[:, c, :], op=ALU.mult),
                 reads=[srcb], writes=[sqb])
            P.op("pe", lambda e, c=c, sq=sq: e.matmul(ps[bank], ones, sq, start=(c == 0), stop=(c == nch - 1)),
                 reads=[sqb, cstb], writes=[psb[bank]], inc=True)
        P.op("act", lambda e: e.activation(out=out_rstd, in_=ps[bank], func=AF.Ln, bias=EPS, scale=1.0 / Dn),
             reads=[psb[bank]], writes=[rstdb])
        P.op("act", lambda e: e.activation(out=out_rstd, in_=out_rstd, func=AF.Exp, scale=-0.5),
             reads=[rstdb], writes=[rstdb])

    class WStream:
        def __init__(self, nslots, nelem=8192):
            self.slots = [ar.alloc([128, nelem], BF16) for _ in range(nslots)]
            self.bufs = [Buf(f"w{i}") for i in range(nslots)]
            self.i = 0

        def load(self, src_ap, kc, ncol):
            s = self.i % len(self.slots)
            self.i += 1
            dst = self.slots[s][:, 0:kc * ncol].rearrange("p (c n) -> p c n", c=kc)
            P.dma("pool", dst, src_ap, writes=[self.bufs[s]])
            return dst, self.bufs[s]

    def wsrc(w2d, c0, ncol):
        return w2d[:, c0:c0 + ncol].rearrange("(c p) n -> p c n", p=128)

    def ffn(l, which, hT, hTb, xn, xnb, hid, hidb, ws, rstd, rstdb, tmps, tmpbs, sg, sgb):
        gc = GC_FFN1 if which == 0 else GC_FFN2
        rms_bc(hT, DC, 6, rstd, rstdb, hTb, float(D), tmps, tmpbs)
        for c in range(DC):
            eng = "dve"
            P.op(eng, lambda e, c=c: e.scalar_tensor_tensor(out=xn[:, c, :], in0=hT[:, c, :], scalar=gcol(l, gc + c),
                                                            in1=rstd, op0=ALU.mult, op1=ALU.mult),
                 reads=[hTb, rstdb, gainsb], writes=[xnb])
        wgu = w_gu[which][l]
        wdn = w_dn[which][l]
        for fb in range(FC // 4):
            wg, wgb = ws.load(wsrc(wgu, fb * 512, 512), DC, 512)
            wu, wub = ws.load(wsrc(wgu, DFF + fb * 512, 512), DC, 512)
            for j in range(4):
                fc = fb * 4 + j
                pg, pu = (0, 1) if fc % 2 == 0 else (2, 3)
                for kc in range(DC):
                    P.op("pe", lambda e, kc=kc, j=j, wg=wg, pg=pg: e.matmul(ps[pg], wg[:, kc, j * 128:(j + 1) * 128], xn[:, kc, :],
                                                                            start=(kc == 0), stop=(kc == DC - 1)),
                         reads=[wgb, xnb], writes=[psb[pg]], inc=(kc == DC - 1))
                for kc in range(DC):
                    P.op("pe", lambda e, kc=kc, j=j, wu=wu, pu=pu: e.matmul(ps[pu], wu[:, kc, j * 128:(j + 1) * 128], xn[:, kc, :],
                                                                            start=(kc == 0), stop=(kc == DC - 1)),
                         reads=[wub, xnb], writes=[psb[pu]], inc=(kc == DC - 1))
                s = fc % 2
                P.op("act", lambda e, pg=pg, s=s: e.activation(out=sg[s], in_=ps[pg], func=AF.Silu), reads=[psb[pg]], writes=[sgb[s]])
                P.op("dve", lambda e, pu=pu, s=s, fc=fc: e.tensor_tensor(out=hid[:, fc, :], in0=sg[s], in1=ps[pu], op=ALU.mult),
                     reads=[sgb[s], psb[pu]], writes=[hidb])
        dbanks = [4, 5, 0, 1]
        for db in range(8):
            halves = []
            for kh in range(2):
                halves.append(ws.load(wdn[kh * 2816:(kh + 1) * 2816, db * 256:(db + 1) * 256].rearrange("(c p) n -> p c n", p=128), 22, 256))
            for kh in range(2):
                wd, wdb = halves[kh]
                for j in range(2):
                    pd = dbanks[(db % 2) * 2 + j]
                    for fc in range(22):
                        last = (kh == 1 and fc == 21)
                        P.op("pe", lambda e, fc=fc, kh=kh, j=j, wd=wd, pd=pd, last=last: e.matmul(
                            ps[pd], wd[:, fc, j * 128:(j + 1) * 128], hid[:, kh * 22 + fc, :], start=(kh == 0 and fc == 0), stop=last),
                             reads=[wdb, hidb], writes=[psb[pd]], inc=last)
            for j in range(2):
                dc = db * 2 + j
                pd = dbanks[(db % 2) * 2 + j]
                P.op("dve", lambda e, dc=dc, pd=pd: e.scalar_tensor_tensor(out=hT[:, dc, :], in0=ps[pd], scalar=0.5, in1=hT[:, dc, :],
                                                                           op0=ALU.mult, op1=ALU.add),
                     reads=[psb[pd], hTb], writes=[hTb])

    def phase_A(li, l, src_h):
        ar.push()
        hT = ar.alloc([128, DC, G], F32); hTb = Buf("hT")
        xn = ar.alloc([128, DC, G], BF16); xnb = Buf("xn")
        hid = ar.alloc([128, FC, G], BF16); hidb = Buf("hid")
        ws = WStream(4)
        rstd = ar.alloc([128, G], F32); rstdb = Buf("rstd")
        rs2 = [ar.alloc([128, G], F32) for _ in range(2)]; rs2b = [Buf("rs20"), Buf("rs21")]
        tmps = [ar.alloc([128, G], BF16) for _ in range(4)]; tmpbs = [Buf(f"tmp{i}") for i in range(4)]
        sg = [ar.alloc([128, G], F32) for _ in range(2)]; sgb = [Buf("sg0"), Buf("sg1")]
        stg = [ar.alloc([128, G], BF16) for _ in range(4)]; stgb = [Buf(f"stg{i}") for i in range(4)]
        xg = [ar.alloc([128, G], BF16) for _ in range(2)]; xgb = [Buf("xg0"), Buf("xg1")]
        x32 = [ar.alloc([128, G], F32) for _ in range(2)]; x32b = [Buf("x320"), Buf("x321")]
        rp = ar.alloc([128, 2, G], F32); rpb = Buf("rope")
        sti = [0]

        def stage():
            i = sti[0] % 4
            sti[0] += 1
            return stg[i], stgb[i]

        for g in range(TG):
            t0 = g * G
            for q4 in range(4):
                P.dma(hwq(), hT[:, q4 * 4:(q4 + 1) * 4, :],
                      src_h[q4 * 512:(q4 + 1) * 512, t0:t0 + G].rearrange("(c p) t -> p c t", p=128), writes=[hTb])
            P.dma(hwq(), rp, rope_d[:, :, t0:t0 + G], writes=[rpb])
            if stop_after != "Aload":
                ffn(li, 0, hT, hTb, xn, xnb, hid, hidb, ws, rstd, rstdb, tmps, tmpbs, sg, sgb)
            for q4 in range(4):
                P.dma(stq(), hS[q4 * 512:(q4 + 1) * 512, t0:t0 + G].rearrange("(c p) t -> p c t", p=128),
                      hT[:, q4 * 4:(q4 + 1) * 4, :], reads=[hTb])
            if stop_after in ("Aload", "Affn"):
                continue
            rms_bc(hT, DC, 6, rstd, rstdb, hTb, float(D), tmps, tmpbs)
            for c in range(DC):
                eng = "dve"
                P.op(eng, lambda e, c=c: e.scalar_tensor_tensor(out=xn[:, c, :], in0=hT[:, c, :], scalar=gcol(li, GC_MIX + c),
                                                                in1=rstd, op0=ALU.mult, op1=ALU.mult),
                     reads=[hTb, rstdb, gainsb], writes=[xnb])
            win = w_in[li]
            pbank = [0]

            def nextbank():
                b = pbank[0] % 4
                pbank[0] += 1
                return b

            for cb in range(20):
                wb_, wbb = ws.load(wsrc(win, cb * 512, 512), DC, 512)
                kind = ["sbq", "sbk", "sbv", "dfq", "dfk", "dfv", "ga", "ga", "gb", "gb"][cb // 2]
                if kind in ("sbv", "dfv"):
                    for tt in range(4):
                        b = nextbank()
                        for kc in range(DC):
                            P.op("pe", lambda e, kc=kc, tt=tt, b=b, wb_=wb_: e.matmul(ps[b], xn[:, kc, tt * 128:(tt + 1) * 128], wb_[:, kc, :],
                                                                                      start=(kc == 0), stop=(kc == DC - 1)),
                                 reads=[wbb, xnb], writes=[psb[b]], inc=(kc == DC - 1))
                        st, stb = stage()
                        P.op("act", lambda e, b=b, st=st: e.activation(out=st, in_=ps[b], func=AF.Copy), reads=[psb[b]], writes=[stb])
                        col0 = (0 if kind == "sbv" else 1024) + (cb % 2) * 512
                        P.dma(stq(), vloc[t0 + tt * 128:t0 + (tt + 1) * 128, col0:col0 + 512], st, reads=[stb])
                    continue
                for j in range(4):
                    b = nextbank()
                    oc = (cb % 2) * 4 + j
                    for kc in range(DC):
                        P.op("pe", lambda e, kc=kc, j=j, b=b, wb_=wb_: e.matmul(ps[b], wb_[:, kc, j * 128:(j + 1) * 128], xn[:, kc, :],
                                                                                start=(kc == 0), stop=(kc == DC - 1)),
                             reads=[wbb, xnb], writes=[psb[b]], inc=(kc == DC - 1))
                    st, stb = stage()
                    if kind == "sbq":
                        P.op("act", lambda e, b=b, st=st: e.activation(out=st, in_=ps[b], func=AF.Copy, scale=SCALE), reads=[psb[b]], writes=[stb])
                        P.dma(stq(), qA[oc * 128:(oc + 1) * 128, t0:t0 + G], st, reads=[stb])
                    elif kind == "sbk":
                        P.op("act", lambda e, b=b, st=st: e.activation(out=st, in_=ps[b], func=AF.Copy), reads=[psb[b]], writes=[stb])
                        P.dma(stq(), kloc[oc * 128:(oc + 1) * 128, t0:t0 + G], st, reads=[stb])
                    elif kind in ("ga", "gb"):
                        gch = ((cb - 12) * 4 + j)
                        P.op("act", lambda e, b=b, st=st: e.activation(out=st, in_=ps[b], func=AF.Sigmoid), reads=[psb[b]], writes=[stb])
                        P.dma(stq(), gts[gch * 128:(gch + 1) * 128, t0:t0 + G], st, reads=[stb])
                    else:
                        isq = kind == "dfq"
                        gcn = GC_DQ if isq else GC_DK
                        s = oc % 2
                        bss, brt = 4 + s, 6 + s
                        P.op("act", lambda e, b=b, s=s: e.activation(out=tmps[s], in_=ps[b], func=AF.Square), reads=[psb[b]], writes=[tmpbs[s]])
                        P.op("act", lambda e, b=b, s=s, gcn=gcn: e.activation(out=x32[s], in_=ps[b], func=AF.Copy, scale=gcol(li, gcn)),
                             reads=[psb[b], gainsb], writes=[x32b[s]])
                        P.op("pe", lambda e, s=s, bss=bss: e.matmul(ps[bss], ones, tmps[s], start=True, stop=True), reads=[tmpbs[s], cstb], writes=[psb[bss]])
                        P.op("dve", lambda e, s=s: e.tensor_copy(out=xg[s], in_=x32[s]), reads=[x32b[s]], writes=[xgb[s]])
                        P.op("pe", lambda e, s=s, brt=brt: e.matmul(ps[brt], rotm, xg[s], start=True, stop=True), reads=[xgb[s], cstb], writes=[psb[brt]])
                        P.op("act", lambda e, s=s, bss=bss: e.activation(out=rs2[s], in_=ps[bss], func=AF.Ln, bias=EPS, scale=1.0 / 128),
                             reads=[psb[bss]], writes=[rs2b[s]])
                        P.op("act", lambda e, s=s, isq=isq: e.activation(out=rs2[s], in_=rs2[s], func=AF.Exp, scale=-0.5, bias=(math.log(SCALE) if isq else 0.0)),
                             reads=[rs2b[s]], writes=[rs2b[s]])
                        P.op("dve", lambda e, s=s: e.tensor_tensor(out=x32[s], in0=x32[s], in1=rp[:, 0, :], op=ALU.mult),
                             reads=[x32b[s], rpb], writes=[x32b[s]])
                        P.op("dve", lambda e, s=s, brt=brt: e.tensor_tensor(out=sg[s], in0=ps[brt], in1=rp[:, 1, :], op=ALU.mult),
                             reads=[psb[brt], rpb], writes=[sgb[s]])
                        P.op("dve", lambda e, s=s: e.tensor_tensor(out=x32[s], in0=x32[s], in1=sg[s], op=ALU.add),
                             reads=[x32b[s], sgb[s]], writes=[x32b[s]])
                        P.op("dve", lambda e, s=s, st=st: e.tensor_tensor(out=st, in0=x32[s], in1=rs2[s], op=ALU.mult),
                             reads=[x32b[s], rs2b[s]], writes=[stb])
                        if isq:
                            P.dma(stq(), qB[oc * 128:(oc + 1) * 128, t0:t0 + G], st, reads=[stb])
                        else:
                            P.dma(stq(), kloc[1024 + oc * 128:1024 + (oc + 1) * 128, t0:t0 + G], st, reads=[stb])
        P.barrier()
        ar.pop()

    RK = min(2048, (1 << 20) // T)
    NKC = 2048 // RK
    RV = min(T, 512)
    NVC = T // RV
    BV = RV // 128

    def kfar(r0, n):
        j, w = r0 // RK, r0 % RK
        return kall[j * 2 * RK + w:j * 2 * RK + w + n, :]

    def exchange():
        groups = [[0, 1], [2, 3], [4, 5], [6, 7]]
        toks = []
        jobs = [(kloc, kall, RK, j) for j in range(NKC)] + [(vloc, vall, RV, j) for j in range(NVC)]
        for (src, dst, R_, j) in jobs:
            sem = nc.alloc_semaphore(f"cc{len(P.extra_sems)}")
            idx = len(P.extra_sems)
            P.extra_sems.append(sem)
            P.streams["pool"].append(([], (lambda e, src=src, dst=dst, sem=sem, R_=R_, j=j: e.collective_compute(
                "AllGather", ALU.bypass, replica_groups=groups, ins=[src[j * R_:(j + 1) * R_, :]],
                outs=[dst[j * 2 * R_:(j + 1) * 2 * R_, :]]).then_inc(sem, 1)), None, None))
            toks.append(("x", idx))
        P.barrier(extra=toks)

    def phase_B(li, l, last):
        ar.push()
        ar.push()
        msk = ar.alloc([128, 4096], BF16); mskb = Buf("msk")
        P.dma("pool", msk, cst_d[:, 384:384 + 4096], writes=[mskb])
        yst = [ar.alloc([128, G], BF16) for _ in range(4)]; ystb = [Buf(f"yst{i}") for i in range(4)]
        ysti = [0]
        kT = [ar.alloc([128, 4, T], BF16) for _ in range(2)]; kTb = [Buf("kT0"), Buf("kT1")]
        vv = [ar.alloc([128, 2 * NKB, 256], BF16) for _ in range(2)]; vvb = [Buf("v0"), Buf("v1")]
        qt = [ar.alloc([128, G], BF16) for _ in range(4)]; qtb = [Buf(f"qt{i}") for i in range(4)]
        e32 = [ar.alloc([128, G], F32) for _ in range(4)]; e32b = [Buf(f"e{i}") for i in range(4)]
        sp = [ar.alloc([128, G], BF16) for _ in range(8)]; spb = [Buf(f"sp{i}") for i in range(8)]
        at = [ar.alloc([128, G], BF16) for _ in range(6)]; atb = [Buf(f"at{i}") for i in range(6)]
        R32 = [ar.alloc([128, G], F32) for _ in range(2)]; R32b = [Buf("R320"), Buf("R321")]
        Rbf = [ar.alloc([128, G], BF16) for _ in range(6)]; Rbfb = [Buf(f"Rbf{i}") for i in range(6)]
        n1 = ar.alloc([128, 2, G], F32); n1b = Buf("n1")
        rc = ar.alloc([128, G], F32); rcb = Buf("rc")
        y32 = ar.alloc([128, 2, G], F32); y32b = Buf("y32")
        tq = ar.alloc([128, G], BF16); tqb = Buf("tq")
        msb = [msk[:, k * 512:(k + 1) * 512] for k in range(4)]
        mdf = [msk[:, 2048 + k * 512:2048 + (k + 1) * 512] for k in range(4)]
        cnt = {"q": 0, "e": 0, "sp": 0, "at": 0, "z": 0, "zs": 0, "p": 0, "y": 0, "R": 0, "hd": 0}

        def rot(name, n):
            i = cnt.get(name, 0) % n
            cnt[name] = cnt.get(name, 0) + 1
            return i

        for h in range(8):
            hb = rot("hd", 2)
            k_, k_b = kT[hb], kTb[hb]
            v_, v_b = vv[hb], vvb[hb]
            P.dma(hwq(), k_[:, 0, :], kloc[h * 128:(h + 1) * 128, :], writes=[k_b])
            P.dma(hwq(), k_[:, 1, :], kfar(h * 128, 128), writes=[k_b])
            P.dma(hwq(), v_[:, 0:NKB, 0:128], vloc[:, h * 128:(h + 1) * 128].rearrange("(b p) c -> p b c", p=128), writes=[v_b])
            for j in range(NVC):
                P.dma(hwq(), v_[:, NKB + BV * j:NKB + BV * (j + 1), 0:128],
                      vall[j * 2 * RV:j * 2 * RV + RV, h * 128:(h + 1) * 128].rearrange("(b p) c -> p b c", p=128), writes=[v_b])
            NSTR = 2 if TG >= 2 else 1
            for g0 in range(0, TG, NSTR):
                streams = []
                for sidx in range(NSTR):
                    g = g0 + sidx
                    qi = rot("q", 4)
                    P.dma(hwq(), qt[qi], qA[h * 128:(h + 1) * 128, g * G:(g + 1) * G], writes=[qtb[qi]])
                    blocks = [(0, kb) for kb in range(4 * g + 3, -1, -1)] + [(1, kb) for kb in range(NKB - 1, -1, -1)]
                    streams.append(dict(g=g, qi=qi, yb=4 + sidx, blocks=blocks, nblk=len(blocks), st={}, sx=sidx))

                def stA(S, bi):
                    sx, g, qi = S["sx"], S["g"], S["qi"]
                    far, kb = S["blocks"][bi]
                    kblk = k_[:, far, kb * 128:(kb + 1) * 128]
                    dk = (kb - 4 * g) if (far == 0 and kb >= 4 * g) else None
                    zb = 2 * sx + rot(f"zs{sx}", 2)
                    P.op("pe", lambda e: e.matmul(ps[zb], kblk, qt[qi], start=True, stop=True),
                         reads=[k_b, qtb[qi]], writes=[psb[zb]])
                    ei = 2 * sx + rot(f"e{sx}", 2)
                    P.op("act", lambda e: e.activation(out=e32[ei], in_=ps[zb], func=AF.Exp), reads=[psb[zb]], writes=[e32b[ei]])
                    si = 4 * sx + rot(f"sp{sx}", 4)
                    P.op("act", lambda e: e.activation(out=sp[si], in_=e32[ei], func=AF.Ln, bias=1.0, scale=1.0),
                         reads=[e32b[ei]], writes=[spb[si]])
                    if dk is not None:
                        P.op("dve", lambda e: e.tensor_tensor(out=sp[si], in0=sp[si], in1=msb[dk], op=ALU.mult),
                             reads=[spb[si], mskb], writes=[spb[si]])
                    Ri = None
                    if bi < S["nblk"] - 1:
                        R32s, R32sb = R32[sx], R32b[sx]
                        if bi == 0:
                            P.op("dve", lambda e: e.tensor_scalar(out=R32s, in0=sp[si], scalar1=-1.0, scalar2=None, op0=ALU.mult),
                                 reads=[spb[si]], writes=[R32sb])
                        else:
                            P.op("dve", lambda e: e.tensor_tensor(out=R32s, in0=R32s, in1=sp[si], op=ALU.subtract),
                                 reads=[spb[si], R32sb], writes=[R32sb])
                        Ri = 3 * sx + rot(f"R{sx}", 3)
                        P.op("dve", lambda e: e.tensor_copy(out=Rbf[Ri], in_=R32s), reads=[R32sb], writes=[Rbfb[Ri]])
                    S["st"][bi] = dict(kblk=kblk, dk=dk, si=si, Ri=Ri, far=far, kb=kb, zb=zb)

                def stB(S, bi):
                    sx = S["sx"]
                    d = S["st"][bi]
                    dk, si, far = d["dk"], d["si"], d["far"]
                    pb = d["zb"]
                    P.op("pe", lambda e: e.matmul(ps[pb], ntri, sp[si], start=False, stop=(bi == 0)),
                         reads=[spb[si], cstb], writes=[psb[pb]], inc=(bi == 0))
                    if bi > 0:
                        Rp = S["st"][bi - 1]["Ri"]
                        P.op("pe", lambda e: e.matmul(ps[pb], ones, Rbf[Rp], start=False, stop=True),
                             reads=[Rbfb[Rp], cstb], writes=[psb[pb]])
                    ai = 3 * sx + rot(f"at{sx}", 3)
                    if far:
                        P.op("act", lambda e: e.activation(out=at[ai], in_=ps[pb], func=AF.Exp, bias=farb[:, 0:1], scale=1.0),
                             reads=[psb[pb], gainsb], writes=[atb[ai]])
                    else:
                        P.op("act", lambda e: e.activation(out=at[ai], in_=ps[pb], func=AF.Exp), reads=[psb[pb]], writes=[atb[ai]])
                    if dk is not None:
                        P.op("dve", lambda e: e.tensor_tensor(out=at[ai], in0=at[ai], in1=msb[dk], op=ALU.mult),
                             reads=[atb[ai], mskb], writes=[atb[ai]])
                    d["ai"] = ai

                def stC(S, bi):
                    d = S["st"][bi]
                    ai, yb, nblk = d["ai"], S["yb"], S["nblk"]
                    vblk = v_[:, d["far"] * NKB + d["kb"], 0:128]
                    P.op("pe", lambda e: e.matmul(ps[yb], vblk, at[ai], start=(bi == 0), stop=(bi == nblk - 1)),
                         reads=[v_b, atb[ai]], writes=[psb[yb]], inc=(bi == nblk - 1))

                sB, sC = SKEW_SB
                nmax = max(S["nblk"] for S in streams)
                for it in range(nmax + sC):
                    for S in streams:
                        if it < S["nblk"]:
                            stA(S, it)
                    for S in streams:
                        if sB <= it < S["nblk"] + sB:
                            stB(S, it - sB)
                    for S in streams:
                        if sC <= it < S["nblk"] + sC:
                            stC(S, it - sC)
                for S in streams:
                    yi = ysti[0] % 4
                    ysti[0] += 1
                    yb, g = S["yb"], S["g"]
                    P.op("act", lambda e, yb=yb, yi=yi: e.activation(out=yst[yi], in_=ps[yb], func=AF.Copy), reads=[psb[yb]], writes=[ystb[yi]])
                    P.dma(stq(), yAd[h * 128:(h + 1) * 128, g * G:(g + 1) * G], yst[yi], reads=[ystb[yi]])

        for h in range(4):
            hb = rot("hd", 2)
            k_, k_b = kT[hb], kTb[hb]
            v_, v_b = vv[hb], vvb[hb]
            for half in range(2):
                r0 = 1024 + (2 * h + half) * 128
                P.dma(hwq(), k_[:, half, :], kloc[r0:r0 + 128, :], writes=[k_b])
                P.dma(hwq(), k_[:, 2 + half, :], kfar(r0, 128), writes=[k_b])
            c0 = 1024 + h * 256
            P.dma(hwq(), v_[:, 0:NKB, :], vloc[:, c0:c0 + 256].rearrange("(b p) c -> p b c", p=128), writes=[v_b])
            for j in range(NVC):
                P.dma(hwq(), v_[:, NKB + BV * j:NKB + BV * (j + 1), :],
                      vall[j * 2 * RV:j * 2 * RV + RV, c0:c0 + 256].rearrange("(b p) c -> p b c", p=128), writes=[v_b])
            for g in range(TG):
                t0 = g * G
                blocks = [(0, kb) for kb in range(4 * g + 3, -1, -1)] + [(1, kb) for kb in range(NKB - 1, -1, -1)]
                nblk = len(blocks)
                dstreams = []
                for half in range(2):
                    qi = rot("q", 4)
                    r0 = (2 * h + half) * 128
                    P.dma(hwq(), qt[qi], qB[r0:r0 + 128, t0:t0 + G], writes=[qtb[qi]])
                    ya, yb2, db = (2, 3, 6) if half == 0 else (4, 5, 7)
                    dstreams.append(dict(half=half, qi=qi, ya=ya, yb2=yb2, db=db, st={}))

                def dA(S, bi):
                    half, qi = S["half"], S["qi"]
                    far, kb = blocks[bi]
                    kblk = k_[:, 2 * far + half, kb * 128:(kb + 1) * 128]
                    dk = (kb - 4 * g) if (far == 0 and kb >= 4 * g) else None
                    zb = half
                    P.op("pe", lambda e: e.matmul(ps[zb], kblk, qt[qi], start=True, stop=True),
                         reads=[k_b, qtb[qi]], writes=[psb[zb]])
                    ai = 3 * half + rot(f"dat{half}", 3)
                    if far:
                        P.op("act", lambda e: e.activation(out=at[ai], in_=ps[zb], func=AF.Exp, bias=farb[:, 0:1], scale=1.0),
                             reads=[psb[zb], gainsb], writes=[atb[ai]])
                    else:
                        P.op("act", lambda e: e.activation(out=at[ai], in_=ps[zb], func=AF.Exp), reads=[psb[zb]], writes=[atb[ai]])
                    if dk is not None:
                        P.op("dve", lambda e: e.tensor_tensor(out=at[ai], in0=at[ai], in1=mdf[dk], op=ALU.mult),
                             reads=[atb[ai], mskb], writes=[atb[ai]])
                    S["st"][bi] = (far, kb, ai)

                def dB(S, bi):
                    far, kb, ai = S["st"][bi]
                    ya, yb2, db = S["ya"], S["yb2"], S["db"]
                    st, sp_ = (bi == 0), (bi == nblk - 1)
                    P.op("pe", lambda e: e.matmul(ps[ya], v_[:, far * NKB + kb, 0:128], at[ai], start=st, stop=sp_),
                         reads=[v_b, atb[ai]], writes=[psb[ya]], inc=sp_)
                    P.op("pe", lambda e: e.matmul(ps[yb2], v_[:, far * NKB + kb, 128:256], at[ai], start=st, stop=sp_),
                         reads=[v_b, atb[ai]], writes=[psb[yb2]], inc=sp_)
                    P.op("pe", lambda e: e.matmul(ps[db], ones, at[ai], start=st, stop=sp_),
                         reads=[cstb, atb[ai]], writes=[psb[db]], inc=sp_)

                for it in range(nblk + SKEW_DF):
                    for S in dstreams:
                        if it < nblk:
                            dA(S, it)
                    for S in dstreams:
                        if it >= SKEW_DF:
                            dB(S, it - SKEW_DF)

                for half in range(2):
                    S = dstreams[half]
                    ya, yb2, db = S["ya"], S["yb2"], S["db"]
                    P.op("dve", lambda e, db=db: e.reciprocal(out=rc, in_=ps[db]), reads=[psb[db]], writes=[rcb])
                    if half == 0:
                        P.op("dve", lambda e, ya=ya: e.tensor_tensor(out=n1[:, 0, :], in0=ps[ya], in1=rc, op=ALU.mult), reads=[psb[ya], rcb], writes=[n1b])
                        P.op("dve", lambda e, yb2=yb2: e.tensor_tensor(out=n1[:, 1, :], in0=ps[yb2], in1=rc, op=ALU.mult), reads=[psb[yb2], rcb], writes=[n1b])
                    else:
                        for c, bk in ((0, ya), (1, yb2)):
                            P.op("dve", lambda e, c=c, bk=bk: e.tensor_tensor(out=y32[:, c, :], in0=ps[bk], in1=rc, op=ALU.mult),
                                 reads=[psb[bk], rcb], writes=[y32b])
                            P.op("dve", lambda e, c=c: e.scalar_tensor_tensor(out=y32[:, c, :], in0=y32[:, c, :], scalar=lam[:, 2 * li + 1:2 * li + 2],
                                                                             in1=n1[:, c, :], op0=ALU.mult, op1=ALU.add),
                                 reads=[y32b, n1b, lamb], writes=[y32b])
                        for c in range(2):
                            P.op("pool", lambda e, c=c: e.tensor_tensor(out=sp[c], in0=y32[:, c, :], in1=y32[:, c, :], op=ALU.mult),
                                 reads=[y32b], writes=[spb[c]])
                            P.op("pe", lambda e, c=c: e.matmul(ps[0], ones, sp[c], start=(c == 0), stop=(c == 1)),
                                 reads=[spb[c], cstb], writes=[psb[0]], inc=(c == 1))
                        P.op("act", lambda e: e.activation(out=rc, in_=ps[0], func=AF.Ln, bias=EPS, scale=1.0 / 256),
                             reads=[psb[0]], writes=[rcb])
                        P.op("act", lambda e: e.activation(out=rc, in_=rc, func=AF.Exp, scale=-0.5),
                             reads=[rcb], writes=[rcb])
                        P.op("dve", lambda e: e.tensor_scalar(out=rc, in0=rc, scalar1=gcol(li, GC_OML), scalar2=None, op0=ALU.mult),
                             reads=[rcb, gainsb], writes=[rcb])
                        for c in range(2):
                            yi = ysti[0] % 4
                            ysti[0] += 1
                            P.op("dve", lambda e, c=c, yi=yi: e.scalar_tensor_tensor(out=yst[yi], in0=y32[:, c, :],
                                                                                   scalar=gcol(li, GC_SUB + c), in1=rc, op0=ALU.mult, op1=ALU.mult),
                                 reads=[y32b, rcb, gainsb], writes=[ystb[yi]])
                            P.dma(stq(), yBd[(2 * h + c) * 128:(2 * h + c + 1) * 128, t0:t0 + G], yst[yi], reads=[ystb[yi]])
        P.barrier()
        ar.pop()

        hT = ar.alloc([128, DC, G], F32); hTb = Buf("hT")
        xn = ar.alloc([128, DC, G], BF16); xnb = Buf("xn")
        hid = ar.alloc([128, FC, G], BF16); hidb = Buf("hid")
        ws = WStream(4)
        rstd = ar.alloc([128, G], F32); rstdb = Buf("rstd")
        tmps = [ar.alloc([128, G], BF16) for _ in range(4)]; tmpbs = [Buf(f"tmp{i}") for i in range(4)]
        sg = [ar.alloc([128, G], F32) for _ in range(2)]; sgb = [Buf("sg0"), Buf("sg1")]
        gt = [ar.alloc([128, 2, G], BF16) for _ in range(4)]; gtb = [Buf(f"gt{i}") for i in range(4)]
        mg = [ar.alloc([128, G], F32) for _ in range(2)]; mgb = [Buf("mg0"), Buf("mg1")]
        ptl = ar.alloc([128, 2, G], BF16); ptlb = Buf("ptl")
        yg = hid[:, 0:16, :]; ygb = hidb
        for g in range(TG):
            t0 = g * G
            for q4 in range(4):
                P.dma(hwq(), hT[:, q4 * 4:(q4 + 1) * 4, :],
                      hS[q4 * 512:(q4 + 1) * 512, t0:t0 + G].rearrange("(c p) t -> p c t", p=128), writes=[hTb])
            P.dma("pool", ptl, pT[li][:, t0:t0 + G].rearrange("(c p) t -> p c t", p=128), writes=[ptlb])
            P.dma(hwq(), yg[:, 0:8, :], yAd[:, t0:t0 + G].rearrange("(c p) t -> p c t", p=128), writes=[ygb])
            P.dma(hwq(), yg[:, 8:16, :], yBd[:, t0:t0 + G].rearrange("(c p) t -> p c t", p=128), writes=[ygb])
            for ob in range(4):
                wa_, wab = ws.load(wsrc(w_a[li], ob * 512, 512), 8, 512)
                wb2, wbb2 = ws.load(wsrc(w_b[li], ob * 512, 512), 8, 512)
                for j in range(4):
                    oc = ob * 4 + j
                    gi = oc % 4
                    P.dma(hwq(), gt[gi], gts.rearrange("(two r) t -> r two t", two=2)[oc * 128:(oc + 1) * 128, :, t0:t0 + G], writes=[gtb[gi]])
                    pa, pb_ = (0, 1) if oc % 2 == 0 else (2, 3)
                    for kc in range(8):
                        P.op("pe", lambda e, kc=kc, j=j, pa=pa, wa_=wa_: e.matmul(ps[pa], wa_[:, kc, j * 128:(j + 1) * 128], yg[:, kc, :],
                                                                                  start=(kc == 0), stop=(kc == 7)),
                             reads=[wab, ygb], writes=[psb[pa]], inc=(kc == 7))
                    for kc in range(8):
                        P.op("pe", lambda e, kc=kc, j=j, pb_=pb_, wb2=wb2: e.matmul(ps[pb_], wb2[:, kc, j * 128:(j + 1) * 128], yg[:, 8 + kc, :],
                                                                                   start=(kc == 0), stop=(kc == 7)),
                             reads=[wbb2, ygb], writes=[psb[pb_]], inc=(kc == 7))
                    s = oc % 2
                    P.op("dve", lambda e, s=s, pa=pa, gi=gi: e.tensor_tensor(out=sg[s], in0=ps[pa], in1=gt[gi][:, 0, :], op=ALU.mult),
                         reads=[psb[pa], gtb[gi]], writes=[sgb[s]])
                    P.op("dve", lambda e, s=s, pb_=pb_, gi=gi: e.tensor_tensor(out=mg[s], in0=ps[pb_], in1=gt[gi][:, 1, :], op=ALU.mult),
                         reads=[psb[pb_], gtb[gi]], writes=[mgb[s]])
                    P.op("pool", lambda e, s=s, oc=oc: e.tensor_tensor(out=xn[:, oc, :], in0=sg[s], in1=mg[s], op=ALU.add),
                         reads=[sgb[s], mgb[s]], writes=[xnb])
            for ob in range(4):
                wo_, wob = ws.load(wsrc(w_o[li], ob * 512, 512), DC, 512)
                for j in range(4):
                    oc = ob * 4 + j
                    pd = 4 + oc % 2
                    for kc in range(DC):
                        P.op("pe", lambda e, kc=kc, j=j, pd=pd, wo_=wo_: e.matmul(ps[pd], wo_[:, kc, j * 128:(j + 1) * 128], xn[:, kc, :],
                                                                                  start=(kc == 0), stop=(kc == DC - 1)),
                             reads=[wob, xnb], writes=[psb[pd]], inc=(kc == DC - 1))
                    P.op("dve", lambda e, oc=oc, pd=pd: e.tensor_tensor(out=hT[:, oc, :], in0=ps[pd], in1=hT[:, oc, :], op=ALU.add),
                         reads=[psb[pd], hTb], writes=[hTb])
            ffn(li, 1, hT, hTb, xn, xnb, hid, hidb, ws, rstd, rstdb, tmps, tmpbs, sg, sgb)
            rms_bc(hT, DC, 6, rstd, rstdb, hTb, float(D), tmps, tmpbs)
            for c in range(DC):
                eng = "dve"
                P.op(eng, lambda e, c=c: e.scalar_tensor_tensor(out=xn[:, c, :], in0=hT[:, c, :], scalar=gcol(li, GC_PLE + c),
                                                                in1=rstd, op0=ALU.mult, op1=ALU.mult),
                     reads=[hTb, rstdb, gainsb], writes=[xnb])
            ppv = hid[:, 0:32, :].rearrange("p a b -> p (a b)").bitcast(F32).rearrange("p (a b) -> p a b", a=16)
            for ob in range(4):
                wp_, wpb = ws.load(wsrc(w_pp[li], ob * 512, 512), 2, 512)
                for j in range(4):
                    oc = ob * 4 + j
                    pd = 4 + oc % 2
                    for kc in range(2):
                        P.op("pe", lambda e, kc=kc, j=j, pd=pd, wp_=wp_: e.matmul(ps[pd], wp_[:, kc, j * 128:(j + 1) * 128], ptl[:, kc, :],
                                                                                  start=(kc == 0), stop=(kc == 1)),
                             reads=[wpb, ptlb], writes=[psb[pd]], inc=(kc == 1))
                    P.op("act", lambda e, oc=oc, pd=pd: e.activation(out=ppv[:, oc, :], in_=ps[pd], func=AF.Copy), reads=[psb[pd]], writes=[hidb])
            rms_bc(ppv, DC, 6, rstd, rstdb, hidb, float(D), tmps, tmpbs)
            for ob in range(4):
                wg_, wgb_ = ws.load(wsrc(w_pg[li], ob * 512, 512), DC, 512)
                for j in range(4):
                    oc = ob * 4 + j
                    pd = (0, 1, 2, 3)[oc % 4]
                    for kc in range(DC):
                        P.op("pe", lambda e, kc=kc, j=j, pd=pd, wg_=wg_: e.matmul(ps[pd], wg_[:, kc, j * 128:(j + 1) * 128], xn[:, kc, :],
                                                                                  start=(kc == 0), stop=(kc == DC - 1)),
                             reads=[wgb_, xnb], writes=[psb[pd]], inc=(kc == DC - 1))
                    s = oc % 2
                    P.op("act", lambda e, s=s, pd=pd: e.activation(out=sg[s], in_=ps[pd], func=AF.Sigmoid), reads=[psb[pd]], writes=[sgb[s]])
                    P.op("dve", lambda e, oc=oc: e.scalar_tensor_tensor(out=ppv[:, oc, :], in0=ppv[:, oc, :], scalar=gcol(li, GC_PLEO + oc),
                                                                        in1=rstd, op0=ALU.mult, op1=ALU.mult),
                         reads=[hidb, rstdb, gainsb], writes=[hidb])
                    P.op("dve", lambda e, s=s, oc=oc: e.tensor_tensor(out=sg[s], in0=sg[s], in1=ppv[:, oc, :], op=ALU.mult),
                         reads=[sgb[s], hidb], writes=[sgb[s]])
                    P.op("dve", lambda e, s=s, oc=oc: e.tensor_tensor(out=hT[:, oc, :], in0=hT[:, oc, :], in1=sg[s], op=ALU.add),
                         reads=[sgb[s], hTb], writes=[hTb])
            dst = outT if last else hS
            for q4 in range(4):
                P.dma(stq(), dst[q4 * 512:(q4 + 1) * 512, t0:t0 + G].rearrange("(c p) t -> p c t", p=128),
                      hT[:, q4 * 4:(q4 + 1) * 4, :], reads=[hTb])
        P.barrier()
        ar.pop()

    for li, l in enumerate(layers):
        if stop_after == "pre":
            break
        phase_A(li, l, xT if li == 0 else hS)
        if stop_after in ("A", "Aload", "Affn"):
            break
        exchange()
        phase_B(li, l, last=(li == L - 1))
    P.barrier()
    P.emit()
    return nc


def _consts():
    c = np.zeros((128, NCC), np.float32)
    c[:, CC_ONES:CC_ONES + 128] = 1.0
    j = np.arange(128)[:, None]
    s = np.arange(128)[None, :]
    c[:, CC_NTRI:CC_NTRI + 128] = -(j >= s).astype(np.float32)
    rot = np.zeros((128, 128), np.float32)
    for i in range(16):
        rot[16 + i, i] = -1.0
        rot[i, 16 + i] = 1.0
    c[:, CC_ROT:CC_ROT + 128] = rot
    tt = np.arange(512)[None, :]
    sp = np.arange(128)[:, None]
    for k in range(4):
        c[:, CC_MSB + k * 512:CC_MSB + (k + 1) * 512] = ((128 * k + sp) < tt).astype(np.float32)
        c[:, CC_MDF + k * 512:CC_MDF + (k + 1) * 512] = ((128 * k + sp) <= tt).astype(np.float32)
    return c


def _rope(pos0, T):
    pos = (pos0 + np.arange(T)).astype(np.float32)
    inv = (np.float32(500000.0) ** (-np.arange(0, 32, 2, dtype=np.float32) / np.float32(32))).astype(np.float32)
    ang = pos[:, None] * inv[None, :]
    cos = np.cos(ang).astype(np.float32).T
    sin = np.sin(ang).astype(np.float32).T
    r = np.zeros((128, 2, T), np.float32)
    r[:, 0, :] = 1.0
    r[0:16, 0, :] = cos
    r[16:32, 0, :] = cos
    r[0:16, 1, :] = sin
    r[16:32, 1, :] = sin
    return r


def _gains(inp, l, lam_layer):
    g = np.zeros((128, NGC), np.float32)

    def col(v):
        return np.asarray(v, np.float32).reshape(-1, 128).T

    g[:, GC_FFN1:GC_FFN1 + 16] = col(inp["ffn1_norm"][l])
    g[:, GC_MIX:GC_MIX + 16] = col(inp["mix_norm"][l])
    g[:, GC_FFN2:GC_FFN2 + 16] = col(inp["ffn2_norm"][l])
    g[:, GC_PLE:GC_PLE + 16] = col(inp["ple_norm"][l])
    g[:, GC_PLEO:GC_PLEO + 16] = col(inp["ple_out_norm"][l])
    g[:, GC_DQ:GC_DQ + 1] = col(inp["diff_q_norm"][l])
    g[:, GC_DK:GC_DK + 1] = col(inp["diff_k_norm"][l])
    g[:, GC_L + 0:GC_L + 1] = col(inp["diff_lambda_q1"][l])
    g[:, GC_L + 1:GC_L + 2] = col(inp["diff_lambda_k1"][l])
    g[:, GC_L + 2:GC_L + 3] = col(inp["diff_lambda_q2"][l])
    g[:, GC_L + 3:GC_L + 4] = col(inp["diff_lambda_k2"][l])
    g[:, GC_SUB:GC_SUB + 2] = col(inp["diff_sub_norm"][l])
    li = 0.8 - 0.6 * math.exp(-0.3 * lam_layer)
    g[:, GC_LI] = li
    g[:, GC_OML] = 1.0 - li
    return g


_NC_CACHE = {}


def _run(inp, TG, layers_list, dbg=None):
    T = TG * G
    S = 2 * T
    x = np.asarray(inp["x"], np.float32)
    B = x.shape[0]
    ncores = 2 * B
    cst = _consts()
    hT = [np.ascontiguousarray(x[c // 2, (c % 2) * T:(c % 2 + 1) * T, :].T) for c in range(ncores)]
    res = None
    for layers in layers_list:
        key = (TG, len(layers), str(dbg))
        if key not in _NC_CACHE:
            _NC_CACHE[key] = build(TG, layers, dbg)
        nc = _NC_CACHE[key]
        sl = slice(layers[0], layers[-1] + 1)
        shared = {
            "w_gu1": np.asarray(inp["ffn1_w_gu"][sl], np.float32), "w_gu2": np.asarray(inp["ffn2_w_gu"][sl], np.float32),
            "w_d1": np.asarray(inp["ffn1_w_down"][sl], np.float32), "w_d2": np.asarray(inp["ffn2_w_down"][sl], np.float32),
            "w_in": np.asarray(inp["w_in"][sl], np.float32), "w_a": np.asarray(inp["w_branch_a"][sl], np.float32),
            "w_b": np.asarray(inp["w_branch_b"][sl], np.float32), "w_o": np.asarray(inp["w_out"][sl], np.float32),
            "w_pg": np.asarray(inp["ple_w_gate"][sl], np.float32), "w_pp": np.asarray(inp["ple_w_proj"][sl], np.float32),
            "gains": np.stack([_gains(inp, l, l) for l in layers]), "cst": cst,
        }
        p = np.asarray(inp["p"], np.float32)
        in_maps = []
        for c in range(ncores):
            b, half = c // 2, c % 2
            m = dict(shared)
            m["xT"] = hT[c]
            m["pT"] = np.ascontiguousarray(np.stack([p[l, b, half * T:(half + 1) * T, :].T for l in layers]))
            m["rope"] = _rope(half * T, T)
            m["farbias"] = np.full((128, 1), 0.0 if half == 1 else -30000.0, np.float32)
            in_maps.append(m)
        res = run_bass_kernel_spmd(nc, in_maps, core_ids=list(range(ncores)))
        hT = [np.asarray(res.results[c]["outT"]) for c in range(ncores)]
    out = np.empty((B, S, D), np.float32)
    for c in range(ncores):
        out[c // 2, (c % 2) * T:(c % 2 + 1) * T, :] = hT[c].T
    return out, res


def kernel(**inputs):
    out, _ = _run(inputs, 4, [[0, 1]])
    return out
```

```python
import math
import types
import numpy as np
import ml_dtypes
import concourse.bass as bass
import concourse.mybir as mybir
from concourse.bass_utils import run_bass_kernel_spmd

F32 = mybir.dt.float32
BF16 = mybir.dt.bfloat16
AF = mybir.ActivationFunctionType
ALU = mybir.AluOpType

D = 2048
DC = 16
DFF = 5632
FC = 44
G = 512
DEPTH = 2
INW = 10240
EPS = 1e-6
CAP = 30000
NDS = 16
SCALE = 128 ** -0.5
import os as _os
SKEW_SB = tuple(int(x) for x in _os.environ.get('SKEW_SB', '1,1').split(','))
SKEW_DF = int(_os.environ.get('SKEW_DF', '1'))

GC_FFN1, GC_MIX, GC_FFN2, GC_PLE, GC_PLEO = 0, 16, 32, 48, 64
GC_DQ, GC_DK, GC_L, GC_SUB, GC_LI, GC_OML = 80, 81, 82, 86, 88, 89
NGC = 90
CC_ONES, CC_NTRI, CC_ROT, CC_MSB, CC_MDF = 0, 128, 256, 384, 384 + 2048
NCC = 384 + 4096


def _freeze(fn):
    if fn is None or fn.__closure__ is None:
        return fn
    cells = []
    for c in fn.__closure__:
        try:
            cells.append(types.CellType(c.cell_contents))
        except ValueError:
            cells.append(c)
    return types.FunctionType(fn.__code__, fn.__globals__, fn.__name__, fn.__defaults__, tuple(cells))


class Buf:
    __slots__ = ("w", "r", "name")

    def __init__(self, name=""):
        self.w = None
        self.r = {}
        self.name = name


class Prog:
    ENGS = ["pe", "act", "dve", "pool", "sp"]

    def __init__(self, nc):
        self.nc = nc
        self.streams = {e: [] for e in self.ENGS}
        self.cnt = {e: 0 for e in self.ENGS}
        self.seen = {e: {} for e in self.ENGS}
        self.ndma = {}
        self.dma_tokens_live = {}
        self.extra_sems = []

    def _need_wait(self, eng, tok):
        if tok[0] == "e":
            _, te, k = tok
            if te == eng and eng == "pe":
                return None
            if self.seen[eng].get(te, 0) >= k:
                return None
            self.seen[eng][te] = k
            return ("e", te, k)
        elif tok[0] == "d":
            _, q, i = tok
            slot = i % NDS
            val = 16 * (i // NDS + 1)
            key = ("d", q, slot)
            if self.seen[eng].get(key, 0) >= val:
                return None
            self.seen[eng][key] = val
            return ("d", q, slot, val)
        else:
            key = tok
            if self.seen[eng].get(key, 0) >= 1:
                return None
            self.seen[eng][key] = 1
            return tok

    def _deps(self, eng, reads, writes):
        toks = []
        for b in reads:
            if b.w is not None:
                toks.append(b.w)
        for b in writes:
            if b.w is not None:
                toks.append(b.w)
            toks.extend(b.r.values())
        waits = []
        for t in toks:
            w = self._need_wait(eng, t)
            if w is not None:
                waits.append(w)
        return waits

    def _mark(self, tok, key, reads, writes):
        for b in reads:
            b.r[key] = tok
        for b in writes:
            b.w = tok
            b.r = {}

    def op(self, eng, fn, reads=(), writes=(), inc=True):
        if eng != "pe":
            inc = True
        waits = self._deps(eng, reads, writes)
        k = self.cnt[eng] + 1
        tok = ("e", eng, k)
        if inc:
            self.cnt[eng] = k
        self.streams[eng].append((waits, _freeze(fn), k if inc else None, None))
        self._mark(tok, eng, reads, writes)
        return tok

    def dma(self, eng, out, in_, reads=(), writes=()):
        waits = self._deps(eng, reads, writes)
        i = self.ndma.get(eng, 0)
        self.ndma[eng] = i + 1
        if i >= NDS:
            w = self._need_wait(eng, ("d", eng, i - NDS))
            if w is not None:
                waits.append(w)
        tok = ("d", eng, i)
        self.streams[eng].append((waits, lambda e: e.dma_start(out=out, in_=in_), None, i))
        self._mark(tok, tok, reads, writes)
        live = self.dma_tokens_live.setdefault(eng, [])
        live.append(tok)
        if len(live) > NDS:
            del live[:-NDS]
        return tok

    def barrier(self, extra=()):
        toks = [("e", e, self.cnt[e]) for e in self.ENGS if self.cnt[e] > 0]
        for live in self.dma_tokens_live.values():
            toks += list(live)
        toks += list(extra)
        for eng in self.ENGS:
            waits = []
            for t in toks:
                if t[0] == "e" and t[1] == eng and eng == "pe":
                    continue
                w = self._need_wait(eng, t)
                if w is not None:
                    waits.append(w)
            if waits:
                self.streams[eng].append((waits, None, None, None))

    def emit(self):
        nc = self.nc
        nsem = {e: (self.cnt[e] + CAP - 1) // CAP + 1 for e in self.ENGS}
        sems = {e: [nc.alloc_semaphore(f"s_{e}_{j}") for j in range(nsem[e])] for e in self.ENGS}
        dsems = {q: [nc.alloc_semaphore(f"s_dma_{q}_{j}") for j in range(NDS)] for q in self.ndma}
        xsems = self.extra_sems
        handles = {"pe": "tensor", "act": "scalar", "dve": "vector", "pool": "gpsimd", "sp": "sync"}

        def run(ename, eng):
            for waits, fn, k, di in self.streams[ename]:
                for w in waits:
                    if w[0] == "e":
                        _, te, kk = w
                        eng.wait_ge(sems[te][(kk - 1) // CAP], (kk - 1) % CAP + 1)
                    elif w[0] == "d":
                        eng.wait_ge(dsems[w[1]][w[2]], w[3])
                    else:
                        eng.wait_ge(xsems[w[1]], 1)
                if fn is None:
                    continue
                ins = fn(eng)
                if k is not None:
                    ins.then_inc(sems[ename][(k - 1) // CAP], 1)
                if di is not None:
                    ins.then_inc(dsems[ename][di % NDS], 16)

        with nc.Block() as block:
            @block.tensor
            def _(e):
                run("pe", e)

            @block.scalar
            def _(e):
                run("act", e)

            @block.vector
            def _(e):
                run("dve", e)

            @block.gpsimd
            def _(e):
                run("pool", e)

            @block.sync
            def _(e):
                run("sp", e)


class Arena:
    def __init__(self, nc, nbytes):
        self.t = nc.alloc_sbuf_tensor("arena", [128, nbytes // 2], BF16)
        self.nbytes = nbytes
        self.off = 0
        self.marks = []

    def push(self):
        self.marks.append(self.off)

    def pop(self):
        self.off = self.marks.pop()

    def alloc(self, shape, dtype):
        es = 4 if dtype == F32 else 2
        n = int(np.prod(shape[1:]))
        nb = n * es
        nb = (nb + 63) // 64 * 64
        assert self.off + nb <= self.nbytes, f"arena overflow {self.off + nb} > {self.nbytes}"
        o = self.off // 2
        ap = self.t[:, o:o + nb // 2]
        self.off += nb
        if dtype == F32:
            ap = ap.bitcast(F32)
        ap = ap[:, 0:n]
        if len(shape) == 3:
            ap = ap.rearrange("p (a b) -> p a b", a=shape[1])
        return ap


def build(TG, layers, dbg=None):
    T = TG * G
    NKB = T // 128
    L = len(layers)
    nc = bass.Bass("TRN2", target_bir_lowering=False)
    P = Prog(nc)

    def din(name, shape, dt=F32):
        return nc.dram_tensor(name, shape, dt, kind="ExternalInput").ap()

    def dscr(name, shape, dt, out=False):
        kind = "ExternalOutput" if (out or (dbg and name in dbg)) else "Internal"
        return nc.dram_tensor(name, shape, dt, kind=kind)

    xT = din("xT", [D, T])
    pT = din("pT", [L, 256, T])
    w_gu = [din("w_gu1", [L, D, 2 * DFF]), din("w_gu2", [L, D, 2 * DFF])]
    w_dn = [din("w_d1", [L, DFF, D]), din("w_d2", [L, DFF, D])]
    w_in = din("w_in", [L, D, INW])
    w_a = din("w_a", [L, 1024, D])
    w_b = din("w_b", [L, 1024, D])
    w_o = din("w_o", [L, D, D])
    w_pg = din("w_pg", [L, D, D])
    w_pp = din("w_pp", [L, 256, D])
    gains_d = din("gains", [L, 128, NGC])
    cst_d = din("cst", [128, NCC])
    rope_d = din("rope", [128, 2, T])
    farb_d = din("farbias", [128, 1])

    outT = nc.dram_tensor("outT", [D, T], F32, kind="ExternalOutput").ap()
    hS = dscr("hS", [D, T], F32).ap()
    qA = dscr("qA", [1024, T], BF16).ap()
    qB = dscr("qB", [1024, T], BF16).ap()
    gts = dscr("gts", [4096, T], BF16).ap()
    uS3 = dscr("uS3", [TG, 128, DC * G], BF16).ap()
    kloc = dscr("kloc", [2048, T], BF16)
    vloc = dscr("vloc", [T, 2048], BF16)
    kall = dscr("kall", [4096, T], BF16)
    vall = dscr("vall", [2 * T, 2048], BF16)
    yAd = dscr("yAd", [1024, T], BF16).ap()
    yBd = dscr("yBd", [1024, T], BF16).ap()

    ar = Arena(nc, 188 * 1024)
    ps = [nc.alloc_psum_tensor(f"ps{i}", [128, 512], F32).ap() for i in range(8)]
    psb = [Buf(f"ps{i}") for i in range(8)]

    cst = ar.alloc([128, 384], BF16)
    cstb = Buf("cst")
    gains = ar.alloc([128, L * NGC], F32)
    gainsb = Buf("gains")
    farb = ar.alloc([128, 1], F32)
    lam = ar.alloc([128, 2 * L], F32)
    lamb = Buf("lam")
    P.dma("pool", cst, cst_d[:, 0:384], writes=[cstb])
    for l in range(L):
        P.dma("sp", gains[:, l * NGC:(l + 1) * NGC], gains_d[l], writes=[gainsb])
    P.dma("sp", farb, farb_d, writes=[gainsb])
    ones = cst[:, CC_ONES:CC_ONES + 128]
    ntri = cst[:, CC_NTRI:CC_NTRI + 128]
    rotm = cst[:, CC_ROT:CC_ROT + 128]

    def gcol(l, c, n=1):
        return gains[:, l * NGC + c:l * NGC + c + n]

    ar.push()
    ltmp = ar.alloc([128, 8], F32)
    ltb = Buf("ltmp")
    ones32 = ar.alloc([128, 128], F32)
    o32b = Buf("ones32")
    P.op("dve", lambda e: e.memset(ones32, 1.0), writes=[o32b])
    for l in range(L):
        P.op("dve", lambda e, l=l: e.tensor_tensor(out=ltmp[:, 0:1], in0=gcol(l, GC_L), in1=gcol(l, GC_L + 1), op=ALU.mult),
             reads=[gainsb], writes=[ltb])
        P.op("dve", lambda e, l=l: e.tensor_tensor(out=ltmp[:, 1:2], in0=gcol(l, GC_L + 2), in1=gcol(l, GC_L + 3), op=ALU.mult),
             reads=[gainsb], writes=[ltb])
        P.op("pe", lambda e: e.matmul(ps[7][:, 0:2], ones32, ltmp[:, 0:2], start=True, stop=True),
             reads=[ltb, o32b], writes=[psb[7]])
        P.op("act", lambda e: e.activation(out=ltmp[:, 2:4], in_=ps[7][:, 0:2], func=AF.Exp), reads=[psb[7]], writes=[ltb])
        P.op("dve", lambda e: e.tensor_tensor(out=ltmp[:, 4:5], in0=ltmp[:, 2:3], in1=ltmp[:, 3:4], op=ALU.subtract),
             reads=[ltb], writes=[ltb])
        P.op("dve", lambda e, l=l: e.tensor_tensor(out=lam[:, 2 * l:2 * l + 1], in0=ltmp[:, 4:5], in1=gcol(l, GC_LI), op=ALU.add),
             reads=[ltb, gainsb], writes=[lamb])
        P.op("dve", lambda e, l=l: e.tensor_scalar(out=lam[:, 2 * l + 1:2 * l + 2], in0=lam[:, 2 * l:2 * l + 1], scalar1=-1.0, scalar2=None, op0=ALU.mult),
             reads=[lamb], writes=[lamb])
    P.barrier()
    ar.pop()
    stop_after = dbg.get("stop") if dbg else None

    dmaq = ["sp", "act"]
    rr = [0]

    def hwq():
        return "sp"

    def stq():
        return "act"

    def rms_bc(src, nch, bank, out_rstd, rstdb, srcb, Dn, tmps, tmpbs):
        nt = len(tmps)
        for c in range(nch):
            sq, sqb = tmps[c % nt], tmpbs[c % nt]
            eng = "dve" if c % 2 == 0 else "pool"
            P.op(eng, lambda e, c=c, sq=sq: e.tensor_tensor(out=sq, in0=src[:, c, :], in1=src[:, c, :], op=ALU.mult),
                 reads=[srcb], writes=[sqb])
            P.op("pe", lambda e, c=c, sq=sq: e.matmul(ps[bank], ones, sq, start=(c == 0), stop=(c == nch - 1)),
                 reads=[sqb, cstb], writes=[psb[bank]], inc=True)
        P.op("act", lambda e: e.activation(out=out_rstd, in_=ps[bank], func=AF.Ln, bias=EPS, scale=1.0 / Dn),
             reads=[psb[bank]], writes=[rstdb])
        P.op("act", lambda e: e.activation(out=out_rstd, in_=out_rstd, func=AF.Exp, scale=-0.5),
             reads=[rstdb], writes=[rstdb])

    class WStream:
        def __init__(self, nslots, nelem=8192):
            self.slots = [ar.alloc([128, nelem], BF16) for _ in range(nslots)]
            self.bufs = [Buf(f"w{i}") for i in range(nslots)]
            self.i = 0

        def load(self, src_ap, kc, ncol):
            s = self.i % len(self.slots)
            self.i += 1
            dst = self.slots[s][:, 0:kc * ncol].rearrange("p (c n) -> p c n", c=kc)
            P.dma("pool", dst, src_ap, writes=[self.bufs[s]])
            return dst, self.bufs[s]

    def wsrc(w2d, c0, ncol):
        return w2d[:, c0:c0 + ncol].rearrange("(c p) n -> p c n", p=128)

    def ffn(l, which, hT, hTb, xn, xnb, hid, hidb, ws, rstd, rstdb, tmps, tmpbs, sg, sgb):
        gc = GC_FFN1 if which == 0 else GC_FFN2
        rms_bc(hT, DC, 6, rstd, rstdb, hTb, float(D), tmps, tmpbs)
        for c in range(DC):
            eng = "dve"
            P.op(eng, lambda e, c=c: e.scalar_tensor_tensor(out=xn[:, c, :], in0=hT[:, c, :], scalar=gcol(l, gc + c),
                                                            in1=rstd, op0=ALU.mult, op1=ALU.mult),
                 reads=[hTb, rstdb, gainsb], writes=[xnb])
        wgu = w_gu[which][l]
        wdn = w_dn[which][l]
        for fb in range(FC // 4):
            wg, wgb = ws.load(wsrc(wgu, fb * 512, 512), DC, 512)
            wu, wub = ws.load(wsrc(wgu, DFF + fb * 512, 512), DC, 512)
            for j in range(4):
                fc = fb * 4 + j
                pg, pu = (0, 1) if fc % 2 == 0 else (2, 3)
                for kc in range(DC):
                    P.op("pe", lambda e, kc=kc, j=j, wg=wg, pg=pg: e.matmul(ps[pg], wg[:, kc, j * 128:(j + 1) * 128], xn[:, kc, :],
                                                                            start=(kc == 0), stop=(kc == DC - 1)),
                         reads=[wgb, xnb], writes=[psb[pg]], inc=(kc == DC - 1))
                for kc in range(DC):
                    P.op("pe", lambda e, kc=kc, j=j, wu=wu, pu=pu: e.matmul(ps[pu], wu[:, kc, j * 128:(j + 1) * 128], xn[:, kc, :],
                                                                            start=(kc == 0), stop=(kc == DC - 1)),
                         reads=[wub, xnb], writes=[psb[pu]], inc=(kc == DC - 1))
                s = fc % 2
                P.op("act", lambda e, pg=pg, s=s: e.activation(out=sg[s], in_=ps[pg], func=AF.Silu), reads=[psb[pg]], writes=[sgb[s]])
                P.op("dve", lambda e, pu=pu, s=s, fc=fc: e.tensor_tensor(out=hid[:, fc, :], in0=sg[s], in1=ps[pu], op=ALU.mult),
                     reads=[sgb[s], psb[pu]], writes=[hidb])
        dbanks = [4, 5, 0, 1]
        for db in range(8):
            halves = []
            for kh in range(2):
                halves.append(ws.load(wdn[kh * 2816:(kh + 1) * 2816, db * 256:(db + 1) * 256].rearrange("(c p) n -> p c n", p=128), 22, 256))
            for kh in range(2):
                wd, wdb = halves[kh]
                for j in range(2):
                    pd = dbanks[(db % 2) * 2 + j]
                    for fc in range(22):
                        last = (kh == 1 and fc == 21)
                        P.op("pe", lambda e, fc=fc, kh=kh, j=j, wd=wd, pd=pd, last=last: e.matmul(
                            ps[pd], wd[:, fc, j * 128:(j + 1) * 128], hid[:, kh * 22 + fc, :], start=(kh == 0 and fc == 0), stop=last),
                             reads=[wdb, hidb], writes=[psb[pd]], inc=last)
            for j in range(2):
                dc = db * 2 + j
                pd = dbanks[(db % 2) * 2 + j]
                P.op("dve", lambda e, dc=dc, pd=pd: e.scalar_tensor_tensor(out=hT[:, dc, :], in0=ps[pd], scalar=0.5, in1=hT[:, dc, :],
                                                                           op0=ALU.mult, op1=ALU.add),
                     reads=[psb[pd], hTb], writes=[hTb])

    def phase_A(li, l, src_h):
        ar.push()
        hT = ar.alloc([128, DC, G], F32); hTb = Buf("hT")
        xn = ar.alloc([128, DC, G], BF16); xnb = Buf("xn")
        hid = ar.alloc([128, FC, G], BF16); hidb = Buf("hid")
        ws = WStream(4)
        rstd = ar.alloc([128, G], F32); rstdb = Buf("rstd")
        rs2 = [ar.alloc([128, G], F32) for _ in range(2)]; rs2b = [Buf("rs20"), Buf("rs21")]
        tmps = [ar.alloc([128, G], BF16) for _ in range(4)]; tmpbs = [Buf(f"tmp{i}") for i in range(4)]
        sg = [ar.alloc([128, G], F32) for _ in range(2)]; sgb = [Buf("sg0"), Buf("sg1")]
        stg = [ar.alloc([128, G], BF16) for _ in range(4)]; stgb = [Buf(f"stg{i}") for i in range(4)]
        xg = [ar.alloc([128, G], BF16) for _ in range(2)]; xgb = [Buf("xg0"), Buf("xg1")]
        x32 = [ar.alloc([128, G], F32) for _ in range(2)]; x32b = [Buf("x320"), Buf("x321")]
        rp = ar.alloc([128, 2, G], F32); rpb = Buf("rope")
        sti = [0]

        def stage():
            i = sti[0] % 4
            sti[0] += 1
            return stg[i], stgb[i]

        for g in range(TG):
            t0 = g * G
            for q4 in range(4):
                P.dma(hwq(), hT[:, q4 * 4:(q4 + 1) * 4, :],
                      src_h[q4 * 512:(q4 + 1) * 512, t0:t0 + G].rearrange("(c p) t -> p c t", p=128), writes=[hTb])
            if stop_after != "Aload":
                ffn(li, 0, hT, hTb, xn, xnb, hid, hidb, ws, rstd, rstdb, tmps, tmpbs, sg, sgb)
            for q4 in range(4):
                P.dma(stq(), hS[q4 * 512:(q4 + 1) * 512, t0:t0 + G].rearrange("(c p) t -> p c t", p=128),
                      hT[:, q4 * 4:(q4 + 1) * 4, :], reads=[hTb])
            if stop_after in ("Aload", "Affn"):
                continue
            rms_bc(hT, DC, 6, rstd, rstdb, hTb, float(D), tmps, tmpbs)
            for c in range(DC):
                eng = "dve"
                P.op(eng, lambda e, c=c: e.scalar_tensor_tensor(out=xn[:, c, :], in0=hT[:, c, :], scalar=gcol(li, GC_MIX + c),
                                                                in1=rstd, op0=ALU.mult, op1=ALU.mult),
                     reads=[hTb, rstdb, gainsb], writes=[xnb])
            P.dma(stq(), uS3[g], xn.rearrange("p a b -> p (a b)"), reads=[xnb])
        P.barrier()
        ar.pop()
        ar.push()
        xn_all = ar.alloc([128, TG * DC, G], BF16); xnb = Buf("xn_all")
        rp_all = ar.alloc([128, 2, T], F32); rpb = Buf("rope_all")
        ws = WStream(4)
        rs2 = [ar.alloc([128, G], F32) for _ in range(2)]; rs2b = [Buf("rs20"), Buf("rs21")]
        tmps = [ar.alloc([128, G], BF16) for _ in range(4)]; tmpbs = [Buf(f"tmp{i}") for i in range(4)]
        sg = [ar.alloc([128, G], F32) for _ in range(2)]; sgb = [Buf("sg0"), Buf("sg1")]
        stg = [ar.alloc([128, G], BF16) for _ in range(4)]; stgb = [Buf(f"stg{i}") for i in range(4)]
        xg = [ar.alloc([128, G], BF16) for _ in range(2)]; xgb = [Buf("xg0"), Buf("xg1")]
        x32 = [ar.alloc([128, G], F32) for _ in range(2)]; x32b = [Buf("x320"), Buf("x321")]
        for g in range(TG):
            P.dma(hwq(), xn_all[:, g * DC:(g + 1) * DC, :].rearrange("p a b -> p (a b)"), uS3[g], writes=[xnb])
        P.dma(hwq(), rp_all, rope_d, writes=[rpb])
        win = w_in[li]
        pbank = [0]

        def nextbank():
            b = pbank[0] % 4
            pbank[0] += 1
            return b

        for cb in range(20):
            wb_, wbb = ws.load(wsrc(win, cb * 512, 512), DC, 512)
            kind = ["sbq", "sbk", "sbv", "dfq", "dfk", "dfv", "ga", "ga", "gb", "gb"][cb // 2]
            for g in range(TG):
                t0 = g * G
                xn = xn_all[:, g * DC:(g + 1) * DC, :]
                rp = rp_all[:, :, t0:t0 + G]
                if kind in ("sbv", "dfv"):
                    for tt in range(4):
                        b = nextbank()
                        for kc in range(DC):
                            P.op("pe", lambda e, kc=kc, tt=tt, b=b, wb_=wb_: e.matmul(ps[b], xn[:, kc, tt * 128:(tt + 1) * 128], wb_[:, kc, :],
                                                                                      start=(kc == 0), stop=(kc == DC - 1)),
                                 reads=[wbb, xnb], writes=[psb[b]], inc=(kc == DC - 1))
                        st, stb = stage()
                        P.op("act", lambda e, b=b, st=st: e.activation(out=st, in_=ps[b], func=AF.Copy), reads=[psb[b]], writes=[stb])
                        col0 = (0 if kind == "sbv" else 1024) + (cb % 2) * 512
                        P.dma(stq(), vloc[t0 + tt * 128:t0 + (tt + 1) * 128, col0:col0 + 512], st, reads=[stb])
                    continue
                for j in range(4):
                    b = nextbank()
                    oc = (cb % 2) * 4 + j
                    for kc in range(DC):
                        P.op("pe", lambda e, kc=kc, j=j, b=b, wb_=wb_: e.matmul(ps[b], wb_[:, kc, j * 128:(j + 1) * 128], xn[:, kc, :],
                                                                                start=(kc == 0), stop=(kc == DC - 1)),
                             reads=[wbb, xnb], writes=[psb[b]], inc=(kc == DC - 1))
                    st, stb = stage()
                    if kind == "sbq":
                        P.op("act", lambda e, b=b, st=st: e.activation(out=st, in_=ps[b], func=AF.Copy, scale=SCALE), reads=[psb[b]], writes=[stb])
                        P.dma(stq(), qA[oc * 128:(oc + 1) * 128, t0:t0 + G], st, reads=[stb])
                    elif kind == "sbk":
                        P.op("act", lambda e, b=b, st=st: e.activation(out=st, in_=ps[b], func=AF.Copy), reads=[psb[b]], writes=[stb])
                        P.dma(stq(), kloc[oc * 128:(oc + 1) * 128, t0:t0 + G], st, reads=[stb])
                    elif kind in ("ga", "gb"):
                        gch = ((cb - 12) * 4 + j)
                        P.op("act", lambda e, b=b, st=st: e.activation(out=st, in_=ps[b], func=AF.Sigmoid), reads=[psb[b]], writes=[stb])
                        P.dma(stq(), gts[gch * 128:(gch + 1) * 128, t0:t0 + G], st, reads=[stb])
                    else:
                        isq = kind == "dfq"
                        gcn = GC_DQ if isq else GC_DK
                        s = oc % 2
                        bss, brt = 4 + s, 6 + s
                        P.op("act", lambda e, b=b, s=s: e.activation(out=tmps[s], in_=ps[b], func=AF.Square), reads=[psb[b]], writes=[tmpbs[s]])
                        P.op("act", lambda e, b=b, s=s, gcn=gcn: e.activation(out=x32[s], in_=ps[b], func=AF.Copy, scale=gcol(li, gcn)),
                             reads=[psb[b], gainsb], writes=[x32b[s]])
                        P.op("pe", lambda e, s=s, bss=bss: e.matmul(ps[bss], ones, tmps[s], start=True, stop=True), reads=[tmpbs[s], cstb], writes=[psb[bss]])
                        P.op("dve", lambda e, s=s: e.tensor_copy(out=xg[s], in_=x32[s]), reads=[x32b[s]], writes=[xgb[s]])
                        P.op("pe", lambda e, s=s, brt=brt: e.matmul(ps[brt], rotm, xg[s], start=True, stop=True), reads=[xgb[s], cstb], writes=[psb[brt]])
                        P.op("act", lambda e, s=s, bss=bss: e.activation(out=rs2[s], in_=ps[bss], func=AF.Ln, bias=EPS, scale=1.0 / 128),
                             reads=[psb[bss]], writes=[rs2b[s]])
                        P.op("act", lambda e, s=s, isq=isq: e.activation(out=rs2[s], in_=rs2[s], func=AF.Exp, scale=-0.5, bias=(math.log(SCALE) if isq else 0.0)),
                             reads=[rs2b[s]], writes=[rs2b[s]])
                        P.op("dve", lambda e, s=s: e.tensor_tensor(out=x32[s], in0=x32[s], in1=rp[:, 0, :], op=ALU.mult),
                             reads=[x32b[s], rpb], writes=[x32b[s]])
                        P.op("dve", lambda e, s=s, brt=brt: e.tensor_tensor(out=sg[s], in0=ps[brt], in1=rp[:, 1, :], op=ALU.mult),
                             reads=[psb[brt], rpb], writes=[sgb[s]])
                        P.op("dve", lambda e, s=s: e.tensor_tensor(out=x32[s], in0=x32[s], in1=sg[s], op=ALU.add),
                             reads=[x32b[s], sgb[s]], writes=[x32b[s]])
                        P.op("dve", lambda e, s=s, st=st: e.tensor_tensor(out=st, in0=x32[s], in1=rs2[s], op=ALU.mult),
                             reads=[x32b[s], rs2b[s]], writes=[stb])
                        if isq:
                            P.dma(stq(), qB[oc * 128:(oc + 1) * 128, t0:t0 + G], st, reads=[stb])
                        else:
                            P.dma(stq(), kloc[1024 + oc * 128:1024 + (oc + 1) * 128, t0:t0 + G], st, reads=[stb])
        P.barrier()
        ar.pop()

    RK = min(2048, (1 << 20) // T)
    NKC = 2048 // RK
    RV = min(T, 512)
    NVC = T // RV
    BV = RV // 128

    def kfar(r0, n):
        j, w = r0 // RK, r0 % RK
        return kall[j * 2 * RK + w:j * 2 * RK + w + n, :]

    def exchange():
        groups = [[0, 1], [2, 3], [4, 5], [6, 7]]
        toks = []
        jobs = [(kloc, kall, RK, j) for j in range(NKC)] + [(vloc, vall, RV, j) for j in range(NVC)]
        for (src, dst, R_, j) in jobs:
            sem = nc.alloc_semaphore(f"cc{len(P.extra_sems)}")
            idx = len(P.extra_sems)
            P.extra_sems.append(sem)
            P.streams["pool"].append(([], (lambda e, src=src, dst=dst, sem=sem, R_=R_, j=j: e.collective_compute(
                "AllGather", ALU.bypass, replica_groups=groups, ins=[src[j * R_:(j + 1) * R_, :]],
                outs=[dst[j * 2 * R_:(j + 1) * 2 * R_, :]]).then_inc(sem, 1)), None, None))
            toks.append(("x", idx))
        P.barrier(extra=toks)

    def phase_B(li, l, last):
        ar.push()
        ar.push()
        msk = ar.alloc([128, 4096], BF16); mskb = Buf("msk")
        P.dma("pool", msk, cst_d[:, 384:384 + 4096], writes=[mskb])
        yst = [ar.alloc([128, G], BF16) for _ in range(4)]; ystb = [Buf(f"yst{i}") for i in range(4)]
        ysti = [0]
        kT = [ar.alloc([128, 4, T], BF16) for _ in range(2)]; kTb = [Buf("kT0"), Buf("kT1")]
        vv = [ar.alloc([128, 2 * NKB, 256], BF16) for _ in range(2)]; vvb = [Buf("v0"), Buf("v1")]
        qt = [ar.alloc([128, G], BF16) for _ in range(4)]; qtb = [Buf(f"qt{i}") for i in range(4)]
        e32 = [ar.alloc([128, G], F32) for _ in range(4)]; e32b = [Buf(f"e{i}") for i in range(4)]
        sp = [ar.alloc([128, G], BF16) for _ in range(8)]; spb = [Buf(f"sp{i}") for i in range(8)]
        at = [ar.alloc([128, G], BF16) for _ in range(6)]; atb = [Buf(f"at{i}") for i in range(6)]
        R32 = [ar.alloc([128, G], F32) for _ in range(2)]; R32b = [Buf("R320"), Buf("R321")]
        Rbf = [ar.alloc([128, G], BF16) for _ in range(6)]; Rbfb = [Buf(f"Rbf{i}") for i in range(6)]
        n1 = ar.alloc([128, 2, G], F32); n1b = Buf("n1")
        rc = ar.alloc([128, G], F32); rcb = Buf("rc")
        y32 = ar.alloc([128, 2, G], F32); y32b = Buf("y32")
        tq = ar.alloc([128, G], BF16); tqb = Buf("tq")
        msb = [msk[:, k * 512:(k + 1) * 512] for k in range(4)]
        mdf = [msk[:, 2048 + k * 512:2048 + (k + 1) * 512] for k in range(4)]
        cnt = {"q": 0, "e": 0, "sp": 0, "at": 0, "z": 0, "zs": 0, "p": 0, "y": 0, "R": 0, "hd": 0}

        def rot(name, n):
            i = cnt.get(name, 0) % n
            cnt[name] = cnt.get(name, 0) + 1
            return i

        for h in range(8):
            hb = rot("hd", 2)
            k_, k_b = kT[hb], kTb[hb]
            v_, v_b = vv[hb], vvb[hb]
            P.dma(hwq(), k_[:, 0, :], kloc[h * 128:(h + 1) * 128, :], writes=[k_b])
            P.dma(hwq(), k_[:, 1, :], kfar(h * 128, 128), writes=[k_b])
            P.dma(hwq(), v_[:, 0:NKB, 0:128], vloc[:, h * 128:(h + 1) * 128].rearrange("(b p) c -> p b c", p=128), writes=[v_b])
            for j in range(NVC):
                P.dma(hwq(), v_[:, NKB + BV * j:NKB + BV * (j + 1), 0:128],
                      vall[j * 2 * RV:j * 2 * RV + RV, h * 128:(h + 1) * 128].rearrange("(b p) c -> p b c", p=128), writes=[v_b])
            NSTR = 2 if TG >= 2 else 1
            for g0 in range(0, TG, NSTR):
                streams = []
                for sidx in range(NSTR):
                    g = g0 + sidx
                    qi = rot("q", 4)
                    P.dma(hwq(), qt[qi], qA[h * 128:(h + 1) * 128, g * G:(g + 1) * G], writes=[qtb[qi]])
                    blocks = [(0, kb) for kb in range(4 * g + 3, -1, -1)] + [(1, kb) for kb in range(NKB - 1, -1, -1)]
                    streams.append(dict(g=g, qi=qi, yb=4 + sidx, blocks=blocks, nblk=len(blocks), st={}, sx=sidx))

                def stA(S, bi):
                    sx, g, qi = S["sx"], S["g"], S["qi"]
                    far, kb = S["blocks"][bi]
                    kblk = k_[:, far, kb * 128:(kb + 1) * 128]
                    dk = (kb - 4 * g) if (far == 0 and kb >= 4 * g) else None
                    zb = 2 * sx + rot(f"zs{sx}", 2)
                    P.op("pe", lambda e: e.matmul(ps[zb], kblk, qt[qi], start=True, stop=True),
                         reads=[k_b, qtb[qi]], writes=[psb[zb]])
                    ei = 2 * sx + rot(f"e{sx}", 2)
                    P.op("act", lambda e: e.activation(out=e32[ei], in_=ps[zb], func=AF.Exp), reads=[psb[zb]], writes=[e32b[ei]])
                    si = 4 * sx + rot(f"sp{sx}", 4)
                    P.op("act", lambda e: e.activation(out=sp[si], in_=e32[ei], func=AF.Ln, bias=1.0, scale=1.0),
                         reads=[e32b[ei]], writes=[spb[si]])
                    if dk is not None:
                        P.op("dve", lambda e: e.tensor_tensor(out=sp[si], in0=sp[si], in1=msb[dk], op=ALU.mult),
                             reads=[spb[si], mskb], writes=[spb[si]])
                    Ri = None
                    if bi < S["nblk"] - 1:
                        R32s, R32sb = R32[sx], R32b[sx]
                        if bi == 0:
                            P.op("dve", lambda e: e.tensor_scalar(out=R32s, in0=sp[si], scalar1=-1.0, scalar2=None, op0=ALU.mult),
                                 reads=[spb[si]], writes=[R32sb])
                        else:
                            P.op("dve", lambda e: e.tensor_tensor(out=R32s, in0=R32s, in1=sp[si], op=ALU.subtract),
                                 reads=[spb[si], R32sb], writes=[R32sb])
                        Ri = 3 * sx + rot(f"R{sx}", 3)
                        P.op("dve", lambda e: e.tensor_copy(out=Rbf[Ri], in_=R32s), reads=[R32sb], writes=[Rbfb[Ri]])
                    S["st"][bi] = dict(kblk=kblk, dk=dk, si=si, Ri=Ri, far=far, kb=kb, zb=zb)

                def stB(S, bi):
                    sx = S["sx"]
                    d = S["st"][bi]
                    dk, si, far = d["dk"], d["si"], d["far"]
                    pb = d["zb"]
                    P.op("pe", lambda e: e.matmul(ps[pb], ntri, sp[si], start=False, stop=(bi == 0)),
                         reads=[spb[si], cstb], writes=[psb[pb]], inc=(bi == 0))
                    if bi > 0:
                        Rp = S["st"][bi - 1]["Ri"]
                        P.op("pe", lambda e: e.matmul(ps[pb], ones, Rbf[Rp], start=False, stop=True),
                             reads=[Rbfb[Rp], cstb], writes=[psb[pb]])
                    ai = 3 * sx + rot(f"at{sx}", 3)
                    if far:
                        P.op("act", lambda e: e.activation(out=at[ai], in_=ps[pb], func=AF.Exp, bias=farb[:, 0:1], scale=1.0),
                             reads=[psb[pb], gainsb], writes=[atb[ai]])
                    else:
                        P.op("act", lambda e: e.activation(out=at[ai], in_=ps[pb], func=AF.Exp), reads=[psb[pb]], writes=[atb[ai]])
                    if dk is not None:
                        P.op("dve", lambda e: e.tensor_tensor(out=at[ai], in0=at[ai], in1=msb[dk], op=ALU.mult),
                             reads=[atb[ai], mskb], writes=[atb[ai]])
                    d["ai"] = ai

                def stC(S, bi):
                    d = S["st"][bi]
                    ai, yb, nblk = d["ai"], S["yb"], S["nblk"]
                    vblk = v_[:, d["far"] * NKB + d["kb"], 0:128]
                    P.op("pe", lambda e: e.matmul(ps[yb], vblk, at[ai], start=(bi == 0), stop=(bi == nblk - 1)),
                         reads=[v_b, atb[ai]], writes=[psb[yb]], inc=(bi == nblk - 1))

                sB, sC = SKEW_SB
                nmax = max(S["nblk"] for S in streams)
                for it in range(nmax + sC):
                    for S in streams:
                        if it < S["nblk"]:
                            stA(S, it)
                    for S in streams:
                        if sB <= it < S["nblk"] + sB:
                            stB(S, it - sB)
                    for S in streams:
                        if sC <= it < S["nblk"] + sC:
                            stC(S, it - sC)
                for S in streams:
                    yi = ysti[0] % 4
                    ysti[0] += 1
                    yb, g = S["yb"], S["g"]
                    P.op("act", lambda e, yb=yb, yi=yi: e.activation(out=yst[yi], in_=ps[yb], func=AF.Copy), reads=[psb[yb]], writes=[ystb[yi]])
                    P.dma(stq(), yAd[h * 128:(h + 1) * 128, g * G:(g + 1) * G], yst[yi], reads=[ystb[yi]])

        for h in range(4):
            hb = rot("hd", 2)
            k_, k_b = kT[hb], kTb[hb]
            v_, v_b = vv[hb], vvb[hb]
            for half in range(2):
                r0 = 1024 + (2 * h + half) * 128
                P.dma(hwq(), k_[:, half, :], kloc[r0:r0 + 128, :], writes=[k_b])
                P.dma(hwq(), k_[:, 2 + half, :], kfar(r0, 128), writes=[k_b])
            c0 = 1024 + h * 256
            P.dma(hwq(), v_[:, 0:NKB, :], vloc[:, c0:c0 + 256].rearrange("(b p) c -> p b c", p=128), writes=[v_b])
            for j in range(NVC):
                P.dma(hwq(), v_[:, NKB + BV * j:NKB + BV * (j + 1), :],
                      vall[j * 2 * RV:j * 2 * RV + RV, c0:c0 + 256].rearrange("(b p) c -> p b c", p=128), writes=[v_b])
            for g in range(TG):
                t0 = g * G
                blocks = [(0, kb) for kb in range(4 * g + 3, -1, -1)] + [(1, kb) for kb in range(NKB - 1, -1, -1)]
                nblk = len(blocks)
                dstreams = []
                for half in range(2):
                    qi = rot("q", 4)
                    r0 = (2 * h + half) * 128
                    P.dma(hwq(), qt[qi], qB[r0:r0 + 128, t0:t0 + G], writes=[qtb[qi]])
                    ya, yb2, db = (2, 3, 6) if half == 0 else (4, 5, 7)
                    dstreams.append(dict(half=half, qi=qi, ya=ya, yb2=yb2, db=db, st={}))

                def dA(S, bi):
                    half, qi = S["half"], S["qi"]
                    far, kb = blocks[bi]
                    kblk = k_[:, 2 * far + half, kb * 128:(kb + 1) * 128]
                    dk = (kb - 4 * g) if (far == 0 and kb >= 4 * g) else None
                    zb = half
                    P.op("pe", lambda e: e.matmul(ps[zb], kblk, qt[qi], start=True, stop=True),
                         reads=[k_b, qtb[qi]], writes=[psb[zb]])
                    ai = 3 * half + rot(f"dat{half}", 3)
                    if far:
                        P.op("act", lambda e: e.activation(out=at[ai], in_=ps[zb], func=AF.Exp, bias=farb[:, 0:1], scale=1.0),
                             reads=[psb[zb], gainsb], writes=[atb[ai]])
                    else:
                        P.op("act", lambda e: e.activation(out=at[ai], in_=ps[zb], func=AF.Exp), reads=[psb[zb]], writes=[atb[ai]])
                    if dk is not None:
                        P.op("dve", lambda e: e.tensor_tensor(out=at[ai], in0=at[ai], in1=mdf[dk], op=ALU.mult),
                             reads=[atb[ai], mskb], writes=[atb[ai]])
                    S["st"][bi] = (far, kb, ai)

                def dB(S, bi):
                    far, kb, ai = S["st"][bi]
                    ya, yb2, db = S["ya"], S["yb2"], S["db"]
                    st, sp_ = (bi == 0), (bi == nblk - 1)
                    P.op("pe", lambda e: e.matmul(ps[ya], v_[:, far * NKB + kb, 0:128], at[ai], start=st, stop=sp_),
                         reads=[v_b, atb[ai]], writes=[psb[ya]], inc=sp_)
                    P.op("pe", lambda e: e.matmul(ps[yb2], v_[:, far * NKB + kb, 128:256], at[ai], start=st, stop=sp_),
                         reads=[v_b, atb[ai]], writes=[psb[yb2]], inc=sp_)
                    P.op("pe", lambda e: e.matmul(ps[db], ones, at[ai], start=st, stop=sp_),
                         reads=[cstb, atb[ai]], writes=[psb[db]], inc=sp_)

                for it in range(nblk + SKEW_DF):
                    for S in dstreams:
                        if it < nblk:
                            dA(S, it)
                    for S in dstreams:
                        if it >= SKEW_DF:
                            dB(S, it - SKEW_DF)

                for half in range(2):
                    S = dstreams[half]
                    ya, yb2, db = S["ya"], S["yb2"], S["db"]
                    P.op("dve", lambda e, db=db: e.reciprocal(out=rc, in_=ps[db]), reads=[psb[db]], writes=[rcb])
                    if half == 0:
                        P.op("dve", lambda e, ya=ya: e.tensor_tensor(out=n1[:, 0, :], in0=ps[ya], in1=rc, op=ALU.mult), reads=[psb[ya], rcb], writes=[n1b])
                        P.op("dve", lambda e, yb2=yb2: e.tensor_tensor(out=n1[:, 1, :], in0=ps[yb2], in1=rc, op=ALU.mult), reads=[psb[yb2], rcb], writes=[n1b])
                    else:
                        for c, bk in ((0, ya), (1, yb2)):
                            P.op("dve", lambda e, c=c, bk=bk: e.tensor_tensor(out=y32[:, c, :], in0=ps[bk], in1=rc, op=ALU.mult),
                                 reads=[psb[bk], rcb], writes=[y32b])
                            P.op("dve", lambda e, c=c: e.scalar_tensor_tensor(out=y32[:, c, :], in0=y32[:, c, :], scalar=lam[:, 2 * li + 1:2 * li + 2],
                                                                             in1=n1[:, c, :], op0=ALU.mult, op1=ALU.add),
                                 reads=[y32b, n1b, lamb], writes=[y32b])
                        for c in range(2):
                            P.op("pool", lambda e, c=c: e.tensor_tensor(out=sp[c], in0=y32[:, c, :], in1=y32[:, c, :], op=ALU.mult),
                                 reads=[y32b], writes=[spb[c]])
                            P.op("pe", lambda e, c=c: e.matmul(ps[0], ones, sp[c], start=(c == 0), stop=(c == 1)),
                                 reads=[spb[c], cstb], writes=[psb[0]], inc=(c == 1))
                        P.op("act", lambda e: e.activation(out=rc, in_=ps[0], func=AF.Ln, bias=EPS, scale=1.0 / 256),
                             reads=[psb[0]], writes=[rcb])
                        P.op("act", lambda e: e.activation(out=rc, in_=rc, func=AF.Exp, scale=-0.5),
                             reads=[rcb], writes=[rcb])
                        P.op("dve", lambda e: e.tensor_scalar(out=rc, in0=rc, scalar1=gcol(li, GC_OML), scalar2=None, op0=ALU.mult),
                             reads=[rcb, gainsb], writes=[rcb])
                        for c in range(2):
                            yi = ysti[0] % 4
                            ysti[0] += 1
                            P.op("dve", lambda e, c=c, yi=yi: e.scalar_tensor_tensor(out=yst[yi], in0=y32[:, c, :],
                                                                                   scalar=gcol(li, GC_SUB + c), in1=rc, op0=ALU.mult, op1=ALU.mult),
                                 reads=[y32b, rcb, gainsb], writes=[ystb[yi]])
                            P.dma(stq(), yBd[(2 * h + c) * 128:(2 * h + c + 1) * 128, t0:t0 + G], yst[yi], reads=[ystb[yi]])
        P.barrier()
        ar.pop()

        hT = ar.alloc([128, DC, G], F32); hTb = Buf("hT")
        xn = ar.alloc([128, DC, G], BF16); xnb = Buf("xn")
        hid = ar.alloc([128, FC, G], BF16); hidb = Buf("hid")
        ws = WStream(4)
        rstd = ar.alloc([128, G], F32); rstdb = Buf("rstd")
        tmps = [ar.alloc([128, G], BF16) for _ in range(4)]; tmpbs = [Buf(f"tmp{i}") for i in range(4)]
        sg = [ar.alloc([128, G], F32) for _ in range(2)]; sgb = [Buf("sg0"), Buf("sg1")]
        gt = [ar.alloc([128, 2, G], BF16) for _ in range(4)]; gtb = [Buf(f"gt{i}") for i in range(4)]
        mg = [ar.alloc([128, G], F32) for _ in range(2)]; mgb = [Buf("mg0"), Buf("mg1")]
        ptl = ar.alloc([128, 2, G], BF16); ptlb = Buf("ptl")
        yg = hid[:, 0:16, :]; ygb = hidb
        for g in range(TG):
            t0 = g * G
            for q4 in range(4):
                P.dma(hwq(), hT[:, q4 * 4:(q4 + 1) * 4, :],
                      hS[q4 * 512:(q4 + 1) * 512, t0:t0 + G].rearrange("(c p) t -> p c t", p=128), writes=[hTb])
            P.dma("pool", ptl, pT[li][:, t0:t0 + G].rearrange("(c p) t -> p c t", p=128), writes=[ptlb])
            P.dma(hwq(), yg[:, 0:8, :], yAd[:, t0:t0 + G].rearrange("(c p) t -> p c t", p=128), writes=[ygb])
            P.dma(hwq(), yg[:, 8:16, :], yBd[:, t0:t0 + G].rearrange("(c p) t -> p c t", p=128), writes=[ygb])
            for ob in range(4):
                wa_, wab = ws.load(wsrc(w_a[li], ob * 512, 512), 8, 512)
                wb2, wbb2 = ws.load(wsrc(w_b[li], ob * 512, 512), 8, 512)
                for j in range(4):
                    oc = ob * 4 + j
                    gi = oc % 4
                    P.dma(hwq(), gt[gi], gts.rearrange("(two r) t -> r two t", two=2)[oc * 128:(oc + 1) * 128, :, t0:t0 + G], writes=[gtb[gi]])
                    pa, pb_ = (0, 1) if oc % 2 == 0 else (2, 3)
                    for kc in range(8):
                        P.op("pe", lambda e, kc=kc, j=j, pa=pa, wa_=wa_: e.matmul(ps[pa], wa_[:, kc, j * 128:(j + 1) * 128], yg[:, kc, :],
                                                                                  start=(kc == 0), stop=(kc == 7)),
                             reads=[wab, ygb], writes=[psb[pa]], inc=(kc == 7))
                    for kc in range(8):
                        P.op("pe", lambda e, kc=kc, j=j, pb_=pb_, wb2=wb2: e.matmul(ps[pb_], wb2[:, kc, j * 128:(j + 1) * 128], yg[:, 8 + kc, :],
                                                                                   start=(kc == 0), stop=(kc == 7)),
                             reads=[wbb2, ygb], writes=[psb[pb_]], inc=(kc == 7))
                    s = oc % 2
                    P.op("dve", lambda e, s=s, pa=pa, gi=gi: e.tensor_tensor(out=sg[s], in0=ps[pa], in1=gt[gi][:, 0, :], op=ALU.mult),
                         reads=[psb[pa], gtb[gi]], writes=[sgb[s]])
                    P.op("dve", lambda e, s=s, pb_=pb_, gi=gi: e.tensor_tensor(out=mg[s], in0=ps[pb_], in1=gt[gi][:, 1, :], op=ALU.mult),
                         reads=[psb[pb_], gtb[gi]], writes=[mgb[s]])
                    P.op("pool", lambda e, s=s, oc=oc: e.tensor_tensor(out=xn[:, oc, :], in0=sg[s], in1=mg[s], op=ALU.add),
                         reads=[sgb[s], mgb[s]], writes=[xnb])
            for ob in range(4):
                wo_, wob = ws.load(wsrc(w_o[li], ob * 512, 512), DC, 512)
                for j in range(4):
                    oc = ob * 4 + j
                    pd = 4 + oc % 2
                    for kc in range(DC):
                        P.op("pe", lambda e, kc=kc, j=j, pd=pd, wo_=wo_: e.matmul(ps[pd], wo_[:, kc, j * 128:(j + 1) * 128], xn[:, kc, :],
                                                                                  start=(kc == 0), stop=(kc == DC - 1)),
                             reads=[wob, xnb], writes=[psb[pd]], inc=(kc == DC - 1))
                    P.op("dve", lambda e, oc=oc, pd=pd: e.tensor_tensor(out=hT[:, oc, :], in0=ps[pd], in1=hT[:, oc, :], op=ALU.add),
                         reads=[psb[pd], hTb], writes=[hTb])
            ffn(li, 1, hT, hTb, xn, xnb, hid, hidb, ws, rstd, rstdb, tmps, tmpbs, sg, sgb)
            rms_bc(hT, DC, 6, rstd, rstdb, hTb, float(D), tmps, tmpbs)
            for c in range(DC):
                eng = "dve"
                P.op(eng, lambda e, c=c: e.scalar_tensor_tensor(out=xn[:, c, :], in0=hT[:, c, :], scalar=gcol(li, GC_PLE + c),
                                                                in1=rstd, op0=ALU.mult, op1=ALU.mult),
                     reads=[hTb, rstdb, gainsb], writes=[xnb])
            ppv = hid[:, 0:32, :].rearrange("p a b -> p (a b)").bitcast(F32).rearrange("p (a b) -> p a b", a=16)
            for ob in range(4):
                wp_, wpb = ws.load(wsrc(w_pp[li], ob * 512, 512), 2, 512)
                for j in range(4):
                    oc = ob * 4 + j
                    pd = 4 + oc % 2
                    for kc in range(2):
                        P.op("pe", lambda e, kc=kc, j=j, pd=pd, wp_=wp_: e.matmul(ps[pd], wp_[:, kc, j * 128:(j + 1) * 128], ptl[:, kc, :],
                                                                                  start=(kc == 0), stop=(kc == 1)),
                             reads=[wpb, ptlb], writes=[psb[pd]], inc=(kc == 1))
                    P.op("act", lambda e, oc=oc, pd=pd: e.activation(out=ppv[:, oc, :], in_=ps[pd], func=AF.Copy), reads=[psb[pd]], writes=[hidb])
            rms_bc(ppv, DC, 6, rstd, rstdb, hidb, float(D), tmps, tmpbs)
            for ob in range(4):
                wg_, wgb_ = ws.load(wsrc(w_pg[li], ob * 512, 512), DC, 512)
                for j in range(4):
                    oc = ob * 4 + j
                    pd = (0, 1, 2, 3)[oc % 4]
                    for kc in range(DC):
                        P.op("pe", lambda e, kc=kc, j=j, pd=pd, wg_=wg_: e.matmul(ps[pd], wg_[:, kc, j * 128:(j + 1) * 128], xn[:, kc, :],
                                                                                  start=(kc == 0), stop=(kc == DC - 1)),
                             reads=[wgb_, xnb], writes=[psb[pd]], inc=(kc == DC - 1))
                    s = oc % 2
                    P.op("act", lambda e, s=s, pd=pd: e.activation(out=sg[s], in_=ps[pd], func=AF.Sigmoid), reads=[psb[pd]], writes=[sgb[s]])
                    P.op("dve", lambda e, oc=oc: e.scalar_tensor_tensor(out=ppv[:, oc, :], in0=ppv[:, oc, :], scalar=gcol(li, GC_PLEO + oc),
                                                                        in1=rstd, op0=ALU.mult, op1=ALU.mult),
                         reads=[hidb, rstdb, gainsb], writes=[hidb])
                    P.op("dve", lambda e, s=s, oc=oc: e.tensor_tensor(out=sg[s], in0=sg[s], in1=ppv[:, oc, :], op=ALU.mult),
                         reads=[sgb[s], hidb], writes=[sgb[s]])
                    P.op("dve", lambda e, s=s, oc=oc: e.tensor_tensor(out=hT[:, oc, :], in0=hT[:, oc, :], in1=sg[s], op=ALU.add),
                         reads=[sgb[s], hTb], writes=[hTb])
            dst = outT if last else hS
            for q4 in range(4):
                P.dma(stq(), dst[q4 * 512:(q4 + 1) * 512, t0:t0 + G].rearrange("(c p) t -> p c t", p=128),
                      hT[:, q4 * 4:(q4 + 1) * 4, :], reads=[hTb])
        P.barrier()
        ar.pop()

    for li, l in enumerate(layers):
        if stop_after == "pre":
            break
        phase_A(li, l, xT if li == 0 else hS)
        if stop_after in ("A", "Aload", "Affn"):
            break
        exchange()
        phase_B(li, l, last=(li == L - 1))
    P.barrier()
    P.emit()
    return nc


def _consts():
    c = np.zeros((128, NCC), np.float32)
    c[:, CC_ONES:CC_ONES + 128] = 1.0
    j = np.arange(128)[:, None]
    s = np.arange(128)[None, :]
    c[:, CC_NTRI:CC_NTRI + 128] = -(j >= s).astype(np.float32)
    rot = np.zeros((128, 128), np.float32)
    for i in range(16):
        rot[16 + i, i] = -1.0
        rot[i, 16 + i] = 1.0
    c[:, CC_ROT:CC_ROT + 128] = rot
    tt = np.arange(512)[None, :]
    sp = np.arange(128)[:, None]
    for k in range(4):
        c[:, CC_MSB + k * 512:CC_MSB + (k + 1) * 512] = ((128 * k + sp) < tt).astype(np.float32)
        c[:, CC_MDF + k * 512:CC_MDF + (k + 1) * 512] = ((128 * k + sp) <= tt).astype(np.float32)
    return c


def _rope(pos0, T):
    pos = (pos0 + np.arange(T)).astype(np.float32)
    inv = (np.float32(500000.0) ** (-np.arange(0, 32, 2, dtype=np.float32) / np.float32(32))).astype(np.float32)
    ang = pos[:, None] * inv[None, :]
    cos = np.cos(ang).astype(np.float32).T
    sin = np.sin(ang).astype(np.float32).T
    r = np.zeros((128, 2, T), np.float32)
    r[:, 0, :] = 1.0
    r[0:16, 0, :] = cos
    r[16:32, 0, :] = cos
    r[0:16, 1, :] = sin
    r[16:32, 1, :] = sin
    return r


def _gains(inp, l, lam_layer):
    g = np.zeros((128, NGC), np.float32)

    def col(v):
        return np.asarray(v, np.float32).reshape(-1, 128).T

    g[:, GC_FFN1:GC_FFN1 + 16] = col(inp["ffn1_norm"][l])
    g[:, GC_MIX:GC_MIX + 16] = col(inp["mix_norm"][l])
    g[:, GC_FFN2:GC_FFN2 + 16] = col(inp["ffn2_norm"][l])
    g[:, GC_PLE:GC_PLE + 16] = col(inp["ple_norm"][l])
    g[:, GC_PLEO:GC_PLEO + 16] = col(inp["ple_out_norm"][l])
    g[:, GC_DQ:GC_DQ + 1] = col(inp["diff_q_norm"][l])
    g[:, GC_DK:GC_DK + 1] = col(inp["diff_k_norm"][l])
    g[:, GC_L + 0:GC_L + 1] = col(inp["diff_lambda_q1"][l])
    g[:, GC_L + 1:GC_L + 2] = col(inp["diff_lambda_k1"][l])
    g[:, GC_L + 2:GC_L + 3] = col(inp["diff_lambda_q2"][l])
    g[:, GC_L + 3:GC_L + 4] = col(inp["diff_lambda_k2"][l])
    g[:, GC_SUB:GC_SUB + 2] = col(inp["diff_sub_norm"][l])
    li = 0.8 - 0.6 * math.exp(-0.3 * lam_layer)
    g[:, GC_LI] = li
    g[:, GC_OML] = 1.0 - li
    return g


_NC_CACHE = {}


def _run(inp, TG, layers_list, dbg=None):
    T = TG * G
    S = 2 * T
    x = np.asarray(inp["x"], np.float32)
    B = x.shape[0]
    ncores = 2 * B
    cst = _consts()
    hT = [np.ascontiguousarray(x[c // 2, (c % 2) * T:(c % 2 + 1) * T, :].T) for c in range(ncores)]
    res = None
    for layers in layers_list:
        key = (TG, len(layers), str(dbg))
        if key not in _NC_CACHE:
            _NC_CACHE[key] = build(TG, layers, dbg)
        nc = _NC_CACHE[key]
        sl = slice(layers[0], layers[-1] + 1)
        shared = {
            "w_gu1": np.asarray(inp["ffn1_w_gu"][sl], np.float32), "w_gu2": np.asarray(inp["ffn2_w_gu"][sl], np.float32),
            "w_d1": np.asarray(inp["ffn1_w_down"][sl], np.float32), "w_d2": np.asarray(inp["ffn2_w_down"][sl], np.float32),
            "w_in": np.asarray(inp["w_in"][sl], np.float32), "w_a": np.asarray(inp["w_branch_a"][sl], np.float32),
            "w_b": np.asarray(inp["w_branch_b"][sl], np.float32), "w_o": np.asarray(inp["w_out"][sl], np.float32),
            "w_pg": np.asarray(inp["ple_w_gate"][sl], np.float32), "w_pp": np.asarray(inp["ple_w_proj"][sl], np.float32),
            "gains": np.stack([_gains(inp, l, l) for l in layers]), "cst": cst,
        }
        p = np.asarray(inp["p"], np.float32)
        in_maps = []
        for c in range(ncores):
            b, half = c // 2, c % 2
            m = dict(shared)
            m["xT"] = hT[c]
            m["pT"] = np.ascontiguousarray(np.stack([p[l, b, half * T:(half + 1) * T, :].T for l in layers]))
            m["rope"] = _rope(half * T, T)
            m["farbias"] = np.full((128, 1), 0.0 if half == 1 else -30000.0, np.float32)
            in_maps.append(m)
        res = run_bass_kernel_spmd(nc, in_maps, core_ids=list(range(ncores)))
        hT = [np.asarray(res.results[c]["outT"]) for c in range(ncores)]
    out = np.empty((B, S, D), np.float32)
    for c in range(ncores):
        out[c // 2, (c % 2) * T:(c % 2 + 1) * T, :] = hT[c].T
    return out, res


def kernel(**inputs):
    out, _ = _run(inputs, 4, [[0, 1]])
    return out
```

```python
import math
import types
import numpy as np
import ml_dtypes
import concourse.bass as bass
import concourse.mybir as mybir
from concourse.bass_utils import run_bass_kernel_spmd

F32 = mybir.dt.float32
BF16 = mybir.dt.bfloat16
AF = mybir.ActivationFunctionType
ALU = mybir.AluOpType

D = 2048
DC = 16
DFF = 5632
FC = 44
G = 512
DEPTH = 2
INW = 10240
EPS = 1e-6
CAP = 30000
NDS = 16
SCALE = 128 ** -0.5
import os as _os
SKEW_SB = tuple(int(x) for x in _os.environ.get('SKEW_SB', '1,1').split(','))
SKEW_DF = int(_os.environ.get('SKEW_DF', '1'))

GC_FFN1, GC_MIX, GC_FFN2, GC_PLE, GC_PLEO = 0, 16, 32, 48, 64
GC_DQ, GC_DK, GC_L, GC_SUB, GC_LI, GC_OML = 80, 81, 82, 86, 88, 89
NGC = 90
CC_ONES, CC_NTRI, CC_ROT, CC_MSB, CC_MDF = 0, 128, 256, 384, 384 + 2048
NCC = 384 + 4096


def _freeze(fn):
    if fn is None or fn.__closure__ is None:
        return fn
    cells = []
    for c in fn.__closure__:
        try:
            cells.append(types.CellType(c.cell_contents))
        except ValueError:
            cells.append(c)
    return types.FunctionType(fn.__code__, fn.__globals__, fn.__name__, fn.__defaults__, tuple(cells))


class Buf:
    __slots__ = ("w", "r", "name")

    def __init__(self, name=""):
        self.w = None
        self.r = {}
        self.name = name


class Prog:
    ENGS = ["pe", "act", "dve", "pool", "sp"]

    def __init__(self, nc):
        self.nc = nc
        self.streams = {e: [] for e in self.ENGS}
        self.cnt = {e: 0 for e in self.ENGS}
        self.seen = {e: {} for e in self.ENGS}
        self.ndma = {}
        self.dma_tokens_live = {}
        self.extra_sems = []

    def _need_wait(self, eng, tok):
        if tok[0] == "e":
            _, te, k = tok
            if te == eng and eng == "pe":
                return None
            if self.seen[eng].get(te, 0) >= k:
                return None
            self.seen[eng][te] = k
            return ("e", te, k)
        elif tok[0] == "d":
            _, q, i = tok
            slot = i % NDS
            val = 16 * (i // NDS + 1)
            key = ("d", q, slot)
            if self.seen[eng].get(key, 0) >= val:
                return None
            self.seen[eng][key] = val
            return ("d", q, slot, val)
        else:
            key = tok
            if self.seen[eng].get(key, 0) >= 1:
                return None
            self.seen[eng][key] = 1
            return tok

    def _deps(self, eng, reads, writes):
        toks = []
        for b in reads:
            if b.w is not None:
                toks.append(b.w)
        for b in writes:
            if b.w is not None:
                toks.append(b.w)
            toks.extend(b.r.values())
        waits = []
        for t in toks:
            w = self._need_wait(eng, t)
            if w is not None:
                waits.append(w)
        return waits

    def _mark(self, tok, key, reads, writes):
        for b in reads:
            b.r[key] = tok
        for b in writes:
            b.w = tok
            b.r = {}

    def op(self, eng, fn, reads=(), writes=(), inc=True):
        if eng != "pe":
            inc = True
        waits = self._deps(eng, reads, writes)
        k = self.cnt[eng] + 1
        tok = ("e", eng, k)
        if inc:
            self.cnt[eng] = k
        self.streams[eng].append((waits, _freeze(fn), k if inc else None, None))
        self._mark(tok, eng, reads, writes)
        return tok

    def dma(self, eng, out, in_, reads=(), writes=()):
        waits = self._deps(eng, reads, writes)
        i = self.ndma.get(eng, 0)
        self.ndma[eng] = i + 1
        if i >= NDS:
            w = self._need_wait(eng, ("d", eng, i - NDS))
            if w is not None:
                waits.append(w)
        tok = ("d", eng, i)
        self.streams[eng].append((waits, lambda e: e.dma_start(out=out, in_=in_), None, i))
        self._mark(tok, tok, reads, writes)
        live = self.dma_tokens_live.setdefault(eng, [])
        live.append(tok)
        if len(live) > NDS:
            del live[:-NDS]
        return tok

    def barrier(self, extra=()):
        toks = [("e", e, self.cnt[e]) for e in self.ENGS if self.cnt[e] > 0]
        for live in self.dma_tokens_live.values():
            toks += list(live)
        toks += list(extra)
        for eng in self.ENGS:
            waits = []
            for t in toks:
                if t[0] == "e" and t[1] == eng and eng == "pe":
                    continue
                w = self._need_wait(eng, t)
                if w is not None:
                    waits.append(w)
            if waits:
                self.streams[eng].append((waits, None, None, None))

    def emit(self):
        nc = self.nc
        nsem = {e: (self.cnt[e] + CAP - 1) // CAP + 1 for e in self.ENGS}
        sems = {e: [nc.alloc_semaphore(f"s_{e}_{j}") for j in range(nsem[e])] for e in self.ENGS}
        dsems = {q: [nc.alloc_semaphore(f"s_dma_{q}_{j}") for j in range(NDS)] for q in self.ndma}
        xsems = self.extra_sems
        handles = {"pe": "tensor", "act": "scalar", "dve": "vector", "pool": "gpsimd", "sp": "sync"}

        def run(ename, eng):
            for waits, fn, k, di in self.streams[ename]:
                for w in waits:
                    if w[0] == "e":
                        _, te, kk = w
                        eng.wait_ge(sems[te][(kk - 1) // CAP], (kk - 1) % CAP + 1)
                    elif w[0] == "d":
                        eng.wait_ge(dsems[w[1]][w[2]], w[3])
                    else:
                        eng.wait_ge(xsems[w[1]], 1)
                if fn is None:
                    continue
                ins = fn(eng)
                if k is not None:
                    ins.then_inc(sems[ename][(k - 1) // CAP], 1)
                if di is not None:
                    ins.then_inc(dsems[ename][di % NDS], 16)

        with nc.Block() as block:
            @block.tensor
            def _(e):
                run("pe", e)

            @block.scalar
            def _(e):
                run("act", e)

            @block.vector
            def _(e):
                run("dve", e)

            @block.gpsimd
            def _(e):
                run("pool", e)

            @block.sync
            def _(e):
                run("sp", e)


class Arena:
    def __init__(self, nc, nbytes):
        self.t = nc.alloc_sbuf_tensor("arena", [128, nbytes // 2], BF16)
        self.nbytes = nbytes
        self.off = 0
        self.marks = []

    def push(self):
        self.marks.append(self.off)

    def pop(self):
        self.off = self.marks.pop()

    def alloc(self, shape, dtype):
        es = 4 if dtype == F32 else 2
        n = int(np.prod(shape[1:]))
        nb = n * es
        nb = (nb + 63) // 64 * 64
        assert self.off + nb <= self.nbytes, f"arena overflow {self.off + nb} > {self.nbytes}"
        o = self.off // 2
        ap = self.t[:, o:o + nb // 2]
        self.off += nb
        if dtype == F32:
            ap = ap.bitcast(F32)
        ap = ap[:, 0:n]
        if len(shape) == 3:
            ap = ap.rearrange("p (a b) -> p a b", a=shape[1])
        return ap


def build(TG, layers, dbg=None):
    T = TG * G
    NKB = T // 128
    L = len(layers)
    nc = bass.Bass("TRN2", target_bir_lowering=False)
    P = Prog(nc)

    def din(name, shape, dt=F32):
        return nc.dram_tensor(name, shape, dt, kind="ExternalInput").ap()

    def dscr(name, shape, dt, out=False):
        kind = "ExternalOutput" if (out or (dbg and name in dbg)) else "Internal"
        return nc.dram_tensor(name, shape, dt, kind=kind)

    xT = din("xT", [D, T])
    pT = din("pT", [L, 256, T])
    w_gu = [din("w_gu1", [L, D, 2 * DFF]), din("w_gu2", [L, D, 2 * DFF])]
    w_dn = [din("w_d1", [L, DFF, D]), din("w_d2", [L, DFF, D])]
    w_in = din("w_in", [L, D, INW])
    w_a = din("w_a", [L, 1024, D])
    w_b = din("w_b", [L, 1024, D])
    w_o = din("w_o", [L, D, D])
    w_pg = din("w_pg", [L, D, D])
    w_pp = din("w_pp", [L, 256, D])
    gains_d = din("gains", [L, 128, NGC])
    cst_d = din("cst", [128, NCC])
    rope_d = din("rope", [128, 2, T])
    farb_d = din("farbias", [128, 1])

    outT = nc.dram_tensor("outT", [D, T], F32, kind="ExternalOutput").ap()
    hS = dscr("hS", [D, T], F32).ap()
    qA = dscr("qA", [1024, T], BF16).ap()
    qB = dscr("qB", [1024, T], BF16).ap()
    gts = dscr("gts", [4096, T], BF16).ap()
    kloc = dscr("kloc", [2048, T], BF16)
    vloc = dscr("vloc", [T, 2048], BF16)
    kall = dscr("kall", [4096, T], BF16)
    vall = dscr("vall", [2 * T, 2048], BF16)
    yAd = dscr("yAd", [1024, T], BF16).ap()
    yBd = dscr("yBd", [1024, T], BF16).ap()

    ar = Arena(nc, 188 * 1024)
    ps = [nc.alloc_psum_tensor(f"ps{i}", [128, 512], F32).ap() for i in range(8)]
    psb = [Buf(f"ps{i}") for i in range(8)]

    cst = ar.alloc([128, 384], BF16)
    cstb = Buf("cst")
    gains = ar.alloc([128, L * NGC], F32)
    gainsb = Buf("gains")
    farb = ar.alloc([128, 1], F32)
    lam = ar.alloc([128, 2 * L], F32)
    lamb = Buf("lam")
    P.dma("pool", cst, cst_d[:, 0:384], writes=[cstb])
    for l in range(L):
        P.dma("sp", gains[:, l * NGC:(l + 1) * NGC], gains_d[l], writes=[gainsb])
    P.dma("sp", farb, farb_d, writes=[gainsb])
    ones = cst[:, CC_ONES:CC_ONES + 128]
    ntri = cst[:, CC_NTRI:CC_NTRI + 128]
    rotm = cst[:, CC_ROT:CC_ROT + 128]

    def gcol(l, c, n=1):
        return gains[:, l * NGC + c:l * NGC + c + n]

    ar.push()
    ltmp = ar.alloc([128, 8], F32)
    ltb = Buf("ltmp")
    ones32 = ar.alloc([128, 128], F32)
    o32b = Buf("ones32")
    P.op("dve", lambda e: e.memset(ones32, 1.0), writes=[o32b])
    for l in range(L):
        P.op("dve", lambda e, l=l: e.tensor_tensor(out=ltmp[:, 0:1], in0=gcol(l, GC_L), in1=gcol(l, GC_L + 1), op=ALU.mult),
             reads=[gainsb], writes=[ltb])
        P.op("dve", lambda e, l=l: e.tensor_tensor(out=ltmp[:, 1:2], in0=gcol(l, GC_L + 2), in1=gcol(l, GC_L + 3), op=ALU.mult),
             reads=[gainsb], writes=[ltb])
        P.op("pe", lambda e: e.matmul(ps[7][:, 0:2], ones32, ltmp[:, 0:2], start=True, stop=True),
             reads=[ltb, o32b], writes=[psb[7]])
        P.op("act", lambda e: e.activation(out=ltmp[:, 2:4], in_=ps[7][:, 0:2], func=AF.Exp), reads=[psb[7]], writes=[ltb])
        P.op("dve", lambda e: e.tensor_tensor(out=ltmp[:, 4:5], in0=ltmp[:, 2:3], in1=ltmp[:, 3:4], op=ALU.subtract),
             reads=[ltb], writes=[ltb])
        P.op("dve", lambda e, l=l: e.tensor_tensor(out=lam[:, 2 * l:2 * l + 1], in0=ltmp[:, 4:5], in1=gcol(l, GC_LI), op=ALU.add),
             reads=[ltb, gainsb], writes=[lamb])
        P.op("dve", lambda e, l=l: e.tensor_scalar(out=lam[:, 2 * l + 1:2 * l + 2], in0=lam[:, 2 * l:2 * l + 1], scalar1=-1.0, scalar2=None, op0=ALU.mult),
             reads=[lamb], writes=[lamb])
    P.barrier()
    ar.pop()
    stop_after = dbg.get("stop") if dbg else None

    dmaq = ["sp", "act"]
    rr = [0]

    def hwq():
        return "sp"

    def stq():
        return "act"

    def rms_bc(src, nch, bank, out_rstd, rstdb, srcb, Dn, tmps, tmpbs):
        nt = len(tmps)
        for c in range(nch):
            sq, sqb = tmps[c % nt], tmpbs[c % nt]
            if c % 2 == 0:
                P.op("dve", lambda e, c=c, sq=sq: e.tensor_tensor(out=sq, in0=src[:, c, :], in1=src[:, c, :], op=ALU.mult),
                     reads=[srcb], writes=[sqb])
            else:
                P.op("act", lambda e, c=c, sq=sq: e.activation(out=sq, in_=src[:, c, :], func=AF.Square),
                     reads=[srcb], writes=[sqb])
            P.op("pe", lambda e, c=c, sq=sq: e.matmul(ps[bank], ones, sq, start=(c == 0), stop=(c == nch - 1)),
                 reads=[sqb, cstb], writes=[psb[bank]], inc=True)
        P.op("act", lambda e: e.activation(out=out_rstd, in_=ps[bank], func=AF.Ln, bias=EPS, scale=1.0 / Dn),
             reads=[psb[bank]], writes=[rstdb])
        P.op("act", lambda e: e.activation(out=out_rstd, in_=out_rstd, func=AF.Exp, scale=-0.5),
             reads=[rstdb], writes=[rstdb])

    class WStream:
        def __init__(self, nslots, nelem=8192):
            self.slots = [ar.alloc([128, nelem], BF16) for _ in range(nslots)]
            self.bufs = [Buf(f"w{i}") for i in range(nslots)]
            self.i = 0

        def load(self, src_ap, kc, ncol):
            s = self.i % len(self.slots)
            self.i += 1
            dst = self.slots[s][:, 0:kc * ncol].rearrange("p (c n) -> p c n", c=kc)
            P.dma("pool", dst, src_ap, writes=[self.bufs[s]])
            return dst, self.bufs[s]

    def wsrc(w2d, c0, ncol):
        return w2d[:, c0:c0 + ncol].rearrange("(c p) n -> p c n", p=128)

    def ffn(l, which, hT, hTb, xn, xnb, hid, hidb, ws, rstd, rstdb, tmps, tmpbs, sg, sgb):
        gc = GC_FFN1 if which == 0 else GC_FFN2
        rms_bc(hT, DC, 6, rstd, rstdb, hTb, float(D), tmps, tmpbs)
        for c in range(DC):
            eng = "dve"
            P.op(eng, lambda e, c=c: e.scalar_tensor_tensor(out=xn[:, c, :], in0=hT[:, c, :], scalar=gcol(l, gc + c),
                                                            in1=rstd, op0=ALU.mult, op1=ALU.mult),
                 reads=[hTb, rstdb, gainsb], writes=[xnb])
        wgu = w_gu[which][l]
        wdn = w_dn[which][l]
        for fb in range(FC // 4):
            wg, wgb = ws.load(wsrc(wgu, fb * 512, 512), DC, 512)
            wu, wub = ws.load(wsrc(wgu, DFF + fb * 512, 512), DC, 512)
            for j in range(4):
                fc = fb * 4 + j
                pg, pu = (0, 1) if fc % 2 == 0 else (2, 3)
                for kc in range(DC):
                    P.op("pe", lambda e, kc=kc, j=j, wg=wg, pg=pg: e.matmul(ps[pg], wg[:, kc, j * 128:(j + 1) * 128], xn[:, kc, :],
                                                                            start=(kc == 0), stop=(kc == DC - 1)),
                         reads=[wgb, xnb], writes=[psb[pg]], inc=(kc == DC - 1))
                for kc in range(DC):
                    P.op("pe", lambda e, kc=kc, j=j, wu=wu, pu=pu: e.matmul(ps[pu], wu[:, kc, j * 128:(j + 1) * 128], xn[:, kc, :],
                                                                            start=(kc == 0), stop=(kc == DC - 1)),
                         reads=[wub, xnb], writes=[psb[pu]], inc=(kc == DC - 1))
                s = fc % 2
                P.op("act", lambda e, pg=pg, s=s: e.activation(out=sg[s], in_=ps[pg], func=AF.Silu), reads=[psb[pg]], writes=[sgb[s]])
                P.op("dve", lambda e, pu=pu, s=s, fc=fc: e.tensor_tensor(out=hid[:, fc, :], in0=sg[s], in1=ps[pu], op=ALU.mult),
                     reads=[sgb[s], psb[pu]], writes=[hidb])
        dbanks = [4, 5, 0, 1]
        for db in range(8):
            halves = []
            for kh in range(2):
                halves.append(ws.load(wdn[kh * 2816:(kh + 1) * 2816, db * 256:(db + 1) * 256].rearrange("(c p) n -> p c n", p=128), 22, 256))
            for kh in range(2):
                wd, wdb = halves[kh]
                for j in range(2):
                    pd = dbanks[(db % 2) * 2 + j]
                    for fc in range(22):
                        last = (kh == 1 and fc == 21)
                        P.op("pe", lambda e, fc=fc, kh=kh, j=j, wd=wd, pd=pd, last=last: e.matmul(
                            ps[pd], wd[:, fc, j * 128:(j + 1) * 128], hid[:, kh * 22 + fc, :], start=(kh == 0 and fc == 0), stop=last),
                             reads=[wdb, hidb], writes=[psb[pd]], inc=last)
            for j in range(2):
                dc = db * 2 + j
                pd = dbanks[(db % 2) * 2 + j]
                P.op("dve", lambda e, dc=dc, pd=pd: e.scalar_tensor_tensor(out=hT[:, dc, :], in0=ps[pd], scalar=0.5, in1=hT[:, dc, :],
                                                                           op0=ALU.mult, op1=ALU.add),
                     reads=[psb[pd], hTb], writes=[hTb])

    def phase_A(li, l, src_h):
        ar.push()
        hT = ar.alloc([128, DC, G], F32); hTb = Buf("hT")
        xn = ar.alloc([128, DC, G], BF16); xnb = Buf("xn")
        hid = ar.alloc([128, FC, G], BF16); hidb = Buf("hid")
        ws = WStream(4)
        rstd = ar.alloc([128, G], F32); rstdb = Buf("rstd")
        rs2 = [ar.alloc([128, G], F32) for _ in range(2)]; rs2b = [Buf("rs20"), Buf("rs21")]
        tmps = [ar.alloc([128, G], BF16) for _ in range(4)]; tmpbs = [Buf(f"tmp{i}") for i in range(4)]
        sg = [ar.alloc([128, G], F32) for _ in range(2)]; sgb = [Buf("sg0"), Buf("sg1")]
        stg = [ar.alloc([128, G], BF16) for _ in range(4)]; stgb = [Buf(f"stg{i}") for i in range(4)]
        xg = [ar.alloc([128, G], BF16) for _ in range(2)]; xgb = [Buf("xg0"), Buf("xg1")]
        x32 = [ar.alloc([128, G], F32) for _ in range(2)]; x32b = [Buf("x320"), Buf("x321")]
        rp = ar.alloc([128, 2, G], F32); rpb = Buf("rope")
        sti = [0]

        def stage():
            i = sti[0] % 4
            sti[0] += 1
            return stg[i], stgb[i]

        for g in range(TG):
            t0 = g * G
            for q4 in range(4):
                P.dma(hwq(), hT[:, q4 * 4:(q4 + 1) * 4, :],
                      src_h[q4 * 512:(q4 + 1) * 512, t0:t0 + G].rearrange("(c p) t -> p c t", p=128), writes=[hTb])
            P.dma(hwq(), rp, rope_d[:, :, t0:t0 + G], writes=[rpb])
            if stop_after != "Aload":
                ffn(li, 0, hT, hTb, xn, xnb, hid, hidb, ws, rstd, rstdb, tmps, tmpbs, sg, sgb)
            for q4 in range(4):
                P.dma(stq(), hS[q4 * 512:(q4 + 1) * 512, t0:t0 + G].rearrange("(c p) t -> p c t", p=128),
                      hT[:, q4 * 4:(q4 + 1) * 4, :], reads=[hTb])
            if stop_after in ("Aload", "Affn"):
                continue
            rms_bc(hT, DC, 6, rstd, rstdb, hTb, float(D), tmps, tmpbs)
            for c in range(DC):
                eng = "dve"
                P.op(eng, lambda e, c=c: e.scalar_tensor_tensor(out=xn[:, c, :], in0=hT[:, c, :], scalar=gcol(li, GC_MIX + c),
                                                                in1=rstd, op0=ALU.mult, op1=ALU.mult),
                     reads=[hTb, rstdb, gainsb], writes=[xnb])
            win = w_in[li]
            pbank = [0]

            def nextbank():
                b = pbank[0] % 4
                pbank[0] += 1
                return b

            for cb in range(20):
                wb_, wbb = ws.load(wsrc(win, cb * 512, 512), DC, 512)
                kind = ["sbq", "sbk", "sbv", "dfq", "dfk", "dfv", "ga", "ga", "gb", "gb"][cb // 2]
                if kind in ("sbv", "dfv"):
                    for tt in range(4):
                        b = nextbank()
                        for kc in range(DC):
                            P.op("pe", lambda e, kc=kc, tt=tt, b=b, wb_=wb_: e.matmul(ps[b], xn[:, kc, tt * 128:(tt + 1) * 128], wb_[:, kc, :],
                                                                                      start=(kc == 0), stop=(kc == DC - 1)),
                                 reads=[wbb, xnb], writes=[psb[b]], inc=(kc == DC - 1))
                        st, stb = stage()
                        P.op("act", lambda e, b=b, st=st: e.activation(out=st, in_=ps[b], func=AF.Copy), reads=[psb[b]], writes=[stb])
                        col0 = (0 if kind == "sbv" else 1024) + (cb % 2) * 512
                        P.dma(stq(), vloc[t0 + tt * 128:t0 + (tt + 1) * 128, col0:col0 + 512], st, reads=[stb])
                    continue
                for j in range(4):
                    b = nextbank()
                    oc = (cb % 2) * 4 + j
                    for kc in range(DC):
                        P.op("pe", lambda e, kc=kc, j=j, b=b, wb_=wb_: e.matmul(ps[b], wb_[:, kc, j * 128:(j + 1) * 128], xn[:, kc, :],
                                                                                start=(kc == 0), stop=(kc == DC - 1)),
                             reads=[wbb, xnb], writes=[psb[b]], inc=(kc == DC - 1))
                    st, stb = stage()
                    if kind == "sbq":
                        P.op("act", lambda e, b=b, st=st: e.activation(out=st, in_=ps[b], func=AF.Copy, scale=SCALE), reads=[psb[b]], writes=[stb])
                        P.dma(stq(), qA[oc * 128:(oc + 1) * 128, t0:t0 + G], st, reads=[stb])
                    elif kind == "sbk":
                        P.op("act", lambda e, b=b, st=st: e.activation(out=st, in_=ps[b], func=AF.Copy), reads=[psb[b]], writes=[stb])
                        P.dma(stq(), kloc[oc * 128:(oc + 1) * 128, t0:t0 + G], st, reads=[stb])
                    elif kind in ("ga", "gb"):
                        gch = ((cb - 12) * 4 + j)
                        P.op("act", lambda e, b=b, st=st: e.activation(out=st, in_=ps[b], func=AF.Sigmoid), reads=[psb[b]], writes=[stb])
                        P.dma(stq(), gts[gch * 128:(gch + 1) * 128, t0:t0 + G], st, reads=[stb])
                    else:
                        isq = kind == "dfq"
                        gcn = GC_DQ if isq else GC_DK
                        s = oc % 2
                        bss, brt = 4 + s, 6 + s
                        P.op("act", lambda e, b=b, s=s: e.activation(out=tmps[s], in_=ps[b], func=AF.Square), reads=[psb[b]], writes=[tmpbs[s]])
                        P.op("act", lambda e, b=b, s=s, gcn=gcn: e.activation(out=x32[s], in_=ps[b], func=AF.Copy, scale=gcol(li, gcn)),
                             reads=[psb[b], gainsb], writes=[x32b[s]])
                        P.op("pe", lambda e, s=s, bss=bss: e.matmul(ps[bss], ones, tmps[s], start=True, stop=True), reads=[tmpbs[s], cstb], writes=[psb[bss]])
                        P.op("dve", lambda e, s=s: e.tensor_copy(out=xg[s], in_=x32[s]), reads=[x32b[s]], writes=[xgb[s]])
                        P.op("pe", lambda e, s=s, brt=brt: e.matmul(ps[brt], rotm, xg[s], start=True, stop=True), reads=[xgb[s], cstb], writes=[psb[brt]])
                        P.op("act", lambda e, s=s, bss=bss: e.activation(out=rs2[s], in_=ps[bss], func=AF.Ln, bias=EPS, scale=1.0 / 128),
                             reads=[psb[bss]], writes=[rs2b[s]])
                        P.op("act", lambda e, s=s, isq=isq: e.activation(out=rs2[s], in_=rs2[s], func=AF.Exp, scale=-0.5, bias=(math.log(SCALE) if isq else 0.0)),
                             reads=[rs2b[s]], writes=[rs2b[s]])
                        P.op("dve", lambda e, s=s: e.tensor_tensor(out=x32[s], in0=x32[s], in1=rp[:, 0, :], op=ALU.mult),
                             reads=[x32b[s], rpb], writes=[x32b[s]])
                        P.op("dve", lambda e, s=s, brt=brt: e.tensor_tensor(out=sg[s], in0=ps[brt], in1=rp[:, 1, :], op=ALU.mult),
                             reads=[psb[brt], rpb], writes=[sgb[s]])
                        P.op("dve", lambda e, s=s: e.tensor_tensor(out=x32[s], in0=x32[s], in1=sg[s], op=ALU.add),
                             reads=[x32b[s], sgb[s]], writes=[x32b[s]])
                        P.op("dve", lambda e, s=s, st=st: e.tensor_tensor(out=st, in0=x32[s], in1=rs2[s], op=ALU.mult),
                             reads=[x32b[s], rs2b[s]], writes=[stb])
                        if isq:
                            P.dma(stq(), qB[oc * 128:(oc + 1) * 128, t0:t0 + G], st, reads=[stb])
                        else:
                            P.dma(stq(), kloc[1024 + oc * 128:1024 + (oc + 1) * 128, t0:t0 + G], st, reads=[stb])
        P.barrier()
        ar.pop()

    RK = min(2048, (1 << 20) // T)
    NKC = 2048 // RK
    RV = min(T, 512)
    NVC = T // RV
    BV = RV // 128

    def kfar(r0, n):
        j, w = r0 // RK, r0 % RK
        return kall[j * 2 * RK + w:j * 2 * RK + w + n, :]

    def exchange():
        groups = [[0, 1], [2, 3], [4, 5], [6, 7]]
        toks = []
        jobs = [(kloc, kall, RK, j) for j in range(NKC)] + [(vloc, vall, RV, j) for j in range(NVC)]
        for (src, dst, R_, j) in jobs:
            sem = nc.alloc_semaphore(f"cc{len(P.extra_sems)}")
            idx = len(P.extra_sems)
            P.extra_sems.append(sem)
            P.streams["pool"].append(([], (lambda e, src=src, dst=dst, sem=sem, R_=R_, j=j: e.collective_compute(
                "AllGather", ALU.bypass, replica_groups=groups, ins=[src[j * R_:(j + 1) * R_, :]],
                outs=[dst[j * 2 * R_:(j + 1) * 2 * R_, :]]).then_inc(sem, 1)), None, None))
            toks.append(("x", idx))
        P.barrier(extra=toks)

    def phase_B(li, l, last):
        ar.push()
        ar.push()
        msk = ar.alloc([128, 4096], BF16); mskb = Buf("msk")
        P.dma("pool", msk, cst_d[:, 384:384 + 4096], writes=[mskb])
        yst = [ar.alloc([128, G], BF16) for _ in range(4)]; ystb = [Buf(f"yst{i}") for i in range(4)]
        ysti = [0]
        kT = [ar.alloc([128, 4, T], BF16) for _ in range(2)]; kTb = [Buf("kT0"), Buf("kT1")]
        vv = [ar.alloc([128, 2 * NKB, 256], BF16) for _ in range(2)]; vvb = [Buf("v0"), Buf("v1")]
        qt = [ar.alloc([128, G], BF16) for _ in range(4)]; qtb = [Buf(f"qt{i}") for i in range(4)]
        e32 = [ar.alloc([128, G], F32) for _ in range(4)]; e32b = [Buf(f"e{i}") for i in range(4)]
        sp = [ar.alloc([128, G], BF16) for _ in range(8)]; spb = [Buf(f"sp{i}") for i in range(8)]
        at = [ar.alloc([128, G], BF16) for _ in range(6)]; atb = [Buf(f"at{i}") for i in range(6)]
        R32 = [ar.alloc([128, G], F32) for _ in range(2)]; R32b = [Buf("R320"), Buf("R321")]
        Rbf = [ar.alloc([128, G], BF16) for _ in range(6)]; Rbfb = [Buf(f"Rbf{i}") for i in range(6)]
        n1 = ar.alloc([128, 2, G], F32); n1b = Buf("n1")
        rc = ar.alloc([128, G], F32); rcb = Buf("rc")
        y32 = ar.alloc([128, 2, G], F32); y32b = Buf("y32")
        tq = ar.alloc([128, G], BF16); tqb = Buf("tq")
        msb = [msk[:, k * 512:(k + 1) * 512] for k in range(4)]
        mdf = [msk[:, 2048 + k * 512:2048 + (k + 1) * 512] for k in range(4)]
        cnt = {"q": 0, "e": 0, "sp": 0, "at": 0, "z": 0, "zs": 0, "p": 0, "y": 0, "R": 0, "hd": 0}

        def rot(name, n):
            i = cnt.get(name, 0) % n
            cnt[name] = cnt.get(name, 0) + 1
            return i

        for h in range(8):
            hb = rot("hd", 2)
            k_, k_b = kT[hb], kTb[hb]
            v_, v_b = vv[hb], vvb[hb]
            P.dma(hwq(), k_[:, 0, :], kloc[h * 128:(h + 1) * 128, :], writes=[k_b])
            P.dma(hwq(), k_[:, 1, :], kfar(h * 128, 128), writes=[k_b])
            P.dma(hwq(), v_[:, 0:NKB, 0:128], vloc[:, h * 128:(h + 1) * 128].rearrange("(b p) c -> p b c", p=128), writes=[v_b])
            for j in range(NVC):
                P.dma(hwq(), v_[:, NKB + BV * j:NKB + BV * (j + 1), 0:128],
                      vall[j * 2 * RV:j * 2 * RV + RV, h * 128:(h + 1) * 128].rearrange("(b p) c -> p b c", p=128), writes=[v_b])
            NSTR = 2 if TG >= 2 else 1
            for g0 in range(0, TG, NSTR):
                streams = []
                for sidx in range(NSTR):
                    g = g0 + sidx
                    qi = rot("q", 4)
                    P.dma(hwq(), qt[qi], qA[h * 128:(h + 1) * 128, g * G:(g + 1) * G], writes=[qtb[qi]])
                    blocks = [(0, kb) for kb in range(4 * g + 3, -1, -1)] + [(1, kb) for kb in range(NKB - 1, -1, -1)]
                    streams.append(dict(g=g, qi=qi, yb=4 + sidx, blocks=blocks, nblk=len(blocks), st={}, sx=sidx))

                def stA(S, bi):
                    sx, g, qi = S["sx"], S["g"], S["qi"]
                    far, kb = S["blocks"][bi]
                    kblk = k_[:, far, kb * 128:(kb + 1) * 128]
                    dk = (kb - 4 * g) if (far == 0 and kb >= 4 * g) else None
                    zb = 2 * sx + rot(f"zs{sx}", 2)
                    P.op("pe", lambda e: e.matmul(ps[zb], kblk, qt[qi], start=True, stop=True),
                         reads=[k_b, qtb[qi]], writes=[psb[zb]])
                    ei = 2 * sx + rot(f"e{sx}", 2)
                    P.op("act", lambda e: e.activation(out=e32[ei], in_=ps[zb], func=AF.Exp), reads=[psb[zb]], writes=[e32b[ei]])
                    si = 4 * sx + rot(f"sp{sx}", 4)
                    P.op("act", lambda e: e.activation(out=sp[si], in_=e32[ei], func=AF.Ln, bias=1.0, scale=1.0),
                         reads=[e32b[ei]], writes=[spb[si]])
                    if dk is not None:
                        P.op("dve", lambda e: e.tensor_tensor(out=sp[si], in0=sp[si], in1=msb[dk], op=ALU.mult),
                             reads=[spb[si], mskb], writes=[spb[si]])
                    Ri = None
                    if bi < S["nblk"] - 1:
                        R32s, R32sb = R32[sx], R32b[sx]
                        if bi == 0:
                            P.op("dve", lambda e: e.tensor_scalar(out=R32s, in0=sp[si], scalar1=-1.0, scalar2=None, op0=ALU.mult),
                                 reads=[spb[si]], writes=[R32sb])
                        else:
                            P.op("dve", lambda e: e.tensor_tensor(out=R32s, in0=R32s, in1=sp[si], op=ALU.subtract),
                                 reads=[spb[si], R32sb], writes=[R32sb])
                        Ri = 3 * sx + rot(f"R{sx}", 3)
                        P.op("dve", lambda e: e.tensor_copy(out=Rbf[Ri], in_=R32s), reads=[R32sb], writes=[Rbfb[Ri]])
                    S["st"][bi] = dict(kblk=kblk, dk=dk, si=si, Ri=Ri, far=far, kb=kb, zb=zb)

                def stB(S, bi):
                    sx = S["sx"]
                    d = S["st"][bi]
                    dk, si, far = d["dk"], d["si"], d["far"]
                    pb = d["zb"]
                    P.op("pe", lambda e: e.matmul(ps[pb], ntri, sp[si], start=False, stop=(bi == 0)),
                         reads=[spb[si], cstb], writes=[psb[pb]], inc=(bi == 0))
                    if bi > 0:
                        Rp = S["st"][bi - 1]["Ri"]
                        P.op("pe", lambda e: e.matmul(ps[pb], ones, Rbf[Rp], start=False, stop=True),
                             reads=[Rbfb[Rp], cstb], writes=[psb[pb]])
                    ai = 3 * sx + rot(f"at{sx}", 3)
                    if far:
                        P.op("act", lambda e: e.activation(out=at[ai], in_=ps[pb], func=AF.Exp, bias=farb[:, 0:1], scale=1.0),
                             reads=[psb[pb], gainsb], writes=[atb[ai]])
                    else:
                        P.op("act", lambda e: e.activation(out=at[ai], in_=ps[pb], func=AF.Exp), reads=[psb[pb]], writes=[atb[ai]])
                    if dk is not None:
                        P.op("dve", lambda e: e.tensor_tensor(out=at[ai], in0=at[ai], in1=msb[dk], op=ALU.mult),
                             reads=[atb[ai], mskb], writes=[atb[ai]])
                    d["ai"] = ai

                def stC(S, bi):
                    d = S["st"][bi]
                    ai, yb, nblk = d["ai"], S["yb"], S["nblk"]
                    vblk = v_[:, d["far"] * NKB + d["kb"], 0:128]
                    P.op("pe", lambda e: e.matmul(ps[yb], vblk, at[ai], start=(bi == 0), stop=(bi == nblk - 1)),
                         reads=[v_b, atb[ai]], writes=[psb[yb]], inc=(bi == nblk - 1))

                sB, sC = SKEW_SB
                nmax = max(S["nblk"] for S in streams)
                for it in range(nmax + sC):
                    for S in streams:
                        if it < S["nblk"]:
                            stA(S, it)
                    for S in streams:
                        if sB <= it < S["nblk"] + sB:
                            stB(S, it - sB)
                    for S in streams:
                        if sC <= it < S["nblk"] + sC:
                            stC(S, it - sC)
                for S in streams:
                    yi = ysti[0] % 4
                    ysti[0] += 1
                    yb, g = S["yb"], S["g"]
                    P.op("act", lambda e, yb=yb, yi=yi: e.activation(out=yst[yi], in_=ps[yb], func=AF.Copy), reads=[psb[yb]], writes=[ystb[yi]])
                    P.dma(stq(), yAd[h * 128:(h + 1) * 128, g * G:(g + 1) * G], yst[yi], reads=[ystb[yi]])

        for h in range(4):
            hb = rot("hd", 2)
            k_, k_b = kT[hb], kTb[hb]
            v_, v_b = vv[hb], vvb[hb]
            for half in range(2):
                r0 = 1024 + (2 * h + half) * 128
                P.dma(hwq(), k_[:, half, :], kloc[r0:r0 + 128, :], writes=[k_b])
                P.dma(hwq(), k_[:, 2 + half, :], kfar(r0, 128), writes=[k_b])
            c0 = 1024 + h * 256
            P.dma(hwq(), v_[:, 0:NKB, :], vloc[:, c0:c0 + 256].rearrange("(b p) c -> p b c", p=128), writes=[v_b])
            for j in range(NVC):
                P.dma(hwq(), v_[:, NKB + BV * j:NKB + BV * (j + 1), :],
                      vall[j * 2 * RV:j * 2 * RV + RV, c0:c0 + 256].rearrange("(b p) c -> p b c", p=128), writes=[v_b])
            for g in range(TG):
                t0 = g * G
                blocks = [(0, kb) for kb in range(4 * g + 3, -1, -1)] + [(1, kb) for kb in range(NKB - 1, -1, -1)]
                nblk = len(blocks)
                dstreams = []
                for half in range(2):
                    qi = rot("q", 4)
                    r0 = (2 * h + half) * 128
                    P.dma(hwq(), qt[qi], qB[r0:r0 + 128, t0:t0 + G], writes=[qtb[qi]])
                    ya, yb2, db = (2, 3, 6) if half == 0 else (4, 5, 7)
                    dstreams.append(dict(half=half, qi=qi, ya=ya, yb2=yb2, db=db, st={}))

                def dA(S, bi):
                    half, qi = S["half"], S["qi"]
                    far, kb = blocks[bi]
                    kblk = k_[:, 2 * far + half, kb * 128:(kb + 1) * 128]
                    dk = (kb - 4 * g) if (far == 0 and kb >= 4 * g) else None
                    zb = half
                    P.op("pe", lambda e: e.matmul(ps[zb], kblk, qt[qi], start=True, stop=True),
                         reads=[k_b, qtb[qi]], writes=[psb[zb]])
                    ai = 3 * half + rot(f"dat{half}", 3)
                    if far:
                        P.op("act", lambda e: e.activation(out=at[ai], in_=ps[zb], func=AF.Exp, bias=farb[:, 0:1], scale=1.0),
                             reads=[psb[zb], gainsb], writes=[atb[ai]])
                    else:
                        P.op("act", lambda e: e.activation(out=at[ai], in_=ps[zb], func=AF.Exp), reads=[psb[zb]], writes=[atb[ai]])
                    if dk is not None:
                        P.op("dve", lambda e: e.tensor_tensor(out=at[ai], in0=at[ai], in1=mdf[dk], op=ALU.mult),
                             reads=[atb[ai], mskb], writes=[atb[ai]])
                    S["st"][bi] = (far, kb, ai)

                def dB(S, bi):
                    far, kb, ai = S["st"][bi]
                    ya, yb2, db = S["ya"], S["yb2"], S["db"]
                    st, sp_ = (bi == 0), (bi == nblk - 1)
                    P.op("pe", lambda e: e.matmul(ps[ya], v_[:, far * NKB + kb, 0:128], at[ai], start=st, stop=sp_),
                         reads=[v_b, atb[ai]], writes=[psb[ya]], inc=sp_)
                    P.op("pe", lambda e: e.matmul(ps[yb2], v_[:, far * NKB + kb, 128:256], at[ai], start=st, stop=sp_),
                         reads=[v_b, atb[ai]], writes=[psb[yb2]], inc=sp_)
                    P.op("pe", lambda e: e.matmul(ps[db], ones, at[ai], start=st, stop=sp_),
                         reads=[cstb, atb[ai]], writes=[psb[db]], inc=sp_)

                for it in range(nblk + SKEW_DF):
                    for S in dstreams:
                        if it < nblk:
                            dA(S, it)
                    for S in dstreams:
                        if it >= SKEW_DF:
                            dB(S, it - SKEW_DF)

                for half in range(2):
                    S = dstreams[half]
                    ya, yb2, db = S["ya"], S["yb2"], S["db"]
                    P.op("dve", lambda e, db=db: e.reciprocal(out=rc, in_=ps[db]), reads=[psb[db]], writes=[rcb])
                    if half == 0:
                        P.op("dve", lambda e, ya=ya: e.tensor_tensor(out=n1[:, 0, :], in0=ps[ya], in1=rc, op=ALU.mult), reads=[psb[ya], rcb], writes=[n1b])
                        P.op("dve", lambda e, yb2=yb2: e.tensor_tensor(out=n1[:, 1, :], in0=ps[yb2], in1=rc, op=ALU.mult), reads=[psb[yb2], rcb], writes=[n1b])
                    else:
                        for c, bk in ((0, ya), (1, yb2)):
                            P.op("dve", lambda e, c=c, bk=bk: e.tensor_tensor(out=y32[:, c, :], in0=ps[bk], in1=rc, op=ALU.mult),
                                 reads=[psb[bk], rcb], writes=[y32b])
                            P.op("dve", lambda e, c=c: e.scalar_tensor_tensor(out=y32[:, c, :], in0=y32[:, c, :], scalar=lam[:, 2 * li + 1:2 * li + 2],
                                                                             in1=n1[:, c, :], op0=ALU.mult, op1=ALU.add),
                                 reads=[y32b, n1b, lamb], writes=[y32b])
                        for c in range(2):
                            P.op("pool", lambda e, c=c: e.tensor_tensor(out=sp[c], in0=y32[:, c, :], in1=y32[:, c, :], op=ALU.mult),
                                 reads=[y32b], writes=[spb[c]])
                            P.op("pe", lambda e, c=c: e.matmul(ps[0], ones, sp[c], start=(c == 0), stop=(c == 1)),
                                 reads=[spb[c], cstb], writes=[psb[0]], inc=(c == 1))
                        P.op("act", lambda e: e.activation(out=rc, in_=ps[0], func=AF.Ln, bias=EPS, scale=1.0 / 256),
                             reads=[psb[0]], writes=[rcb])
                        P.op("act", lambda e: e.activation(out=rc, in_=rc, func=AF.Exp, scale=-0.5),
                             reads=[rcb], writes=[rcb])
                        P.op("dve", lambda e: e.tensor_scalar(out=rc, in0=rc, scalar1=gcol(li, GC_OML), scalar2=None, op0=ALU.mult),
                             reads=[rcb, gainsb], writes=[rcb])
                        for c in range(2):
                            yi = ysti[0] % 4
                            ysti[0] += 1
                            P.op("dve", lambda e, c=c, yi=yi: e.scalar_tensor_tensor(out=yst[yi], in0=y32[:, c, :],
                                                                                   scalar=gcol(li, GC_SUB + c), in1=rc, op0=ALU.mult, op1=ALU.mult),
                                 reads=[y32b, rcb, gainsb], writes=[ystb[yi]])
                            P.dma(stq(), yBd[(2 * h + c) * 128:(2 * h + c + 1) * 128, t0:t0 + G], yst[yi], reads=[ystb[yi]])
        P.barrier()
        ar.pop()

        hT = ar.alloc([128, DC, G], F32); hTb = Buf("hT")
        xn = ar.alloc([128, DC, G], BF16); xnb = Buf("xn")
        hid = ar.alloc([128, FC, G], BF16); hidb = Buf("hid")
        ws = WStream(4)
        rstd = ar.alloc([128, G], F32); rstdb = Buf("rstd")
        tmps = [ar.alloc([128, G], BF16) for _ in range(4)]; tmpbs = [Buf(f"tmp{i}") for i in range(4)]
        sg = [ar.alloc([128, G], F32) for _ in range(2)]; sgb = [Buf("sg0"), Buf("sg1")]
        gt = [ar.alloc([128, 2, G], BF16) for _ in range(4)]; gtb = [Buf(f"gt{i}") for i in range(4)]
        mg = [ar.alloc([128, G], F32) for _ in range(2)]; mgb = [Buf("mg0"), Buf("mg1")]
        ptl = ar.alloc([128, 2, G], BF16); ptlb = Buf("ptl")
        yg = hid[:, 0:16, :]; ygb = hidb
        for g in range(TG):
            t0 = g * G
            for q4 in range(4):
                P.dma(hwq(), hT[:, q4 * 4:(q4 + 1) * 4, :],
                      hS[q4 * 512:(q4 + 1) * 512, t0:t0 + G].rearrange("(c p) t -> p c t", p=128), writes=[hTb])
            P.dma("pool", ptl, pT[li][:, t0:t0 + G].rearrange("(c p) t -> p c t", p=128), writes=[ptlb])
            P.dma(hwq(), yg[:, 0:8, :], yAd[:, t0:t0 + G].rearrange("(c p) t -> p c t", p=128), writes=[ygb])
            P.dma(hwq(), yg[:, 8:16, :], yBd[:, t0:t0 + G].rearrange("(c p) t -> p c t", p=128), writes=[ygb])
            for ob in range(4):
                wa_, wab = ws.load(wsrc(w_a[li], ob * 512, 512), 8, 512)
                wb2, wbb2 = ws.load(wsrc(w_b[li], ob * 512, 512), 8, 512)
                for j in range(4):
                    oc = ob * 4 + j
                    gi = oc % 4
                    P.dma(hwq(), gt[gi], gts.rearrange("(two r) t -> r two t", two=2)[oc * 128:(oc + 1) * 128, :, t0:t0 + G], writes=[gtb[gi]])
                    pa, pb_ = (0, 1) if oc % 2 == 0 else (2, 3)
                    for kc in range(8):
                        P.op("pe", lambda e, kc=kc, j=j, pa=pa, wa_=wa_: e.matmul(ps[pa], wa_[:, kc, j * 128:(j + 1) * 128], yg[:, kc, :],
                                                                                  start=(kc == 0), stop=(kc == 7)),
                             reads=[wab, ygb], writes=[psb[pa]], inc=(kc == 7))
                    for kc in range(8):
                        P.op("pe", lambda e, kc=kc, j=j, pb_=pb_, wb2=wb2: e.matmul(ps[pb_], wb2[:, kc, j * 128:(j + 1) * 128], yg[:, 8 + kc, :],
                                                                                   start=(kc == 0), stop=(kc == 7)),
                             reads=[wbb2, ygb], writes=[psb[pb_]], inc=(kc == 7))
                    s = oc % 2
                    P.op("dve", lambda e, s=s, pa=pa, gi=gi: e.tensor_tensor(out=sg[s], in0=ps[pa], in1=gt[gi][:, 0, :], op=ALU.mult),
                         reads=[psb[pa], gtb[gi]], writes=[sgb[s]])
                    P.op("dve", lambda e, s=s, pb_=pb_, gi=gi: e.tensor_tensor(out=mg[s], in0=ps[pb_], in1=gt[gi][:, 1, :], op=ALU.mult),
                         reads=[psb[pb_], gtb[gi]], writes=[mgb[s]])
                    P.op("dve", lambda e, s=s, oc=oc: e.tensor_tensor(out=xn[:, oc, :], in0=sg[s], in1=mg[s], op=ALU.add),
                         reads=[sgb[s], mgb[s]], writes=[xnb])
            for ob in range(4):
                wo_, wob = ws.load(wsrc(w_o[li], ob * 512, 512), DC, 512)
                for j in range(4):
                    oc = ob * 4 + j
                    pd = 4 + oc % 2
                    for kc in range(DC):
                        P.op("pe", lambda e, kc=kc, j=j, pd=pd, wo_=wo_: e.matmul(ps[pd], wo_[:, kc, j * 128:(j + 1) * 128], xn[:, kc, :],
                                                                                  start=(kc == 0), stop=(kc == DC - 1)),
                             reads=[wob, xnb], writes=[psb[pd]], inc=(kc == DC - 1))
                    P.op("dve", lambda e, oc=oc, pd=pd: e.tensor_tensor(out=hT[:, oc, :], in0=ps[pd], in1=hT[:, oc, :], op=ALU.add),
                         reads=[psb[pd], hTb], writes=[hTb])
            ffn(li, 1, hT, hTb, xn, xnb, hid, hidb, ws, rstd, rstdb, tmps, tmpbs, sg, sgb)
            rms_bc(hT, DC, 6, rstd, rstdb, hTb, float(D), tmps, tmpbs)
            for c in range(DC):
                eng = "dve"
                P.op(eng, lambda e, c=c: e.scalar_tensor_tensor(out=xn[:, c, :], in0=hT[:, c, :], scalar=gcol(li, GC_PLE + c),
                                                                in1=rstd, op0=ALU.mult, op1=ALU.mult),
                     reads=[hTb, rstdb, gainsb], writes=[xnb])
            ppv = hid[:, 0:32, :].rearrange("p a b -> p (a b)").bitcast(F32).rearrange("p (a b) -> p a b", a=16)
            for ob in range(4):
                wp_, wpb = ws.load(wsrc(w_pp[li], ob * 512, 512), 2, 512)
                for j in range(4):
                    oc = ob * 4 + j
                    pd = 4 + oc % 2
                    for kc in range(2):
                        P.op("pe", lambda e, kc=kc, j=j, pd=pd, wp_=wp_: e.matmul(ps[pd], wp_[:, kc, j * 128:(j + 1) * 128], ptl[:, kc, :],
                                                                                  start=(kc == 0), stop=(kc == 1)),
                             reads=[wpb, ptlb], writes=[psb[pd]], inc=(kc == 1))
                    P.op("act", lambda e, oc=oc, pd=pd: e.activation(out=ppv[:, oc, :], in_=ps[pd], func=AF.Copy), reads=[psb[pd]], writes=[hidb])
            rms_bc(ppv, DC, 6, rstd, rstdb, hidb, float(D), tmps, tmpbs)
            for ob in range(4):
                wg_, wgb_ = ws.load(wsrc(w_pg[li], ob * 512, 512), DC, 512)
                for j in range(4):
                    oc = ob * 4 + j
                    pd = (0, 1, 2, 3)[oc % 4]
                    for kc in range(DC):
                        P.op("pe", lambda e, kc=kc, j=j, pd=pd, wg_=wg_: e.matmul(ps[pd], wg_[:, kc, j * 128:(j + 1) * 128], xn[:, kc, :],
                                                                                  start=(kc == 0), stop=(kc == DC - 1)),
                             reads=[wgb_, xnb], writes=[psb[pd]], inc=(kc == DC - 1))
                    s = oc % 2
                    P.op("act", lambda e, s=s, pd=pd: e.activation(out=sg[s], in_=ps[pd], func=AF.Sigmoid), reads=[psb[pd]], writes=[sgb[s]])
                    P.op("dve", lambda e, oc=oc: e.scalar_tensor_tensor(out=ppv[:, oc, :], in0=ppv[:, oc, :], scalar=gcol(li, GC_PLEO + oc),
                                                                        in1=rstd, op0=ALU.mult, op1=ALU.mult),
                         reads=[hidb, rstdb, gainsb], writes=[hidb])
                    P.op("dve", lambda e, s=s, oc=oc: e.tensor_tensor(out=sg[s], in0=sg[s], in1=ppv[:, oc, :], op=ALU.mult),
                         reads=[sgb[s], hidb], writes=[sgb[s]])
                    P.op("dve", lambda e, s=s, oc=oc: e.tensor_tensor(out=hT[:, oc, :], in0=hT[:, oc, :], in1=sg[s], op=ALU.add),
                         reads=[sgb[s], hTb], writes=[hTb])
            dst = outT if last else hS
            for q4 in range(4):
                P.dma(stq(), dst[q4 * 512:(q4 + 1) * 512, t0:t0 + G].rearrange("(c p) t -> p c t", p=128),
                      hT[:, q4 * 4:(q4 + 1) * 4, :], reads=[hTb])
        P.barrier()
        ar.pop()

    for li, l in enumerate(layers):
        if stop_after == "pre":
            break
        phase_A(li, l, xT if li == 0 else hS)
        if stop_after in ("A", "Aload", "Affn"):
            break
        exchange()
        phase_B(li, l, last=(li == L - 1))
    P.barrier()
    P.emit()
    return nc


def _consts():
    c = np.zeros((128, NCC), np.float32)
    c[:, CC_ONES:CC_ONES + 128] = 1.0
    j = np.arange(128)[:, None]
    s = np.arange(128)[None, :]
    c[:, CC_NTRI:CC_NTRI + 128] = -(j >= s).astype(np.float32)
    rot = np.zeros((128, 128), np.float32)
    for i in range(16):
        rot[16 + i, i] = -1.0
        rot[i, 16 + i] = 1.0
    c[:, CC_ROT:CC_ROT + 128] = rot
    tt = np.arange(512)[None, :]
    sp = np.arange(128)[:, None]
    for k in range(4):
        c[:, CC_MSB + k * 512:CC_MSB + (k + 1) * 512] = ((128 * k + sp) < tt).astype(np.float32)
        c[:, CC_MDF + k * 512:CC_MDF + (k + 1) * 512] = ((128 * k + sp) <= tt).astype(np.float32)
    return c


def _rope(pos0, T):
    pos = (pos0 + np.arange(T)).astype(np.float32)
    inv = (np.float32(500000.0) ** (-np.arange(0, 32, 2, dtype=np.float32) / np.float32(32))).astype(np.float32)
    ang = pos[:, None] * inv[None, :]
    cos = np.cos(ang).astype(np.float32).T
    sin = np.sin(ang).astype(np.float32).T
    r = np.zeros((128, 2, T), np.float32)
    r[:, 0, :] = 1.0
    r[0:16, 0, :] = cos
    r[16:32, 0, :] = cos
    r[0:16, 1, :] = sin
    r[16:32, 1, :] = sin
    return r


def _gains(inp, l, lam_layer):
    g = np.zeros((128, NGC), np.float32)

    def col(v):
        return np.asarray(v, np.float32).reshape(-1, 128).T

    g[:, GC_FFN1:GC_FFN1 + 16] = col(inp["ffn1_norm"][l])
    g[:, GC_MIX:GC_MIX + 16] = col(inp["mix_norm"][l])
    g[:, GC_FFN2:GC_FFN2 + 16] = col(inp["ffn2_norm"][l])
    g[:, GC_PLE:GC_PLE + 16] = col(inp["ple_norm"][l])
    g[:, GC_PLEO:GC_PLEO + 16] = col(inp["ple_out_norm"][l])
    g[:, GC_DQ:GC_DQ + 1] = col(inp["diff_q_norm"][l])
    g[:, GC_DK:GC_DK + 1] = col(inp["diff_k_norm"][l])
    g[:, GC_L + 0:GC_L + 1] = col(inp["diff_lambda_q1"][l])
    g[:, GC_L + 1:GC_L + 2] = col(inp["diff_lambda_k1"][l])
    g[:, GC_L + 2:GC_L + 3] = col(inp["diff_lambda_q2"][l])
    g[:, GC_L + 3:GC_L + 4] = col(inp["diff_lambda_k2"][l])
    g[:, GC_SUB:GC_SUB + 2] = col(inp["diff_sub_norm"][l])
    li = 0.8 - 0.6 * math.exp(-0.3 * lam_layer)
    g[:, GC_LI] = li
    g[:, GC_OML] = 1.0 - li
    return g


_NC_CACHE = {}


def _run(inp, TG, layers_list, dbg=None):
    T = TG * G
    S = 2 * T
    x = np.asarray(inp["x"], np.float32)
    B = x.shape[0]
    ncores = 2 * B
    cst = _consts()
    hT = [np.ascontiguousarray(x[c // 2, (c % 2) * T:(c % 2 + 1) * T, :].T) for c in range(ncores)]
    res = None
    for layers in layers_list:
        key = (TG, len(layers), str(dbg))
        if key not in _NC_CACHE:
            _NC_CACHE[key] = build(TG, layers, dbg)
        nc = _NC_CACHE[key]
        sl = slice(layers[0], layers[-1] + 1)
        shared = {
            "w_gu1": np.asarray(inp["ffn1_w_gu"][sl], np.float32), "w_gu2": np.asarray(inp["ffn2_w_gu"][sl], np.float32),
            "w_d1": np.asarray(inp["ffn1_w_down"][sl], np.float32), "w_d2": np.asarray(inp["ffn2_w_down"][sl], np.float32),
            "w_in": np.asarray(inp["w_in"][sl], np.float32), "w_a": np.asarray(inp["w_branch_a"][sl], np.float32),
            "w_b": np.asarray(inp["w_branch_b"][sl], np.float32), "w_o": np.asarray(inp["w_out"][sl], np.float32),
            "w_pg": np.asarray(inp["ple_w_gate"][sl], np.float32), "w_pp": np.asarray(inp["ple_w_proj"][sl], np.float32),
            "gains": np.stack([_gains(inp, l, l) for l in layers]), "cst": cst,
        }
        p = np.asarray(inp["p"], np.float32)
        in_maps = []
        for c in range(ncores):
            b, half = c // 2, c % 2
            m = dict(shared)
            m["xT"] = hT[c]
            m["pT"] = np.ascontiguousarray(np.stack([p[l, b, half * T:(half + 1) * T, :].T for l in layers]))
            m["rope"] = _rope(half * T, T)
            m["farbias"] = np.full((128, 1), 0.0 if half == 1 else -30000.0, np.float32)
            in_maps.append(m)
        res = run_bass_kernel_spmd(nc, in_maps, core_ids=list(range(ncores)))
        hT = [np.asarray(res.results[c]["outT"]) for c in range(ncores)]
    out = np.empty((B, S, D), np.float32)
    for c in range(ncores):
        out[c // 2, (c % 2) * T:(c % 2 + 1) * T, :] = hT[c].T
    return out, res


def kernel(**inputs):
    out, _ = _run(inputs, 4, [[0, 1]])
    return out
```

```python
import math
import types
import numpy as np
import ml_dtypes
import concourse.bass as bass
import concourse.mybir as mybir
from concourse.bass_utils import run_bass_kernel_spmd

F32 = mybir.dt.float32
BF16 = mybir.dt.bfloat16
AF = mybir.ActivationFunctionType
ALU = mybir.AluOpType

D = 2048
DC = 16
DFF = 5632
FC = 44
G = 512
DEPTH = 2
INW = 10240
EPS = 1e-6
CAP = 30000
NDS = 16
SCALE = 128 ** -0.5
import os as _os
SKEW_SB = tuple(int(x) for x in _os.environ.get('SKEW_SB', '0,0').split(','))
SKEW_DF = int(_os.environ.get('SKEW_DF', '1'))

GC_FFN1, GC_MIX, GC_FFN2, GC_PLE, GC_PLEO = 0, 16, 32, 48, 64
GC_DQ, GC_DK, GC_L, GC_SUB, GC_LI, GC_OML = 80, 81, 82, 86, 88, 89
NGC = 90
CC_ONES, CC_NTRI, CC_ROT, CC_MSB, CC_MDF = 0, 128, 256, 384, 384 + 2048
NCC = 384 + 4096


def _freeze(fn):
    if fn is None or fn.__closure__ is None:
        return fn
    cells = []
    for c in fn.__closure__:
        try:
            cells.append(types.CellType(c.cell_contents))
        except ValueError:
            cells.append(c)
    return types.FunctionType(fn.__code__, fn.__globals__, fn.__name__, fn.__defaults__, tuple(cells))


class Buf:
    __slots__ = ("w", "r", "name")

    def __init__(self, name=""):
        self.w = None
        self.r = {}
        self.name = name


class Prog:
    ENGS = ["pe", "act", "dve", "pool", "sp"]

    def __init__(self, nc):
        self.nc = nc
        self.streams = {e: [] for e in self.ENGS}
        self.cnt = {e: 0 for e in self.ENGS}
        self.seen = {e: {} for e in self.ENGS}
        self.ndma = {}
        self.dma_tokens_live = {}
        self.extra_sems = []

    def _need_wait(self, eng, tok):
        if tok[0] == "e":
            _, te, k = tok
            if te == eng and eng == "pe":
                return None
            if self.seen[eng].get(te, 0) >= k:
                return None
            self.seen[eng][te] = k
            return ("e", te, k)
        elif tok[0] == "d":
            _, q, i = tok
            slot = i % NDS
            val = 16 * (i // NDS + 1)
            key = ("d", q, slot)
            if self.seen[eng].get(key, 0) >= val:
                return None
            self.seen[eng][key] = val
            return ("d", q, slot, val)
        else:
            key = tok
            if self.seen[eng].get(key, 0) >= 1:
                return None
            self.seen[eng][key] = 1
            return tok

    def _deps(self, eng, reads, writes):
        toks = []
        for b in reads:
            if b.w is not None:
                toks.append(b.w)
        for b in writes:
            if b.w is not None:
                toks.append(b.w)
            toks.extend(b.r.values())
        waits = []
        for t in toks:
            w = self._need_wait(eng, t)
            if w is not None:
                waits.append(w)
        return waits

    def _mark(self, tok, key, reads, writes):
        for b in reads:
            b.r[key] = tok
        for b in writes:
            b.w = tok
            b.r = {}

    def op(self, eng, fn, reads=(), writes=(), inc=True):
        if eng != "pe":
            inc = True
        waits = self._deps(eng, reads, writes)
        k = self.cnt[eng] + 1
        tok = ("e", eng, k)
        if inc:
            self.cnt[eng] = k
        self.streams[eng].append((waits, _freeze(fn), k if inc else None, None))
        self._mark(tok, eng, reads, writes)
        return tok

    def dma(self, eng, out, in_, reads=(), writes=()):
        waits = self._deps(eng, reads, writes)
        i = self.ndma.get(eng, 0)
        self.ndma[eng] = i + 1
        if i >= NDS:
            w = self._need_wait(eng, ("d", eng, i - NDS))
            if w is not None:
                waits.append(w)
        tok = ("d", eng, i)
        self.streams[eng].append((waits, lambda e: e.dma_start(out=out, in_=in_), None, i))
        self._mark(tok, tok, reads, writes)
        live = self.dma_tokens_live.setdefault(eng, [])
        live.append(tok)
        if len(live) > NDS:
            del live[:-NDS]
        return tok

    def barrier(self, extra=()):
        toks = [("e", e, self.cnt[e]) for e in self.ENGS if self.cnt[e] > 0]
        for live in self.dma_tokens_live.values():
            toks += list(live)
        toks += list(extra)
        for eng in self.ENGS:
            waits = []
            for t in toks:
                if t[0] == "e" and t[1] == eng and eng == "pe":
                    continue
                w = self._need_wait(eng, t)
                if w is not None:
                    waits.append(w)
            if waits:
                self.streams[eng].append((waits, None, None, None))

    def emit(self):
        nc = self.nc
        nsem = {e: (self.cnt[e] + CAP - 1) // CAP + 1 for e in self.ENGS}
        sems = {e: [nc.alloc_semaphore(f"s_{e}_{j}") for j in range(nsem[e])] for e in self.ENGS}
        dsems = {q: [nc.alloc_semaphore(f"s_dma_{q}_{j}") for j in range(NDS)] for q in self.ndma}
        xsems = self.extra_sems
        handles = {"pe": "tensor", "act": "scalar", "dve": "vector", "pool": "gpsimd", "sp": "sync"}

        def run(ename, eng):
            for waits, fn, k, di in self.streams[ename]:
                for w in waits:
                    if w[0] == "e":
                        _, te, kk = w
                        eng.wait_ge(sems[te][(kk - 1) // CAP], (kk - 1) % CAP + 1)
                    elif w[0] == "d":
                        eng.wait_ge(dsems[w[1]][w[2]], w[3])
                    else:
                        eng.wait_ge(xsems[w[1]], 1)
                if fn is None:
                    continue
                ins = fn(eng)
                if k is not None:
                    ins.then_inc(sems[ename][(k - 1) // CAP], 1)
                if di is not None:
                    ins.then_inc(dsems[ename][di % NDS], 16)

        with nc.Block() as block:
            @block.tensor
            def _(e):
                run("pe", e)

            @block.scalar
            def _(e):
                run("act", e)

            @block.vector
            def _(e):
                run("dve", e)

            @block.gpsimd
            def _(e):
                run("pool", e)

            @block.sync
            def _(e):
                run("sp", e)


class Arena:
    def __init__(self, nc, nbytes):
        self.t = nc.alloc_sbuf_tensor("arena", [128, nbytes // 2], BF16)
        self.nbytes = nbytes
        self.off = 0
        self.marks = []

    def push(self):
        self.marks.append(self.off)

    def pop(self):
        self.off = self.marks.pop()

    def alloc(self, shape, dtype):
        es = 4 if dtype == F32 else 2
        n = int(np.prod(shape[1:]))
        nb = n * es
        nb = (nb + 63) // 64 * 64
        assert self.off + nb <= self.nbytes, f"arena overflow {self.off + nb} > {self.nbytes}"
        o = self.off // 2
        ap = self.t[:, o:o + nb // 2]
        self.off += nb
        if dtype == F32:
            ap = ap.bitcast(F32)
        ap = ap[:, 0:n]
        if len(shape) == 3:
            ap = ap.rearrange("p (a b) -> p a b", a=shape[1])
        return ap


def build(TG, layers, dbg=None):
    T = TG * G
    NKB = T // 128
    L = len(layers)
    nc = bass.Bass("TRN2", target_bir_lowering=False)
    P = Prog(nc)

    def din(name, shape, dt=F32):
        return nc.dram_tensor(name, shape, dt, kind="ExternalInput").ap()

    def dscr(name, shape, dt, out=False):
        kind = "ExternalOutput" if (out or (dbg and name in dbg)) else "Internal"
        return nc.dram_tensor(name, shape, dt, kind=kind)

    xT = din("xT", [D, T])
    pT = din("pT", [L, 256, T])
    w_gu = [din("w_gu1", [L, D, 2 * DFF]), din("w_gu2", [L, D, 2 * DFF])]
    w_dn = [din("w_d1", [L, DFF, D]), din("w_d2", [L, DFF, D])]
    w_in = din("w_in", [L, D, INW])
    w_a = din("w_a", [L, 1024, D])
    w_b = din("w_b", [L, 1024, D])
    w_o = din("w_o", [L, D, D])
    w_pg = din("w_pg", [L, D, D])
    w_pp = din("w_pp", [L, 256, D])
    gains_d = din("gains", [L, 128, NGC])
    cst_d = din("cst", [128, NCC])
    rope_d = din("rope", [128, 2, T])
    farb_d = din("farbias", [128, 1])

    outT = nc.dram_tensor("outT", [D, T], F32, kind="ExternalOutput").ap()
    hS = dscr("hS", [D, T], F32).ap()
    qA = dscr("qA", [1024, T], BF16).ap()
    qB = dscr("qB", [1024, T], BF16).ap()
    gts = dscr("gts", [4096, T], BF16).ap()
    kloc = dscr("kloc", [2048, T], BF16)
    vloc = dscr("vloc", [T, 2048], BF16)
    kall = dscr("kall", [4096, T], BF16)
    vall = dscr("vall", [2 * T, 2048], BF16)
    yAd = dscr("yAd", [1024, T], BF16).ap()
    yBd = dscr("yBd", [1024, T], BF16).ap()

    ar = Arena(nc, 188 * 1024)
    ps = [nc.alloc_psum_tensor(f"ps{i}", [128, 512], F32).ap() for i in range(8)]
    psb = [Buf(f"ps{i}") for i in range(8)]

    cst = ar.alloc([128, 384], BF16)
    cstb = Buf("cst")
    gains = ar.alloc([128, L * NGC], F32)
    gainsb = Buf("gains")
    farb = ar.alloc([128, 1], F32)
    lam = ar.alloc([128, 2 * L], F32)
    lamb = Buf("lam")
    P.dma("pool", cst, cst_d[:, 0:384], writes=[cstb])
    for l in range(L):
        P.dma("sp", gains[:, l * NGC:(l + 1) * NGC], gains_d[l], writes=[gainsb])
    P.dma("sp", farb, farb_d, writes=[gainsb])
    ones = cst[:, CC_ONES:CC_ONES + 128]
    ntri = cst[:, CC_NTRI:CC_NTRI + 128]
    rotm = cst[:, CC_ROT:CC_ROT + 128]

    def gcol(l, c, n=1):
        return gains[:, l * NGC + c:l * NGC + c + n]

    ar.push()
    ltmp = ar.alloc([128, 8], F32)
    ltb = Buf("ltmp")
    ones32 = ar.alloc([128, 128], F32)
    o32b = Buf("ones32")
    P.op("dve", lambda e: e.memset(ones32, 1.0), writes=[o32b])
    for l in range(L):
        P.op("dve", lambda e, l=l: e.tensor_tensor(out=ltmp[:, 0:1], in0=gcol(l, GC_L), in1=gcol(l, GC_L + 1), op=ALU.mult),
             reads=[gainsb], writes=[ltb])
        P.op("dve", lambda e, l=l: e.tensor_tensor(out=ltmp[:, 1:2], in0=gcol(l, GC_L + 2), in1=gcol(l, GC_L + 3), op=ALU.mult),
             reads=[gainsb], writes=[ltb])
        P.op("pe", lambda e: e.matmul(ps[7][:, 0:2], ones32, ltmp[:, 0:2], start=True, stop=True),
             reads=[ltb, o32b], writes=[psb[7]])
        P.op("act", lambda e: e.activation(out=ltmp[:, 2:4], in_=ps[7][:, 0:2], func=AF.Exp), reads=[psb[7]], writes=[ltb])
        P.op("dve", lambda e: e.tensor_tensor(out=ltmp[:, 4:5], in0=ltmp[:, 2:3], in1=ltmp[:, 3:4], op=ALU.subtract),
             reads=[ltb], writes=[ltb])
        P.op("dve", lambda e, l=l: e.tensor_tensor(out=lam[:, 2 * l:2 * l + 1], in0=ltmp[:, 4:5], in1=gcol(l, GC_LI), op=ALU.add),
             reads=[ltb, gainsb], writes=[lamb])
        P.op("dve", lambda e, l=l: e.tensor_scalar(out=lam[:, 2 * l + 1:2 * l + 2], in0=lam[:, 2 * l:2 * l + 1], scalar1=-1.0, scalar2=None, op0=ALU.mult),
             reads=[lamb], writes=[lamb])
    P.barrier()
    ar.pop()
    stop_after = dbg.get("stop") if dbg else None

    dmaq = ["sp", "act"]
    rr = [0]

    def hwq():
        return "sp"

    def stq():
        return "act"

    def rms_bc(src, nch, bank, out_rstd, rstdb, srcb, Dn, tmps, tmpbs):
        nt = len(tmps)
        for c in range(nch):
            sq, sqb = tmps[c % nt], tmpbs[c % nt]
            if c % 2 == 0:
                P.op("dve", lambda e, c=c, sq=sq: e.tensor_tensor(out=sq, in0=src[:, c, :], in1=src[:, c, :], op=ALU.mult),
                     reads=[srcb], writes=[sqb])
            else:
                P.op("act", lambda e, c=c, sq=sq: e.activation(out=sq, in_=src[:, c, :], func=AF.Square),
                     reads=[srcb], writes=[sqb])
            P.op("pe", lambda e, c=c, sq=sq: e.matmul(ps[bank], ones, sq, start=(c == 0), stop=(c == nch - 1)),
                 reads=[sqb, cstb], writes=[psb[bank]], inc=True)
        P.op("act", lambda e: e.activation(out=out_rstd, in_=ps[bank], func=AF.Ln, bias=EPS, scale=1.0 / Dn),
             reads=[psb[bank]], writes=[rstdb])
        P.op("act", lambda e: e.activation(out=out_rstd, in_=out_rstd, func=AF.Exp, scale=-0.5),
             reads=[rstdb], writes=[rstdb])

    class WStream:
        def __init__(self, nslots, nelem=8192):
            self.slots = [ar.alloc([128, nelem], BF16) for _ in range(nslots)]
            self.bufs = [Buf(f"w{i}") for i in range(nslots)]
            self.i = 0

        def load(self, src_ap, kc, ncol):
            s = self.i % len(self.slots)
            self.i += 1
            dst = self.slots[s][:, 0:kc * ncol].rearrange("p (c n) -> p c n", c=kc)
            P.dma("pool", dst, src_ap, writes=[self.bufs[s]])
            return dst, self.bufs[s]

    def wsrc(w2d, c0, ncol):
        return w2d[:, c0:c0 + ncol].rearrange("(c p) n -> p c n", p=128)

    def ffn(l, which, hT, hTb, xn, xnb, hid, hidb, ws, rstd, rstdb, tmps, tmpbs, sg, sgb):
        gc = GC_FFN1 if which == 0 else GC_FFN2
        rms_bc(hT, DC, 6, rstd, rstdb, hTb, float(D), tmps, tmpbs)
        for c in range(DC):
            eng = "dve"
            P.op(eng, lambda e, c=c: e.scalar_tensor_tensor(out=xn[:, c, :], in0=hT[:, c, :], scalar=gcol(l, gc + c),
                                                            in1=rstd, op0=ALU.mult, op1=ALU.mult),
                 reads=[hTb, rstdb, gainsb], writes=[xnb])
        wgu = w_gu[which][l]
        wdn = w_dn[which][l]
        for fb in range(FC // 4):
            wg, wgb = ws.load(wsrc(wgu, fb * 512, 512), DC, 512)
            wu, wub = ws.load(wsrc(wgu, DFF + fb * 512, 512), DC, 512)
            for j in range(4):
                fc = fb * 4 + j
                pg, pu = (0, 1) if fc % 2 == 0 else (2, 3)
                for kc in range(DC):
                    P.op("pe", lambda e, kc=kc, j=j, wg=wg, pg=pg: e.matmul(ps[pg], wg[:, kc, j * 128:(j + 1) * 128], xn[:, kc, :],
                                                                            start=(kc == 0), stop=(kc == DC - 1)),
                         reads=[wgb, xnb], writes=[psb[pg]], inc=(kc == DC - 1))
                for kc in range(DC):
                    P.op("pe", lambda e, kc=kc, j=j, wu=wu, pu=pu: e.matmul(ps[pu], wu[:, kc, j * 128:(j + 1) * 128], xn[:, kc, :],
                                                                            start=(kc == 0), stop=(kc == DC - 1)),
                         reads=[wub, xnb], writes=[psb[pu]], inc=(kc == DC - 1))
                s = fc % 2
                P.op("act", lambda e, pg=pg, s=s: e.activation(out=sg[s], in_=ps[pg], func=AF.Silu), reads=[psb[pg]], writes=[sgb[s]])
                P.op("dve", lambda e, pu=pu, s=s, fc=fc: e.tensor_tensor(out=hid[:, fc, :], in0=sg[s], in1=ps[pu], op=ALU.mult),
                     reads=[sgb[s], psb[pu]], writes=[hidb])
        dbanks = [4, 5, 0, 1]
        for db in range(8):
            halves = []
            for kh in range(2):
                halves.append(ws.load(wdn[kh * 2816:(kh + 1) * 2816, db * 256:(db + 1) * 256].rearrange("(c p) n -> p c n", p=128), 22, 256))
            for kh in range(2):
                wd, wdb = halves[kh]
                for j in range(2):
                    pd = dbanks[(db % 2) * 2 + j]
                    for fc in range(22):
                        last = (kh == 1 and fc == 21)
                        P.op("pe", lambda e, fc=fc, kh=kh, j=j, wd=wd, pd=pd, last=last: e.matmul(
                            ps[pd], wd[:, fc, j * 128:(j + 1) * 128], hid[:, kh * 22 + fc, :], start=(kh == 0 and fc == 0), stop=last),
                             reads=[wdb, hidb], writes=[psb[pd]], inc=last)
            for j in range(2):
                dc = db * 2 + j
                pd = dbanks[(db % 2) * 2 + j]
                P.op("dve", lambda e, dc=dc, pd=pd: e.scalar_tensor_tensor(out=hT[:, dc, :], in0=ps[pd], scalar=0.5, in1=hT[:, dc, :],
                                                                           op0=ALU.mult, op1=ALU.add),
                     reads=[psb[pd], hTb], writes=[hTb])

    def phase_A(li, l, src_h):
        ar.push()
        hT = ar.alloc([128, DC, G], F32); hTb = Buf("hT")
        xn = ar.alloc([128, DC, G], BF16); xnb = Buf("xn")
        hid = ar.alloc([128, FC, G], BF16); hidb = Buf("hid")
        ws = WStream(4)
        rstd = ar.alloc([128, G], F32); rstdb = Buf("rstd")
        rs2 = [ar.alloc([128, G], F32) for _ in range(2)]; rs2b = [Buf("rs20"), Buf("rs21")]
        tmps = [ar.alloc([128, G], BF16) for _ in range(4)]; tmpbs = [Buf(f"tmp{i}") for i in range(4)]
        sg = [ar.alloc([128, G], F32) for _ in range(2)]; sgb = [Buf("sg0"), Buf("sg1")]
        stg = [ar.alloc([128, G], BF16) for _ in range(4)]; stgb = [Buf(f"stg{i}") for i in range(4)]
        xg = [ar.alloc([128, G], BF16) for _ in range(2)]; xgb = [Buf("xg0"), Buf("xg1")]
        x32 = [ar.alloc([128, G], F32) for _ in range(2)]; x32b = [Buf("x320"), Buf("x321")]
        rp = ar.alloc([128, 2, G], F32); rpb = Buf("rope")
        sti = [0]

        def stage():
            i = sti[0] % 4
            sti[0] += 1
            return stg[i], stgb[i]

        for g in range(TG):
            t0 = g * G
            for q4 in range(4):
                P.dma(hwq(), hT[:, q4 * 4:(q4 + 1) * 4, :],
                      src_h[q4 * 512:(q4 + 1) * 512, t0:t0 + G].rearrange("(c p) t -> p c t", p=128), writes=[hTb])
            P.dma(hwq(), rp, rope_d[:, :, t0:t0 + G], writes=[rpb])
            if stop_after != "Aload":
                ffn(li, 0, hT, hTb, xn, xnb, hid, hidb, ws, rstd, rstdb, tmps, tmpbs, sg, sgb)
            for q4 in range(4):
                P.dma(stq(), hS[q4 * 512:(q4 + 1) * 512, t0:t0 + G].rearrange("(c p) t -> p c t", p=128),
                      hT[:, q4 * 4:(q4 + 1) * 4, :], reads=[hTb])
            if stop_after in ("Aload", "Affn"):
                continue
            rms_bc(hT, DC, 6, rstd, rstdb, hTb, float(D), tmps, tmpbs)
            for c in range(DC):
                eng = "dve"
                P.op(eng, lambda e, c=c: e.scalar_tensor_tensor(out=xn[:, c, :], in0=hT[:, c, :], scalar=gcol(li, GC_MIX + c),
                                                                in1=rstd, op0=ALU.mult, op1=ALU.mult),
                     reads=[hTb, rstdb, gainsb], writes=[xnb])
            win = w_in[li]
            pbank = [0]

            def nextbank():
                b = pbank[0] % 4
                pbank[0] += 1
                return b

            for cb in range(20):
                wb_, wbb = ws.load(wsrc(win, cb * 512, 512), DC, 512)
                kind = ["sbq", "sbk", "sbv", "dfq", "dfk", "dfv", "ga", "ga", "gb", "gb"][cb // 2]
                if kind in ("sbv", "dfv"):
                    for tt in range(4):
                        b = nextbank()
                        for kc in range(DC):
                            P.op("pe", lambda e, kc=kc, tt=tt, b=b, wb_=wb_: e.matmul(ps[b], xn[:, kc, tt * 128:(tt + 1) * 128], wb_[:, kc, :],
                                                                                      start=(kc == 0), stop=(kc == DC - 1)),
                                 reads=[wbb, xnb], writes=[psb[b]], inc=(kc == DC - 1))
                        st, stb = stage()
                        P.op("act", lambda e, b=b, st=st: e.activation(out=st, in_=ps[b], func=AF.Copy), reads=[psb[b]], writes=[stb])
                        col0 = (0 if kind == "sbv" else 1024) + (cb % 2) * 512
                        P.dma(stq(), vloc[t0 + tt * 128:t0 + (tt + 1) * 128, col0:col0 + 512], st, reads=[stb])
                    continue
                for j in range(4):
                    b = nextbank()
                    oc = (cb % 2) * 4 + j
                    for kc in range(DC):
                        P.op("pe", lambda e, kc=kc, j=j, b=b, wb_=wb_: e.matmul(ps[b], wb_[:, kc, j * 128:(j + 1) * 128], xn[:, kc, :],
                                                                                start=(kc == 0), stop=(kc == DC - 1)),
                             reads=[wbb, xnb], writes=[psb[b]], inc=(kc == DC - 1))
                    st, stb = stage()
                    if kind == "sbq":
                        P.op("act", lambda e, b=b, st=st: e.activation(out=st, in_=ps[b], func=AF.Copy, scale=SCALE), reads=[psb[b]], writes=[stb])
                        P.dma(stq(), qA[oc * 128:(oc + 1) * 128, t0:t0 + G], st, reads=[stb])
                    elif kind == "sbk":
                        P.op("act", lambda e, b=b, st=st: e.activation(out=st, in_=ps[b], func=AF.Copy), reads=[psb[b]], writes=[stb])
                        P.dma(stq(), kloc[oc * 128:(oc + 1) * 128, t0:t0 + G], st, reads=[stb])
                    elif kind in ("ga", "gb"):
                        gch = ((cb - 12) * 4 + j)
                        P.op("act", lambda e, b=b, st=st: e.activation(out=st, in_=ps[b], func=AF.Sigmoid), reads=[psb[b]], writes=[stb])
                        P.dma(stq(), gts[gch * 128:(gch + 1) * 128, t0:t0 + G], st, reads=[stb])
                    else:
                        isq = kind == "dfq"
                        gcn = GC_DQ if isq else GC_DK
                        s = oc % 2
                        bss, brt = 4 + s, 6 + s
                        P.op("act", lambda e, b=b, s=s: e.activation(out=tmps[s], in_=ps[b], func=AF.Square), reads=[psb[b]], writes=[tmpbs[s]])
                        P.op("act", lambda e, b=b, s=s, gcn=gcn: e.activation(out=x32[s], in_=ps[b], func=AF.Copy, scale=gcol(li, gcn)),
                             reads=[psb[b], gainsb], writes=[x32b[s]])
                        P.op("pe", lambda e, s=s, bss=bss: e.matmul(ps[bss], ones, tmps[s], start=True, stop=True), reads=[tmpbs[s], cstb], writes=[psb[bss]])
                        P.op("dve", lambda e, s=s: e.tensor_copy(out=xg[s], in_=x32[s]), reads=[x32b[s]], writes=[xgb[s]])
                        P.op("pe", lambda e, s=s, brt=brt: e.matmul(ps[brt], rotm, xg[s], start=True, stop=True), reads=[xgb[s], cstb], writes=[psb[brt]])
                        P.op("act", lambda e, s=s, bss=bss: e.activation(out=rs2[s], in_=ps[bss], func=AF.Ln, bias=EPS, scale=1.0 / 128),
                             reads=[psb[bss]], writes=[rs2b[s]])
                        P.op("act", lambda e, s=s, isq=isq: e.activation(out=rs2[s], in_=rs2[s], func=AF.Exp, scale=-0.5, bias=(math.log(SCALE) if isq else 0.0)),
                             reads=[rs2b[s]], writes=[rs2b[s]])
                        P.op("dve", lambda e, s=s: e.tensor_tensor(out=x32[s], in0=x32[s], in1=rp[:, 0, :], op=ALU.mult),
                             reads=[x32b[s], rpb], writes=[x32b[s]])
                        P.op("dve", lambda e, s=s, brt=brt: e.tensor_tensor(out=sg[s], in0=ps[brt], in1=rp[:, 1, :], op=ALU.mult),
                             reads=[psb[brt], rpb], writes=[sgb[s]])
                        P.op("dve", lambda e, s=s: e.tensor_tensor(out=x32[s], in0=x32[s], in1=sg[s], op=ALU.add),
                             reads=[x32b[s], sgb[s]], writes=[x32b[s]])
                        P.op("dve", lambda e, s=s, st=st: e.tensor_tensor(out=st, in0=x32[s], in1=rs2[s], op=ALU.mult),
                             reads=[x32b[s], rs2b[s]], writes=[stb])
                        if isq:
                            P.dma(stq(), qB[oc * 128:(oc + 1) * 128, t0:t0 + G], st, reads=[stb])
                        else:
                            P.dma(stq(), kloc[1024 + oc * 128:1024 + (oc + 1) * 128, t0:t0 + G], st, reads=[stb])
        P.barrier()
        ar.pop()

    RK = min(2048, (1 << 20) // T)
    NKC = 2048 // RK
    RV = min(T, 512)
    NVC = T // RV
    BV = RV // 128

    def kfar(r0, n):
        j, w = r0 // RK, r0 % RK
        return kall[j * 2 * RK + w:j * 2 * RK + w + n, :]

    def exchange():
        groups = [[0, 1], [2, 3], [4, 5], [6, 7]]
        toks = []
        jobs = [(kloc, kall, RK, j) for j in range(NKC)] + [(vloc, vall, RV, j) for j in range(NVC)]
        for (src, dst, R_, j) in jobs:
            sem = nc.alloc_semaphore(f"cc{len(P.extra_sems)}")
            idx = len(P.extra_sems)
            P.extra_sems.append(sem)
            P.streams["pool"].append(([], (lambda e, src=src, dst=dst, sem=sem, R_=R_, j=j: e.collective_compute(
                "AllGather", ALU.bypass, replica_groups=groups, ins=[src[j * R_:(j + 1) * R_, :]],
                outs=[dst[j * 2 * R_:(j + 1) * 2 * R_, :]]).then_inc(sem, 1)), None, None))
            toks.append(("x", idx))
        P.barrier(extra=toks)

    def phase_B(li, l, last):
        ar.push()
        ar.push()
        msk = ar.alloc([128, 4096], BF16); mskb = Buf("msk")
        P.dma("pool", msk, cst_d[:, 384:384 + 4096], writes=[mskb])
        yst = [ar.alloc([128, G], BF16) for _ in range(4)]; ystb = [Buf(f"yst{i}") for i in range(4)]
        ysti = [0]
        kT = [ar.alloc([128, 4, T], BF16) for _ in range(2)]; kTb = [Buf("kT0"), Buf("kT1")]
        vv = [ar.alloc([128, 2 * NKB, 256], BF16) for _ in range(2)]; vvb = [Buf("v0"), Buf("v1")]
        qt = [ar.alloc([128, G], BF16) for _ in range(4)]; qtb = [Buf(f"qt{i}") for i in range(4)]
        e32 = [ar.alloc([128, G], F32) for _ in range(4)]; e32b = [Buf(f"e{i}") for i in range(4)]
        sp = [ar.alloc([128, G], BF16) for _ in range(8)]; spb = [Buf(f"sp{i}") for i in range(8)]
        at = [ar.alloc([128, G], BF16) for _ in range(8)]; atb = [Buf(f"at{i}") for i in range(8)]
        R32 = [ar.alloc([128, G], F32) for _ in range(4)]; R32b = [Buf(f"R32{i}") for i in range(4)]
        Rbf = [ar.alloc([128, G], BF16) for _ in range(8)]; Rbfb = [Buf(f"Rbf{i}") for i in range(8)]
        n1 = ar.alloc([128, 2, G], F32); n1b = Buf("n1")
        rc = ar.alloc([128, G], F32); rcb = Buf("rc")
        y32 = ar.alloc([128, 2, G], F32); y32b = Buf("y32")
        tq = ar.alloc([128, G], BF16); tqb = Buf("tq")
        msb = [msk[:, k * 512:(k + 1) * 512] for k in range(4)]
        mdf = [msk[:, 2048 + k * 512:2048 + (k + 1) * 512] for k in range(4)]
        cnt = {"q": 0, "e": 0, "sp": 0, "at": 0, "z": 0, "zs": 0, "p": 0, "y": 0, "R": 0, "hd": 0}

        def rot(name, n):
            i = cnt.get(name, 0) % n
            cnt[name] = cnt.get(name, 0) + 1
            return i

        for h in range(8):
            hb = rot("hd", 2)
            k_, k_b = kT[hb], kTb[hb]
            v_, v_b = vv[hb], vvb[hb]
            P.dma(hwq(), k_[:, 0, :], kloc[h * 128:(h + 1) * 128, :], writes=[k_b])
            P.dma(hwq(), k_[:, 1, :], kfar(h * 128, 128), writes=[k_b])
            P.dma(hwq(), v_[:, 0:NKB, 0:128], vloc[:, h * 128:(h + 1) * 128].rearrange("(b p) c -> p b c", p=128), writes=[v_b])
            for j in range(NVC):
                P.dma(hwq(), v_[:, NKB + BV * j:NKB + BV * (j + 1), 0:128],
                      vall[j * 2 * RV:j * 2 * RV + RV, h * 128:(h + 1) * 128].rearrange("(b p) c -> p b c", p=128), writes=[v_b])
            NSTR = min(TG, 4)
            for g0 in range(0, TG, NSTR):
                streams = []
                for sidx in range(NSTR):
                    g = g0 + sidx
                    qi = rot("q", 4)
                    P.dma(hwq(), qt[qi], qA[h * 128:(h + 1) * 128, g * G:(g + 1) * G], writes=[qtb[qi]])
                    blocks = [(0, kb) for kb in range(4 * g + 3, -1, -1)] + [(1, kb) for kb in range(NKB - 1, -1, -1)]
                    streams.append(dict(g=g, qi=qi, yb=4 + sidx, blocks=blocks, nblk=len(blocks), st={}, sx=sidx))

                def stA(S, bi):
                    sx, g, qi = S["sx"], S["g"], S["qi"]
                    far, kb = S["blocks"][bi]
                    kblk = k_[:, far, kb * 128:(kb + 1) * 128]
                    dk = (kb - 4 * g) if (far == 0 and kb >= 4 * g) else None
                    zb = sx
                    P.op("pe", lambda e: e.matmul(ps[zb], kblk, qt[qi], start=True, stop=True),
                         reads=[k_b, qtb[qi]], writes=[psb[zb]])
                    ei = sx
                    P.op("act", lambda e: e.activation(out=e32[ei], in_=ps[zb], func=AF.Exp), reads=[psb[zb]], writes=[e32b[ei]])
                    si = 2 * sx + rot(f"sp{sx}", 2)
                    P.op("act", lambda e: e.activation(out=sp[si], in_=e32[ei], func=AF.Ln, bias=1.0, scale=1.0),
                         reads=[e32b[ei]], writes=[spb[si]])
                    if dk is not None:
                        P.op("dve", lambda e: e.tensor_tensor(out=sp[si], in0=sp[si], in1=msb[dk], op=ALU.mult),
                             reads=[spb[si], mskb], writes=[spb[si]])
                    Ri = None
                    if bi < S["nblk"] - 1:
                        R32s, R32sb = R32[sx], R32b[sx]
                        if bi == 0:
                            P.op("dve", lambda e: e.tensor_scalar(out=R32s, in0=sp[si], scalar1=-1.0, scalar2=None, op0=ALU.mult),
                                 reads=[spb[si]], writes=[R32sb])
                        else:
                            P.op("dve", lambda e: e.tensor_tensor(out=R32s, in0=R32s, in1=sp[si], op=ALU.subtract),
                                 reads=[spb[si], R32sb], writes=[R32sb])
                        Ri = 2 * sx + rot(f"R{sx}", 2)
                        P.op("dve", lambda e: e.tensor_copy(out=Rbf[Ri], in_=R32s), reads=[R32sb], writes=[Rbfb[Ri]])
                    S["st"][bi] = dict(kblk=kblk, dk=dk, si=si, Ri=Ri, far=far, kb=kb, zb=zb)

                def stB(S, bi):
                    sx = S["sx"]
                    d = S["st"][bi]
                    dk, si, far = d["dk"], d["si"], d["far"]
                    pb = d["zb"]
                    P.op("pe", lambda e: e.matmul(ps[pb], ntri, sp[si], start=False, stop=(bi == 0)),
                         reads=[spb[si], cstb], writes=[psb[pb]], inc=(bi == 0))
                    if bi > 0:
                        Rp = S["st"][bi - 1]["Ri"]
                        P.op("pe", lambda e: e.matmul(ps[pb], ones, Rbf[Rp], start=False, stop=True),
                             reads=[Rbfb[Rp], cstb], writes=[psb[pb]])
                    ai = 2 * sx + rot(f"at{sx}", 2)
                    if far:
                        P.op("act", lambda e: e.activation(out=at[ai], in_=ps[pb], func=AF.Exp, bias=farb[:, 0:1], scale=1.0),
                             reads=[psb[pb], gainsb], writes=[atb[ai]])
                    else:
                        P.op("act", lambda e: e.activation(out=at[ai], in_=ps[pb], func=AF.Exp), reads=[psb[pb]], writes=[atb[ai]])
                    if dk is not None:
                        P.op("dve", lambda e: e.tensor_tensor(out=at[ai], in0=at[ai], in1=msb[dk], op=ALU.mult),
                             reads=[atb[ai], mskb], writes=[atb[ai]])
                    d["ai"] = ai

                def stC(S, bi):
                    d = S["st"][bi]
                    ai, yb, nblk = d["ai"], S["yb"], S["nblk"]
                    vblk = v_[:, d["far"] * NKB + d["kb"], 0:128]
                    P.op("pe", lambda e: e.matmul(ps[yb], vblk, at[ai], start=(bi == 0), stop=(bi == nblk - 1)),
                         reads=[v_b, atb[ai]], writes=[psb[yb]], inc=(bi == nblk - 1))

                sB, sC = SKEW_SB
                nmax = max(S["nblk"] for S in streams)
                for it in range(nmax + sC):
                    for S in streams:
                        if it < S["nblk"]:
                            stA(S, it)
                    for S in streams:
                        if sB <= it < S["nblk"] + sB:
                            stB(S, it - sB)
                    for S in streams:
                        if sC <= it < S["nblk"] + sC:
                            stC(S, it - sC)
                for S in streams:
                    yi = ysti[0] % 4
                    ysti[0] += 1
                    yb, g = S["yb"], S["g"]
                    P.op("act", lambda e, yb=yb, yi=yi: e.activation(out=yst[yi], in_=ps[yb], func=AF.Copy), reads=[psb[yb]], writes=[ystb[yi]])
                    P.dma(stq(), yAd[h * 128:(h + 1) * 128, g * G:(g + 1) * G], yst[yi], reads=[ystb[yi]])

        for h in range(4):
            hb = rot("hd", 2)
            k_, k_b = kT[hb], kTb[hb]
            v_, v_b = vv[hb], vvb[hb]
            for half in range(2):
                r0 = 1024 + (2 * h + half) * 128
                P.dma(hwq(), k_[:, half, :], kloc[r0:r0 + 128, :], writes=[k_b])
                P.dma(hwq(), k_[:, 2 + half, :], kfar(r0, 128), writes=[k_b])
            c0 = 1024 + h * 256
            P.dma(hwq(), v_[:, 0:NKB, :], vloc[:, c0:c0 + 256].rearrange("(b p) c -> p b c", p=128), writes=[v_b])
            for j in range(NVC):
                P.dma(hwq(), v_[:, NKB + BV * j:NKB + BV * (j + 1), :],
                      vall[j * 2 * RV:j * 2 * RV + RV, c0:c0 + 256].rearrange("(b p) c -> p b c", p=128), writes=[v_b])
            for g in range(TG):
                t0 = g * G
                blocks = [(0, kb) for kb in range(4 * g + 3, -1, -1)] + [(1, kb) for kb in range(NKB - 1, -1, -1)]
                nblk = len(blocks)
                dstreams = []
                for half in range(2):
                    qi = rot("q", 4)
                    r0 = (2 * h + half) * 128
                    P.dma(hwq(), qt[qi], qB[r0:r0 + 128, t0:t0 + G], writes=[qtb[qi]])
                    ya, yb2, db = (2, 3, 6) if half == 0 else (4, 5, 7)
                    dstreams.append(dict(half=half, qi=qi, ya=ya, yb2=yb2, db=db, st={}))

                def dA(S, bi):
                    half, qi = S["half"], S["qi"]
                    far, kb = blocks[bi]
                    kblk = k_[:, 2 * far + half, kb * 128:(kb + 1) * 128]
                    dk = (kb - 4 * g) if (far == 0 and kb >= 4 * g) else None
                    zb = half
                    P.op("pe", lambda e: e.matmul(ps[zb], kblk, qt[qi], start=True, stop=True),
                         reads=[k_b, qtb[qi]], writes=[psb[zb]])
                    ai = 3 * half + rot(f"dat{half}", 3)
                    if far:
                        P.op("act", lambda e: e.activation(out=at[ai], in_=ps[zb], func=AF.Exp, bias=farb[:, 0:1], scale=1.0),
                             reads=[psb[zb], gainsb], writes=[atb[ai]])
                    else:
                        P.op("act", lambda e: e.activation(out=at[ai], in_=ps[zb], func=AF.Exp), reads=[psb[zb]], writes=[atb[ai]])
                    if dk is not None:
                        P.op("dve", lambda e: e.tensor_tensor(out=at[ai], in0=at[ai], in1=mdf[dk], op=ALU.mult),
                             reads=[atb[ai], mskb], writes=[atb[ai]])
                    S["st"][bi] = (far, kb, ai)

                def dB(S, bi):
                    far, kb, ai = S["st"][bi]
                    ya, yb2, db = S["ya"], S["yb2"], S["db"]
                    st, sp_ = (bi == 0), (bi == nblk - 1)
                    P.op("pe", lambda e: e.matmul(ps[ya], v_[:, far * NKB + kb, 0:128], at[ai], start=st, stop=sp_),
                         reads=[v_b, atb[ai]], writes=[psb[ya]], inc=sp_)
                    P.op("pe", lambda e: e.matmul(ps[yb2], v_[:, far * NKB + kb, 128:256], at[ai], start=st, stop=sp_),
                         reads=[v_b, atb[ai]], writes=[psb[yb2]], inc=sp_)
                    P.op("pe", lambda e: e.matmul(ps[db], ones, at[ai], start=st, stop=sp_),
                         reads=[cstb, atb[ai]], writes=[psb[db]], inc=sp_)

                for it in range(nblk + SKEW_DF):
                    for S in dstreams:
                        if it < nblk:
                            dA(S, it)
                    for S in dstreams:
                        if it >= SKEW_DF:
                            dB(S, it - SKEW_DF)

                for half in range(2):
                    S = dstreams[half]
                    ya, yb2, db = S["ya"], S["yb2"], S["db"]
                    P.op("dve", lambda e, db=db: e.reciprocal(out=rc, in_=ps[db]), reads=[psb[db]], writes=[rcb])
                    if half == 0:
                        P.op("dve", lambda e, ya=ya: e.tensor_tensor(out=n1[:, 0, :], in0=ps[ya], in1=rc, op=ALU.mult), reads=[psb[ya], rcb], writes=[n1b])
                        P.op("dve", lambda e, yb2=yb2: e.tensor_tensor(out=n1[:, 1, :], in0=ps[yb2], in1=rc, op=ALU.mult), reads=[psb[yb2], rcb], writes=[n1b])
                    else:
                        for c, bk in ((0, ya), (1, yb2)):
                            P.op("dve", lambda e, c=c, bk=bk: e.tensor_tensor(out=y32[:, c, :], in0=ps[bk], in1=rc, op=ALU.mult),
                                 reads=[psb[bk], rcb], writes=[y32b])
                            P.op("dve", lambda e, c=c: e.scalar_tensor_tensor(out=y32[:, c, :], in0=y32[:, c, :], scalar=lam[:, 2 * li + 1:2 * li + 2],
                                                                             in1=n1[:, c, :], op0=ALU.mult, op1=ALU.add),
                                 reads=[y32b, n1b, lamb], writes=[y32b])
                        for c in range(2):
                            P.op("pool", lambda e, c=c: e.tensor_tensor(out=sp[c], in0=y32[:, c, :], in1=y32[:, c, :], op=ALU.mult),
                                 reads=[y32b], writes=[spb[c]])
                            P.op("pe", lambda e, c=c: e.matmul(ps[0], ones, sp[c], start=(c == 0), stop=(c == 1)),
                                 reads=[spb[c], cstb], writes=[psb[0]], inc=(c == 1))
                        P.op("act", lambda e: e.activation(out=rc, in_=ps[0], func=AF.Ln, bias=EPS, scale=1.0 / 256),
                             reads=[psb[0]], writes=[rcb])
                        P.op("act", lambda e: e.activation(out=rc, in_=rc, func=AF.Exp, scale=-0.5),
                             reads=[rcb], writes=[rcb])
                        P.op("dve", lambda e: e.tensor_scalar(out=rc, in0=rc, scalar1=gcol(li, GC_OML), scalar2=None, op0=ALU.mult),
                             reads=[rcb, gainsb], writes=[rcb])
                        for c in range(2):
                            yi = ysti[0] % 4
                            ysti[0] += 1
                            P.op("dve", lambda e, c=c, yi=yi: e.scalar_tensor_tensor(out=yst[yi], in0=y32[:, c, :],
                                                                                   scalar=gcol(li, GC_SUB + c), in1=rc, op0=ALU.mult, op1=ALU.mult),
                                 reads=[y32b, rcb, gainsb], writes=[ystb[yi]])
                            P.dma(stq(), yBd[(2 * h + c) * 128:(2 * h + c + 1) * 128, t0:t0 + G], yst[yi], reads=[ystb[yi]])
        P.barrier()
        ar.pop()

        hT = ar.alloc([128, DC, G], F32); hTb = Buf("hT")
        xn = ar.alloc([128, DC, G], BF16); xnb = Buf("xn")
        hid = ar.alloc([128, FC, G], BF16); hidb = Buf("hid")
        ws = WStream(4)
        rstd = ar.alloc([128, G], F32); rstdb = Buf("rstd")
        tmps = [ar.alloc([128, G], BF16) for _ in range(4)]; tmpbs = [Buf(f"tmp{i}") for i in range(4)]
        sg = [ar.alloc([128, G], F32) for _ in range(2)]; sgb = [Buf("sg0"), Buf("sg1")]
        gt = [ar.alloc([128, 2, G], BF16) for _ in range(4)]; gtb = [Buf(f"gt{i}") for i in range(4)]
        mg = [ar.alloc([128, G], F32) for _ in range(2)]; mgb = [Buf("mg0"), Buf("mg1")]
        ptl = ar.alloc([128, 2, G], BF16); ptlb = Buf("ptl")
        yg = hid[:, 0:16, :]; ygb = hidb
        for g in range(TG):
            t0 = g * G
            for q4 in range(4):
                P.dma(hwq(), hT[:, q4 * 4:(q4 + 1) * 4, :],
                      hS[q4 * 512:(q4 + 1) * 512, t0:t0 + G].rearrange("(c p) t -> p c t", p=128), writes=[hTb])
            P.dma("pool", ptl, pT[li][:, t0:t0 + G].rearrange("(c p) t -> p c t", p=128), writes=[ptlb])
            P.dma(hwq(), yg[:, 0:8, :], yAd[:, t0:t0 + G].rearrange("(c p) t -> p c t", p=128), writes=[ygb])
            P.dma(hwq(), yg[:, 8:16, :], yBd[:, t0:t0 + G].rearrange("(c p) t -> p c t", p=128), writes=[ygb])
            for ob in range(4):
                wa_, wab = ws.load(wsrc(w_a[li], ob * 512, 512), 8, 512)
                wb2, wbb2 = ws.load(wsrc(w_b[li], ob * 512, 512), 8, 512)
                for j in range(4):
                    oc = ob * 4 + j
                    gi = oc % 4
                    P.dma(hwq(), gt[gi], gts.rearrange("(two r) t -> r two t", two=2)[oc * 128:(oc + 1) * 128, :, t0:t0 + G], writes=[gtb[gi]])
                    pa, pb_ = (0, 1) if oc % 2 == 0 else (2, 3)
                    for kc in range(8):
                        P.op("pe", lambda e, kc=kc, j=j, pa=pa, wa_=wa_: e.matmul(ps[pa], wa_[:, kc, j * 128:(j + 1) * 128], yg[:, kc, :],
                                                                                  start=(kc == 0), stop=(kc == 7)),
                             reads=[wab, ygb], writes=[psb[pa]], inc=(kc == 7))
                    for kc in range(8):
                        P.op("pe", lambda e, kc=kc, j=j, pb_=pb_, wb2=wb2: e.matmul(ps[pb_], wb2[:, kc, j * 128:(j + 1) * 128], yg[:, 8 + kc, :],
                                                                                   start=(kc == 0), stop=(kc == 7)),
                             reads=[wbb2, ygb], writes=[psb[pb_]], inc=(kc == 7))
                    s = oc % 2
                    P.op("dve", lambda e, s=s, pa=pa, gi=gi: e.tensor_tensor(out=sg[s], in0=ps[pa], in1=gt[gi][:, 0, :], op=ALU.mult),
                         reads=[psb[pa], gtb[gi]], writes=[sgb[s]])
                    P.op("dve", lambda e, s=s, pb_=pb_, gi=gi: e.tensor_tensor(out=mg[s], in0=ps[pb_], in1=gt[gi][:, 1, :], op=ALU.mult),
                         reads=[psb[pb_], gtb[gi]], writes=[mgb[s]])
                    P.op("dve", lambda e, s=s, oc=oc: e.tensor_tensor(out=xn[:, oc, :], in0=sg[s], in1=mg[s], op=ALU.add),
                         reads=[sgb[s], mgb[s]], writes=[xnb])
            for ob in range(4):
                wo_, wob = ws.load(wsrc(w_o[li], ob * 512, 512), DC, 512)
                for j in range(4):
                    oc = ob * 4 + j
                    pd = 4 + oc % 2
                    for kc in range(DC):
                        P.op("pe", lambda e, kc=kc, j=j, pd=pd, wo_=wo_: e.matmul(ps[pd], wo_[:, kc, j * 128:(j + 1) * 128], xn[:, kc, :],
                                                                                  start=(kc == 0), stop=(kc == DC - 1)),
                             reads=[wob, xnb], writes=[psb[pd]], inc=(kc == DC - 1))
                    P.op("dve", lambda e, oc=oc, pd=pd: e.tensor_tensor(out=hT[:, oc, :], in0=ps[pd], in1=hT[:, oc, :], op=ALU.add),
                         reads=[psb[pd], hTb], writes=[hTb])
            ffn(li, 1, hT, hTb, xn, xnb, hid, hidb, ws, rstd, rstdb, tmps, tmpbs, sg, sgb)
            rms_bc(hT, DC, 6, rstd, rstdb, hTb, float(D), tmps, tmpbs)
            for c in range(DC):
                eng = "dve"
                P.op(eng, lambda e, c=c: e.scalar_tensor_tensor(out=xn[:, c, :], in0=hT[:, c, :], scalar=gcol(li, GC_PLE + c),
                                                                in1=rstd, op0=ALU.mult, op1=ALU.mult),
                     reads=[hTb, rstdb, gainsb], writes=[xnb])
            ppv = hid[:, 0:32, :].rearrange("p a b -> p (a b)").bitcast(F32).rearrange("p (a b) -> p a b", a=16)
            for ob in range(4):
                wp_, wpb = ws.load(wsrc(w_pp[li], ob * 512, 512), 2, 512)
                for j in range(4):
                    oc = ob * 4 + j
                    pd = 4 + oc % 2
                    for kc in range(2):
                        P.op("pe", lambda e, kc=kc, j=j, pd=pd, wp_=wp_: e.matmul(ps[pd], wp_[:, kc, j * 128:(j + 1) * 128], ptl[:, kc, :],
                                                                                  start=(kc == 0), stop=(kc == 1)),
                             reads=[wpb, ptlb], writes=[psb[pd]], inc=(kc == 1))
                    P.op("act", lambda e, oc=oc, pd=pd: e.activation(out=ppv[:, oc, :], in_=ps[pd], func=AF.Copy), reads=[psb[pd]], writes=[hidb])
            rms_bc(ppv, DC, 6, rstd, rstdb, hidb, float(D), tmps, tmpbs)
            for ob in range(4):
                wg_, wgb_ = ws.load(wsrc(w_pg[li], ob * 512, 512), DC, 512)
                for j in range(4):
                    oc = ob * 4 + j
                    pd = (0, 1, 2, 3)[oc % 4]
                    for kc in range(DC):
                        P.op("pe", lambda e, kc=kc, j=j, pd=pd, wg_=wg_: e.matmul(ps[pd], wg_[:, kc, j * 128:(j + 1) * 128], xn[:, kc, :],
                                                                                  start=(kc == 0), stop=(kc == DC - 1)),
                             reads=[wgb_, xnb], writes=[psb[pd]], inc=(kc == DC - 1))
                    s = oc % 2
                    P.op("act", lambda e, s=s, pd=pd: e.activation(out=sg[s], in_=ps[pd], func=AF.Sigmoid), reads=[psb[pd]], writes=[sgb[s]])
                    P.op("dve", lambda e, oc=oc: e.scalar_tensor_tensor(out=ppv[:, oc, :], in0=ppv[:, oc, :], scalar=gcol(li, GC_PLEO + oc),
                                                                        in1=rstd, op0=ALU.mult, op1=ALU.mult),
                         reads=[hidb, rstdb, gainsb], writes=[hidb])
                    P.op("dve", lambda e, s=s, oc=oc: e.tensor_tensor(out=sg[s], in0=sg[s], in1=ppv[:, oc, :], op=ALU.mult),
                         reads=[sgb[s], hidb], writes=[sgb[s]])
                    P.op("dve", lambda e, s=s, oc=oc: e.tensor_tensor(out=hT[:, oc, :], in0=hT[:, oc, :], in1=sg[s], op=ALU.add),
                         reads=[sgb[s], hTb], writes=[hTb])
            dst = outT if last else hS
            for q4 in range(4):
                P.dma(stq(), dst[q4 * 512:(q4 + 1) * 512, t0:t0 + G].rearrange("(c p) t -> p c t", p=128),
                      hT[:, q4 * 4:(q4 + 1) * 4, :], reads=[hTb])
        P.barrier()
        ar.pop()

    for li, l in enumerate(layers):
        if stop_after == "pre":
            break
        phase_A(li, l, xT if li == 0 else hS)
        if stop_after in ("A", "Aload", "Affn"):
            break
        exchange()
        phase_B(li, l, last=(li == L - 1))
    P.barrier()
    P.emit()
    return nc


def _consts():
    c = np.zeros((128, NCC), np.float32)
    c[:, CC_ONES:CC_ONES + 128] = 1.0
    j = np.arange(128)[:, None]
    s = np.arange(128)[None, :]
    c[:, CC_NTRI:CC_NTRI + 128] = -(j >= s).astype(np.float32)
    rot = np.zeros((128, 128), np.float32)
    for i in range(16):
        rot[16 + i, i] = -1.0
        rot[i, 16 + i] = 1.0
    c[:, CC_ROT:CC_ROT + 128] = rot
    tt = np.arange(512)[None, :]
    sp = np.arange(128)[:, None]
    for k in range(4):
        c[:, CC_MSB + k * 512:CC_MSB + (k + 1) * 512] = ((128 * k + sp) < tt).astype(np.float32)
        c[:, CC_MDF + k * 512:CC_MDF + (k + 1) * 512] = ((128 * k + sp) <= tt).astype(np.float32)
    return c


def _rope(pos0, T):
    pos = (pos0 + np.arange(T)).astype(np.float32)
    inv = (np.float32(500000.0) ** (-np.arange(0, 32, 2, dtype=np.float32) / np.float32(32))).astype(np.float32)
    ang = pos[:, None] * inv[None, :]
    cos = np.cos(ang).astype(np.float32).T
    sin = np.sin(ang).astype(np.float32).T
    r = np.zeros((128, 2, T), np.float32)
    r[:, 0, :] = 1.0
    r[0:16, 0, :] = cos
    r[16:32, 0, :] = cos
    r[0:16, 1, :] = sin
    r[16:32, 1, :] = sin
    return r


def _gains(inp, l, lam_layer):
    g = np.zeros((128, NGC), np.float32)

    def col(v):
        return np.asarray(v, np.float32).reshape(-1, 128).T

    g[:, GC_FFN1:GC_FFN1 + 16] = col(inp["ffn1_norm"][l])
    g[:, GC_MIX:GC_MIX + 16] = col(inp["mix_norm"][l])
    g[:, GC_FFN2:GC_FFN2 + 16] = col(inp["ffn2_norm"][l])
    g[:, GC_PLE:GC_PLE + 16] = col(inp["ple_norm"][l])
    g[:, GC_PLEO:GC_PLEO + 16] = col(inp["ple_out_norm"][l])
    g[:, GC_DQ:GC_DQ + 1] = col(inp["diff_q_norm"][l])
    g[:, GC_DK:GC_DK + 1] = col(inp["diff_k_norm"][l])
    g[:, GC_L + 0:GC_L + 1] = col(inp["diff_lambda_q1"][l])
    g[:, GC_L + 1:GC_L + 2] = col(inp["diff_lambda_k1"][l])
    g[:, GC_L + 2:GC_L + 3] = col(inp["diff_lambda_q2"][l])
    g[:, GC_L + 3:GC_L + 4] = col(inp["diff_lambda_k2"][l])
    g[:, GC_SUB:GC_SUB + 2] = col(inp["diff_sub_norm"][l])
    li = 0.8 - 0.6 * math.exp(-0.3 * lam_layer)
    g[:, GC_LI] = li
    g[:, GC_OML] = 1.0 - li
    return g


_NC_CACHE = {}


def _run(inp, TG, layers_list, dbg=None):
    T = TG * G
    S = 2 * T
    x = np.asarray(inp["x"], np.float32)
    B = x.shape[0]
    ncores = 2 * B
    cst = _consts()
    hT = [np.ascontiguousarray(x[c // 2, (c % 2) * T:(c % 2 + 1) * T, :].T) for c in range(ncores)]
    res = None
    for layers in layers_list:
        key = (TG, len(layers), str(dbg))
        if key not in _NC_CACHE:
            _NC_CACHE[key] = build(TG, layers, dbg)
        nc = _NC_CACHE[key]
        sl = slice(layers[0], layers[-1] + 1)
        shared = {
            "w_gu1": np.asarray(inp["ffn1_w_gu"][sl], np.float32), "w_gu2": np.asarray(inp["ffn2_w_gu"][sl], np.float32),
            "w_d1": np.asarray(inp["ffn1_w_down"][sl], np.float32), "w_d2": np.asarray(inp["ffn2_w_down"][sl], np.float32),
            "w_in": np.asarray(inp["w_in"][sl], np.float32), "w_a": np.asarray(inp["w_branch_a"][sl], np.float32),
            "w_b": np.asarray(inp["w_branch_b"][sl], np.float32), "w_o": np.asarray(inp["w_out"][sl], np.float32),
            "w_pg": np.asarray(inp["ple_w_gate"][sl], np.float32), "w_pp": np.asarray(inp["ple_w_proj"][sl], np.float32),
            "gains": np.stack([_gains(inp, l, l) for l in layers]), "cst": cst,
        }
        p = np.asarray(inp["p"], np.float32)
        in_maps = []
        for c in range(ncores):
            b, half = c // 2, c % 2
            m = dict(shared)
            m["xT"] = hT[c]
            m["pT"] = np.ascontiguousarray(np.stack([p[l, b, half * T:(half + 1) * T, :].T for l in layers]))
            m["rope"] = _rope(half * T, T)
            m["farbias"] = np.full((128, 1), 0.0 if half == 1 else -30000.0, np.float32)
            in_maps.append(m)
        res = run_bass_kernel_spmd(nc, in_maps, core_ids=list(range(ncores)))
        hT = [np.asarray(res.results[c]["outT"]) for c in range(ncores)]
    out = np.empty((B, S, D), np.float32)
    for c in range(ncores):
        out[c // 2, (c % 2) * T:(c % 2 + 1) * T, :] = hT[c].T
    return out, res


def kernel(**inputs):
    out, _ = _run(inputs, 4, [[0, 1]])
    return out
```
